# Optimizing a Trainium2 kernel written in Bass

```python
import math
import jax
import jax.numpy as jnp
from jax import lax
import numpy as np

D_MODEL = 1024
BATCH = 8
SEQ = 2048
DEPTH = 1

N_SUBLAYERS = 3
ADA_COLS = 3 * N_SUBLAYERS * D_MODEL
D_FF = 2816
MLA_HEADS = 8
MLA_NOPE_DIM = 64
MLA_ROPE_DIM = 32
MLA_QK_DIM = MLA_NOPE_DIM + MLA_ROPE_DIM
MLA_V_DIM = 64
MLA_Q_LORA = 384
MLA_KV_LORA = 256
MLA_ROPE_THETA = 10000.0
MLA_WIDTH = MLA_HEADS * MLA_V_DIM
DIFF_HEADS = 4
DIFF_HEAD_DIM = 64
DIFF_V_DIM = 2 * DIFF_HEAD_DIM
DIFF_WIDTH = DIFF_HEADS * DIFF_V_DIM
ROPE_THETA = 500000.0
ROT_DIM = DIFF_HEAD_DIM // 4
N_BRANCHES = 2
IN_SIZES = (MLA_Q_LORA, MLA_KV_LORA, MLA_ROPE_DIM, DIFF_WIDTH, DIFF_WIDTH, DIFF_WIDTH, N_BRANCHES * D_MODEL)
IN_COLS = sum(IN_SIZES)
IN_SPLIT_POINTS = tuple(int(v) for v in np.cumsum(IN_SIZES)[:-1])
Q_BLOCK = 128
NORM_EPS = 1e-6

kernel_name = 'hybrid_mla_diffattn_macaron_adaln_block'


def rmsnorm(x, gain):
    xf = x.astype(jnp.float32)
    y = xf * lax.rsqrt(jnp.mean(xf * xf, axis=-1, keepdims=True) + NORM_EPS)
    return (y * gain.astype(jnp.float32)).astype(x.dtype)


def rope(x, pos, theta):
    half = x.shape[-1] // 2
    freqs = 1.0 / (theta ** (jnp.arange(half, dtype=jnp.float32) / half))
    ang = pos.astype(jnp.float32)[..., None] * freqs
    ang = ang.reshape(ang.shape[:2] + (1,) * (x.ndim - 3) + (half,))
    cos, sin = jnp.cos(ang), jnp.sin(ang)
    xf = x.astype(jnp.float32)
    x1, x2 = xf[..., :half], xf[..., half:]
    return jnp.concatenate([x1 * cos - x2 * sin, x2 * cos + x1 * sin], axis=-1).astype(x.dtype)


def partial_rope(x, pos):
    return jnp.concatenate([rope(x[..., :ROT_DIM], pos, ROPE_THETA), x[..., ROT_DIM:]], axis=-1)


def swiglu(x, w_gate, w_up, w_down):
    return (jax.nn.silu(x @ w_gate) * (x @ w_up)) @ w_down


def _to_blocks(t):
    b, s = t.shape[:2]
    return jnp.moveaxis(t.reshape((b, s // Q_BLOCK, Q_BLOCK) + t.shape[2:]), 1, 0)


def _from_blocks(t):
    nb, b, q = t.shape[:3]
    return jnp.moveaxis(t, 0, 1).reshape((b, nb * q) + t.shape[3:])


def softmax_attention(q, k, v, scale):
    def block(qb):
        s = jnp.einsum('bqhd,bkhd->bhqk', qb, k).astype(jnp.float32) * scale
        p = jax.nn.softmax(s, axis=-1).astype(v.dtype)
        return jnp.einsum('bhqk,bkhd->bqhd', p, v)
    return _from_blocks(lax.map(block, _to_blocks(q)))


def differential_attention(q1, q2, k1, k2, v, lam, scale):
    def block(qs):
        q1b, q2b = qs
        s1 = jnp.einsum('bqhd,bkhd->bhqk', q1b, k1).astype(jnp.float32) * scale
        s2 = jnp.einsum('bqhd,bkhd->bhqk', q2b, k2).astype(jnp.float32) * scale
        p = jax.nn.softmax(s1, axis=-1) - lam * jax.nn.softmax(s2, axis=-1)
        return jnp.einsum('bhqk,bkhd->bqhd', p.astype(v.dtype), v)
    return _from_blocks(lax.map(block, (_to_blocks(q1), _to_blocks(q2))))


def mla_mixer(z_q, z_kv, k_rope, pos, q_norm, w_uq, kv_norm, w_ukv, q_gain, k_gain):
    b, s, _ = z_q.shape
    q = (rmsnorm(z_q, q_norm) @ w_uq).reshape(b, s, MLA_HEADS, MLA_QK_DIM)
    kv = (rmsnorm(z_kv, kv_norm) @ w_ukv).reshape(b, s, MLA_HEADS, MLA_NOPE_DIM + MLA_V_DIM)
    k_nope, v = kv[..., :MLA_NOPE_DIM], kv[..., MLA_NOPE_DIM:]
    k_pe = jnp.broadcast_to(k_rope[:, :, None, :], (b, s, MLA_HEADS, MLA_ROPE_DIM))
    k = jnp.concatenate([k_nope, k_pe], axis=-1)
    q = rmsnorm(q, q_gain)
    k = rmsnorm(k, k_gain)
    q = jnp.concatenate([q[..., :MLA_NOPE_DIM], rope(q[..., MLA_NOPE_DIM:], pos, MLA_ROPE_THETA)], axis=-1)
    k = jnp.concatenate([k[..., :MLA_NOPE_DIM], rope(k[..., MLA_NOPE_DIM:], pos, MLA_ROPE_THETA)], axis=-1)
    o = softmax_attention(q, k, v, 1.0 / math.sqrt(MLA_QK_DIM))
    return o.reshape(b, s, MLA_WIDTH)


def diff_lambda_init(layer_idx):
    return 0.8 - 0.6 * math.exp(-0.3 * layer_idx)


def diff_mixer(z_q, z_k, z_v, pos, q_gain, k_gain, lq1, lk1, lq2, lk2, subln, lambda_init):
    b, s, _ = z_q.shape
    q = z_q.reshape(b, s, DIFF_HEADS, 2, DIFF_HEAD_DIM)
    k = z_k.reshape(b, s, DIFF_HEADS, 2, DIFF_HEAD_DIM)
    v = z_v.reshape(b, s, DIFF_HEADS, DIFF_V_DIM)
    q = partial_rope(rmsnorm(q, q_gain), pos)
    k = partial_rope(rmsnorm(k, k_gain), pos)
    f32 = jnp.float32
    lam = (jnp.exp(jnp.sum(lq1.astype(f32) * lk1.astype(f32)))
           - jnp.exp(jnp.sum(lq2.astype(f32) * lk2.astype(f32))) + lambda_init)
    o = differential_attention(q[..., 0, :], q[..., 1, :], k[..., 0, :], k[..., 1, :], v,
                               lam, 1.0 / math.sqrt(DIFF_HEAD_DIM))
    o = rmsnorm(o, subln) * (1.0 - lambda_init)
    return o.reshape(b, s, DIFF_WIDTH)


def setup_inputs(seed: int = 0) -> dict:
    key = jax.random.key(seed)
    ks = iter(jax.random.split(key, 40))
    L, D = DEPTH, D_MODEL

    def w(shape, fan_in, gain=1.0):
        return gain * fan_in ** -0.5 * jax.random.normal(next(ks), shape, jnp.float32)

    def g(shape):
        return 1.0 + 0.05 * jax.random.normal(next(ks), shape, jnp.float32)

    def small(shape, scale):
        return scale * jax.random.normal(next(ks), shape, jnp.float32)

    x = jax.random.normal(next(ks), (BATCH, SEQ, D), jnp.float32)
    c = jax.random.normal(next(ks), (BATCH, D), jnp.float32)
    positions = (jnp.arange(SEQ, dtype=jnp.int32)[None, :]
                 + jax.random.randint(next(ks), (BATCH, 1), 0, SEQ, dtype=jnp.int32))
    return {
        'x': x,
        'c': c,
        'positions': positions,
        'w_ada': w((L, D, ADA_COLS), D, 0.5),
        'b_ada': small((L, ADA_COLS), 0.02),
        'ffn1_norm': g((L, D)),
        'ffn1_w_gate': w((L, D, D_FF), D),
        'ffn1_w_up': w((L, D, D_FF), D),
        'ffn1_w_down': w((L, D_FF, D), D_FF),
        'mix_norm': g((L, D)),
        'w_in': w((L, D, IN_COLS), D),
        'mla_q_norm': g((L, MLA_Q_LORA)),
        'mla_w_uq': w((L, MLA_Q_LORA, MLA_HEADS * MLA_QK_DIM), MLA_Q_LORA),
        'mla_kv_norm': g((L, MLA_KV_LORA)),
        'mla_w_ukv': w((L, MLA_KV_LORA, MLA_HEADS * (MLA_NOPE_DIM + MLA_V_DIM)), MLA_KV_LORA),
        'mla_q_gain': g((L, MLA_QK_DIM)),
        'mla_k_gain': g((L, MLA_QK_DIM)),
        'mla_w_o': w((L, MLA_WIDTH, D), MLA_WIDTH),
        'diff_q_gain': g((L, DIFF_HEAD_DIM)),
        'diff_k_gain': g((L, DIFF_HEAD_DIM)),
        'diff_lambda_q1': small((L, DIFF_HEAD_DIM), 0.1),
        'diff_lambda_k1': small((L, DIFF_HEAD_DIM), 0.1),
        'diff_lambda_q2': small((L, DIFF_HEAD_DIM), 0.1),
        'diff_lambda_k2': small((L, DIFF_HEAD_DIM), 0.1),
        'diff_subln': g((L, DIFF_V_DIM)),
        'diff_w_o': w((L, DIFF_WIDTH, D), DIFF_WIDTH),
        'w_out': w((L, D, D), D),
        'ffn2_norm': g((L, D)),
        'ffn2_w_gate': w((L, D, D_FF), D),
        'ffn2_w_up': w((L, D, D_FF), D),
        'ffn2_w_down': w((L, D_FF, D), D_FF),
        'final_norm': g((L, D)),
    }


def reference(x, c, positions, w_ada, b_ada, ffn1_norm, ffn1_w_gate, ffn1_w_up, ffn1_w_down,
              mix_norm, w_in, mla_q_norm, mla_w_uq, mla_kv_norm, mla_w_ukv, mla_q_gain, mla_k_gain,
              mla_w_o, diff_q_gain, diff_k_gain, diff_lambda_q1, diff_lambda_k1, diff_lambda_q2,
              diff_lambda_k2, diff_subln, diff_w_o, w_out, ffn2_norm, ffn2_w_gate, ffn2_w_up,
              ffn2_w_down, final_norm):
    h = x
    cond = jax.nn.silu(c)
    for l in range(DEPTH):
        mod = cond @ w_ada[l] + b_ada[l]
        sh1, sc1, gt1, sh2, sc2, gt2, sh3, sc3, gt3 = [
            m[:, None, :] for m in jnp.split(mod, 3 * N_SUBLAYERS, axis=-1)]

        n = rmsnorm(h, ffn1_norm[l]) * (1 + sc1) + sh1
        h = h + 0.5 * gt1 * swiglu(n, ffn1_w_gate[l], ffn1_w_up[l], ffn1_w_down[l])

        n = rmsnorm(h, mix_norm[l]) * (1 + sc2) + sh2
        z = n @ w_in[l]
        zq_a, zkv_a, krope_a, zq_b, zk_b, zv_b, gate_logits = jnp.split(z, IN_SPLIT_POINTS, axis=-1)
        y_a = mla_mixer(zq_a, zkv_a, krope_a, positions, mla_q_norm[l], mla_w_uq[l],
                        mla_kv_norm[l], mla_w_ukv[l], mla_q_gain[l], mla_k_gain[l]) @ mla_w_o[l]
        y_b = diff_mixer(zq_b, zk_b, zv_b, positions, diff_q_gain[l], diff_k_gain[l],
                         diff_lambda_q1[l], diff_lambda_k1[l], diff_lambda_q2[l], diff_lambda_k2[l],
                         diff_subln[l], diff_lambda_init(l)) @ diff_w_o[l]
        gate_a, gate_b = jnp.split(jax.nn.sigmoid(gate_logits), N_BRANCHES, axis=-1)
        h = h + gt2 * ((gate_a * y_a + gate_b * y_b) @ w_out[l])

        n = rmsnorm(h, ffn2_norm[l]) * (1 + sc3) + sh3
        h = h + 0.5 * gt3 * swiglu(n, ffn2_w_gate[l], ffn2_w_up[l], ffn2_w_down[l])

        h = rmsnorm(h, final_norm[l])
    return h
```

```python
import math
import os as _os
import numpy as np
import concourse.bass as bass
import concourse.mybir as mybir
from concourse.bass_utils import run_bass_kernel_spmd

F32 = mybir.dt.float32
BF16 = mybir.dt.bfloat16
I32 = mybir.dt.int32
AF = mybir.ActivationFunctionType
ALU = mybir.AluOpType

T = 2048
D = 1024
DFF = 2816
NG = 4
GS = 512
KC = 8
EPS = 1e-6
IN_COLS = 4256
LAMBDA_INIT = 0.8 - 0.6 * math.exp(-0.3 * 0)


class Buf:
    __slots__ = ("name", "w", "r", "dsem", "dcnt")

    def __init__(self, name):
        self.name = name
        self.w = None
        self.r = {}
        self.dsem = None
        self.dcnt = 0


class Prog:
    ENG = ("pe", "act", "dve", "pool", "sp")

    def __init__(self, nc, same_engine_sync=True):
        self.nc = nc
        self.streams = {e: [] for e in self.ENG}
        self.sems = {}
        self.cnt = {}
        self.waited = {e: {} for e in self.ENG}
        self.same = same_engine_sync
        for e in self.ENG:
            self.sems["E_" + e] = nc.alloc_semaphore("sem_e_" + e)
            self.cnt[e] = 0
        self.nd = 0
        self.retired = {}

    def new_dsem(self, name):
        k = "D_%d_%s" % (self.nd, name)
        self.nd += 1
        self.sems[k] = self.nc.alloc_semaphore("sem_d%d" % self.nd)
        return k

    def buf(self, name):
        b = Buf(name)
        b.r = dict(self.retired)
        return b

    def bufs(self, name, n):
        return [self.buf("%s%d" % (name, i)) for i in range(n)]

    def retire(self, *bufs):
        for b in bufs:
            if b.w is not None:
                s, v = b.w
                self.retired[s] = max(self.retired.get(s, 0), v)
            for s, v in b.r.items():
                self.retired[s] = max(self.retired.get(s, 0), v)

    def _waits(self, eng, reads, writes):
        waits = {}
        own = "E_" + eng

        def need(s, v):
            if s == own and (eng == "pe" or not self.same):
                return
            if self.waited[eng].get(s, 0) >= v:
                return
            if waits.get(s, 0) < v:
                waits[s] = v

        for b in reads:
            if b.w is not None:
                need(*b.w)
        for b in writes:
            if b.w is not None:
                need(*b.w)
            for s, v in b.r.items():
                need(s, v)
        for s, v in waits.items():
            self.waited[eng][s] = v
        return list(waits.items())

    def _mark(self, tok, reads, writes):
        s, v = tok
        for b in reads:
            if b.r.get(s, 0) < v:
                b.r[s] = v
        for b in writes:
            b.w = tok
            b.r = {}

    def op(self, eng, fn, reads=(), writes=(), inc=True):
        waits = self._waits(eng, reads, writes)
        own = "E_" + eng
        if inc:
            self.cnt[eng] += 1
            tok = (own, self.cnt[eng])
        else:
            tok = (own, self.cnt[eng] + 1)
        self._mark(tok, reads, writes)
        self.streams[eng].append((waits, fn, (own, 1) if inc else None))

    def dma(self, q, out_ap, in_ap, reads=(), writes=(), owner=None):
        if owner is None:
            owner = writes[0] if writes else reads[0]
        if owner.dsem is None:
            owner.dsem = self.new_dsem(owner.name)
        waits = self._waits(q, reads, writes)
        owner.dcnt += 16
        tok = (owner.dsem, owner.dcnt)
        self._mark(tok, reads, writes)
        self.streams[q].append(
            (waits, lambda e: e.dma_start(out=out_ap, in_=in_ap), (owner.dsem, 16)))

    def wait_all(self, eng, bufs):
        waits = self._waits(eng, (), bufs)
        self.streams[eng].append((waits, None, None))

    def emit(self):
        nc = self.nc
        sems = self.sems
        streams = self.streams
        with nc.Block() as block:
            def run(e, name):
                for waits, fn, inc in streams[name]:
                    for s, v in waits:
                        e.wait_ge(sems[s], v)
                    if fn is None:
                        continue
                    ins = fn(e)
                    if inc is not None:
                        ins.then_inc(sems[inc[0]], inc[1])

            @block.tensor
            def _(e):
                run(e, "pe")

            @block.scalar
            def _(e):
                run(e, "act")

            @block.vector
            def _(e):
                run(e, "dve")

            @block.gpsimd
            def _(e):
                run(e, "pool")

            @block.sync
            def _(e):
                run(e, "sp")


class Arena:
    def __init__(self, nc, reserve=1024):
        rem = nc.sbuf_bytes_remaining
        self.nbytes = ((rem - reserve) // 64) * 64
        self.t = nc.alloc_sbuf_tensor("arena", [128, self.nbytes // 4], F32)

    def view(self, off, dtype, shape):
        assert off % 4 == 0
        esz = 2 if dtype == BF16 else 4
        n = 1
        for s in shape:
            n *= s
        nb = n * esz
        assert nb % 4 == 0 and off + nb <= self.nbytes, (off, nb, self.nbytes)
        a = self.t[:, off // 4:(off + nb) // 4]
        if dtype != F32:
            a = a.bitcast(dtype)
        if len(shape) == 2:
            a = a.rearrange("p (a b) -> p a b", a=shape[0])
        elif len(shape) == 3:
            a = a.rearrange("p (a b c) -> p a b c", a=shape[0], b=shape[1])
        return a


def MM(out, lhsT, rhs, start, stop):
    return lambda e: e.matmul(out, lhsT=lhsT, rhs=rhs, start=start, stop=stop)


def TR(out, in_, ident):
    return lambda e: e.transpose(out, in_, ident)


def ACT(out, in_, func, bias=None, scale=None):
    kw = {}
    if bias is not None:
        kw["bias"] = bias
    if scale is not None:
        kw["scale"] = scale
    return lambda e: e.activation(out=out, in_=in_, func=func, **kw)


def TT(out, in0, in1, op):
    return lambda e: e.tensor_tensor(out=out, in0=in0, in1=in1, op=op)


def TS(out, in0, s1, s2, op0, op1=None):
    if op1 is None:
        return lambda e: e.tensor_scalar(out=out, in0=in0, scalar1=s1, scalar2=None, op0=op0)
    return lambda e: e.tensor_scalar(out=out, in0=in0, scalar1=s1, scalar2=s2, op0=op0, op1=op1)


def STT(out, in0, scalar, in1, op0, op1):
    return lambda e: e.scalar_tensor_tensor(out=out, in0=in0, scalar=scalar, in1=in1, op0=op0, op1=op1)


def CP(out, in_):
    return lambda e: e.tensor_copy(out=out, in_=in_)


def RECIP(out, in_):
    return lambda e: e.reciprocal(out=out, in_=in_)


def MEMSET(ap, v):
    return lambda e: e.memset(ap, v)


PVB = {}
_n = 0
for _name, _cnt in [("g1", 8), ("g2", 8), ("g3", 8), ("gf", 8), ("qnorm", 3), ("kvnorm", 2),
                    ("qg", 1), ("qgp", 1), ("kg", 1), ("kgp", 1), ("dqg", 1), ("dqgp", 1),
                    ("dkg", 1), ("dkgp", 1), ("lq1", 1), ("lk1", 1), ("lq2", 1), ("lk2", 1),
                    ("subln", 1), ("freqa", 1), ("signa", 1), ("freqb", 1), ("signb", 1)]:
    PVB[_name] = _n
    _n += _cnt
NPVB = _n
NPVA = 80


def build_nc(stages=("ffn1", "mix", "ffn2"), debug=None):
    nc = bass.Bass("TRN2", target_bir_lowering=False)
    P = Prog(nc)
    A = Arena(nc)

    def dram_in(name, shape, dt=F32):
        return nc.dram_tensor(name, shape, dt, kind="ExternalInput").ap()

    x_d = dram_in("x", [T, D])
    pos_d = dram_in("pos", [1, T], I32)
    pva_d = dram_in("pva", [NPVA, 128])
    pvb_d = dram_in("pvb", [NPVB, 128])
    ident_d = dram_in("ident", [128, 128])
    wada_d = dram_in("w_ada", [D, 9 * D])
    f1g_d = dram_in("f1g", [D, DFF])
    f1u_d = dram_in("f1u", [D, DFF])
    f1d_d = dram_in("f1d", [DFF, D])
    f2g_d = dram_in("f2g", [D, DFF])
    f2u_d = dram_in("f2u", [D, DFF])
    f2d_d = dram_in("f2d", [DFF, D])
    win_d = dram_in("w_in", [D, IN_COLS])
    wuq_d = dram_in("w_uq", [384, 768])
    wukv_d = dram_in("w_ukv", [256, 1024])
    woa_d = dram_in("w_oa", [512, D])
    wob_d = dram_in("w_ob", [512, D])
    wout_d = dram_in("w_out", [D, D])
    out_d = nc.dram_tensor("out", [T, D], F32, kind="ExternalOutput").ap()
    dbg_d = None
    if debug is not None:
        dbg_d = nc.dram_tensor("dbg", [128] + list(debug[1]), F32, kind="ExternalOutput").ap()

    H_OFF = 0
    MISC = 65536
    NT_OFF = MISC + 14336
    R_OFF = NT_OFF + 32768
    R_SIZE = A.nbytes - R_OFF
    assert R_SIZE >= 98304, R_SIZE

    hT = A.view(H_OFF, F32, [KC, T])
    nT = A.view(NT_OFF, BF16, [KC, T])
    ident = A.view(MISC + 0, F32, [128])
    ones = A.view(MISC + 512, BF16, [128])
    ones2 = A.view(MISC + 768, BF16, [128])
    pvT = A.view(MISC + 1024, F32, [256])
    modT = A.view(MISC + 2048, F32, [72])
    der = A.view(MISC + 2560, F32, [128])
    sqs = [A.view(MISC + 3072 + i * 1024, BF16, [GS]) for i in range(2)]
    fsc = [A.view(MISC + 5120 + i * 2048, F32, [GS]) for i in range(2)]
    rsb = [A.view(MISC + 9216 + i * 2048, F32, [GS]) for i in range(2)]
    condb = A.view(MISC + 13312, BF16, [8])
    ps = nc.alloc_psum_tensor("ps", [128, 8, GS], F32)

    b_h = [[P.buf("h%d_%d" % (c, g)) for g in range(NG)] for c in range(KC)]
    b_n = [[P.buf("n%d_%d" % (c, g)) for g in range(NG)] for c in range(KC)]
    b_ps = P.bufs("ps", 8)
    b_const = P.buf("const")
    b_pvT = P.buf("pvT")
    b_mod = P.buf("mod")
    b_der = P.buf("der")
    b_sqs = P.bufs("sqs", 2)
    b_fsc = P.bufs("fsc", 2)
    b_rsb = P.bufs("rsb", 2)
    b_condb = P.buf("condb")

    def pvb(name, i=0):
        c = 80 + PVB[name] + i
        return pvT[:, c:c + 1]

    DER = {"A1": 0, "G1": 8, "A2": 16, "G2": 24, "A3": 32, "G3": 40, "SH1": 48, "SH2": 56, "SH3": 64,
           "nlam": 72, "subs": 73, "lamt": 74}

    def dcol(name, i=0):
        c = DER[name] + i
        return der[:, c:c + 1]

    P.dma("sp", ident[:], ident_d, writes=[b_const])
    P.op("pool", MEMSET(ones[:], 1.0), writes=[b_const])
    P.op("pool", MEMSET(ones2[:], 0.0), writes=[b_const])
    P.op("pool", MEMSET(ones2[0:64, 0:64], 1.0), writes=[b_const])
    P.op("pool", MEMSET(ones2[64:128, 64:128], 1.0), writes=[b_const])
    pstage = A.view(R_OFF + 32768, F32, [2, 128])
    b_pstage = P.buf("pstage")
    P.op("pool", MEMSET(pstage[:], 0.0), writes=[b_pstage])
    P.dma("sp", pstage[0:NPVA, 0, :], pva_d, writes=[b_pstage])
    P.dma("sp", pstage[0:NPVB, 1, :], pvb_d, writes=[b_pstage])
    P.op("pe", TR(ps[:, 7, 0:NPVA], pstage[0:NPVA, 0, :], ident[0:NPVA, 0:NPVA]),
         reads=[b_pstage, b_const], writes=[b_ps[7]], inc=False)
    P.op("pe", TR(ps[:, 7, 128:128 + NPVB], pstage[0:NPVB, 1, :], ident[0:NPVB, 0:NPVB]),
         reads=[b_pstage, b_const], writes=[b_ps[7]])
    P.op("dve", CP(pvT[:, 0:NPVA], ps[:, 7, 0:NPVA]), reads=[b_ps[7]], writes=[b_pvT])
    P.op("dve", CP(pvT[:, 80:80 + NPVB], ps[:, 7, 128:128 + NPVB]), reads=[b_ps[7]], writes=[b_pvT])
    P.retire(b_pstage)
    P.op("act", ACT(condb[:], pvT[:, 0:8], AF.Silu), reads=[b_pvT], writes=[b_condb])

    ring = {"wd_i": 0, "gu_i": 0, "b_wd": P.bufs("wdp", 2), "b_gu": P.bufs("gu", 2)}

    AT_OFF = R_OFF
    GU_OFF = R_OFF + 32768
    WD_OFF = R_OFF + 65536

    def wada_block(m, dst, b_dst):
        src = wada_d.rearrange("(kc p) n -> p kc n", p=128)[:, :, m * 1024:(m + 1) * 1024]
        for hh in range(2):
            P.dma("pool", dst[:, hh * 4:(hh + 1) * 4, :], src[:, hh * 4:(hh + 1) * 4, :], writes=[b_dst])

    def mod_block(m, wt, b_wt):
        for j in range(8):
            for kc in range(KC):
                P.op("pe", MM(ps[:, 6, m * 8 + j:m * 8 + j + 1], wt[:, kc, j * 128:(j + 1) * 128],
                              condb[:, kc:kc + 1], kc == 0, kc == KC - 1),
                     reads=[b_wt, b_condb], writes=[b_ps[6]], inc=(j == 7 and kc == KC - 1))
        P.op("dve", TT(modT[:, m * 8:(m + 1) * 8], ps[:, 6, m * 8:(m + 1) * 8],
                       pvT[:, 8 + m * 8:8 + (m + 1) * 8], ALU.add),
             reads=[b_ps[6], b_pvT], writes=[b_mod])

    def derive_A(dst, gname, m_sc, m_sh, shname):
        P.op("dve", STT(der[:, DER[dst]:DER[dst] + 8], modT[:, m_sc * 8:(m_sc + 1) * 8], 1.0,
                        pvT[:, 80 + PVB[gname]:80 + PVB[gname] + 8], ALU.add, ALU.mult),
             reads=[b_mod, b_pvT], writes=[b_der])
        P.op("dve", CP(der[:, DER[shname]:DER[shname] + 8], modT[:, m_sh * 8:(m_sh + 1) * 8]),
             reads=[b_mod], writes=[b_der])

    def derive_G(dst, m_gt, scale):
        P.op("dve", TS(der[:, DER[dst]:DER[dst] + 8], modT[:, m_gt * 8:(m_gt + 1) * 8], float(scale), None, ALU.mult),
             reads=[b_mod], writes=[b_der])

    def norm_group(g, Aname, SHname, sq_from_psum=None):
        sl = slice(g * GS, (g + 1) * GS)
        sb = 6 + (g % 2)
        for c in range(KC):
            i = c % 2
            P.op("act", ACT(sqs[i][:], hT[:, c, sl], AF.Square), reads=[b_h[c][g]], writes=[b_sqs[i]])
            P.op("pe", MM(ps[:, sb, :], ones[:], sqs[i][:], c == 0, c == KC - 1),
                 reads=[b_const, b_sqs[i]], writes=[b_ps[sb]], inc=True)
        r = g % 2
        P.op("act", ACT(rsb[r][:], ps[:, sb, :], AF.Sqrt, bias=epsc[:], scale=1.0 / D),
             reads=[b_ps[sb], b_const], writes=[b_rsb[r]])
        P.op("dve", RECIP(rsb[r][:], rsb[r][:]), reads=[b_rsb[r]], writes=[b_rsb[r]])
        if Aname is None:
            return r
        for c in range(KC):
            i = c % 2
            P.op("dve", TT(fsc[i][:], hT[:, c, sl], rsb[r][:], ALU.mult),
                 reads=[b_h[c][g], b_rsb[r]], writes=[b_fsc[i]])
            P.op("dve", TS(nT[:, c, sl], fsc[i][:], dcol(Aname, c), dcol(SHname, c), ALU.mult, ALU.add),
                 reads=[b_fsc[i], b_der], writes=[b_n[c][g]])
        return r

    epsc = A.view(MISC + 13312 + 64, F32, [1])
    P.op("pool", MEMSET(epsc[:], EPS), writes=[b_const])

    aT = A.view(AT_OFF, BF16, [8, T])
    gu = [A.view(GU_OFF + i * 16384, BF16, [2, KC, GS]) for i in range(2)]
    wdp = [A.view(WD_OFF + i * 16384, BF16, [8, D]) for i in range(2)]
    PASSES = [(0, 8), (8, 16), (16, 22)]

    def ffn(wg_d, wu_d, wd_d, Gname, st, pre_hooks):
        wg_v = wg_d.rearrange("(kc p) n -> p kc n", p=128)
        wu_v = wu_d.rearrange("(kc p) n -> p kc n", p=128)
        wd_v = wd_d.rearrange("(j p) n -> p j n", p=128)
        b_a = [[P.buf("a%d_%d" % (j, g)) for g in range(NG)] for j in range(8)]
        it = 0
        for pi, (c0, c1) in enumerate(PASSES):
            npc = c1 - c0
            wslot = st["wd_i"] % 2
            st["wd_i"] += 1
            b_wd = st["b_wd"][wslot]
            for hh in range(0, npc, 4):
                h1 = min(hh + 4, npc)
                P.dma("pool", wdp[wslot][:, hh:h1, :], wd_v[:, c0 + hh:c0 + h1, :], writes=[b_wd])
            groups = [(a, min(a + 4, c1)) for a in range(c0, c1, 4)]
            loaded = []
            for (a, b) in groups:
                gslot = st["gu_i"] % 2
                st["gu_i"] += 1
                b_g = st["b_gu"][gslot]
                ncol = (b - a) * 128
                P.dma("pool", gu[gslot][:, 0, :, 0:ncol], wg_v[:, :, a * 128:b * 128], writes=[b_g])
                P.dma("pool", gu[gslot][:, 1, :, 0:ncol], wu_v[:, :, a * 128:b * 128], writes=[b_g])
                loaded.append((a, b, gslot, b_g))
                for j in range(a, b):
                    jl = j - c0
                    jo = (j - a) * 128
                    for g in range(NG):
                        sl = slice(g * GS, (g + 1) * GS)
                        bg = it % 2
                        bu = 2 + it % 2
                        it += 1
                        for kc in range(KC):
                            P.op("pe", MM(ps[:, bg, :], gu[gslot][:, 0, kc, jo:jo + 128], nT[:, kc, sl], kc == 0, kc == KC - 1),
                                 reads=[b_g, b_n[kc][g]], writes=[b_ps[bg]], inc=(kc == KC - 1))
                        for kc in range(KC):
                            P.op("pe", MM(ps[:, bu, :], gu[gslot][:, 1, kc, jo:jo + 128], nT[:, kc, sl], kc == 0, kc == KC - 1),
                                 reads=[b_g, b_n[kc][g]], writes=[b_ps[bu]], inc=(kc == KC - 1))
                        i = it % 2
                        P.op("act", ACT(fsc[i][:], ps[:, bg, :], AF.Silu), reads=[b_ps[bg]], writes=[b_fsc[i]])
                        P.op("dve", TT(aT[:, jl, sl], fsc[i][:], ps[:, bu, :], ALU.mult),
                             reads=[b_fsc[i], b_ps[bu]], writes=[b_a[jl][g]])
                hk = pre_hooks.get((pi, len(loaded) - 1))
                if hk is not None:
                    hk()
            hk = pre_hooks.get((pi, "down"))
            if hk is not None:
                hk()
            dn = 0
            for g in range(NG):
                sl = slice(g * GS, (g + 1) * GS)
                for oc in range(KC):
                    bd = 4 + dn % 2
                    dn += 1
                    for jl in range(npc):
                        P.op("pe", MM(ps[:, bd, :], wdp[wslot][:, jl, oc * 128:(oc + 1) * 128], aT[:, jl, sl], jl == 0, jl == npc - 1),
                             reads=[b_wd, b_a[jl][g]], writes=[b_ps[bd]], inc=(jl == npc - 1))
                    P.op("dve", STT(hT[:, oc, sl], ps[:, bd, :], dcol(Gname, oc), hT[:, oc, sl], ALU.mult, ALU.add),
                         reads=[b_ps[bd], b_der, b_h[oc][g]], writes=[b_h[oc][g]])
        P.retire(*[b for row in b_a for b in row])


    xs = [A.view(AT_OFF + i * 4096, F32, [D]) for i in range(8)]
    b_xs = P.bufs("xs", 8)
    wa = [A.view(WD_OFF + i * 16384, BF16, [KC, D]) for i in range(2)]
    wada_block(0, wa[0], ring["b_wd"][0])
    wada_block(1, wa[1], ring["b_wd"][1])
    x_v = x_d.rearrange("(t p) d -> t p d", p=128)
    for g in range(NG):
        for j in range(4):
            tt = g * 4 + j
            s = tt % 8
            P.dma("sp", xs[s][:], x_v[tt], writes=[b_xs[s]])
        for c in range(KC):
            bk = c % 2
            for j in range(4):
                s = (g * 4 + j) % 8
                P.op("pe", TR(ps[:, bk, j * 128:(j + 1) * 128], xs[s][:, c * 128:(c + 1) * 128], ident[:]),
                     reads=[b_xs[s], b_const], writes=[b_ps[bk]], inc=(j == 3))
            P.op("act", ACT(hT[:, c, g * GS:(g + 1) * GS], ps[:, bk, :], AF.Copy), reads=[b_ps[bk]], writes=[b_h[c][g]])
        if g == 0:
            mod_block(0, wa[0], ring["b_wd"][0])
            mod_block(1, wa[1], ring["b_wd"][1])
            derive_A("A1", "g1", 1, 0, "SH1")
    P.retire(*b_xs)
    ring["wd_i"] = 0

    for g in range(NG):
        norm_group(g, "A1", "SH1")

    def make_mod_hook(ms):
        def hk():
            for m in ms:
                wslot = ring["wd_i"] % 2
                b_w_ = ring["b_wd"][wslot]
                wv = A.view(WD_OFF + wslot * 16384, BF16, [KC, D])
                wada_block(m, wv, b_w_)
                mod_block(m, wv, b_w_)
        return hk

    def hook_g1():
        derive_G("G1", 2, 0.5)

    if "ffn1" in stages:
        ffn(f1g_d, f1u_d, f1d_d, "G1", ring,
            {(0, 0): make_mod_hook([2]), (0, "down"): hook_g1,
             (0, 1): make_mod_hook([3]), (1, 0): make_mod_hook([4])})
    else:
        make_mod_hook([2, 3, 4])()
    derive_A("A2", "g2", 4, 3, "SH2")
    LATE_MOD = "mix" in stages
    if not LATE_MOD:
        make_mod_hook([5, 6, 7, 8])()
        derive_G("G2", 5, 1.0)
        derive_A("A3", "g3", 7, 6, "SH3")
        derive_G("G3", 8, 0.5)

    if "mix" in stages:
        for g in range(NG):
            norm_group(g, "A2", "SH2")
        hsp = nc.dram_tensor("hspill", [128, KC, T], F32).ap()
        b_hsp = P.buf("hsp")
        for c in range(KC):
            P.dma("sp", hsp[:, c, :], hT[:, c, :], reads=[b_h[c][g] for g in range(NG)], writes=[b_hsp])
        P.retire(*[b for row in b_h for b in row])
        P.retire(*ring["b_wd"])
        P.retire(*ring["b_gu"])
        M2 = R_OFF
        SC_A = 1.0 / math.sqrt(96.0)
        SC_B = 1.0 / math.sqrt(64.0)
        win_v = win_d.rearrange("(kc p) n -> p kc n", p=128)
        oaT = A.view(0, BF16, [4, T])
        obT = A.view(16384, BF16, [4, T])
        cosF = A.view(32768, F32, [T])
        sinF = A.view(40960, F32, [T])
        kpeR = A.view(49152, F32, [T])
        sqpe = A.view(57344, BF16, [T])
        b_oa = [[P.buf("oa") for g in range(NG)] for c in range(4)]
        b_ob = [[P.buf("ob") for g in range(NG)] for c in range(4)]
        b_tab = P.buf("tab")
        b_kpe = P.bufs("kpe", NG)
        wlat = A.view(M2 + 0, BF16, [KC, 640])
        wkr = A.view(M2 + 10240, BF16, [KC, 96])
        wkrp = A.view(M2 + 11776, BF16, [KC, 96])
        wuq = A.view(M2 + 13312, BF16, [3, 768])
        wuqp = A.view(M2 + 17920, BF16, [3, 8, 96])
        wukk = A.view(M2 + 22528, BF16, [2, 8, 64])
        wukv = A.view(M2 + 24576, BF16, [2, 512])
        wuqp_f = A.view(M2 + 17920, BF16, [3, 768])
        wukk_f = A.view(M2 + 22528, BF16, [2, 512])
        cqT = A.view(M2 + 26624, BF16, [3, T])
        ckvT = A.view(M2 + 38912, BF16, [2, T])
        vh = [A.view(M2 + 47104 + i * 4096, BF16, [16, 128]) for i in range(2)]
        qh = [A.view(M2 + 55296 + i * 4096, BF16, [T]) for i in range(2)]
        kh = [A.view(M2 + 63488 + i * 4096, BF16, [T]) for i in range(2)]
        PT = [A.view(M2 + 71680 + i * 1024, BF16, [GS]) for i in range(4)]
        oraw = [A.view(M2 + 75776 + i * 2048, F32, [GS]) for i in range(2)]
        dn = [A.view(M2 + 79872 + i * 2048, F32, [GS]) for i in range(2)]
        onesf = A.view(M2 + 83968, F32, [128])
        rin = A.view(M2 + 84480, F32, [GS])
        posi = A.view(M2 + 26624, I32, [T])
        tmpf = A.view(M2 + 34816, F32, [T])
        kfi = A.view(M2 + 43008, I32, [T])
        kff = A.view(M2 + 43008, F32, [T])
        b_tt = P.buf("tabtmp")
        b_w = {k: P.buf("w_" + k) for k in ("lat", "kr", "uq", "uqp", "ukk", "ukv")}
        b_PT = P.bufs("PT", 4)
        b_oraw = P.bufs("oraw", 2)
        b_dn = P.bufs("dn", 2)
        b_rin = P.buf("rin")
        sq_e = A.view(M2 + 86528, BF16, [GS])
        od_e = A.view(M2 + 87552, F32, [GS])
        rs_e = A.view(M2 + 89600, F32, [GS])
        b_sqe, b_ode, b_rse = P.buf("sqe"), P.buf("ode"), P.buf("rse")
        b_onesf = P.buf("onesf")
        P.op("pool", MEMSET(onesf[:], 1.0), writes=[b_onesf])
        self_ = [A.view(M2 + 91648 + i * 512, F32, [128]) for i in range(2)]
        rq_t = [A.view(M2 + 92672 + i * 128, F32, [32]) for i in range(2)]
        rk_t = [A.view(M2 + 92928 + i * 128, F32, [32]) for i in range(2)]
        t4 = [A.view(M2 + 93184 + i * 32, F32, [8]) for i in range(2)]
        dg = [A.view(M2 + 93248 + i * 512, F32, [128]) for i in range(8)]
        sel2b = A.view(M2 + 97344, BF16, [2])
        b_rq, b_rk = P.bufs("rq", 2), P.bufs("rk", 2)
        b_t4 = P.bufs("t4", 2)
        b_dg = P.bufs("dg", 8)
        P.op("pool", MEMSET(self_[0][:], 0.0), writes=[b_onesf])
        P.op("pool", MEMSET(self_[1][:], 0.0), writes=[b_onesf])
        P.op("pool", MEMSET(self_[0][:, 0:64], 1.0), writes=[b_onesf])
        P.op("pool", MEMSET(self_[1][:, 64:128], 1.0), writes=[b_onesf])
        P.op("pool", MEMSET(sel2b[:], 0.0), writes=[b_onesf])
        P.op("pool", MEMSET(sel2b[0:64, 0:1], 1.0), writes=[b_onesf])
        P.op("pool", MEMSET(sel2b[64:128, 1:2], 1.0), writes=[b_onesf])
        cmh = A.view(MISC + 13312 + 72, F32, [1])
        P.op("pool", MEMSET(cmh[:], -0.5), writes=[b_onesf])

        def POWH(out, in0, ncol):
            return lambda e: e.tensor_tensor(out=out, in0=in0, in1=cmh[:, 0:1].to_broadcast([128, ncol]), op=ALU.pow)

        TWO_PI = 2.0 * math.pi
        C1 = 6.28125
        C2 = TWO_PI - C1

        def build_tables(fname, sname):
            P.dma("sp", posi[:], pos_d.partition_broadcast(128), writes=[b_tt])
            P.op("dve", CP(tmpf[:], posi[:]), reads=[b_tt], writes=[b_tt])
            P.op("dve", TS(tmpf[:], tmpf[:], pvb(fname), None, ALU.mult), reads=[b_tt, b_pvT], writes=[b_tt])
            P.op("dve", TS(cosF[:], tmpf[:], 1.0 / TWO_PI, None, ALU.mult), reads=[b_tt], writes=[b_tab])
            P.op("dve", CP(kfi[:], cosF[:]), reads=[b_tab], writes=[b_tt])
            P.op("dve", CP(cosF[:], kfi[:]), reads=[b_tt], writes=[b_tab])
            P.op("dve", STT(tmpf[:], cosF[:], -C1, tmpf[:], ALU.mult, ALU.add), reads=[b_tab, b_tt], writes=[b_tt])
            P.op("dve", STT(tmpf[:], cosF[:], -C2, tmpf[:], ALU.mult, ALU.add), reads=[b_tab, b_tt], writes=[b_tt])
            P.op("dve", TS(tmpf[:], tmpf[:], math.pi, -math.pi, ALU.min, ALU.max), reads=[b_tt], writes=[b_tt])
            P.op("act", ACT(sinF[:], tmpf[:], AF.Sin, scale=pvb(sname)), reads=[b_tt, b_pvT], writes=[b_tab])
            P.op("dve", TS(tmpf[:], tmpf[:], math.pi / 2, None, ALU.add), reads=[b_tt], writes=[b_tt])
            P.op("dve", TS(kff[:], tmpf[:], math.pi, -TWO_PI, ALU.is_gt, ALU.mult), reads=[b_tt], writes=[b_tt])
            P.op("dve", TT(tmpf[:], tmpf[:], kff[:], ALU.add), reads=[b_tt], writes=[b_tt])
            P.op("dve", TS(tmpf[:], tmpf[:], math.pi, -math.pi, ALU.min, ALU.max), reads=[b_tt], writes=[b_tt])
            P.op("act", ACT(cosF[:], tmpf[:], AF.Sin), reads=[b_tt], writes=[b_tab])

        def rstd_from(bank, rows, n, r):
            if _os.environ.get("MK_LN") != "1":
                P.op("act", ACT(rsb[r][rows, :], ps[rows, bank, :], AF.Sqrt, bias=epsc[rows, :], scale=1.0 / n),
                     reads=[b_ps[bank], b_const], writes=[b_rsb[r]])
                P.op("dve", RECIP(rsb[r][rows, :], rsb[r][rows, :]), reads=[b_rsb[r]], writes=[b_rsb[r]])
                return
            P.op("act", ACT(rsb[r][rows, :], ps[rows, bank, :], AF.Ln, bias=epsc[rows, :], scale=1.0 / n),
                 reads=[b_ps[bank], b_const], writes=[b_rsb[r]])
            P.op("act", ACT(rsb[r][rows, :], rsb[r][rows, :], AF.Exp, scale=-0.5), reads=[b_rsb[r]], writes=[b_rsb[r]])

        def rope_parts(nrows, gcol, gpcol, sl, out_ap, b_out, after=()):
            rows = slice(0, nrows)
            P.op("dve", STT(fsc[1][rows, :], ps[rows, 5, :], gpcol[rows, :], sinF[rows, sl], ALU.mult, ALU.mult),
                 reads=[b_ps[5], b_pvT, b_tab], writes=[b_fsc[1]])
            P.op("dve", STT(fsc[0][rows, :], ps[rows, 4, :], gcol[rows, :], cosF[rows, sl], ALU.mult, ALU.mult),
                 reads=[b_ps[4], b_pvT, b_tab] + list(after), writes=[b_fsc[0]])
            P.op("dve", TT(out_ap, fsc[0][rows, :], fsc[1][rows, :], ALU.add), reads=[b_fsc[0], b_fsc[1]], writes=[b_out])

        def tiny_rstd(i, which, g, ncomp, n, mult, stat_fn):
            base = 0 if which == "q" else 32
            dst_t, b_dst_t = (rq_t[i], b_rq[i]) if which == "q" else (rk_t[i], b_rk[i])
            c0 = g * 4 * ncomp
            if _os.environ.get("MK_SKIP") == which:
                P.op("pool", MEMSET(dst_t[:, c0:c0 + 4 * ncomp], 0.1), writes=[b_dst_t])
                return
            for j in range(4):
                stat_fn(j, ps[:, 6, base + c0 + j * ncomp:base + c0 + (j + 1) * ncomp])
            ti = 0 if which == "q" else 1
            P.op("dve", TS(t4[ti][:, 0:4 * ncomp], ps[:, 6, base + c0:base + c0 + 4 * ncomp], mult / n, EPS * mult, ALU.mult, ALU.add),
                 reads=[b_ps[6]], writes=[b_t4[ti]])
            P.op("pool", POWH(dst_t[:, c0:c0 + 4 * ncomp], t4[ti][:, 0:4 * ncomp], 4 * ncomp), reads=[b_t4[ti], b_onesf], writes=[b_dst_t])

        def qprep_gen(i, nrows, ncomp, n, gcol, gpcol, dst, b_dst, sl, g):
            rows = slice(0, nrows)
            P.op("act", ACT(sqs[0][rows, :], ps[rows, 4, :], AF.Square), reads=[b_ps[4]], writes=[b_sqs[0]])
            rope_parts(nrows, gcol, gpcol, sl, fsc[0][rows, :], b_fsc[0], after=[b_sqs[0]])
            yield

            def stat(j, out_ap):
                rhs = ones[rows, 0:1] if ncomp == 1 else sel2b[:, 0:2]
                P.op("pe", MM(out_ap, sqs[0][rows, j * 128:(j + 1) * 128], rhs, True, True),
                     reads=[b_sqs[0], b_const, b_onesf], writes=[b_ps[6]])
            tiny_rstd(i, "q", g, ncomp, n, 1.0, stat)
            yield
            c0 = g * 4 * ncomp
            for j in range(4):
                for c in range(ncomp):
                    d = j * ncomp + c
                    col = c0 + j * ncomp + c
                    P.op("dve", TS(dg[d][:], ident[:], rq_t[i][:, col:col + 1], None, ALU.mult),
                         reads=[b_const, b_rq[i]], writes=[b_dg[d]])
            for j in range(4):
                for c in range(ncomp):
                    d = j * ncomp + c
                    lhsT = onesf[:, 0:128] if ncomp == 1 else self_[c][:, 0:128]
                    P.op("pe", MM(ps[0:128, 4, j * 128:(j + 1) * 128], lhsT, dg[d][:], c == 0, c == ncomp - 1),
                         reads=[b_onesf, b_dg[d]], writes=[b_ps[4]], inc=(j == 3 and c == ncomp - 1))
            P.op("dve", TT(dst, fsc[0][rows, :], ps[rows, 4, :], ALU.mult), reads=[b_fsc[0], b_ps[4]], writes=[b_dst])
            yield

        ring_i = [0]
        pend = []

        def step_pend():
            for gen_ in list(pend):
                try:
                    next(gen_)
                except StopIteration:
                    pend.remove(gen_)

        def attention(kT, b_k, qT, b_q, rows, scale, vts, qc, filler, fill_at):
            qsl = slice(qc * GS, (qc + 1) * GS)

            def S(kt):
                it = ring_i[0]
                ring_i[0] += 1
                sb = it % 2
                pr = it % 4
                P.op("pe", MM(ps[:, sb, :], kT[rows, kt * 128:(kt + 1) * 128], qT[rows, qsl], True, True),
                     reads=[b_k, b_q], writes=[b_ps[sb]])
                sc_ap, b_sc = scale(kt)
                if _os.environ.get("MK_CONSTSC") == "1":
                    sc_ap = 0.1
                P.op("act", ACT(PT[pr][:], ps[:, sb, :], AF.Exp, scale=sc_ap), reads=[b_ps[sb], b_sc], writes=[b_PT[pr]])
                return pr

            prs = {0: S(0)}
            for kt in range(16):
                if kt + 1 < 16:
                    prs[kt + 1] = S(kt + 1)
                pr = prs[kt]
                for vi, (vfn, M, bank, b_v) in enumerate(vts):
                    P.op("pe", MM(ps[0:M, bank, :], vfn(kt), PT[pr][:], kt == 0, kt == 15),
                         reads=[b_v, b_PT[pr]], writes=[b_ps[bank]], inc=(kt == 15 and vi == len(vts) - 1))
                if kt in fill_at:
                    filler()
                if kt in (4, 8, 12):
                    step_pend()

        def recip_bcast(src, drow):
            P.op("dve", RECIP(rin[drow:drow + 1, :], src[drow:drow + 1, :]), reads=[b_oraw[0], b_oraw[1]], writes=[b_rin])
            P.op("pe", MM(ps[:, 7, :], onesf[drow:drow + 1, :], rin[drow:drow + 1, :], True, True),
                 reads=[b_onesf, b_rin], writes=[b_ps[7]])

        NOFILL = _os.environ.get("MK_NOFILL") == "1"

        def make_filler(gen):
            def f():
                if gen is not None and not NOFILL:
                    next(gen, None)
            return f

        def drain(gen):
            if gen is not None:
                for _ in gen:
                    pass

        for hh in range(2):
            P.dma("pool", wlat[:, hh * 4:(hh + 1) * 4, :], win_v[:, hh * 4:(hh + 1) * 4, 0:640], writes=[b_w["lat"]])
        P.op("pool", MEMSET(wkr[:], 0.0), writes=[b_w["kr"]])
        P.op("pool", MEMSET(wkrp[:], 0.0), writes=[b_w["kr"]])
        P.dma("pool", wkr[:, :, 64:96], win_v[:, :, 640:672], writes=[b_w["kr"]])
        P.dma("pool", wkrp[:, :, 64:80], win_v[:, :, 656:672], writes=[b_w["kr"]])
        P.dma("pool", wkrp[:, :, 80:96], win_v[:, :, 640:656], writes=[b_w["kr"]])
        P.dma("pool", wuq[:], wuq_d.rearrange("(kc p) n -> p kc n", p=128), writes=[b_w["uq"]])
        P.op("pool", MEMSET(wuqp[:], 0.0), writes=[b_w["uqp"]])
        wuq_v4 = wuq_d.rearrange("(kc p) (h d) -> p kc h d", p=128, d=96)
        for kc in range(3):
            P.dma("pool", wuqp[:, kc, :, 64:80], wuq_v4[:, kc, :, 80:96], writes=[b_w["uqp"]])
            P.dma("pool", wuqp[:, kc, :, 80:96], wuq_v4[:, kc, :, 64:80], writes=[b_w["uqp"]])
        wukv_v4 = wukv_d.rearrange("(kc p) (h d) -> p kc h d", p=128, d=128)
        wukv_s = wukv.rearrange("p kc (h d) -> p kc h d", d=64)
        for kc in range(2):
            P.dma("pool", wukk[:, kc, :, :], wukv_v4[:, kc, :, 0:64], writes=[b_w["ukk"]])
            P.dma("pool", wukv_s[:, kc, :, :], wukv_v4[:, kc, :, 64:128], writes=[b_w["ukv"]])

        build_tables("freqa", "signa")
        P.retire(b_tt)
        b_cq = [[P.buf("cq") for g in range(NG)] for c in range(3)]
        b_ckv = [[P.buf("ckv") for g in range(NG)] for c in range(2)]
        b_vh = P.bufs("vh", 2)
        b_qh = P.bufs("qh", 2)
        b_kh = P.bufs("kh", 2)

        zst = [oraw[0], oraw[1], dn[0]]
        b_zst = [b_oraw[0], b_oraw[1], b_dn[0]]
        for g in range(NG):
            sl = slice(g * GS, (g + 1) * GS)
            for (lc0, nl, dstT, b_dst, nname, nfeat, sbank, r) in ((0, 3, cqT, b_cq, "qnorm", 384.0, 6, 0), (3, 2, ckvT, b_ckv, "kvnorm", 256.0, 7, 1)):
                for l in range(nl):
                    bk = 4 + l % 2
                    for kc in range(KC):
                        P.op("pe", MM(ps[:, bk, :], wlat[:, kc, (lc0 + l) * 128:(lc0 + l + 1) * 128], nT[:, kc, sl], kc == 0, kc == KC - 1),
                             reads=[b_w["lat"], b_n[kc][g]], writes=[b_ps[bk]], inc=(kc == KC - 1))
                    P.op("act", ACT(zst[l][:], ps[:, bk, :], AF.Copy), reads=[b_ps[bk]], writes=[b_zst[l]])
                    P.op("act", ACT(sqs[l % 2][:], ps[:, bk, :], AF.Square), reads=[b_ps[bk]], writes=[b_sqs[l % 2]])
                    P.op("pe", MM(ps[:, sbank, :], ones[:], sqs[l % 2][:], l == 0, l == nl - 1),
                         reads=[b_const, b_sqs[l % 2]], writes=[b_ps[sbank]])
                rstd_from(sbank, slice(0, 128), nfeat, r)
                for l in range(nl):
                    P.op("dve", STT(dstT[:, l, sl], zst[l][:], pvb(nname, l), rsb[r][:], ALU.mult, ALU.mult),
                         reads=[b_zst[l], b_pvT, b_rsb[r]], writes=[b_dst[l][g]])
            for (wt, bk) in ((wkr, 4), (wkrp, 5)):
                for kc in range(KC):
                    P.op("pe", MM(ps[0:96, bk, :], wt[:, kc, :], nT[:, kc, sl], kc == 0, kc == KC - 1),
                         reads=[b_w["kr"], b_n[kc][g]], writes=[b_ps[bk]], inc=(kc == KC - 1))
            R96 = slice(0, 96)
            P.op("act", ACT(sqpe[R96, sl], ps[R96, 4, :], AF.Square), reads=[b_ps[4]], writes=[b_kpe[g]])
            P.op("dve", STT(fsc[0][R96, :], ps[R96, 4, :], pvb("kg")[R96, :], cosF[R96, sl], ALU.mult, ALU.mult),
                 reads=[b_ps[4], b_pvT, b_tab, b_kpe[g]], writes=[b_fsc[0]])
            P.op("dve", STT(fsc[1][R96, :], ps[R96, 5, :], pvb("kgp")[R96, :], sinF[R96, sl], ALU.mult, ALU.mult),
                 reads=[b_ps[5], b_pvT, b_tab], writes=[b_fsc[1]])
            P.op("dve", TT(kpeR[R96, sl], fsc[0][R96, :], fsc[1][R96, :], ALU.add), reads=[b_fsc[0], b_fsc[1]], writes=[b_kpe[g]])

        R96 = slice(0, 96)

        def mla_prep(h):
            i = h % 2
            kindB = (h % 2 == 1)
            for g in range(NG):
                sl = slice(g * GS, (g + 1) * GS)
                Mq = 128 if h < 7 else 96
                Mk = 128 if h < 7 else 64
                for kc in range(3):
                    P.op("pe", MM(ps[0:Mq, 4, :], wuq[:, kc, h * 96:h * 96 + Mq], cqT[:, kc, sl], kc == 0, kc == 2),
                         reads=[b_w["uq"], b_cq[kc][g]], writes=[b_ps[4]], inc=(kc == 2))
                for kc in range(3):
                    P.op("pe", MM(ps[0:Mq, 5, :], wuqp_f[:, kc, h * 96:h * 96 + Mq], cqT[:, kc, sl], kc == 0, kc == 2),
                         reads=[b_w["uqp"], b_cq[kc][g]], writes=[b_ps[5]], inc=(kc == 2))
                yield from qprep_gen(i, 96, 1, 96.0, pvb("qg"), pvb("qgp"), qh[i][R96, sl], b_qh[i], sl, g)
                for kc in range(2):
                    P.op("pe", MM(ps[0:Mk, 5, :], wukk_f[:, kc, h * 64:h * 64 + Mk], ckvT[:, kc, sl], kc == 0, kc == 1),
                         reads=[b_w["ukk"], b_ckv[kc][g]], writes=[b_ps[5]], inc=(kc == 1))
                P.op("act", ACT(sqs[1][0:64, :], ps[0:64, 5, :], AF.Square), reads=[b_ps[5]], writes=[b_sqs[1]])
                P.op("dve", TS(kh[i][0:64, sl], ps[0:64, 5, :], pvb("kg")[0:64, :], None, ALU.mult),
                     reads=[b_ps[5], b_pvT, b_sqs[1]], writes=[b_kh[i]])
                P.op("dve", CP(kh[i][64:96, sl], kpeR[64:96, sl]), reads=[b_kpe[g]], writes=[b_kh[i]])
                yield

                def kstat(j, out_ap, g=g):
                    P.op("pe", MM(out_ap, sqs[1][0:64, j * 128:(j + 1) * 128], ones[0:64, 0:1], True, False),
                         reads=[b_sqs[1], b_const], writes=[b_ps[6]], inc=False)
                    P.op("pe", MM(out_ap, sqpe[64:96, g * GS + j * 128:g * GS + (j + 1) * 128], ones[64:96, 0:1], False, True),
                         reads=[b_kpe[g], b_const], writes=[b_ps[6]])
                tiny_rstd(i, "k", g, 1, 96.0, 96.0, kstat)
                yield
            P.op("pool", MEMSET(vh[i][:], 0.0), writes=[b_vh[i]])
            onescol = 32 if kindB else 64
            c0 = 64 if kindB else 0
            P.op("pool", MEMSET(vh[i][:, :, onescol:onescol + 1], 1.0), writes=[b_vh[i]])
            for t0 in range(0, 16, 8):
                for tt in range(t0, t0 + 8):
                    for kc in range(2):
                        P.op("pe", MM(ps[:, 5, (tt - t0) * 64:(tt - t0 + 1) * 64], ckvT[:, kc, tt * 128:(tt + 1) * 128],
                                      wukv[:, kc, h * 64:(h + 1) * 64], kc == 0, kc == 1),
                             reads=[b_ckv[kc][tt // 4], b_w["ukv"]], writes=[b_ps[5]], inc=(tt == t0 + 7 and kc == 1))
                P.op("dve", CP(vh[i][:, t0:t0 + 8, c0:c0 + 64], ps[:, 5, :].rearrange("p (a b) -> p a b", b=64)),
                     reads=[b_ps[5]], writes=[b_vh[i]])
                yield

        def mla_attn(h, filler):
            i = h % 2
            kindB = (h % 2 == 1)
            if kindB:
                vt = (lambda kt, i=i: vh[i][:, kt, 0:128], 128, 3, b_vh[i])
                drow, ob = 32, 1
                drows = slice(64, 128)
            else:
                vt = (lambda kt, i=i: vh[i][:, kt, 0:128], 128, 2, b_vh[i])
                drow, ob = 64, 0
                drows = slice(0, 64)
            for qc in range(NG):
                qsl = slice(qc * GS, (qc + 1) * GS)
                attention(kh[i], b_kh[i], qh[i], b_qh[i], R96, (lambda kt, i=i: (rk_t[i][:, kt:kt + 1], b_rk[i])), [vt], qc, filler, (1, 3, 5, 7, 9, 11, 13))
                if kindB:
                    P.op("dve", CP(oraw[ob][32:33, :], ps[32:33, 3, :]), reads=[b_ps[3]], writes=[b_oraw[ob]])
                    P.op("dve", CP(oraw[ob][64:128, :], ps[64:128, 3, :]), reads=[b_ps[3]], writes=[b_oraw[ob]])
                else:
                    P.op("dve", CP(oraw[ob][0:65, :], ps[0:65, 2, :]), reads=[b_ps[2]], writes=[b_oraw[ob]])
                P.op("dve", RECIP(rin[drow:drow + 1, :], oraw[ob][drow:drow + 1, :]), reads=[b_oraw[ob]], writes=[b_rin])

                def epi(ob=ob, drow=drow, drows=drows, qsl=qsl, qc=qc):
                    P.op("pe", MM(ps[:, 7, :], onesf[drow:drow + 1, :], rin[drow:drow + 1, :], True, True),
                         reads=[b_onesf, b_rin], writes=[b_ps[7]])
                    P.op("dve", TT(oaT[drows, h // 2, qsl], oraw[ob][drows, :], ps[drows, 7, :], ALU.mult),
                         reads=[b_oraw[ob], b_ps[7]], writes=[b_oa[h // 2][qc]])
                    yield
                pend.append(epi())

        wdh = [A.view(M2 + i * 10240, BF16, [KC, 640]) for i in range(2)]
        vb = [A.view(M2 + 26624 + i * 6400, BF16, [16, 200]) for i in range(2)]
        QB0, KB0, VB0 = 672, 1184, 1696
        dstate = {}

        def diff_setup():
            P.retire(*[b for row in b_cq for b in row], *[b for row in b_ckv for b in row], *b_vh, *b_w.values(), *b_kpe)
            nonlocal_b_tt = P.buf("tabtmp2")
            dstate["b_tt"] = nonlocal_b_tt
            yield

        def build_tables2(fname, sname, bt):
            P.dma("sp", posi[:], pos_d.partition_broadcast(128), writes=[bt])
            P.op("dve", CP(tmpf[:], posi[:]), reads=[bt], writes=[bt])
            P.op("dve", TS(tmpf[:], tmpf[:], pvb(fname), None, ALU.mult), reads=[bt, b_pvT], writes=[bt])
            yield
            P.op("dve", TS(cosF[:], tmpf[:], 1.0 / TWO_PI, None, ALU.mult), reads=[bt], writes=[b_tab])
            P.op("dve", CP(kfi[:], cosF[:]), reads=[b_tab], writes=[bt])
            P.op("dve", CP(cosF[:], kfi[:]), reads=[bt], writes=[b_tab])
            yield
            P.op("dve", STT(tmpf[:], cosF[:], -C1, tmpf[:], ALU.mult, ALU.add), reads=[b_tab, bt], writes=[bt])
            P.op("dve", STT(tmpf[:], cosF[:], -C2, tmpf[:], ALU.mult, ALU.add), reads=[b_tab, bt], writes=[bt])
            P.op("dve", TS(tmpf[:], tmpf[:], math.pi, -math.pi, ALU.min, ALU.max), reads=[bt], writes=[bt])
            P.op("act", ACT(sinF[:], tmpf[:], AF.Sin, scale=pvb(sname)), reads=[bt, b_pvT], writes=[b_tab])
            yield
            P.op("dve", TS(tmpf[:], tmpf[:], math.pi / 2, None, ALU.add), reads=[bt], writes=[bt])
            P.op("dve", TS(kff[:], tmpf[:], math.pi, -TWO_PI, ALU.is_gt, ALU.mult), reads=[bt], writes=[bt])
            P.op("dve", TT(tmpf[:], tmpf[:], kff[:], ALU.add), reads=[bt], writes=[bt])
            P.op("dve", TS(tmpf[:], tmpf[:], math.pi, -math.pi, ALU.min, ALU.max), reads=[bt], writes=[bt])
            P.op("act", ACT(cosF[:], tmpf[:], AF.Sin), reads=[bt], writes=[b_tab])
            yield

        def diff_prep(h):
            i = h % 2
            if h == 0:
                P.retire(*[b for row in b_cq for b in row], *[b for row in b_ckv for b in row], *b_vh, *b_w.values(), *b_kpe)
                bt = P.buf("tabtmp2")
                yield from build_tables2("freqb", "signb", bt)
                P.retire(bt)
                dstate["b_wdh"] = P.bufs("wdh", 2)
                dstate["b_vb"] = P.bufs("vb", 2)
                P.op("dve", TT(der[:, 74:75], pvb("lq1"), pvb("lk1"), ALU.mult), reads=[b_pvT], writes=[b_der])
                P.op("dve", TT(der[:, 75:76], pvb("lq2"), pvb("lk2"), ALU.mult), reads=[b_pvT], writes=[b_der])
                yield
                P.op("pe", MM(ps[:, 6, 0:2], onesf[:], der[:, 74:76], True, True), reads=[b_onesf, b_der], writes=[b_ps[6]])
                P.op("act", ACT(der[:, 74:76], ps[:, 6, 0:2], AF.Exp), reads=[b_ps[6]], writes=[b_der])
                P.op("dve", TT(der[:, 72:73], der[:, 75:76], der[:, 74:75], ALU.subtract), reads=[b_der], writes=[b_der])
                P.op("dve", TS(der[:, 72:73], der[:, 72:73], -LAMBDA_INIT, None, ALU.add), reads=[b_der], writes=[b_der])
                P.op("dve", TS(der[:, 73:74], pvb("subln"), 1.0 - LAMBDA_INIT, None, ALU.mult), reads=[b_pvT], writes=[b_der])
            b_wdh, b_vb = dstate["b_wdh"], dstate["b_vb"]

            def diff_load(hh):
                ii = hh % 2
                P.op("pool", MEMSET(wdh[ii][:, :, 128:256], 0.0), writes=[b_wdh[ii]])
                P.op("pool", MEMSET(wdh[ii][:, :, 384:512], 0.0), writes=[b_wdh[ii]])
                for (src0, d0) in ((QB0, 0), (KB0, 256)):
                    s0 = src0 + hh * 128
                    P.dma("pool", wdh[ii][:, :, d0:d0 + 128], win_v[:, :, s0:s0 + 128], writes=[b_wdh[ii]])
                    for base in (0, 64):
                        P.dma("pool", wdh[ii][:, :, d0 + 128 + base:d0 + 128 + base + 8], win_v[:, :, s0 + base + 8:s0 + base + 16], writes=[b_wdh[ii]])
                        P.dma("pool", wdh[ii][:, :, d0 + 128 + base + 8:d0 + 128 + base + 16], win_v[:, :, s0 + base:s0 + base + 8], writes=[b_wdh[ii]])
                P.dma("pool", wdh[ii][:, :, 512:640], win_v[:, :, VB0 + hh * 128:VB0 + (hh + 1) * 128], writes=[b_wdh[ii]])

            if h == 0:
                diff_load(0)
                diff_load(1)
            elif h + 1 < 4:
                diff_load(h + 1)
            P.op("pool", MEMSET(vb[i][:], 0.0), writes=[b_vb[i]])
            P.op("pool", MEMSET(vb[i][:, :, 64:65], 1.0), writes=[b_vb[i]])
            P.op("pool", MEMSET(vb[i][:, :, 72 + 32:72 + 33], 1.0), writes=[b_vb[i]])
            yield
            for t0 in range(0, 16, 4):
                for tt in range(t0, t0 + 4):
                    for kc in range(KC):
                        P.op("pe", MM(ps[:, 5, (tt - t0) * 128:(tt - t0 + 1) * 128], nT[:, kc, tt * 128:(tt + 1) * 128],
                                      wdh[i][:, kc, 512:640], kc == 0, kc == KC - 1),
                             reads=[b_n[kc][tt // 4], b_wdh[i]], writes=[b_ps[5]], inc=(tt == t0 + 3 and kc == KC - 1))
                pv = ps[:, 5, :].rearrange("p (a b) -> p a b", b=128)
                P.op("dve", CP(vb[i][:, t0:t0 + 4, 0:64], pv[:, :, 0:64]), reads=[b_ps[5]], writes=[b_vb[i]])
                P.op("dve", CP(vb[i][:, t0:t0 + 4, 136:200], pv[:, :, 64:128]), reads=[b_ps[5]], writes=[b_vb[i]])
                yield
            for g in range(NG):
                sl = slice(g * GS, (g + 1) * GS)
                for (d0, gname, gpname, dst, b_dst) in ((0, "dqg", "dqgp", qh[i], b_qh[i]), (256, "dkg", "dkgp", kh[i], b_kh[i])):
                    for kc in range(KC):
                        P.op("pe", MM(ps[:, 4, :], wdh[i][:, kc, d0:d0 + 128], nT[:, kc, sl], kc == 0, kc == KC - 1),
                             reads=[b_wdh[i], b_n[kc][g]], writes=[b_ps[4]], inc=(kc == KC - 1))
                    for kc in range(KC):
                        P.op("pe", MM(ps[:, 5, :], wdh[i][:, kc, d0 + 128:d0 + 256], nT[:, kc, sl], kc == 0, kc == KC - 1),
                             reads=[b_wdh[i], b_n[kc][g]], writes=[b_ps[5]], inc=(kc == KC - 1))
                    if d0 == 0:
                        yield from qprep_gen(i, 128, 2, 64.0, pvb(gname), pvb(gpname), dst[:, sl], b_dst, sl, g)
                    else:
                        P.op("act", ACT(sqs[1][:], ps[:, 4, :], AF.Square), reads=[b_ps[4]], writes=[b_sqs[1]])
                        rope_parts(128, pvb(gname), pvb(gpname), sl, dst[:, sl], b_dst, after=[b_sqs[1]])
                        yield

                        def kstat2(j, out_ap):
                            P.op("pe", MM(out_ap, sqs[1][:, j * 128:(j + 1) * 128], sel2b[:, 0:2], True, True),
                                 reads=[b_sqs[1], b_onesf], writes=[b_ps[6]])
                        tiny_rstd(i, "k", g, 2, 64.0, 64.0, kstat2)
                        yield

        def diff_attn(h, filler):
            i = h % 2
            b_vb = dstate["b_vb"]
            RA = slice(0, 128)
            vtA = (lambda kt, i=i: vb[i][:, kt, 0:128], 128, 2, b_vb[i])
            vtB = (lambda kt, i=i: vb[i][:, kt, 72:200], 128, 3, b_vb[i])
            for qc in range(NG):
                qsl = slice(qc * GS, (qc + 1) * GS)
                for comp in range(2):
                    rows = slice(comp * 64, comp * 64 + 64)
                    attention(kh[i], b_kh[i], qh[i], b_qh[i], rows, (lambda kt, i=i, comp=comp: (rk_t[i][:, kt * 2 + comp:kt * 2 + comp + 1], b_rk[i])), [vtA, vtB], qc, filler, (3, 7, 11, 15))
                    P.op("dve", CP(oraw[0][0:65, :], ps[0:65, 2, :]), reads=[b_ps[2]], writes=[b_oraw[0]])
                    P.op("dve", CP(oraw[1][64:128, :], ps[64:128, 3, :]), reads=[b_ps[3]], writes=[b_oraw[1]])
                    P.op("dve", RECIP(rin[64:65, :], oraw[0][64:65, :]), reads=[b_oraw[0]], writes=[b_rin])

                    def epi(comp=comp, qsl=qsl, qc=qc):
                        P.op("pe", MM(ps[:, 7, :], onesf[64:65, :], rin[64:65, :], True, True),
                             reads=[b_onesf, b_rin], writes=[b_ps[7]])
                        P.op("dve", TT(dn[comp][0:64, :], oraw[0][0:64, :], ps[0:64, 7, :], ALU.mult),
                             reads=[b_oraw[0], b_ps[7]], writes=[b_dn[comp]])
                        P.op("dve", TT(dn[comp][64:128, :], oraw[1][64:128, :], ps[64:128, 7, :], ALU.mult),
                             reads=[b_oraw[1], b_ps[7]], writes=[b_dn[comp]])
                        if comp == 0:
                            yield
                            return
                        P.op("dve", STT(od_e[:], dn[1][:], der[:, 72:73], dn[0][:], ALU.mult, ALU.add),
                             reads=[b_dn[0], b_dn[1], b_der], writes=[b_ode])
                        P.op("act", ACT(sq_e[:], od_e[:], AF.Square), reads=[b_ode], writes=[b_sqe])
                        yield
                        P.op("pe", MM(ps[:, 7, :], ones[:], sq_e[:], True, True), reads=[b_const, b_sqe], writes=[b_ps[7]])
                        P.op("act", ACT(rs_e[:], ps[:, 7, :], AF.Sqrt, bias=epsc[:], scale=1.0 / 128.0),
                             reads=[b_ps[7], b_const], writes=[b_rse])
                        P.op("dve", RECIP(rs_e[:], rs_e[:]), reads=[b_rse], writes=[b_rse])
                        yield
                        P.op("dve", STT(obT[:, h, qsl], od_e[:], der[:, 73:74], rs_e[:], ALU.mult, ALU.mult),
                             reads=[b_ode, b_der, b_rse], writes=[b_ob[h][qc]])
                        yield
                    pend.append(epi())

        units = [("mla", h) for h in range(8)] + [("diff", h) for h in range(4)]

        def prep_of(u):
            return mla_prep(u[1]) if u[0] == "mla" else diff_prep(u[1])

        NU = int(_os.environ.get("MK_UNITS", "12"))
        if NU < 12:
            for c in range(4):
                for g in range(NG):
                    P.op("pool", MEMSET(oaT[:, c, g * GS:(g + 1) * GS], 0.0), writes=[b_oa[c][g]])
                    P.op("pool", MEMSET(obT[:, c, g * GS:(g + 1) * GS], 0.0), writes=[b_ob[c][g]])
            if NU <= 8:
                dstate["b_wdh"] = P.bufs("wdh", 2)
                dstate["b_vb"] = P.bufs("vb", 2)
        units = units[:NU]
        def warm(n):
            for k in range(n):
                P.op("pe", MM(ps[:, 7, :], ones[:], nT[:, k % KC, 0:GS], True, True),
                     reads=[b_const, b_n[k % KC][0]], writes=[b_ps[7]], inc=(k == n - 1))

        NWARM = int(_os.environ.get("MK_WARM", "0"))
        drain(prep_of(units[0]))
        for ui, u in enumerate(units):
            if NWARM:
                warm(NWARM)
            gen = prep_of(units[ui + 1]) if ui + 1 < len(units) else None
            if NOFILL and _os.environ.get("MK_PREFIRST") == "1":
                drain(gen)
            filler = make_filler(gen)
            if u[0] == "mla":
                mla_attn(u[1], filler)
            else:
                diff_attn(u[1], filler)
            drain(gen)
        while pend:
            step_pend()
        b_wdh, b_vb = dstate["b_wdh"], dstate["b_vb"]

        P.retire(*b_wdh, *b_vb, *b_qh, *b_kh, *b_PT, *b_oraw, *b_dn, b_rin, b_onesf, b_tab, b_sqe, b_ode, b_rse, *b_rq, *b_rk, *b_t4, *b_dg)
        woa = A.view(M2 + 0, BF16, [4, D])
        wob = A.view(M2 + 8192, BF16, [4, D])
        wout = A.view(M2 + 16384, BF16, [KC, D])
        wgt = [A.view(M2 + 32768 + i * 4096, BF16, [KC, 256]) for i in range(2)]
        hst = [A.view(M2 + 40960 + i * 2048, F32, [GS]) for i in range(2)]
        gA = A.view(M2 + 45056, F32, [GS])
        gB = A.view(M2 + 47104, F32, [GS])
        mT = A.view(M2 + 49152, BF16, [KC, T])
        b_woa, b_wob, b_wout = P.buf("woa"), P.buf("wob"), P.buf("wout")
        b_wgt = P.bufs("wgt", 2)
        b_hst = P.bufs("hst", 2)
        b_gA, b_gB = P.buf("gA"), P.buf("gB")
        b_m = [[P.buf("m") for g in range(NG)] for c in range(KC)]
        P.dma("pool", woa[:], woa_d.rearrange("(c p) n -> p c n", p=128), writes=[b_woa])
        P.dma("pool", wob[:], wob_d.rearrange("(c p) n -> p c n", p=128), writes=[b_wob])
        for hh in range(2):
            P.dma("pool", wout[:, hh * 4:(hh + 1) * 4, :], wout_d.rearrange("(c p) n -> p c n", p=128)[:, hh * 4:(hh + 1) * 4, :], writes=[b_wout])
        GA0, GB0 = 2208, 3232
        wad = A.view(M2 + 81920, BF16, [KC, D])
        b_wad = P.buf("wad")
        wada_block(5, wad, b_wad)
        late = {0: (5, 6), 2: (6, 7), 4: (7, 8), 6: (8, None)}
        nit = 0
        def load_wgt(oc_):
            wi_ = oc_ % 2
            P.dma("pool", wgt[wi_][:, :, 0:128], win_v[:, :, GA0 + oc_ * 128:GA0 + (oc_ + 1) * 128], writes=[b_wgt[wi_]])
            P.dma("pool", wgt[wi_][:, :, 128:256], win_v[:, :, GB0 + oc_ * 128:GB0 + (oc_ + 1) * 128], writes=[b_wgt[wi_]])

        load_wgt(0)
        for oc in range(KC):
            wi = oc % 2
            if oc + 1 < KC:
                load_wgt(oc + 1)
            for g in range(NG):
                sl = slice(g * GS, (g + 1) * GS)
                b0 = 4 * (nit % 2)
                nit += 1
                for c in range(4):
                    P.op("pe", MM(ps[:, b0, :], woa[:, c, oc * 128:(oc + 1) * 128], oaT[:, c, sl], c == 0, c == 3),
                         reads=[b_woa, b_oa[c][g]], writes=[b_ps[b0]], inc=(c == 3))
                for c in range(4):
                    P.op("pe", MM(ps[:, b0 + 1, :], wob[:, c, oc * 128:(oc + 1) * 128], obT[:, c, sl], c == 0, c == 3),
                         reads=[b_wob, b_ob[c][g]], writes=[b_ps[b0 + 1]], inc=(c == 3))
                for kc in range(KC):
                    P.op("pe", MM(ps[:, b0 + 2, :], wgt[wi][:, kc, 0:128], nT[:, kc, sl], kc == 0, kc == KC - 1),
                         reads=[b_wgt[wi], b_n[kc][g]], writes=[b_ps[b0 + 2]], inc=(kc == KC - 1))
                for kc in range(KC):
                    P.op("pe", MM(ps[:, b0 + 3, :], wgt[wi][:, kc, 128:256], nT[:, kc, sl], kc == 0, kc == KC - 1),
                         reads=[b_wgt[wi], b_n[kc][g]], writes=[b_ps[b0 + 3]], inc=(kc == KC - 1))
                P.op("act", ACT(gA[:], ps[:, b0 + 2, :], AF.Sigmoid), reads=[b_ps[b0 + 2]], writes=[b_gA])
                P.op("act", ACT(gB[:], ps[:, b0 + 3, :], AF.Sigmoid), reads=[b_ps[b0 + 3]], writes=[b_gB])
                P.op("dve", TT(gA[:], gA[:], ps[:, b0, :], ALU.mult), reads=[b_gA, b_ps[b0]], writes=[b_gA])
                P.op("dve", TT(gB[:], gB[:], ps[:, b0 + 1, :], ALU.mult), reads=[b_gB, b_ps[b0 + 1]], writes=[b_gB])
                P.op("dve", TT(mT[:, oc, sl], gA[:], gB[:], ALU.add), reads=[b_gA, b_gB], writes=[b_m[oc][g]])
            if oc in late:
                mcur, mnext = late[oc]
                mod_block(mcur, wad, b_wad)
                if mnext is not None:
                    wada_block(mnext, wad, b_wad)
        derive_G("G2", 5, 1.0)
        derive_A("A3", "g3", 7, 6, "SH3")
        derive_G("G3", 8, 0.5)
        P.retire(*[b for row in b_oa for b in row], *[b for row in b_ob for b in row])
        b_h = [[P.buf("h%d_%d" % (c, g)) for g in range(NG)] for c in range(KC)]
        nit = 0
        for g in range(NG):
            sl = slice(g * GS, (g + 1) * GS)
            for oc in range(KC):
                bk = nit % 2
                hi = nit % 2
                nit += 1
                P.dma("sp", hst[hi][:], hsp[:, oc, sl], reads=[b_hsp], writes=[b_hst[hi]], owner=b_hst[hi])
                for c in range(KC):
                    P.op("pe", MM(ps[:, bk, :], wout[:, c, oc * 128:(oc + 1) * 128], mT[:, c, sl], c == 0, c == KC - 1),
                         reads=[b_wout, b_m[c][g]], writes=[b_ps[bk]], inc=(c == KC - 1))
                P.op("dve", STT(hT[:, oc, sl], ps[:, bk, :], dcol("G2", oc), hst[hi][:], ALU.mult, ALU.add),
                     reads=[b_ps[bk], b_der, b_hst[hi]], writes=[b_h[oc][g]])
        P.retire(*[b for row in b_m for b in row], b_woa, b_wob, b_wout, *b_wgt, *b_hst, b_gA, b_gB, b_wad)
        P.retire(*[b for row in b_n for b in row])
        b_n = [[P.buf("n%d_%d" % (c, g)) for g in range(NG)] for c in range(KC)]
        ring["b_wd"] = P.bufs("wdp", 2)
        ring["b_gu"] = P.bufs("gu", 2)

    if "ffn2" in stages:
        for g in range(NG):
            norm_group(g, "A3", "SH3")
        ffn(f2g_d, f2u_d, f2d_d, "G3", ring, {})

    ost = [A.view(AT_OFF + i * 4096, F32, [D]) for i in range(2)]
    b_ost = P.bufs("ost", 2)
    ytmp = [A.view(AT_OFF + 8192 + i * 2048, F32, [GS]) for i in range(8)]
    b_ytmp = P.bufs("ytmp", 8)
    b_out = P.buf("out")
    for g in range(NG):
        r = norm_group(g, None, None)
        sl = slice(g * GS, (g + 1) * GS)
        for c in range(KC):
            P.op("dve", STT(ytmp[c][:], hT[:, c, sl], pvb("gf", c), rsb[r][:], ALU.mult, ALU.mult),
                 reads=[b_h[c][g], b_pvT, b_rsb[r]], writes=[b_ytmp[c]])
        for j in range(4):
            tt = g * 4 + j
            o = tt % 2
            for half in range(2):
                bk = (tt * 2 + half) % 4
                for cc in range(4):
                    c = half * 4 + cc
                    P.op("pe", TR(ps[:, bk, cc * 128:(cc + 1) * 128], ytmp[c][:, j * 128:(j + 1) * 128], ident[:]),
                         reads=[b_ytmp[c], b_const], writes=[b_ps[bk]], inc=(cc == 3))
                P.op("act", ACT(ost[o][:, half * 512:(half + 1) * 512], ps[:, bk, :], AF.Copy),
                     reads=[b_ps[bk]], writes=[b_ost[o]])
            P.dma("sp", out_d[tt * 128:(tt + 1) * 128, :], ost[o][:], reads=[b_ost[o]], writes=[b_out], owner=b_ost[o])
    P.wait_all("sp", b_ost + [b_out])
    P.emit()
    return nc


def _prep_inputs(inp):
    L = 0
    f32 = np.float32

    def pad128(v):
        o = np.zeros(128, f32)
        o[:v.shape[0]] = v
        return o

    def mla_perm(g):
        o = np.zeros(128, f32)
        o[64:80] = g[80:96]
        o[80:96] = g[64:80]
        return o

    def diff_rep(g):
        return np.concatenate([g, g]).astype(f32)

    def diff_perm(g):
        o = np.zeros(128, f32)
        for base in (0, 64):
            o[base:base + 8] = g[8:16]
            o[base + 8:base + 16] = g[0:8]
        return o

    freqa = np.zeros(128, f32)
    signa = np.ones(128, f32)
    fa = (1.0 / (np.float32(10000.0) ** (np.arange(16, dtype=f32) / np.float32(16)))).astype(f32)
    freqa[64:80] = fa
    freqa[80:96] = fa
    signa[64:80] = -1.0
    freqb = np.zeros(128, f32)
    signb = np.ones(128, f32)
    fb = (1.0 / (np.float32(500000.0) ** (np.arange(8, dtype=f32) / np.float32(8)))).astype(f32)
    for base in (0, 64):
        freqb[base:base + 8] = fb
        freqb[base + 8:base + 16] = fb
        signb[base:base + 8] = -1.0

    rows = []
    for nm in ("ffn1_norm", "mix_norm", "ffn2_norm", "final_norm"):
        rows.append(np.asarray(inp[nm][L], f32).reshape(8, 128))
    rows.append(np.asarray(inp["mla_q_norm"][L], f32).reshape(3, 128))
    rows.append(np.asarray(inp["mla_kv_norm"][L], f32).reshape(2, 128))
    qg = np.asarray(inp["mla_q_gain"][L], f32)
    kg = np.asarray(inp["mla_k_gain"][L], f32)
    dqg = np.asarray(inp["diff_q_gain"][L], f32)
    dkg = np.asarray(inp["diff_k_gain"][L], f32)
    singles = [pad128(qg), mla_perm(qg), pad128(kg), mla_perm(kg),
               diff_rep(dqg), diff_perm(dqg), diff_rep(dkg), diff_perm(dkg),
               pad128(np.asarray(inp["diff_lambda_q1"][L], f32)), pad128(np.asarray(inp["diff_lambda_k1"][L], f32)),
               pad128(np.asarray(inp["diff_lambda_q2"][L], f32)), pad128(np.asarray(inp["diff_lambda_k2"][L], f32)),
               np.asarray(inp["diff_subln"][L], f32), freqa, signa, freqb, signb]
    rows.append(np.stack(singles))
    pvb = np.ascontiguousarray(np.concatenate(rows, axis=0), dtype=f32)
    assert pvb.shape == (NPVB, 128), pvb.shape
    shared = {
        "pvb": pvb,
        "ident": np.eye(128, dtype=f32),
        "w_ada": np.ascontiguousarray(inp["w_ada"][L], f32),
        "f1g": np.ascontiguousarray(inp["ffn1_w_gate"][L], f32),
        "f1u": np.ascontiguousarray(inp["ffn1_w_up"][L], f32),
        "f1d": np.ascontiguousarray(inp["ffn1_w_down"][L], f32),
        "f2g": np.ascontiguousarray(inp["ffn2_w_gate"][L], f32),
        "f2u": np.ascontiguousarray(inp["ffn2_w_up"][L], f32),
        "f2d": np.ascontiguousarray(inp["ffn2_w_down"][L], f32),
        "w_in": np.ascontiguousarray(inp["w_in"][L], f32),
        "w_uq": np.ascontiguousarray(inp["mla_w_uq"][L], f32),
        "w_ukv": np.ascontiguousarray(inp["mla_w_ukv"][L], f32),
        "w_oa": np.ascontiguousarray(inp["mla_w_o"][L], f32),
        "w_ob": np.ascontiguousarray(inp["diff_w_o"][L], f32),
        "w_out": np.ascontiguousarray(inp["w_out"][L], f32),
    }
    b_ada = np.asarray(inp["b_ada"][L], f32).reshape(72, 128)
    maps = []
    for b in range(8):
        m = dict(shared)
        m["x"] = np.ascontiguousarray(inp["x"][b], f32)
        m["pos"] = np.ascontiguousarray(np.asarray(inp["positions"][b]).reshape(1, T).astype(np.int32))
        m["pva"] = np.ascontiguousarray(np.concatenate([np.asarray(inp["c"][b], f32).reshape(8, 128), b_ada], axis=0))
        maps.append(m)
    return maps


_NC_CACHE = {}


def kernel(**inputs):
    maps = _prep_inputs(inputs)
    if "nc" not in _NC_CACHE:
        _NC_CACHE["nc"] = build_nc()
    nc = _NC_CACHE["nc"]
    res = run_bass_kernel_spmd(nc, maps, core_ids=list(range(8)))
    out = np.stack([np.asarray(r["out"], np.float32) for r in res.results], axis=0)
    return out
```

```python
import math
import os as _os
import numpy as np
import concourse.bass as bass
import concourse.mybir as mybir
from concourse.bass_utils import run_bass_kernel_spmd

F32 = mybir.dt.float32
BF16 = mybir.dt.bfloat16
I32 = mybir.dt.int32
AF = mybir.ActivationFunctionType
ALU = mybir.AluOpType

T = 2048
D = 1024
DFF = 2816
NG = 4
GS = 512
KC = 8
EPS = 1e-6
IN_COLS = 4256
LAMBDA_INIT = 0.8 - 0.6 * math.exp(-0.3 * 0)


class Buf:
    __slots__ = ("name", "w", "r", "dsem", "dcnt")

    def __init__(self, name):
        self.name = name
        self.w = None
        self.r = {}
        self.dsem = None
        self.dcnt = 0


class Prog:
    ENG = ("pe", "act", "dve", "pool", "sp")

    def __init__(self, nc, same_engine_sync=True):
        self.nc = nc
        self.streams = {e: [] for e in self.ENG}
        self.sems = {}
        self.cnt = {}
        self.waited = {e: {} for e in self.ENG}
        self.same = same_engine_sync
        for e in self.ENG:
            self.sems["E_" + e] = nc.alloc_semaphore("sem_e_" + e)
            self.cnt[e] = 0
        self.nd = 0
        self.retired = {}

    def new_dsem(self, name):
        k = "D_%d_%s" % (self.nd, name)
        self.nd += 1
        self.sems[k] = self.nc.alloc_semaphore("sem_d%d" % self.nd)
        return k

    def buf(self, name):
        b = Buf(name)
        b.r = dict(self.retired)
        return b

    def bufs(self, name, n):
        return [self.buf("%s%d" % (name, i)) for i in range(n)]

    def retire(self, *bufs):
        for b in bufs:
            if b.w is not None:
                s, v = b.w
                self.retired[s] = max(self.retired.get(s, 0), v)
            for s, v in b.r.items():
                self.retired[s] = max(self.retired.get(s, 0), v)

    def _waits(self, eng, reads, writes):
        waits = {}
        own = "E_" + eng

        def need(s, v):
            if s == own and (eng == "pe" or not self.same):
                return
            if self.waited[eng].get(s, 0) >= v:
                return
            if waits.get(s, 0) < v:
                waits[s] = v

        for b in reads:
            if b.w is not None:
                need(*b.w)
        for b in writes:
            if b.w is not None:
                need(*b.w)
            for s, v in b.r.items():
                need(s, v)
        for s, v in waits.items():
            self.waited[eng][s] = v
        return list(waits.items())

    def _mark(self, tok, reads, writes):
        s, v = tok
        for b in reads:
            if b.r.get(s, 0) < v:
                b.r[s] = v
        for b in writes:
            b.w = tok
            b.r = {}

    def op(self, eng, fn, reads=(), writes=(), inc=True):
        waits = self._waits(eng, reads, writes)
        own = "E_" + eng
        if inc:
            self.cnt[eng] += 1
            tok = (own, self.cnt[eng])
        else:
            tok = (own, self.cnt[eng] + 1)
        self._mark(tok, reads, writes)
        self.streams[eng].append((waits, fn, (own, 1) if inc else None))

    def dma(self, q, out_ap, in_ap, reads=(), writes=(), owner=None):
        if owner is None:
            owner = writes[0] if writes else reads[0]
        if owner.dsem is None:
            owner.dsem = self.new_dsem(owner.name)
        waits = self._waits(q, reads, writes)
        owner.dcnt += 16
        tok = (owner.dsem, owner.dcnt)
        self._mark(tok, reads, writes)
        self.streams[q].append(
            (waits, lambda e: e.dma_start(out=out_ap, in_=in_ap), (owner.dsem, 16)))

    def wait_all(self, eng, bufs):
        waits = self._waits(eng, (), bufs)
        self.streams[eng].append((waits, None, None))

    def emit(self):
        nc = self.nc
        sems = self.sems
        streams = self.streams
        with nc.Block() as block:
            def run(e, name):
                for waits, fn, inc in streams[name]:
                    for s, v in waits:
                        e.wait_ge(sems[s], v)
                    if fn is None:
                        continue
                    ins = fn(e)
                    if inc is not None:
                        ins.then_inc(sems[inc[0]], inc[1])

            @block.tensor
            def _(e):
                run(e, "pe")

            @block.scalar
            def _(e):
                run(e, "act")

            @block.vector
            def _(e):
                run(e, "dve")

            @block.gpsimd
            def _(e):
                run(e, "pool")

            @block.sync
            def _(e):
                run(e, "sp")


class Arena:
    def __init__(self, nc, reserve=1024):
        rem = nc.sbuf_bytes_remaining
        self.nbytes = ((rem - reserve) // 64) * 64
        self.t = nc.alloc_sbuf_tensor("arena", [128, self.nbytes // 4], F32)

    def view(self, off, dtype, shape):
        assert off % 4 == 0
        esz = 2 if dtype == BF16 else 4
        n = 1
        for s in shape:
            n *= s
        nb = n * esz
        assert nb % 4 == 0 and off + nb <= self.nbytes, (off, nb, self.nbytes)
        a = self.t[:, off // 4:(off + nb) // 4]
        if dtype != F32:
            a = a.bitcast(dtype)
        if len(shape) == 2:
            a = a.rearrange("p (a b) -> p a b", a=shape[0])
        elif len(shape) == 3:
            a = a.rearrange("p (a b c) -> p a b c", a=shape[0], b=shape[1])
        return a


def MM(out, lhsT, rhs, start, stop):
    return lambda e: e.matmul(out, lhsT=lhsT, rhs=rhs, start=start, stop=stop)


def TR(out, in_, ident):
    return lambda e: e.transpose(out, in_, ident)


def ACT(out, in_, func, bias=None, scale=None):
    kw = {}
    if bias is not None:
        kw["bias"] = bias
    if scale is not None:
        kw["scale"] = scale
    return lambda e: e.activation(out=out, in_=in_, func=func, **kw)


def TT(out, in0, in1, op):
    return lambda e: e.tensor_tensor(out=out, in0=in0, in1=in1, op=op)


def TS(out, in0, s1, s2, op0, op1=None):
    if op1 is None:
        return lambda e: e.tensor_scalar(out=out, in0=in0, scalar1=s1, scalar2=None, op0=op0)
    return lambda e: e.tensor_scalar(out=out, in0=in0, scalar1=s1, scalar2=s2, op0=op0, op1=op1)


def STT(out, in0, scalar, in1, op0, op1):
    return lambda e: e.scalar_tensor_tensor(out=out, in0=in0, scalar=scalar, in1=in1, op0=op0, op1=op1)


def CP(out, in_):
    return lambda e: e.tensor_copy(out=out, in_=in_)


def RECIP(out, in_):
    return lambda e: e.reciprocal(out=out, in_=in_)


def MEMSET(ap, v):
    return lambda e: e.memset(ap, v)


PVB = {}
_n = 0
for _name, _cnt in [("g1", 8), ("g2", 8), ("g3", 8), ("gf", 8), ("qnorm", 3), ("kvnorm", 2),
                    ("qg", 1), ("qgp", 1), ("kg", 1), ("kgp", 1), ("dqg", 1), ("dqgp", 1),
                    ("dkg", 1), ("dkgp", 1), ("lq1", 1), ("lk1", 1), ("lq2", 1), ("lk2", 1),
                    ("subln", 1), ("freqa", 1), ("signa", 1), ("freqb", 1), ("signb", 1)]:
    PVB[_name] = _n
    _n += _cnt
NPVB = _n
NPVA = 80


def build_nc(stages=("ffn1", "mix", "ffn2"), debug=None):
    nc = bass.Bass("TRN2", target_bir_lowering=False)
    P = Prog(nc)
    A = Arena(nc)

    def dram_in(name, shape, dt=F32):
        return nc.dram_tensor(name, shape, dt, kind="ExternalInput").ap()

    x_d = dram_in("x", [T, D])
    pos_d = dram_in("pos", [1, T], I32)
    pva_d = dram_in("pva", [NPVA, 128])
    pvb_d = dram_in("pvb", [NPVB, 128])
    ident_d = dram_in("ident", [128, 128])
    wada_d = dram_in("w_ada", [D, 9 * D])
    f1g_d = dram_in("f1g", [D, DFF])
    f1u_d = dram_in("f1u", [D, DFF])
    f1d_d = dram_in("f1d", [DFF, D])
    f2g_d = dram_in("f2g", [D, DFF])
    f2u_d = dram_in("f2u", [D, DFF])
    f2d_d = dram_in("f2d", [DFF, D])
    win_d = dram_in("w_in", [D, IN_COLS])
    wuq_d = dram_in("w_uq", [384, 768])
    wukv_d = dram_in("w_ukv", [256, 1024])
    woa_d = dram_in("w_oa", [512, D])
    wob_d = dram_in("w_ob", [512, D])
    wout_d = dram_in("w_out", [D, D])
    out_d = nc.dram_tensor("out", [T, D], F32, kind="ExternalOutput").ap()
    dbg_d = None
    if debug is not None:
        dbg_d = nc.dram_tensor("dbg", [128] + list(debug[1]), F32, kind="ExternalOutput").ap()

    H_OFF = 0
    MISC = 65536
    NT_OFF = MISC + 14336
    R_OFF = NT_OFF + 32768
    R_SIZE = A.nbytes - R_OFF
    assert R_SIZE >= 98304, R_SIZE

    hT = A.view(H_OFF, F32, [KC, T])
    nT = A.view(NT_OFF, BF16, [KC, T])
    ident = A.view(MISC + 0, F32, [128])
    ones = A.view(MISC + 512, BF16, [128])
    ones2 = A.view(MISC + 768, BF16, [128])
    pvT = A.view(MISC + 1024, F32, [256])
    modT = A.view(MISC + 2048, F32, [72])
    der = A.view(MISC + 2560, F32, [128])
    sqs = [A.view(MISC + 3072 + i * 1024, BF16, [GS]) for i in range(2)]
    fsc = [A.view(MISC + 5120 + i * 2048, F32, [GS]) for i in range(2)]
    rsb = [A.view(MISC + 9216 + i * 2048, F32, [GS]) for i in range(2)]
    condb = A.view(MISC + 13312, BF16, [8])
    ps = nc.alloc_psum_tensor("ps", [128, 8, GS], F32)

    b_h = [[P.buf("h%d_%d" % (c, g)) for g in range(NG)] for c in range(KC)]
    b_n = [[P.buf("n%d_%d" % (c, g)) for g in range(NG)] for c in range(KC)]
    b_ps = P.bufs("ps", 8)
    b_const = P.buf("const")
    b_pvT = P.buf("pvT")
    b_mod = P.buf("mod")
    b_der = P.buf("der")
    b_sqs = P.bufs("sqs", 2)
    b_fsc = P.bufs("fsc", 2)
    b_rsb = P.bufs("rsb", 2)
    b_condb = P.buf("condb")

    def pvb(name, i=0):
        c = 80 + PVB[name] + i
        return pvT[:, c:c + 1]

    DER = {"A1": 0, "G1": 8, "A2": 16, "G2": 24, "A3": 32, "G3": 40, "SH1": 48, "SH2": 56, "SH3": 64,
           "nlam": 72, "subs": 73, "lamt": 74}

    def dcol(name, i=0):
        c = DER[name] + i
        return der[:, c:c + 1]

    P.dma("sp", ident[:], ident_d, writes=[b_const])
    P.op("pool", MEMSET(ones[:], 1.0), writes=[b_const])
    P.op("pool", MEMSET(ones2[:], 0.0), writes=[b_const])
    P.op("pool", MEMSET(ones2[0:64, 0:64], 1.0), writes=[b_const])
    P.op("pool", MEMSET(ones2[64:128, 64:128], 1.0), writes=[b_const])
    pstage = A.view(R_OFF + 32768, F32, [2, 128])
    b_pstage = P.buf("pstage")
    P.op("pool", MEMSET(pstage[:], 0.0), writes=[b_pstage])
    P.dma("sp", pstage[0:NPVA, 0, :], pva_d, writes=[b_pstage])
    P.dma("sp", pstage[0:NPVB, 1, :], pvb_d, writes=[b_pstage])
    P.op("pe", TR(ps[:, 7, 0:NPVA], pstage[0:NPVA, 0, :], ident[0:NPVA, 0:NPVA]),
         reads=[b_pstage, b_const], writes=[b_ps[7]], inc=False)
    P.op("pe", TR(ps[:, 7, 128:128 + NPVB], pstage[0:NPVB, 1, :], ident[0:NPVB, 0:NPVB]),
         reads=[b_pstage, b_const], writes=[b_ps[7]])
    P.op("dve", CP(pvT[:, 0:NPVA], ps[:, 7, 0:NPVA]), reads=[b_ps[7]], writes=[b_pvT])
    P.op("dve", CP(pvT[:, 80:80 + NPVB], ps[:, 7, 128:128 + NPVB]), reads=[b_ps[7]], writes=[b_pvT])
    P.retire(b_pstage)
    P.op("act", ACT(condb[:], pvT[:, 0:8], AF.Silu), reads=[b_pvT], writes=[b_condb])

    ring = {"wd_i": 0, "gu_i": 0, "b_wd": P.bufs("wdp", 2), "b_gu": P.bufs("gu", 2)}

    AT_OFF = R_OFF
    GU_OFF = R_OFF + 32768
    WD_OFF = R_OFF + 65536

    def wada_block(m, dst, b_dst):
        src = wada_d.rearrange("(kc p) n -> p kc n", p=128)[:, :, m * 1024:(m + 1) * 1024]
        for hh in range(2):
            P.dma("pool", dst[:, hh * 4:(hh + 1) * 4, :], src[:, hh * 4:(hh + 1) * 4, :], writes=[b_dst])

    def mod_block(m, wt, b_wt):
        for j in range(8):
            for kc in range(KC):
                P.op("pe", MM(ps[:, 6, m * 8 + j:m * 8 + j + 1], wt[:, kc, j * 128:(j + 1) * 128],
                              condb[:, kc:kc + 1], kc == 0, kc == KC - 1),
                     reads=[b_wt, b_condb], writes=[b_ps[6]], inc=(j == 7 and kc == KC - 1))
        P.op("dve", TT(modT[:, m * 8:(m + 1) * 8], ps[:, 6, m * 8:(m + 1) * 8],
                       pvT[:, 8 + m * 8:8 + (m + 1) * 8], ALU.add),
             reads=[b_ps[6], b_pvT], writes=[b_mod])

    def derive_A(dst, gname, m_sc, m_sh, shname):
        P.op("dve", STT(der[:, DER[dst]:DER[dst] + 8], modT[:, m_sc * 8:(m_sc + 1) * 8], 1.0,
                        pvT[:, 80 + PVB[gname]:80 + PVB[gname] + 8], ALU.add, ALU.mult),
             reads=[b_mod, b_pvT], writes=[b_der])
        P.op("dve", CP(der[:, DER[shname]:DER[shname] + 8], modT[:, m_sh * 8:(m_sh + 1) * 8]),
             reads=[b_mod], writes=[b_der])

    def derive_G(dst, m_gt, scale):
        P.op("dve", TS(der[:, DER[dst]:DER[dst] + 8], modT[:, m_gt * 8:(m_gt + 1) * 8], float(scale), None, ALU.mult),
             reads=[b_mod], writes=[b_der])

    def norm_group(g, Aname, SHname, sq_from_psum=None):
        sl = slice(g * GS, (g + 1) * GS)
        sb = 6 + (g % 2)
        for c in range(KC):
            i = c % 2
            P.op("act", ACT(sqs[i][:], hT[:, c, sl], AF.Square), reads=[b_h[c][g]], writes=[b_sqs[i]])
            P.op("pe", MM(ps[:, sb, :], ones[:], sqs[i][:], c == 0, c == KC - 1),
                 reads=[b_const, b_sqs[i]], writes=[b_ps[sb]], inc=True)
        r = g % 2
        P.op("act", ACT(rsb[r][:], ps[:, sb, :], AF.Sqrt, bias=epsc[:], scale=1.0 / D),
             reads=[b_ps[sb], b_const], writes=[b_rsb[r]])
        P.op("dve", RECIP(rsb[r][:], rsb[r][:]), reads=[b_rsb[r]], writes=[b_rsb[r]])
        if Aname is None:
            return r
        for c in range(KC):
            i = c % 2
            P.op("dve", TT(fsc[i][:], hT[:, c, sl], rsb[r][:], ALU.mult),
                 reads=[b_h[c][g], b_rsb[r]], writes=[b_fsc[i]])
            P.op("dve", TS(nT[:, c, sl], fsc[i][:], dcol(Aname, c), dcol(SHname, c), ALU.mult, ALU.add),
                 reads=[b_fsc[i], b_der], writes=[b_n[c][g]])
        return r

    epsc = A.view(MISC + 13312 + 64, F32, [1])
    P.op("pool", MEMSET(epsc[:], EPS), writes=[b_const])

    aT = A.view(AT_OFF, BF16, [8, T])
    gu = [A.view(GU_OFF + i * 16384, BF16, [2, KC, GS]) for i in range(2)]
    wdp = [A.view(WD_OFF + i * 16384, BF16, [8, D]) for i in range(2)]
    PASSES = [(0, 8), (8, 16), (16, 22)]

    def ffn(wg_d, wu_d, wd_d, Gname, st, pre_hooks):
        wg_v = wg_d.rearrange("(kc p) n -> p kc n", p=128)
        wu_v = wu_d.rearrange("(kc p) n -> p kc n", p=128)
        wd_v = wd_d.rearrange("(j p) n -> p j n", p=128)
        b_a = [[P.buf("a%d_%d" % (j, g)) for g in range(NG)] for j in range(8)]
        it = 0
        for pi, (c0, c1) in enumerate(PASSES):
            npc = c1 - c0
            wslot = st["wd_i"] % 2
            st["wd_i"] += 1
            b_wd = st["b_wd"][wslot]
            for hh in range(0, npc, 4):
                h1 = min(hh + 4, npc)
                P.dma("pool", wdp[wslot][:, hh:h1, :], wd_v[:, c0 + hh:c0 + h1, :], writes=[b_wd])
            groups = [(a, min(a + 4, c1)) for a in range(c0, c1, 4)]
            loaded = []
            for (a, b) in groups:
                gslot = st["gu_i"] % 2
                st["gu_i"] += 1
                b_g = st["b_gu"][gslot]
                ncol = (b - a) * 128
                P.dma("pool", gu[gslot][:, 0, :, 0:ncol], wg_v[:, :, a * 128:b * 128], writes=[b_g])
                P.dma("pool", gu[gslot][:, 1, :, 0:ncol], wu_v[:, :, a * 128:b * 128], writes=[b_g])
                loaded.append((a, b, gslot, b_g))
                for j in range(a, b):
                    jl = j - c0
                    jo = (j - a) * 128
                    for g in range(NG):
                        sl = slice(g * GS, (g + 1) * GS)
                        bg = it % 2
                        bu = 2 + it % 2
                        it += 1
                        for kc in range(KC):
                            P.op("pe", MM(ps[:, bg, :], gu[gslot][:, 0, kc, jo:jo + 128], nT[:, kc, sl], kc == 0, kc == KC - 1),
                                 reads=[b_g, b_n[kc][g]], writes=[b_ps[bg]], inc=(kc == KC - 1))
                        for kc in range(KC):
                            P.op("pe", MM(ps[:, bu, :], gu[gslot][:, 1, kc, jo:jo + 128], nT[:, kc, sl], kc == 0, kc == KC - 1),
                                 reads=[b_g, b_n[kc][g]], writes=[b_ps[bu]], inc=(kc == KC - 1))
                        i = it % 2
                        P.op("act", ACT(fsc[i][:], ps[:, bg, :], AF.Silu), reads=[b_ps[bg]], writes=[b_fsc[i]])
                        P.op("dve", TT(aT[:, jl, sl], fsc[i][:], ps[:, bu, :], ALU.mult),
                             reads=[b_fsc[i], b_ps[bu]], writes=[b_a[jl][g]])
                hk = pre_hooks.get((pi, len(loaded) - 1))
                if hk is not None:
                    hk()
            hk = pre_hooks.get((pi, "down"))
            if hk is not None:
                hk()
            dn = 0
            for g in range(NG):
                sl = slice(g * GS, (g + 1) * GS)
                for oc in range(KC):
                    bd = 4 + dn % 2
                    dn += 1
                    for jl in range(npc):
                        P.op("pe", MM(ps[:, bd, :], wdp[wslot][:, jl, oc * 128:(oc + 1) * 128], aT[:, jl, sl], jl == 0, jl == npc - 1),
                             reads=[b_wd, b_a[jl][g]], writes=[b_ps[bd]], inc=(jl == npc - 1))
                    P.op("dve", STT(hT[:, oc, sl], ps[:, bd, :], dcol(Gname, oc), hT[:, oc, sl], ALU.mult, ALU.add),
                         reads=[b_ps[bd], b_der, b_h[oc][g]], writes=[b_h[oc][g]])
        P.retire(*[b for row in b_a for b in row])


    xs = [A.view(AT_OFF + i * 4096, F32, [D]) for i in range(8)]
    b_xs = P.bufs("xs", 8)
    wa = [A.view(WD_OFF + i * 16384, BF16, [KC, D]) for i in range(2)]
    wada_block(0, wa[0], ring["b_wd"][0])
    wada_block(1, wa[1], ring["b_wd"][1])
    x_v = x_d.rearrange("(t p) d -> t p d", p=128)
    for g in range(NG):
        for j in range(4):
            tt = g * 4 + j
            s = tt % 8
            P.dma("sp", xs[s][:], x_v[tt], writes=[b_xs[s]])
        for c in range(KC):
            bk = c % 2
            for j in range(4):
                s = (g * 4 + j) % 8
                P.op("pe", TR(ps[:, bk, j * 128:(j + 1) * 128], xs[s][:, c * 128:(c + 1) * 128], ident[:]),
                     reads=[b_xs[s], b_const], writes=[b_ps[bk]], inc=(j == 3))
            P.op("act", ACT(hT[:, c, g * GS:(g + 1) * GS], ps[:, bk, :], AF.Copy), reads=[b_ps[bk]], writes=[b_h[c][g]])
        if g == 0:
            mod_block(0, wa[0], ring["b_wd"][0])
            mod_block(1, wa[1], ring["b_wd"][1])
            derive_A("A1", "g1", 1, 0, "SH1")
    P.retire(*b_xs)
    ring["wd_i"] = 0

    for g in range(NG):
        norm_group(g, "A1", "SH1")

    def make_mod_hook(ms):
        def hk():
            for m in ms:
                wslot = ring["wd_i"] % 2
                b_w_ = ring["b_wd"][wslot]
                wv = A.view(WD_OFF + wslot * 16384, BF16, [KC, D])
                wada_block(m, wv, b_w_)
                mod_block(m, wv, b_w_)
        return hk

    def hook_g1():
        derive_G("G1", 2, 0.5)

    if "ffn1" in stages:
        ffn(f1g_d, f1u_d, f1d_d, "G1", ring,
            {(0, 0): make_mod_hook([2]), (0, "down"): hook_g1,
             (0, 1): make_mod_hook([3]), (1, 0): make_mod_hook([4])})
    else:
        make_mod_hook([2, 3, 4])()
    derive_A("A2", "g2", 4, 3, "SH2")
    LATE_MOD = "mix" in stages
    if not LATE_MOD:
        make_mod_hook([5, 6, 7, 8])()
        derive_G("G2", 5, 1.0)
        derive_A("A3", "g3", 7, 6, "SH3")
        derive_G("G3", 8, 0.5)

    if "mix" in stages:
        for g in range(NG):
            norm_group(g, "A2", "SH2")
        hsp = nc.dram_tensor("hspill", [128, KC, T], F32).ap()
        b_hsp = P.buf("hsp")
        for c in range(KC):
            P.dma("sp", hsp[:, c, :], hT[:, c, :], reads=[b_h[c][g] for g in range(NG)], writes=[b_hsp])
        P.retire(*[b for row in b_h for b in row])
        P.retire(*ring["b_wd"])
        P.retire(*ring["b_gu"])
        M2 = R_OFF
        SC_A = 1.0 / math.sqrt(96.0)
        SC_B = 1.0 / math.sqrt(64.0)
        win_v = win_d.rearrange("(kc p) n -> p kc n", p=128)
        oaT = A.view(0, BF16, [4, T])
        obT = A.view(16384, BF16, [4, T])
        cosF = A.view(32768, F32, [T])
        sinF = A.view(40960, F32, [T])
        kpeR = A.view(49152, F32, [T])
        sqpe = A.view(57344, BF16, [T])
        b_oa = [[P.buf("oa") for g in range(NG)] for c in range(4)]
        b_ob = [[P.buf("ob") for g in range(NG)] for c in range(4)]
        b_tab = P.buf("tab")
        b_kpe = P.bufs("kpe", NG)
        wlat = A.view(M2 + 0, BF16, [KC, 640])
        wkr = A.view(M2 + 10240, BF16, [KC, 96])
        wkrp = A.view(M2 + 11776, BF16, [KC, 96])
        wuq = A.view(M2 + 13312, BF16, [3, 768])
        wuqp = A.view(M2 + 17920, BF16, [3, 8, 96])
        wukk = A.view(M2 + 22528, BF16, [2, 8, 64])
        wukv = A.view(M2 + 24576, BF16, [2, 512])
        wuqp_f = A.view(M2 + 17920, BF16, [3, 768])
        wukk_f = A.view(M2 + 22528, BF16, [2, 512])
        cqT = A.view(M2 + 26624, BF16, [3, T])
        ckvT = A.view(M2 + 38912, BF16, [2, T])
        vh = [A.view(M2 + 47104 + i * 4096, BF16, [16, 128]) for i in range(2)]
        qh = [A.view(M2 + 55296 + i * 4096, BF16, [T]) for i in range(2)]
        kh = [A.view(M2 + 63488 + i * 4096, BF16, [T]) for i in range(2)]
        PT = [A.view(M2 + 71680 + i * 1024, BF16, [GS]) for i in range(4)]
        oraw = [A.view(M2 + 75776 + i * 2048, F32, [GS]) for i in range(2)]
        dn = [A.view(M2 + 79872 + i * 2048, F32, [GS]) for i in range(2)]
        onesf = A.view(M2 + 83968, F32, [128])
        rin = A.view(M2 + 84480, F32, [GS])
        posi = A.view(M2 + 26624, I32, [T])
        tmpf = A.view(M2 + 34816, F32, [T])
        kfi = A.view(M2 + 43008, I32, [T])
        kff = A.view(M2 + 43008, F32, [T])
        b_tt = P.buf("tabtmp")
        b_w = {k: P.buf("w_" + k) for k in ("lat", "kr", "uq", "uqp", "ukk", "ukv")}
        b_PT = P.bufs("PT", 4)
        b_oraw = P.bufs("oraw", 2)
        b_dn = P.bufs("dn", 2)
        b_rin = P.buf("rin")
        sq_e = A.view(M2 + 86528, BF16, [GS])
        od_e = A.view(M2 + 87552, F32, [GS])
        rs_e = A.view(M2 + 89600, F32, [GS])
        b_sqe, b_ode, b_rse = P.buf("sqe"), P.buf("ode"), P.buf("rse")
        b_onesf = P.buf("onesf")
        P.op("pool", MEMSET(onesf[:], 1.0), writes=[b_onesf])
        self_ = [A.view(M2 + 91648 + i * 512, F32, [128]) for i in range(2)]
        rq_t = [A.view(M2 + 92672 + i * 128, F32, [32]) for i in range(2)]
        rk_t = [A.view(M2 + 92928 + i * 128, F32, [32]) for i in range(2)]
        t4 = [A.view(M2 + 93184 + i * 32, F32, [8]) for i in range(2)]
        dg = [A.view(M2 + 93248 + i * 512, F32, [128]) for i in range(8)]
        sel2b = A.view(M2 + 97344, BF16, [2])
        b_rq, b_rk = P.bufs("rq", 2), P.bufs("rk", 2)
        b_t4 = P.bufs("t4", 2)
        b_dg = P.bufs("dg", 8)
        P.op("pool", MEMSET(self_[0][:], 0.0), writes=[b_onesf])
        P.op("pool", MEMSET(self_[1][:], 0.0), writes=[b_onesf])
        P.op("pool", MEMSET(self_[0][:, 0:64], 1.0), writes=[b_onesf])
        P.op("pool", MEMSET(self_[1][:, 64:128], 1.0), writes=[b_onesf])
        P.op("pool", MEMSET(sel2b[:], 0.0), writes=[b_onesf])
        P.op("pool", MEMSET(sel2b[0:64, 0:1], 1.0), writes=[b_onesf])
        P.op("pool", MEMSET(sel2b[64:128, 1:2], 1.0), writes=[b_onesf])
        cmh = A.view(MISC + 13312 + 72, F32, [1])
        P.op("pool", MEMSET(cmh[:], -0.5), writes=[b_onesf])

        def POWH(out, in0, ncol):
            return lambda e: e.tensor_tensor(out=out, in0=in0, in1=cmh[:, 0:1].to_broadcast([128, ncol]), op=ALU.pow)

        TWO_PI = 2.0 * math.pi
        C1 = 6.28125
        C2 = TWO_PI - C1

        def build_tables(fname, sname):
            P.dma("sp", posi[:], pos_d.partition_broadcast(128), writes=[b_tt])
            P.op("dve", CP(tmpf[:], posi[:]), reads=[b_tt], writes=[b_tt])
            P.op("dve", TS(tmpf[:], tmpf[:], pvb(fname), None, ALU.mult), reads=[b_tt, b_pvT], writes=[b_tt])
            P.op("dve", TS(cosF[:], tmpf[:], 1.0 / TWO_PI, None, ALU.mult), reads=[b_tt], writes=[b_tab])
            P.op("dve", CP(kfi[:], cosF[:]), reads=[b_tab], writes=[b_tt])
            P.op("dve", CP(cosF[:], kfi[:]), reads=[b_tt], writes=[b_tab])
            P.op("dve", STT(tmpf[:], cosF[:], -C1, tmpf[:], ALU.mult, ALU.add), reads=[b_tab, b_tt], writes=[b_tt])
            P.op("dve", STT(tmpf[:], cosF[:], -C2, tmpf[:], ALU.mult, ALU.add), reads=[b_tab, b_tt], writes=[b_tt])
            P.op("dve", TS(tmpf[:], tmpf[:], math.pi, -math.pi, ALU.min, ALU.max), reads=[b_tt], writes=[b_tt])
            P.op("act", ACT(sinF[:], tmpf[:], AF.Sin, scale=pvb(sname)), reads=[b_tt, b_pvT], writes=[b_tab])
            P.op("dve", TS(tmpf[:], tmpf[:], math.pi / 2, None, ALU.add), reads=[b_tt], writes=[b_tt])
            P.op("dve", TS(kff[:], tmpf[:], math.pi, -TWO_PI, ALU.is_gt, ALU.mult), reads=[b_tt], writes=[b_tt])
            P.op("dve", TT(tmpf[:], tmpf[:], kff[:], ALU.add), reads=[b_tt], writes=[b_tt])
            P.op("dve", TS(tmpf[:], tmpf[:], math.pi, -math.pi, ALU.min, ALU.max), reads=[b_tt], writes=[b_tt])
            P.op("act", ACT(cosF[:], tmpf[:], AF.Sin), reads=[b_tt], writes=[b_tab])

        def rstd_from(bank, rows, n, r):
            if _os.environ.get("MK_LN") != "1":
                P.op("act", ACT(rsb[r][rows, :], ps[rows, bank, :], AF.Sqrt, bias=epsc[rows, :], scale=1.0 / n),
                     reads=[b_ps[bank], b_const], writes=[b_rsb[r]])
                P.op("dve", RECIP(rsb[r][rows, :], rsb[r][rows, :]), reads=[b_rsb[r]], writes=[b_rsb[r]])
                return
            P.op("act", ACT(rsb[r][rows, :], ps[rows, bank, :], AF.Ln, bias=epsc[rows, :], scale=1.0 / n),
                 reads=[b_ps[bank], b_const], writes=[b_rsb[r]])
            P.op("act", ACT(rsb[r][rows, :], rsb[r][rows, :], AF.Exp, scale=-0.5), reads=[b_rsb[r]], writes=[b_rsb[r]])

        def rope_parts(nrows, gcol, gpcol, sl, out_ap, b_out, after=()):
            rows = slice(0, nrows)
            P.op("dve", STT(fsc[1][rows, :], ps[rows, 5, :], gpcol[rows, :], sinF[rows, sl], ALU.mult, ALU.mult),
                 reads=[b_ps[5], b_pvT, b_tab], writes=[b_fsc[1]])
            P.op("dve", STT(fsc[0][rows, :], ps[rows, 4, :], gcol[rows, :], cosF[rows, sl], ALU.mult, ALU.mult),
                 reads=[b_ps[4], b_pvT, b_tab] + list(after), writes=[b_fsc[0]])
            P.op("dve", TT(out_ap, fsc[0][rows, :], fsc[1][rows, :], ALU.add), reads=[b_fsc[0], b_fsc[1]], writes=[b_out])

        def tiny_rstd(i, which, g, ncomp, n, mult, stat_fn):
            base = 0 if which == "q" else 32
            dst_t, b_dst_t = (rq_t[i], b_rq[i]) if which == "q" else (rk_t[i], b_rk[i])
            c0 = g * 4 * ncomp
            if _os.environ.get("MK_SKIP") == which:
                P.op("pool", MEMSET(dst_t[:, c0:c0 + 4 * ncomp], 0.1), writes=[b_dst_t])
                return
            for j in range(4):
                stat_fn(j, ps[:, 6, base + c0 + j * ncomp:base + c0 + (j + 1) * ncomp])
            ti = 0 if which == "q" else 1
            P.op("dve", TS(t4[ti][:, 0:4 * ncomp], ps[:, 6, base + c0:base + c0 + 4 * ncomp], mult / n, EPS * mult, ALU.mult, ALU.add),
                 reads=[b_ps[6]], writes=[b_t4[ti]])
            P.op("pool", POWH(dst_t[:, c0:c0 + 4 * ncomp], t4[ti][:, 0:4 * ncomp], 4 * ncomp), reads=[b_t4[ti], b_onesf], writes=[b_dst_t])

        def qprep_gen(i, nrows, ncomp, n, gcol, gpcol, dst, b_dst, sl, g):
            rows = slice(0, nrows)
            P.op("act", ACT(sqs[0][rows, :], ps[rows, 4, :], AF.Square), reads=[b_ps[4]], writes=[b_sqs[0]])
            rope_parts(nrows, gcol, gpcol, sl, fsc[0][rows, :], b_fsc[0], after=[b_sqs[0]])
            yield

            def stat(j, out_ap):
                rhs = ones[rows, 0:1] if ncomp == 1 else sel2b[:, 0:2]
                P.op("pe", MM(out_ap, sqs[0][rows, j * 128:(j + 1) * 128], rhs, True, True),
                     reads=[b_sqs[0], b_const, b_onesf], writes=[b_ps[6]])
            tiny_rstd(i, "q", g, ncomp, n, 1.0, stat)
            yield
            c0 = g * 4 * ncomp
            for j in range(4):
                for c in range(ncomp):
                    d = j * ncomp + c
                    col = c0 + j * ncomp + c
                    P.op("dve", TS(dg[d][:], ident[:], rq_t[i][:, col:col + 1], None, ALU.mult),
                         reads=[b_const, b_rq[i]], writes=[b_dg[d]])
            for j in range(4):
                for c in range(ncomp):
                    d = j * ncomp + c
                    lhsT = onesf[:, 0:128] if ncomp == 1 else self_[c][:, 0:128]
                    P.op("pe", MM(ps[0:128, 4, j * 128:(j + 1) * 128], lhsT, dg[d][:], c == 0, c == ncomp - 1),
                         reads=[b_onesf, b_dg[d]], writes=[b_ps[4]], inc=(j == 3 and c == ncomp - 1))
            P.op("dve", TT(dst, fsc[0][rows, :], ps[rows, 4, :], ALU.mult), reads=[b_fsc[0], b_ps[4]], writes=[b_dst])
            yield

        ring_i = [0]
        pend = []

        def step_pend():
            for gen_ in list(pend):
                try:
                    next(gen_)
                except StopIteration:
                    pend.remove(gen_)

        def attention(kT, b_k, qT, b_q, rows, scale, vts, qc, filler, fill_at):
            qsl = slice(qc * GS, (qc + 1) * GS)

            def S(kt):
                it = ring_i[0]
                ring_i[0] += 1
                sb = it % 2
                pr = it % 4
                P.op("pe", MM(ps[:, sb, :], kT[rows, kt * 128:(kt + 1) * 128], qT[rows, qsl], True, True),
                     reads=[b_k, b_q], writes=[b_ps[sb]])
                sc_ap, b_sc = scale(kt)
                if _os.environ.get("MK_CONSTSC") == "1":
                    sc_ap = 0.1
                P.op("act", ACT(PT[pr][:], ps[:, sb, :], AF.Exp, scale=sc_ap), reads=[b_ps[sb], b_sc], writes=[b_PT[pr]])
                return pr

            prs = {0: S(0)}
            for kt in range(16):
                if kt + 1 < 16:
                    prs[kt + 1] = S(kt + 1)
                pr = prs[kt]
                for vi, (vfn, M, bank, b_v) in enumerate(vts):
                    P.op("pe", MM(ps[0:M, bank, :], vfn(kt), PT[pr][:], kt == 0, kt == 15),
                         reads=[b_v, b_PT[pr]], writes=[b_ps[bank]], inc=(kt == 15 and vi == len(vts) - 1))
                if kt in fill_at:
                    filler()
                if kt in (4, 8, 12):
                    step_pend()

        def recip_bcast(src, drow):
            P.op("dve", RECIP(rin[drow:drow + 1, :], src[drow:drow + 1, :]), reads=[b_oraw[0], b_oraw[1]], writes=[b_rin])
            P.op("pe", MM(ps[:, 7, :], onesf[drow:drow + 1, :], rin[drow:drow + 1, :], True, True),
                 reads=[b_onesf, b_rin], writes=[b_ps[7]])

        NOFILL = _os.environ.get("MK_NOFILL") == "1"

        def make_filler(gen):
            def f():
                if gen is not None and not NOFILL:
                    next(gen, None)
            return f

        def drain(gen):
            if gen is not None:
                for _ in gen:
                    pass

        for hh in range(2):
            P.dma("pool", wlat[:, hh * 4:(hh + 1) * 4, :], win_v[:, hh * 4:(hh + 1) * 4, 0:640], writes=[b_w["lat"]])
        P.op("pool", MEMSET(wkr[:], 0.0), writes=[b_w["kr"]])
        P.op("pool", MEMSET(wkrp[:], 0.0), writes=[b_w["kr"]])
        P.dma("pool", wkr[:, :, 64:96], win_v[:, :, 640:672], writes=[b_w["kr"]])
        P.dma("pool", wkrp[:, :, 64:80], win_v[:, :, 656:672], writes=[b_w["kr"]])
        P.dma("pool", wkrp[:, :, 80:96], win_v[:, :, 640:656], writes=[b_w["kr"]])
        P.dma("pool", wuq[:], wuq_d.rearrange("(kc p) n -> p kc n", p=128), writes=[b_w["uq"]])
        P.op("pool", MEMSET(wuqp[:], 0.0), writes=[b_w["uqp"]])
        wuq_v4 = wuq_d.rearrange("(kc p) (h d) -> p kc h d", p=128, d=96)
        for kc in range(3):
            P.dma("pool", wuqp[:, kc, :, 64:80], wuq_v4[:, kc, :, 80:96], writes=[b_w["uqp"]])
            P.dma("pool", wuqp[:, kc, :, 80:96], wuq_v4[:, kc, :, 64:80], writes=[b_w["uqp"]])
        wukv_v4 = wukv_d.rearrange("(kc p) (h d) -> p kc h d", p=128, d=128)
        wukv_s = wukv.rearrange("p kc (h d) -> p kc h d", d=64)
        for kc in range(2):
            P.dma("pool", wukk[:, kc, :, :], wukv_v4[:, kc, :, 0:64], writes=[b_w["ukk"]])
            P.dma("pool", wukv_s[:, kc, :, :], wukv_v4[:, kc, :, 64:128], writes=[b_w["ukv"]])

        build_tables("freqa", "signa")
        P.retire(b_tt)
        b_cq = [[P.buf("cq") for g in range(NG)] for c in range(3)]
        b_ckv = [[P.buf("ckv") for g in range(NG)] for c in range(2)]
        b_vh = P.bufs("vh", 2)
        b_qh = P.bufs("qh", 2)
        b_kh = P.bufs("kh", 2)

        zst = [oraw[0], oraw[1], dn[0]]
        b_zst = [b_oraw[0], b_oraw[1], b_dn[0]]
        for g in range(NG):
            sl = slice(g * GS, (g + 1) * GS)
            for (lc0, nl, dstT, b_dst, nname, nfeat, sbank, r) in ((0, 3, cqT, b_cq, "qnorm", 384.0, 6, 0), (3, 2, ckvT, b_ckv, "kvnorm", 256.0, 7, 1)):
                for l in range(nl):
                    bk = 4 + l % 2
                    for kc in range(KC):
                        P.op("pe", MM(ps[:, bk, :], wlat[:, kc, (lc0 + l) * 128:(lc0 + l + 1) * 128], nT[:, kc, sl], kc == 0, kc == KC - 1),
                             reads=[b_w["lat"], b_n[kc][g]], writes=[b_ps[bk]], inc=(kc == KC - 1))
                    P.op("act", ACT(zst[l][:], ps[:, bk, :], AF.Copy), reads=[b_ps[bk]], writes=[b_zst[l]])
                    P.op("act", ACT(sqs[l % 2][:], ps[:, bk, :], AF.Square), reads=[b_ps[bk]], writes=[b_sqs[l % 2]])
                    P.op("pe", MM(ps[:, sbank, :], ones[:], sqs[l % 2][:], l == 0, l == nl - 1),
                         reads=[b_const, b_sqs[l % 2]], writes=[b_ps[sbank]])
                rstd_from(sbank, slice(0, 128), nfeat, r)
                for l in range(nl):
                    P.op("dve", STT(dstT[:, l, sl], zst[l][:], pvb(nname, l), rsb[r][:], ALU.mult, ALU.mult),
                         reads=[b_zst[l], b_pvT, b_rsb[r]], writes=[b_dst[l][g]])
            for (wt, bk) in ((wkr, 4), (wkrp, 5)):
                for kc in range(KC):
                    P.op("pe", MM(ps[0:96, bk, :], wt[:, kc, :], nT[:, kc, sl], kc == 0, kc == KC - 1),
                         reads=[b_w["kr"], b_n[kc][g]], writes=[b_ps[bk]], inc=(kc == KC - 1))
            R96 = slice(0, 96)
            P.op("act", ACT(sqpe[R96, sl], ps[R96, 4, :], AF.Square), reads=[b_ps[4]], writes=[b_kpe[g]])
            P.op("dve", STT(fsc[0][R96, :], ps[R96, 4, :], pvb("kg")[R96, :], cosF[R96, sl], ALU.mult, ALU.mult),
                 reads=[b_ps[4], b_pvT, b_tab, b_kpe[g]], writes=[b_fsc[0]])
            P.op("dve", STT(fsc[1][R96, :], ps[R96, 5, :], pvb("kgp")[R96, :], sinF[R96, sl], ALU.mult, ALU.mult),
                 reads=[b_ps[5], b_pvT, b_tab], writes=[b_fsc[1]])
            P.op("dve", TT(kpeR[R96, sl], fsc[0][R96, :], fsc[1][R96, :], ALU.add), reads=[b_fsc[0], b_fsc[1]], writes=[b_kpe[g]])

        R96 = slice(0, 96)

        def mla_prep(h):
            i = h % 2
            kindB = (h % 2 == 1)
            for g in range(NG):
                sl = slice(g * GS, (g + 1) * GS)
                Mq = 128 if h < 7 else 96
                Mk = 128 if h < 7 else 64
                for kc in range(3):
                    P.op("pe", MM(ps[0:Mq, 4, :], wuq[:, kc, h * 96:h * 96 + Mq], cqT[:, kc, sl], kc == 0, kc == 2),
                         reads=[b_w["uq"], b_cq[kc][g]], writes=[b_ps[4]], inc=(kc == 2))
                for kc in range(3):
                    P.op("pe", MM(ps[0:Mq, 5, :], wuqp_f[:, kc, h * 96:h * 96 + Mq], cqT[:, kc, sl], kc == 0, kc == 2),
                         reads=[b_w["uqp"], b_cq[kc][g]], writes=[b_ps[5]], inc=(kc == 2))
                yield from qprep_gen(i, 96, 1, 96.0, pvb("qg"), pvb("qgp"), qh[i][R96, sl], b_qh[i], sl, g)
                for kc in range(2):
                    P.op("pe", MM(ps[0:Mk, 5, :], wukk_f[:, kc, h * 64:h * 64 + Mk], ckvT[:, kc, sl], kc == 0, kc == 1),
                         reads=[b_w["ukk"], b_ckv[kc][g]], writes=[b_ps[5]], inc=(kc == 1))
                P.op("act", ACT(sqs[1][0:64, :], ps[0:64, 5, :], AF.Square), reads=[b_ps[5]], writes=[b_sqs[1]])
                P.op("dve", TS(kh[i][0:64, sl], ps[0:64, 5, :], pvb("kg")[0:64, :], None, ALU.mult),
                     reads=[b_ps[5], b_pvT, b_sqs[1]], writes=[b_kh[i]])
                P.op("dve", CP(kh[i][64:96, sl], kpeR[64:96, sl]), reads=[b_kpe[g]], writes=[b_kh[i]])
                yield

                def kstat(j, out_ap, g=g):
                    P.op("pe", MM(out_ap, sqs[1][0:64, j * 128:(j + 1) * 128], ones[0:64, 0:1], True, False),
                         reads=[b_sqs[1], b_const], writes=[b_ps[6]], inc=False)
                    P.op("pe", MM(out_ap, sqpe[64:96, g * GS + j * 128:g * GS + (j + 1) * 128], ones[64:96, 0:1], False, True),
                         reads=[b_kpe[g], b_const], writes=[b_ps[6]])
                tiny_rstd(i, "k", g, 1, 96.0, 96.0, kstat)
                yield
            P.op("pool", MEMSET(vh[i][:], 0.0), writes=[b_vh[i]])
            onescol = 32 if kindB else 64
            c0 = 64 if kindB else 0
            P.op("pool", MEMSET(vh[i][:, :, onescol:onescol + 1], 1.0), writes=[b_vh[i]])
            for t0 in range(0, 16, 8):
                for tt in range(t0, t0 + 8):
                    for kc in range(2):
                        P.op("pe", MM(ps[:, 5, (tt - t0) * 64:(tt - t0 + 1) * 64], ckvT[:, kc, tt * 128:(tt + 1) * 128],
                                      wukv[:, kc, h * 64:(h + 1) * 64], kc == 0, kc == 1),
                             reads=[b_ckv[kc][tt // 4], b_w["ukv"]], writes=[b_ps[5]], inc=(tt == t0 + 7 and kc == 1))
                P.op("dve", CP(vh[i][:, t0:t0 + 8, c0:c0 + 64], ps[:, 5, :].rearrange("p (a b) -> p a b", b=64)),
                     reads=[b_ps[5]], writes=[b_vh[i]])
                yield

        def mla_attn(h, filler):
            i = h % 2
            kindB = (h % 2 == 1)
            if kindB:
                vt = (lambda kt, i=i: vh[i][:, kt, 0:128], 128, 3, b_vh[i])
                drow, ob = 32, 1
                drows = slice(64, 128)
            else:
                vt = (lambda kt, i=i: vh[i][:, kt, 0:128], 128, 2, b_vh[i])
                drow, ob = 64, 0
                drows = slice(0, 64)
            for qc in range(NG):
                qsl = slice(qc * GS, (qc + 1) * GS)
                attention(kh[i], b_kh[i], qh[i], b_qh[i], R96, (lambda kt, i=i: (rk_t[i][:, kt:kt + 1], b_rk[i])), [vt], qc, filler, (1, 3, 5, 7, 9, 11, 13))
                if kindB:
                    P.op("dve", CP(oraw[ob][32:33, :], ps[32:33, 3, :]), reads=[b_ps[3]], writes=[b_oraw[ob]])
                    P.op("dve", CP(oraw[ob][64:128, :], ps[64:128, 3, :]), reads=[b_ps[3]], writes=[b_oraw[ob]])
                else:
                    P.op("dve", CP(oraw[ob][0:65, :], ps[0:65, 2, :]), reads=[b_ps[2]], writes=[b_oraw[ob]])
                P.op("dve", RECIP(rin[drow:drow + 1, :], oraw[ob][drow:drow + 1, :]), reads=[b_oraw[ob]], writes=[b_rin])

                def epi(ob=ob, drow=drow, drows=drows, qsl=qsl, qc=qc):
                    P.op("pe", MM(ps[:, 7, :], onesf[drow:drow + 1, :], rin[drow:drow + 1, :], True, True),
                         reads=[b_onesf, b_rin], writes=[b_ps[7]])
                    P.op("dve", TT(oaT[drows, h // 2, qsl], oraw[ob][drows, :], ps[drows, 7, :], ALU.mult),
                         reads=[b_oraw[ob], b_ps[7]], writes=[b_oa[h // 2][qc]])
                    yield
                pend.append(epi())

        wdh = [A.view(M2 + i * 10240, BF16, [KC, 640]) for i in range(2)]
        vb = [A.view(M2 + 26624 + i * 6400, BF16, [16, 200]) for i in range(2)]
        QB0, KB0, VB0 = 672, 1184, 1696
        dstate = {}

        def diff_setup():
            P.retire(*[b for row in b_cq for b in row], *[b for row in b_ckv for b in row], *b_vh, *b_w.values(), *b_kpe)
            nonlocal_b_tt = P.buf("tabtmp2")
            dstate["b_tt"] = nonlocal_b_tt
            yield

        def build_tables2(fname, sname, bt):
            P.dma("sp", posi[:], pos_d.partition_broadcast(128), writes=[bt])
            P.op("dve", CP(tmpf[:], posi[:]), reads=[bt], writes=[bt])
            P.op("dve", TS(tmpf[:], tmpf[:], pvb(fname), None, ALU.mult), reads=[bt, b_pvT], writes=[bt])
            yield
            P.op("dve", TS(cosF[:], tmpf[:], 1.0 / TWO_PI, None, ALU.mult), reads=[bt], writes=[b_tab])
            P.op("dve", CP(kfi[:], cosF[:]), reads=[b_tab], writes=[bt])
            P.op("dve", CP(cosF[:], kfi[:]), reads=[bt], writes=[b_tab])
            yield
            P.op("dve", STT(tmpf[:], cosF[:], -C1, tmpf[:], ALU.mult, ALU.add), reads=[b_tab, bt], writes=[bt])
            P.op("dve", STT(tmpf[:], cosF[:], -C2, tmpf[:], ALU.mult, ALU.add), reads=[b_tab, bt], writes=[bt])
            P.op("dve", TS(tmpf[:], tmpf[:], math.pi, -math.pi, ALU.min, ALU.max), reads=[bt], writes=[bt])
            P.op("act", ACT(sinF[:], tmpf[:], AF.Sin, scale=pvb(sname)), reads=[bt, b_pvT], writes=[b_tab])
            yield
            P.op("dve", TS(tmpf[:], tmpf[:], math.pi / 2, None, ALU.add), reads=[bt], writes=[bt])
            P.op("dve", TS(kff[:], tmpf[:], math.pi, -TWO_PI, ALU.is_gt, ALU.mult), reads=[bt], writes=[bt])
            P.op("dve", TT(tmpf[:], tmpf[:], kff[:], ALU.add), reads=[bt], writes=[bt])
            P.op("dve", TS(tmpf[:], tmpf[:], math.pi, -math.pi, ALU.min, ALU.max), reads=[bt], writes=[bt])
            P.op("act", ACT(cosF[:], tmpf[:], AF.Sin), reads=[bt], writes=[b_tab])
            yield

        def diff_prep(h):
            i = h % 2
            if h == 0:
                P.retire(*[b for row in b_cq for b in row], *[b for row in b_ckv for b in row], *b_vh, *b_w.values(), *b_kpe)
                bt = P.buf("tabtmp2")
                yield from build_tables2("freqb", "signb", bt)
                P.retire(bt)
                dstate["b_wdh"] = P.bufs("wdh", 2)
                dstate["b_vb"] = P.bufs("vb", 2)
                P.op("dve", TT(der[:, 74:75], pvb("lq1"), pvb("lk1"), ALU.mult), reads=[b_pvT], writes=[b_der])
                P.op("dve", TT(der[:, 75:76], pvb("lq2"), pvb("lk2"), ALU.mult), reads=[b_pvT], writes=[b_der])
                yield
                P.op("pe", MM(ps[:, 6, 0:2], onesf[:], der[:, 74:76], True, True), reads=[b_onesf, b_der], writes=[b_ps[6]])
                P.op("act", ACT(der[:, 74:76], ps[:, 6, 0:2], AF.Exp), reads=[b_ps[6]], writes=[b_der])
                P.op("dve", TT(der[:, 72:73], der[:, 75:76], der[:, 74:75], ALU.subtract), reads=[b_der], writes=[b_der])
                P.op("dve", TS(der[:, 72:73], der[:, 72:73], -LAMBDA_INIT, None, ALU.add), reads=[b_der], writes=[b_der])
                P.op("dve", TS(der[:, 73:74], pvb("subln"), 1.0 - LAMBDA_INIT, None, ALU.mult), reads=[b_pvT], writes=[b_der])
            b_wdh, b_vb = dstate["b_wdh"], dstate["b_vb"]

            def diff_load(hh):
                ii = hh % 2
                P.op("pool", MEMSET(wdh[ii][:, :, 128:256], 0.0), writes=[b_wdh[ii]])
                P.op("pool", MEMSET(wdh[ii][:, :, 384:512], 0.0), writes=[b_wdh[ii]])
                for (src0, d0) in ((QB0, 0), (KB0, 256)):
                    s0 = src0 + hh * 128
                    P.dma("pool", wdh[ii][:, :, d0:d0 + 128], win_v[:, :, s0:s0 + 128], writes=[b_wdh[ii]])
                    for base in (0, 64):
                        P.op("pool", CP(wdh[ii][:, :, d0 + 128 + base:d0 + 128 + base + 8], wdh[ii][:, :, d0 + base + 8:d0 + base + 16]),
                             reads=[b_wdh[ii]], writes=[b_wdh[ii]])
                        P.op("pool", CP(wdh[ii][:, :, d0 + 128 + base + 8:d0 + 128 + base + 16], wdh[ii][:, :, d0 + base:d0 + base + 8]),
                             reads=[b_wdh[ii]], writes=[b_wdh[ii]])
                P.dma("pool", wdh[ii][:, :, 512:640], win_v[:, :, VB0 + hh * 128:VB0 + (hh + 1) * 128], writes=[b_wdh[ii]])

            if h == 0:
                diff_load(0)
                diff_load(1)
            elif h + 1 < 4:
                diff_load(h + 1)
            P.op("pool", MEMSET(vb[i][:], 0.0), writes=[b_vb[i]])
            P.op("pool", MEMSET(vb[i][:, :, 64:65], 1.0), writes=[b_vb[i]])
            P.op("pool", MEMSET(vb[i][:, :, 72 + 32:72 + 33], 1.0), writes=[b_vb[i]])
            yield
            for t0 in range(0, 16, 4):
                for tt in range(t0, t0 + 4):
                    for kc in range(KC):
                        P.op("pe", MM(ps[:, 5, (tt - t0) * 128:(tt - t0 + 1) * 128], nT[:, kc, tt * 128:(tt + 1) * 128],
                                      wdh[i][:, kc, 512:640], kc == 0, kc == KC - 1),
                             reads=[b_n[kc][tt // 4], b_wdh[i]], writes=[b_ps[5]], inc=(tt == t0 + 3 and kc == KC - 1))
                pv = ps[:, 5, :].rearrange("p (a b) -> p a b", b=128)
                P.op("dve", CP(vb[i][:, t0:t0 + 4, 0:64], pv[:, :, 0:64]), reads=[b_ps[5]], writes=[b_vb[i]])
                P.op("dve", CP(vb[i][:, t0:t0 + 4, 136:200], pv[:, :, 64:128]), reads=[b_ps[5]], writes=[b_vb[i]])
                yield
            for g in range(NG):
                sl = slice(g * GS, (g + 1) * GS)
                for (d0, gname, gpname, dst, b_dst) in ((0, "dqg", "dqgp", qh[i], b_qh[i]), (256, "dkg", "dkgp", kh[i], b_kh[i])):
                    for kc in range(KC):
                        P.op("pe", MM(ps[:, 4, :], wdh[i][:, kc, d0:d0 + 128], nT[:, kc, sl], kc == 0, kc == KC - 1),
                             reads=[b_wdh[i], b_n[kc][g]], writes=[b_ps[4]], inc=(kc == KC - 1))
                    for kc in range(KC):
                        P.op("pe", MM(ps[:, 5, :], wdh[i][:, kc, d0 + 128:d0 + 256], nT[:, kc, sl], kc == 0, kc == KC - 1),
                             reads=[b_wdh[i], b_n[kc][g]], writes=[b_ps[5]], inc=(kc == KC - 1))
                    if d0 == 0:
                        yield from qprep_gen(i, 128, 2, 64.0, pvb(gname), pvb(gpname), dst[:, sl], b_dst, sl, g)
                    else:
                        P.op("act", ACT(sqs[1][:], ps[:, 4, :], AF.Square), reads=[b_ps[4]], writes=[b_sqs[1]])
                        rope_parts(128, pvb(gname), pvb(gpname), sl, dst[:, sl], b_dst, after=[b_sqs[1]])
                        yield

                        def kstat2(j, out_ap):
                            P.op("pe", MM(out_ap, sqs[1][:, j * 128:(j + 1) * 128], sel2b[:, 0:2], True, True),
                                 reads=[b_sqs[1], b_onesf], writes=[b_ps[6]])
                        tiny_rstd(i, "k", g, 2, 64.0, 64.0, kstat2)
                        yield

        def diff_attn(h, filler):
            i = h % 2
            b_vb = dstate["b_vb"]
            RA = slice(0, 128)
            vtA = (lambda kt, i=i: vb[i][:, kt, 0:128], 128, 2, b_vb[i])
            vtB = (lambda kt, i=i: vb[i][:, kt, 72:200], 128, 3, b_vb[i])
            for qc in range(NG):
                qsl = slice(qc * GS, (qc + 1) * GS)
                for comp in range(2):
                    rows = slice(comp * 64, comp * 64 + 64)
                    attention(kh[i], b_kh[i], qh[i], b_qh[i], rows, (lambda kt, i=i, comp=comp: (rk_t[i][:, kt * 2 + comp:kt * 2 + comp + 1], b_rk[i])), [vtA, vtB], qc, filler, (3, 7, 11, 15))
                    P.op("dve", CP(oraw[0][0:65, :], ps[0:65, 2, :]), reads=[b_ps[2]], writes=[b_oraw[0]])
                    P.op("dve", CP(oraw[1][64:128, :], ps[64:128, 3, :]), reads=[b_ps[3]], writes=[b_oraw[1]])
                    P.op("dve", RECIP(rin[64:65, :], oraw[0][64:65, :]), reads=[b_oraw[0]], writes=[b_rin])

                    def epi(comp=comp, qsl=qsl, qc=qc):
                        P.op("pe", MM(ps[:, 7, :], onesf[64:65, :], rin[64:65, :], True, True),
                             reads=[b_onesf, b_rin], writes=[b_ps[7]])
                        P.op("dve", TT(dn[comp][0:64, :], oraw[0][0:64, :], ps[0:64, 7, :], ALU.mult),
                             reads=[b_oraw[0], b_ps[7]], writes=[b_dn[comp]])
                        P.op("dve", TT(dn[comp][64:128, :], oraw[1][64:128, :], ps[64:128, 7, :], ALU.mult),
                             reads=[b_oraw[1], b_ps[7]], writes=[b_dn[comp]])
                        if comp == 0:
                            yield
                            return
                        P.op("dve", STT(od_e[:], dn[1][:], der[:, 72:73], dn[0][:], ALU.mult, ALU.add),
                             reads=[b_dn[0], b_dn[1], b_der], writes=[b_ode])
                        P.op("act", ACT(sq_e[:], od_e[:], AF.Square), reads=[b_ode], writes=[b_sqe])
                        yield
                        P.op("pe", MM(ps[:, 7, :], ones[:], sq_e[:], True, True), reads=[b_const, b_sqe], writes=[b_ps[7]])
                        P.op("act", ACT(rs_e[:], ps[:, 7, :], AF.Sqrt, bias=epsc[:], scale=1.0 / 128.0),
                             reads=[b_ps[7], b_const], writes=[b_rse])
                        P.op("dve", RECIP(rs_e[:], rs_e[:]), reads=[b_rse], writes=[b_rse])
                        yield
                        P.op("dve", STT(obT[:, h, qsl], od_e[:], der[:, 73:74], rs_e[:], ALU.mult, ALU.mult),
                             reads=[b_ode, b_der, b_rse], writes=[b_ob[h][qc]])
                        yield
                    pend.append(epi())

        units = [("mla", h) for h in range(8)] + [("diff", h) for h in range(4)]

        def prep_of(u):
            return mla_prep(u[1]) if u[0] == "mla" else diff_prep(u[1])

        NU = int(_os.environ.get("MK_UNITS", "12"))
        if NU < 12:
            for c in range(4):
                for g in range(NG):
                    P.op("pool", MEMSET(oaT[:, c, g * GS:(g + 1) * GS], 0.0), writes=[b_oa[c][g]])
                    P.op("pool", MEMSET(obT[:, c, g * GS:(g + 1) * GS], 0.0), writes=[b_ob[c][g]])
            if NU <= 8:
                dstate["b_wdh"] = P.bufs("wdh", 2)
                dstate["b_vb"] = P.bufs("vb", 2)
        units = units[:NU]
        def warm(n):
            for k in range(n):
                P.op("pe", MM(ps[:, 7, :], ones[:], nT[:, k % KC, 0:GS], True, True),
                     reads=[b_const, b_n[k % KC][0]], writes=[b_ps[7]], inc=(k == n - 1))

        NWARM = int(_os.environ.get("MK_WARM", "0"))
        drain(prep_of(units[0]))
        for ui, u in enumerate(units):
            if NWARM:
                warm(NWARM)
            gen = prep_of(units[ui + 1]) if ui + 1 < len(units) else None
            if NOFILL and _os.environ.get("MK_PREFIRST") == "1":
                drain(gen)
            filler = make_filler(gen)
            if u[0] == "mla":
                mla_attn(u[1], filler)
            else:
                diff_attn(u[1], filler)
            drain(gen)
        while pend:
            step_pend()
        b_wdh, b_vb = dstate["b_wdh"], dstate["b_vb"]

        P.retire(*b_wdh, *b_vb, *b_qh, *b_kh, *b_PT, *b_oraw, *b_dn, b_rin, b_onesf, b_tab, b_sqe, b_ode, b_rse, *b_rq, *b_rk, *b_t4, *b_dg)
        woa = A.view(M2 + 0, BF16, [4, D])
        wob = A.view(M2 + 8192, BF16, [4, D])
        wout = A.view(M2 + 16384, BF16, [KC, D])
        wgt = [A.view(M2 + 32768 + i * 4096, BF16, [KC, 256]) for i in range(2)]
        hst = [A.view(M2 + 40960 + i * 2048, F32, [GS]) for i in range(2)]
        gA = A.view(M2 + 45056, F32, [GS])
        gB = A.view(M2 + 47104, F32, [GS])
        mT = A.view(M2 + 49152, BF16, [KC, T])
        b_woa, b_wob, b_wout = P.buf("woa"), P.buf("wob"), P.buf("wout")
        b_wgt = P.bufs("wgt", 2)
        b_hst = P.bufs("hst", 2)
        b_gA, b_gB = P.buf("gA"), P.buf("gB")
        b_m = [[P.buf("m") for g in range(NG)] for c in range(KC)]
        P.dma("pool", woa[:], woa_d.rearrange("(c p) n -> p c n", p=128), writes=[b_woa])
        P.dma("pool", wob[:], wob_d.rearrange("(c p) n -> p c n", p=128), writes=[b_wob])
        for hh in range(2):
            P.dma("pool", wout[:, hh * 4:(hh + 1) * 4, :], wout_d.rearrange("(c p) n -> p c n", p=128)[:, hh * 4:(hh + 1) * 4, :], writes=[b_wout])
        GA0, GB0 = 2208, 3232
        wad = A.view(M2 + 81920, BF16, [KC, D])
        b_wad = P.buf("wad")
        wada_block(5, wad, b_wad)
        late = {0: (5, 6), 2: (6, 7), 4: (7, 8), 6: (8, None)}
        nit = 0
        def load_wgt(oc_):
            wi_ = oc_ % 2
            P.dma("pool", wgt[wi_][:, :, 0:128], win_v[:, :, GA0 + oc_ * 128:GA0 + (oc_ + 1) * 128], writes=[b_wgt[wi_]])
            P.dma("pool", wgt[wi_][:, :, 128:256], win_v[:, :, GB0 + oc_ * 128:GB0 + (oc_ + 1) * 128], writes=[b_wgt[wi_]])

        load_wgt(0)
        for oc in range(KC):
            wi = oc % 2
            if oc + 1 < KC:
                load_wgt(oc + 1)
            for g in range(NG):
                sl = slice(g * GS, (g + 1) * GS)
                b0 = 4 * (nit % 2)
                nit += 1
                for c in range(4):
                    P.op("pe", MM(ps[:, b0, :], woa[:, c, oc * 128:(oc + 1) * 128], oaT[:, c, sl], c == 0, c == 3),
                         reads=[b_woa, b_oa[c][g]], writes=[b_ps[b0]], inc=(c == 3))
                for c in range(4):
                    P.op("pe", MM(ps[:, b0 + 1, :], wob[:, c, oc * 128:(oc + 1) * 128], obT[:, c, sl], c == 0, c == 3),
                         reads=[b_wob, b_ob[c][g]], writes=[b_ps[b0 + 1]], inc=(c == 3))
                for kc in range(KC):
                    P.op("pe", MM(ps[:, b0 + 2, :], wgt[wi][:, kc, 0:128], nT[:, kc, sl], kc == 0, kc == KC - 1),
                         reads=[b_wgt[wi], b_n[kc][g]], writes=[b_ps[b0 + 2]], inc=(kc == KC - 1))
                for kc in range(KC):
                    P.op("pe", MM(ps[:, b0 + 3, :], wgt[wi][:, kc, 128:256], nT[:, kc, sl], kc == 0, kc == KC - 1),
                         reads=[b_wgt[wi], b_n[kc][g]], writes=[b_ps[b0 + 3]], inc=(kc == KC - 1))
                P.op("act", ACT(gA[:], ps[:, b0 + 2, :], AF.Sigmoid), reads=[b_ps[b0 + 2]], writes=[b_gA])
                P.op("act", ACT(gB[:], ps[:, b0 + 3, :], AF.Sigmoid), reads=[b_ps[b0 + 3]], writes=[b_gB])
                P.op("dve", TT(gA[:], gA[:], ps[:, b0, :], ALU.mult), reads=[b_gA, b_ps[b0]], writes=[b_gA])
                P.op("dve", TT(gB[:], gB[:], ps[:, b0 + 1, :], ALU.mult), reads=[b_gB, b_ps[b0 + 1]], writes=[b_gB])
                P.op("dve", TT(mT[:, oc, sl], gA[:], gB[:], ALU.add), reads=[b_gA, b_gB], writes=[b_m[oc][g]])
            if oc in late:
                mcur, mnext = late[oc]
                mod_block(mcur, wad, b_wad)
                if mnext is not None:
                    wada_block(mnext, wad, b_wad)
        derive_G("G2", 5, 1.0)
        derive_A("A3", "g3", 7, 6, "SH3")
        derive_G("G3", 8, 0.5)
        P.retire(*[b for row in b_oa for b in row], *[b for row in b_ob for b in row])
        b_h = [[P.buf("h%d_%d" % (c, g)) for g in range(NG)] for c in range(KC)]
        nit = 0
        for g in range(NG):
            sl = slice(g * GS, (g + 1) * GS)
            for oc in range(KC):
                bk = nit % 2
                hi = nit % 2
                nit += 1
                P.dma("sp", hst[hi][:], hsp[:, oc, sl], reads=[b_hsp], writes=[b_hst[hi]], owner=b_hst[hi])
                for c in range(KC):
                    P.op("pe", MM(ps[:, bk, :], wout[:, c, oc * 128:(oc + 1) * 128], mT[:, c, sl], c == 0, c == KC - 1),
                         reads=[b_wout, b_m[c][g]], writes=[b_ps[bk]], inc=(c == KC - 1))
                P.op("dve", STT(hT[:, oc, sl], ps[:, bk, :], dcol("G2", oc), hst[hi][:], ALU.mult, ALU.add),
                     reads=[b_ps[bk], b_der, b_hst[hi]], writes=[b_h[oc][g]])
        P.retire(*[b for row in b_m for b in row], b_woa, b_wob, b_wout, *b_wgt, *b_hst, b_gA, b_gB, b_wad)
        P.retire(*[b for row in b_n for b in row])
        b_n = [[P.buf("n%d_%d" % (c, g)) for g in range(NG)] for c in range(KC)]
        ring["b_wd"] = P.bufs("wdp", 2)
        ring["b_gu"] = P.bufs("gu", 2)

    if "ffn2" in stages:
        for g in range(NG):
            norm_group(g, "A3", "SH3")
        ffn(f2g_d, f2u_d, f2d_d, "G3", ring, {})

    ost = [A.view(AT_OFF + i * 4096, F32, [D]) for i in range(2)]
    b_ost = P.bufs("ost", 2)
    ytmp = [A.view(AT_OFF + 8192 + i * 2048, F32, [GS]) for i in range(8)]
    b_ytmp = P.bufs("ytmp", 8)
    b_out = P.buf("out")
    for g in range(NG):
        r = norm_group(g, None, None)
        sl = slice(g * GS, (g + 1) * GS)
        for c in range(KC):
            P.op("dve", STT(ytmp[c][:], hT[:, c, sl], pvb("gf", c), rsb[r][:], ALU.mult, ALU.mult),
                 reads=[b_h[c][g], b_pvT, b_rsb[r]], writes=[b_ytmp[c]])
        for j in range(4):
            tt = g * 4 + j
            o = tt % 2
            for half in range(2):
                bk = (tt * 2 + half) % 4
                for cc in range(4):
                    c = half * 4 + cc
                    P.op("pe", TR(ps[:, bk, cc * 128:(cc + 1) * 128], ytmp[c][:, j * 128:(j + 1) * 128], ident[:]),
                         reads=[b_ytmp[c], b_const], writes=[b_ps[bk]], inc=(cc == 3))
                P.op("act", ACT(ost[o][:, half * 512:(half + 1) * 512], ps[:, bk, :], AF.Copy),
                     reads=[b_ps[bk]], writes=[b_ost[o]])
            P.dma("sp", out_d[tt * 128:(tt + 1) * 128, :], ost[o][:], reads=[b_ost[o]], writes=[b_out], owner=b_ost[o])
    P.wait_all("sp", b_ost + [b_out])
    P.emit()
    return nc


def _prep_inputs(inp):
    L = 0
    f32 = np.float32

    def pad128(v):
        o = np.zeros(128, f32)
        o[:v.shape[0]] = v
        return o

    def mla_perm(g):
        o = np.zeros(128, f32)
        o[64:80] = g[80:96]
        o[80:96] = g[64:80]
        return o

    def diff_rep(g):
        return np.concatenate([g, g]).astype(f32)

    def diff_perm(g):
        o = np.zeros(128, f32)
        for base in (0, 64):
            o[base:base + 8] = g[8:16]
            o[base + 8:base + 16] = g[0:8]
        return o

    freqa = np.zeros(128, f32)
    signa = np.ones(128, f32)
    fa = (1.0 / (np.float32(10000.0) ** (np.arange(16, dtype=f32) / np.float32(16)))).astype(f32)
    freqa[64:80] = fa
    freqa[80:96] = fa
    signa[64:80] = -1.0
    freqb = np.zeros(128, f32)
    signb = np.ones(128, f32)
    fb = (1.0 / (np.float32(500000.0) ** (np.arange(8, dtype=f32) / np.float32(8)))).astype(f32)
    for base in (0, 64):
        freqb[base:base + 8] = fb
        freqb[base + 8:base + 16] = fb
        signb[base:base + 8] = -1.0

    rows = []
    for nm in ("ffn1_norm", "mix_norm", "ffn2_norm", "final_norm"):
        rows.append(np.asarray(inp[nm][L], f32).reshape(8, 128))
    rows.append(np.asarray(inp["mla_q_norm"][L], f32).reshape(3, 128))
    rows.append(np.asarray(inp["mla_kv_norm"][L], f32).reshape(2, 128))
    qg = np.asarray(inp["mla_q_gain"][L], f32)
    kg = np.asarray(inp["mla_k_gain"][L], f32)
    dqg = np.asarray(inp["diff_q_gain"][L], f32)
    dkg = np.asarray(inp["diff_k_gain"][L], f32)
    singles = [pad128(qg), mla_perm(qg), pad128(kg), mla_perm(kg),
               diff_rep(dqg), diff_perm(dqg), diff_rep(dkg), diff_perm(dkg),
               pad128(np.asarray(inp["diff_lambda_q1"][L], f32)), pad128(np.asarray(inp["diff_lambda_k1"][L], f32)),
               pad128(np.asarray(inp["diff_lambda_q2"][L], f32)), pad128(np.asarray(inp["diff_lambda_k2"][L], f32)),
               np.asarray(inp["diff_subln"][L], f32), freqa, signa, freqb, signb]
    rows.append(np.stack(singles))
    pvb = np.ascontiguousarray(np.concatenate(rows, axis=0), dtype=f32)
    assert pvb.shape == (NPVB, 128), pvb.shape
    shared = {
        "pvb": pvb,
        "ident": np.eye(128, dtype=f32),
        "w_ada": np.ascontiguousarray(inp["w_ada"][L], f32),
        "f1g": np.ascontiguousarray(inp["ffn1_w_gate"][L], f32),
        "f1u": np.ascontiguousarray(inp["ffn1_w_up"][L], f32),
        "f1d": np.ascontiguousarray(inp["ffn1_w_down"][L], f32),
        "f2g": np.ascontiguousarray(inp["ffn2_w_gate"][L], f32),
        "f2u": np.ascontiguousarray(inp["ffn2_w_up"][L], f32),
        "f2d": np.ascontiguousarray(inp["ffn2_w_down"][L], f32),
        "w_in": np.ascontiguousarray(inp["w_in"][L], f32),
        "w_uq": np.ascontiguousarray(inp["mla_w_uq"][L], f32),
        "w_ukv": np.ascontiguousarray(inp["mla_w_ukv"][L], f32),
        "w_oa": np.ascontiguousarray(inp["mla_w_o"][L], f32),
        "w_ob": np.ascontiguousarray(inp["diff_w_o"][L], f32),
        "w_out": np.ascontiguousarray(inp["w_out"][L], f32),
    }
    b_ada = np.asarray(inp["b_ada"][L], f32).reshape(72, 128)
    maps = []
    for b in range(8):
        m = dict(shared)
        m["x"] = np.ascontiguousarray(inp["x"][b], f32)
        m["pos"] = np.ascontiguousarray(np.asarray(inp["positions"][b]).reshape(1, T).astype(np.int32))
        m["pva"] = np.ascontiguousarray(np.concatenate([np.asarray(inp["c"][b], f32).reshape(8, 128), b_ada], axis=0))
        maps.append(m)
    return maps


_NC_CACHE = {}


def kernel(**inputs):
    maps = _prep_inputs(inputs)
    if "nc" not in _NC_CACHE:
        _NC_CACHE["nc"] = build_nc()
    nc = _NC_CACHE["nc"]
    res = run_bass_kernel_spmd(nc, maps, core_ids=list(range(8)))
    out = np.stack([np.asarray(r["out"], np.float32) for r in res.results], axis=0)
    return out
```

```python
import math
import os as _os
import numpy as np
import concourse.bass as bass
import concourse.mybir as mybir
from concourse.bass_utils import run_bass_kernel_spmd

F32 = mybir.dt.float32
BF16 = mybir.dt.bfloat16
I32 = mybir.dt.int32
AF = mybir.ActivationFunctionType
ALU = mybir.AluOpType

T = 2048
D = 1024
DFF = 2816
NG = 4
GS = 512
KC = 8
EPS = 1e-6
IN_COLS = 4256
LAMBDA_INIT = 0.8 - 0.6 * math.exp(-0.3 * 0)


class Buf:
    __slots__ = ("name", "w", "r", "dsem", "dcnt")

    def __init__(self, name):
        self.name = name
        self.w = None
        self.r = {}
        self.dsem = None
        self.dcnt = 0


class Prog:
    ENG = ("pe", "act", "dve", "pool", "sp")

    def __init__(self, nc, same_engine_sync=True):
        self.nc = nc
        self.streams = {e: [] for e in self.ENG}
        self.sems = {}
        self.cnt = {}
        self.waited = {e: {} for e in self.ENG}
        self.same = same_engine_sync
        for e in self.ENG:
            self.sems["E_" + e] = nc.alloc_semaphore("sem_e_" + e)
            self.cnt[e] = 0
        self.nd = 0
        self.retired = {}

    def new_dsem(self, name):
        k = "D_%d_%s" % (self.nd, name)
        self.nd += 1
        self.sems[k] = self.nc.alloc_semaphore("sem_d%d" % self.nd)
        return k

    def buf(self, name):
        b = Buf(name)
        b.r = dict(self.retired)
        return b

    def bufs(self, name, n):
        return [self.buf("%s%d" % (name, i)) for i in range(n)]

    def retire(self, *bufs):
        for b in bufs:
            if b.w is not None:
                s, v = b.w
                self.retired[s] = max(self.retired.get(s, 0), v)
            for s, v in b.r.items():
                self.retired[s] = max(self.retired.get(s, 0), v)

    def _waits(self, eng, reads, writes):
        waits = {}
        own = "E_" + eng

        def need(s, v):
            if s == own and (eng == "pe" or not self.same):
                return
            if self.waited[eng].get(s, 0) >= v:
                return
            if waits.get(s, 0) < v:
                waits[s] = v

        for b in reads:
            if b.w is not None:
                need(*b.w)
        for b in writes:
            if b.w is not None:
                need(*b.w)
            for s, v in b.r.items():
                need(s, v)
        for s, v in waits.items():
            self.waited[eng][s] = v
        return list(waits.items())

    def _mark(self, tok, reads, writes):
        s, v = tok
        for b in reads:
            if b.r.get(s, 0) < v:
                b.r[s] = v
        for b in writes:
            b.w = tok
            b.r = {}

    def op(self, eng, fn, reads=(), writes=(), inc=True):
        waits = self._waits(eng, reads, writes)
        own = "E_" + eng
        if inc:
            self.cnt[eng] += 1
            tok = (own, self.cnt[eng])
        else:
            tok = (own, self.cnt[eng] + 1)
        self._mark(tok, reads, writes)
        self.streams[eng].append((waits, fn, (own, 1) if inc else None))

    def dma(self, q, out_ap, in_ap, reads=(), writes=(), owner=None):
        if owner is None:
            owner = writes[0] if writes else reads[0]
        if owner.dsem is None:
            owner.dsem = self.new_dsem(owner.name)
        waits = self._waits(q, reads, writes)
        owner.dcnt += 16
        tok = (owner.dsem, owner.dcnt)
        self._mark(tok, reads, writes)
        self.streams[q].append(
            (waits, lambda e: e.dma_start(out=out_ap, in_=in_ap), (owner.dsem, 16)))

    def wait_all(self, eng, bufs):
        waits = self._waits(eng, (), bufs)
        self.streams[eng].append((waits, None, None))

    def emit(self):
        nc = self.nc
        sems = self.sems
        streams = self.streams
        with nc.Block() as block:
            def run(e, name):
                for waits, fn, inc in streams[name]:
                    for s, v in waits:
                        e.wait_ge(sems[s], v)
                    if fn is None:
                        continue
                    ins = fn(e)
                    if inc is not None:
                        ins.then_inc(sems[inc[0]], inc[1])

            @block.tensor
            def _(e):
                run(e, "pe")

            @block.scalar
            def _(e):
                run(e, "act")

            @block.vector
            def _(e):
                run(e, "dve")

            @block.gpsimd
            def _(e):
                run(e, "pool")

            @block.sync
            def _(e):
                run(e, "sp")


class Arena:
    def __init__(self, nc, reserve=1024):
        rem = nc.sbuf_bytes_remaining
        self.nbytes = ((rem - reserve) // 64) * 64
        self.t = nc.alloc_sbuf_tensor("arena", [128, self.nbytes // 4], F32)

    def view(self, off, dtype, shape):
        assert off % 4 == 0
        esz = 2 if dtype == BF16 else 4
        n = 1
        for s in shape:
            n *= s
        nb = n * esz
        assert nb % 4 == 0 and off + nb <= self.nbytes, (off, nb, self.nbytes)
        a = self.t[:, off // 4:(off + nb) // 4]
        if dtype != F32:
            a = a.bitcast(dtype)
        if len(shape) == 2:
            a = a.rearrange("p (a b) -> p a b", a=shape[0])
        elif len(shape) == 3:
            a = a.rearrange("p (a b c) -> p a b c", a=shape[0], b=shape[1])
        return a


def MM(out, lhsT, rhs, start, stop):
    return lambda e: e.matmul(out, lhsT=lhsT, rhs=rhs, start=start, stop=stop)


def TR(out, in_, ident):
    return lambda e: e.transpose(out, in_, ident)


def ACT(out, in_, func, bias=None, scale=None):
    kw = {}
    if bias is not None:
        kw["bias"] = bias
    if scale is not None:
        kw["scale"] = scale
    return lambda e: e.activation(out=out, in_=in_, func=func, **kw)


def TT(out, in0, in1, op):
    return lambda e: e.tensor_tensor(out=out, in0=in0, in1=in1, op=op)


def TS(out, in0, s1, s2, op0, op1=None):
    if op1 is None:
        return lambda e: e.tensor_scalar(out=out, in0=in0, scalar1=s1, scalar2=None, op0=op0)
    return lambda e: e.tensor_scalar(out=out, in0=in0, scalar1=s1, scalar2=s2, op0=op0, op1=op1)


def STT(out, in0, scalar, in1, op0, op1):
    return lambda e: e.scalar_tensor_tensor(out=out, in0=in0, scalar=scalar, in1=in1, op0=op0, op1=op1)


def CP(out, in_):
    return lambda e: e.tensor_copy(out=out, in_=in_)


def RECIP(out, in_):
    return lambda e: e.reciprocal(out=out, in_=in_)


def MEMSET(ap, v):
    return lambda e: e.memset(ap, v)


PVB = {}
_n = 0
for _name, _cnt in [("g1", 8), ("g2", 8), ("g3", 8), ("gf", 8), ("qnorm", 3), ("kvnorm", 2),
                    ("qg", 1), ("qgp", 1), ("kg", 1), ("kgp", 1), ("dqg", 1), ("dqgp", 1),
                    ("dkg", 1), ("dkgp", 1), ("lq1", 1), ("lk1", 1), ("lq2", 1), ("lk2", 1),
                    ("subln", 1), ("freqa", 1), ("signa", 1), ("freqb", 1), ("signb", 1)]:
    PVB[_name] = _n
    _n += _cnt
NPVB = _n
NPVA = 80


def build_nc(stages=("ffn1", "mix", "ffn2"), debug=None):
    nc = bass.Bass("TRN2", target_bir_lowering=False)
    P = Prog(nc)
    A = Arena(nc)

    def dram_in(name, shape, dt=F32):
        return nc.dram_tensor(name, shape, dt, kind="ExternalInput").ap()

    x_d = dram_in("x", [T, D])
    pos_d = dram_in("pos", [1, T], I32)
    pva_d = dram_in("pva", [NPVA, 128])
    pvb_d = dram_in("pvb", [NPVB, 128])
    ident_d = dram_in("ident", [128, 128])
    wada_d = dram_in("w_ada", [D, 9 * D])
    f1g_d = dram_in("f1g", [D, DFF])
    f1u_d = dram_in("f1u", [D, DFF])
    f1d_d = dram_in("f1d", [DFF, D])
    f2g_d = dram_in("f2g", [D, DFF])
    f2u_d = dram_in("f2u", [D, DFF])
    f2d_d = dram_in("f2d", [DFF, D])
    win_d = dram_in("w_in", [D, IN_COLS])
    wuq_d = dram_in("w_uq", [384, 768])
    wukv_d = dram_in("w_ukv", [256, 1024])
    woa_d = dram_in("w_oa", [512, D])
    wob_d = dram_in("w_ob", [512, D])
    wout_d = dram_in("w_out", [D, D])
    out_d = nc.dram_tensor("out", [T, D], F32, kind="ExternalOutput").ap()
    dbg_d = None
    if debug is not None:
        dbg_d = nc.dram_tensor("dbg", [128] + list(debug[1]), F32, kind="ExternalOutput").ap()

    H_OFF = 0
    MISC = 65536
    NT_OFF = MISC + 14336
    R_OFF = NT_OFF + 32768
    R_SIZE = A.nbytes - R_OFF
    assert R_SIZE >= 98304, R_SIZE

    hT = A.view(H_OFF, F32, [KC, T])
    nT = A.view(NT_OFF, BF16, [KC, T])
    ident = A.view(MISC + 0, F32, [128])
    ones = A.view(MISC + 512, BF16, [128])
    ones2 = A.view(MISC + 768, BF16, [128])
    pvT = A.view(MISC + 1024, F32, [256])
    modT = A.view(MISC + 2048, F32, [72])
    der = A.view(MISC + 2560, F32, [128])
    sqs = [A.view(MISC + 3072 + i * 1024, BF16, [GS]) for i in range(2)]
    fsc = [A.view(MISC + 5120 + i * 2048, F32, [GS]) for i in range(2)]
    rsb = [A.view(MISC + 9216 + i * 2048, F32, [GS]) for i in range(2)]
    condb = A.view(MISC + 13312, BF16, [8])
    ps = nc.alloc_psum_tensor("ps", [128, 8, GS], F32)

    b_h = [[P.buf("h%d_%d" % (c, g)) for g in range(NG)] for c in range(KC)]
    b_n = [[P.buf("n%d_%d" % (c, g)) for g in range(NG)] for c in range(KC)]
    b_ps = P.bufs("ps", 8)
    b_const = P.buf("const")
    b_pvT = P.buf("pvT")
    b_mod = P.buf("mod")
    b_der = P.buf("der")
    b_sqs = P.bufs("sqs", 2)
    b_fsc = P.bufs("fsc", 2)
    b_rsb = P.bufs("rsb", 2)
    b_condb = P.buf("condb")

    def pvb(name, i=0):
        c = 80 + PVB[name] + i
        return pvT[:, c:c + 1]

    DER = {"A1": 0, "G1": 8, "A2": 16, "G2": 24, "A3": 32, "G3": 40, "SH1": 48, "SH2": 56, "SH3": 64,
           "nlam": 72, "subs": 73, "lamt": 74}

    def dcol(name, i=0):
        c = DER[name] + i
        return der[:, c:c + 1]

    P.dma("sp", ident[:], ident_d, writes=[b_const])
    P.op("pool", MEMSET(ones[:], 1.0), writes=[b_const])
    P.op("pool", MEMSET(ones2[:], 0.0), writes=[b_const])
    P.op("pool", MEMSET(ones2[0:64, 0:64], 1.0), writes=[b_const])
    P.op("pool", MEMSET(ones2[64:128, 64:128], 1.0), writes=[b_const])
    pstage = A.view(R_OFF + 32768, F32, [2, 128])
    b_pstage = P.buf("pstage")
    P.op("pool", MEMSET(pstage[:], 0.0), writes=[b_pstage])
    P.dma("sp", pstage[0:NPVA, 0, :], pva_d, writes=[b_pstage])
    P.dma("sp", pstage[0:NPVB, 1, :], pvb_d, writes=[b_pstage])
    P.op("pe", TR(ps[:, 7, 0:NPVA], pstage[0:NPVA, 0, :], ident[0:NPVA, 0:NPVA]),
         reads=[b_pstage, b_const], writes=[b_ps[7]], inc=False)
    P.op("pe", TR(ps[:, 7, 128:128 + NPVB], pstage[0:NPVB, 1, :], ident[0:NPVB, 0:NPVB]),
         reads=[b_pstage, b_const], writes=[b_ps[7]])
    P.op("dve", CP(pvT[:, 0:NPVA], ps[:, 7, 0:NPVA]), reads=[b_ps[7]], writes=[b_pvT])
    P.op("dve", CP(pvT[:, 80:80 + NPVB], ps[:, 7, 128:128 + NPVB]), reads=[b_ps[7]], writes=[b_pvT])
    P.retire(b_pstage)
    P.op("act", ACT(condb[:], pvT[:, 0:8], AF.Silu), reads=[b_pvT], writes=[b_condb])

    ring = {"wd_i": 0, "gu_i": 0, "b_wd": P.bufs("wdp", 2), "b_gu": P.bufs("gu", 2)}

    AT_OFF = R_OFF
    GU_OFF = R_OFF + 32768
    WD_OFF = R_OFF + 65536

    def wada_block(m, dst, b_dst):
        src = wada_d.rearrange("(kc p) n -> p kc n", p=128)[:, :, m * 1024:(m + 1) * 1024]
        for hh in range(2):
            P.dma("pool", dst[:, hh * 4:(hh + 1) * 4, :], src[:, hh * 4:(hh + 1) * 4, :], writes=[b_dst])

    def mod_block(m, wt, b_wt):
        for j in range(8):
            for kc in range(KC):
                P.op("pe", MM(ps[:, 6, m * 8 + j:m * 8 + j + 1], wt[:, kc, j * 128:(j + 1) * 128],
                              condb[:, kc:kc + 1], kc == 0, kc == KC - 1),
                     reads=[b_wt, b_condb], writes=[b_ps[6]], inc=(j == 7 and kc == KC - 1))
        P.op("dve", TT(modT[:, m * 8:(m + 1) * 8], ps[:, 6, m * 8:(m + 1) * 8],
                       pvT[:, 8 + m * 8:8 + (m + 1) * 8], ALU.add),
             reads=[b_ps[6], b_pvT], writes=[b_mod])

    def derive_A(dst, gname, m_sc, m_sh, shname):
        P.op("dve", STT(der[:, DER[dst]:DER[dst] + 8], modT[:, m_sc * 8:(m_sc + 1) * 8], 1.0,
                        pvT[:, 80 + PVB[gname]:80 + PVB[gname] + 8], ALU.add, ALU.mult),
             reads=[b_mod, b_pvT], writes=[b_der])
        P.op("dve", CP(der[:, DER[shname]:DER[shname] + 8], modT[:, m_sh * 8:(m_sh + 1) * 8]),
             reads=[b_mod], writes=[b_der])

    def derive_G(dst, m_gt, scale):
        P.op("dve", TS(der[:, DER[dst]:DER[dst] + 8], modT[:, m_gt * 8:(m_gt + 1) * 8], float(scale), None, ALU.mult),
             reads=[b_mod], writes=[b_der])

    def norm_group(g, Aname, SHname, sq_from_psum=None):
        sl = slice(g * GS, (g + 1) * GS)
        sb = 6 + (g % 2)
        for c in range(KC):
            i = c % 2
            P.op("act", ACT(sqs[i][:], hT[:, c, sl], AF.Square), reads=[b_h[c][g]], writes=[b_sqs[i]])
            P.op("pe", MM(ps[:, sb, :], ones[:], sqs[i][:], c == 0, c == KC - 1),
                 reads=[b_const, b_sqs[i]], writes=[b_ps[sb]], inc=True)
        r = g % 2
        P.op("act", ACT(rsb[r][:], ps[:, sb, :], AF.Sqrt, bias=epsc[:], scale=1.0 / D),
             reads=[b_ps[sb], b_const], writes=[b_rsb[r]])
        P.op("dve", RECIP(rsb[r][:], rsb[r][:]), reads=[b_rsb[r]], writes=[b_rsb[r]])
        if Aname is None:
            return r
        for c in range(KC):
            i = c % 2
            P.op("dve", TT(fsc[i][:], hT[:, c, sl], rsb[r][:], ALU.mult),
                 reads=[b_h[c][g], b_rsb[r]], writes=[b_fsc[i]])
            P.op("act", ACT(nT[:, c, sl], fsc[i][:], AF.Identity, bias=dcol(SHname, c), scale=dcol(Aname, c)),
                 reads=[b_fsc[i], b_der], writes=[b_n[c][g]])
        return r

    epsc = A.view(MISC + 13312 + 64, F32, [1])
    P.op("pool", MEMSET(epsc[:], EPS), writes=[b_const])

    aT = A.view(AT_OFF, BF16, [8, T])
    gu = [A.view(GU_OFF + i * 16384, BF16, [2, KC, GS]) for i in range(2)]
    wdp = [A.view(WD_OFF + i * 16384, BF16, [8, D]) for i in range(2)]
    PASSES = [(0, 8), (8, 16), (16, 22)]

    def ffn(wg_d, wu_d, wd_d, Gname, st, pre_hooks):
        wg_v = wg_d.rearrange("(kc p) n -> p kc n", p=128)
        wu_v = wu_d.rearrange("(kc p) n -> p kc n", p=128)
        wd_v = wd_d.rearrange("(j p) n -> p j n", p=128)
        b_a = [[P.buf("a%d_%d" % (j, g)) for g in range(NG)] for j in range(8)]
        it = 0
        for pi, (c0, c1) in enumerate(PASSES):
            npc = c1 - c0
            wslot = st["wd_i"] % 2
            st["wd_i"] += 1
            b_wd = st["b_wd"][wslot]
            for hh in range(0, npc, 4):
                h1 = min(hh + 4, npc)
                P.dma("pool", wdp[wslot][:, hh:h1, :], wd_v[:, c0 + hh:c0 + h1, :], writes=[b_wd])
            groups = [(a, min(a + 4, c1)) for a in range(c0, c1, 4)]
            loaded = []
            for (a, b) in groups:
                gslot = st["gu_i"] % 2
                st["gu_i"] += 1
                b_g = st["b_gu"][gslot]
                ncol = (b - a) * 128
                P.dma("pool", gu[gslot][:, 0, :, 0:ncol], wg_v[:, :, a * 128:b * 128], writes=[b_g])
                P.dma("pool", gu[gslot][:, 1, :, 0:ncol], wu_v[:, :, a * 128:b * 128], writes=[b_g])
                loaded.append((a, b, gslot, b_g))
                for j in range(a, b):
                    jl = j - c0
                    jo = (j - a) * 128
                    for g in range(NG):
                        sl = slice(g * GS, (g + 1) * GS)
                        bg = it % 2
                        bu = 2 + it % 2
                        it += 1
                        for kc in range(KC):
                            P.op("pe", MM(ps[:, bg, :], gu[gslot][:, 0, kc, jo:jo + 128], nT[:, kc, sl], kc == 0, kc == KC - 1),
                                 reads=[b_g, b_n[kc][g]], writes=[b_ps[bg]], inc=(kc == KC - 1))
                        for kc in range(KC):
                            P.op("pe", MM(ps[:, bu, :], gu[gslot][:, 1, kc, jo:jo + 128], nT[:, kc, sl], kc == 0, kc == KC - 1),
                                 reads=[b_g, b_n[kc][g]], writes=[b_ps[bu]], inc=(kc == KC - 1))
                        i = it % 2
                        P.op("act", ACT(fsc[i][:], ps[:, bg, :], AF.Silu), reads=[b_ps[bg]], writes=[b_fsc[i]])
                        P.op("dve", TT(aT[:, jl, sl], fsc[i][:], ps[:, bu, :], ALU.mult),
                             reads=[b_fsc[i], b_ps[bu]], writes=[b_a[jl][g]])
                hk = pre_hooks.get((pi, len(loaded) - 1))
                if hk is not None:
                    hk()
            hk = pre_hooks.get((pi, "down"))
            if hk is not None:
                hk()
            dn = 0
            for g in range(NG):
                sl = slice(g * GS, (g + 1) * GS)
                for oc in range(KC):
                    bd = 4 + dn % 2
                    dn += 1
                    for jl in range(npc):
                        P.op("pe", MM(ps[:, bd, :], wdp[wslot][:, jl, oc * 128:(oc + 1) * 128], aT[:, jl, sl], jl == 0, jl == npc - 1),
                             reads=[b_wd, b_a[jl][g]], writes=[b_ps[bd]], inc=(jl == npc - 1))
                    P.op("dve", STT(hT[:, oc, sl], ps[:, bd, :], dcol(Gname, oc), hT[:, oc, sl], ALU.mult, ALU.add),
                         reads=[b_ps[bd], b_der, b_h[oc][g]], writes=[b_h[oc][g]])
        P.retire(*[b for row in b_a for b in row])


    xs = [A.view(AT_OFF + i * 4096, F32, [D]) for i in range(8)]
    b_xs = P.bufs("xs", 8)
    wa = [A.view(WD_OFF + i * 16384, BF16, [KC, D]) for i in range(2)]
    wada_block(0, wa[0], ring["b_wd"][0])
    wada_block(1, wa[1], ring["b_wd"][1])
    x_v = x_d.rearrange("(t p) d -> t p d", p=128)
    for g in range(NG):
        for j in range(4):
            tt = g * 4 + j
            s = tt % 8
            P.dma("sp", xs[s][:], x_v[tt], writes=[b_xs[s]])
        for c in range(KC):
            bk = c % 2
            for j in range(4):
                s = (g * 4 + j) % 8
                P.op("pe", TR(ps[:, bk, j * 128:(j + 1) * 128], xs[s][:, c * 128:(c + 1) * 128], ident[:]),
                     reads=[b_xs[s], b_const], writes=[b_ps[bk]], inc=(j == 3))
            P.op("act", ACT(hT[:, c, g * GS:(g + 1) * GS], ps[:, bk, :], AF.Copy), reads=[b_ps[bk]], writes=[b_h[c][g]])
        if g == 0:
            mod_block(0, wa[0], ring["b_wd"][0])
            mod_block(1, wa[1], ring["b_wd"][1])
            derive_A("A1", "g1", 1, 0, "SH1")
    P.retire(*b_xs)
    ring["wd_i"] = 0

    for g in range(NG):
        norm_group(g, "A1", "SH1")

    def make_mod_hook(ms):
        def hk():
            for m in ms:
                wslot = ring["wd_i"] % 2
                b_w_ = ring["b_wd"][wslot]
                wv = A.view(WD_OFF + wslot * 16384, BF16, [KC, D])
                wada_block(m, wv, b_w_)
                mod_block(m, wv, b_w_)
        return hk

    def hook_g1():
        derive_G("G1", 2, 0.5)

    if "ffn1" in stages:
        ffn(f1g_d, f1u_d, f1d_d, "G1", ring,
            {(0, 0): make_mod_hook([2]), (0, "down"): hook_g1,
             (0, 1): make_mod_hook([3]), (1, 0): make_mod_hook([4])})
    else:
        make_mod_hook([2, 3, 4])()
    derive_A("A2", "g2", 4, 3, "SH2")
    LATE_MOD = "mix" in stages
    if not LATE_MOD:
        make_mod_hook([5, 6, 7, 8])()
        derive_G("G2", 5, 1.0)
        derive_A("A3", "g3", 7, 6, "SH3")
        derive_G("G3", 8, 0.5)

    if "mix" in stages:
        for g in range(NG):
            norm_group(g, "A2", "SH2")
        hsp = nc.dram_tensor("hspill", [128, KC, T], F32).ap()
        b_hsp = P.buf("hsp")
        for c in range(KC):
            P.dma("sp", hsp[:, c, :], hT[:, c, :], reads=[b_h[c][g] for g in range(NG)], writes=[b_hsp])
        P.retire(*[b for row in b_h for b in row])
        P.retire(*ring["b_wd"])
        P.retire(*ring["b_gu"])
        M2 = R_OFF
        SC_A = 1.0 / math.sqrt(96.0)
        SC_B = 1.0 / math.sqrt(64.0)
        win_v = win_d.rearrange("(kc p) n -> p kc n", p=128)
        oaT = A.view(0, BF16, [4, T])
        obT = A.view(16384, BF16, [4, T])
        cosF = A.view(32768, F32, [T])
        sinF = A.view(40960, F32, [T])
        kpeR = A.view(49152, F32, [T])
        sqpe = A.view(57344, BF16, [T])
        b_oa = [[P.buf("oa") for g in range(NG)] for c in range(4)]
        b_ob = [[P.buf("ob") for g in range(NG)] for c in range(4)]
        b_tab = P.buf("tab")
        b_kpe = P.bufs("kpe", NG)
        wlat = A.view(M2 + 0, BF16, [KC, 640])
        wkr = A.view(M2 + 10240, BF16, [KC, 96])
        wkrp = A.view(M2 + 11776, BF16, [KC, 96])
        wuq = A.view(M2 + 13312, BF16, [3, 768])
        wuqp = A.view(M2 + 17920, BF16, [3, 8, 96])
        wukk = A.view(M2 + 22528, BF16, [2, 8, 64])
        wukv = A.view(M2 + 24576, BF16, [2, 512])
        wuqp_f = A.view(M2 + 17920, BF16, [3, 768])
        wukk_f = A.view(M2 + 22528, BF16, [2, 512])
        cqT = A.view(M2 + 26624, BF16, [3, T])
        ckvT = A.view(M2 + 38912, BF16, [2, T])
        vh = [A.view(M2 + 47104 + i * 4096, BF16, [16, 128]) for i in range(2)]
        qh = [A.view(M2 + 55296 + i * 4096, BF16, [T]) for i in range(2)]
        kh = [A.view(M2 + 63488 + i * 4096, BF16, [T]) for i in range(2)]
        PT = [A.view(M2 + 71680 + i * 1024, BF16, [GS]) for i in range(4)]
        oraw = [A.view(M2 + 75776 + i * 2048, F32, [GS]) for i in range(2)]
        dn = [A.view(M2 + 79872 + i * 2048, F32, [GS]) for i in range(2)]
        onesf = A.view(M2 + 83968, F32, [128])
        rin = A.view(M2 + 84480, F32, [GS])
        posi = A.view(M2 + 26624, I32, [T])
        tmpf = A.view(M2 + 34816, F32, [T])
        kfi = A.view(M2 + 43008, I32, [T])
        kff = A.view(M2 + 43008, F32, [T])
        b_tt = P.buf("tabtmp")
        b_w = {k: P.buf("w_" + k) for k in ("lat", "kr", "uq", "uqp", "ukk", "ukv")}
        b_PT = P.bufs("PT", 4)
        b_oraw = P.bufs("oraw", 2)
        b_dn = P.bufs("dn", 2)
        b_rin = P.buf("rin")
        sq_e = A.view(M2 + 86528, BF16, [GS])
        od_e = A.view(M2 + 87552, F32, [GS])
        rs_e = A.view(M2 + 89600, F32, [GS])
        b_sqe, b_ode, b_rse = P.buf("sqe"), P.buf("ode"), P.buf("rse")
        b_onesf = P.buf("onesf")
        P.op("pool", MEMSET(onesf[:], 1.0), writes=[b_onesf])
        self_ = [A.view(M2 + 91648 + i * 512, F32, [128]) for i in range(2)]
        rq_t = [A.view(M2 + 92672 + i * 128, F32, [32]) for i in range(2)]
        rk_t = [A.view(M2 + 92928 + i * 128, F32, [32]) for i in range(2)]
        t4 = [A.view(M2 + 93184 + i * 32, F32, [8]) for i in range(2)]
        dg = [A.view(M2 + 93248 + i * 512, F32, [128]) for i in range(8)]
        sel2b = A.view(M2 + 97344, BF16, [2])
        b_rq, b_rk = P.bufs("rq", 2), P.bufs("rk", 2)
        b_t4 = P.bufs("t4", 2)
        b_dg = P.bufs("dg", 8)
        P.op("pool", MEMSET(self_[0][:], 0.0), writes=[b_onesf])
        P.op("pool", MEMSET(self_[1][:], 0.0), writes=[b_onesf])
        P.op("pool", MEMSET(self_[0][:, 0:64], 1.0), writes=[b_onesf])
        P.op("pool", MEMSET(self_[1][:, 64:128], 1.0), writes=[b_onesf])
        P.op("pool", MEMSET(sel2b[:], 0.0), writes=[b_onesf])
        P.op("pool", MEMSET(sel2b[0:64, 0:1], 1.0), writes=[b_onesf])
        P.op("pool", MEMSET(sel2b[64:128, 1:2], 1.0), writes=[b_onesf])
        cmh = A.view(MISC + 13312 + 72, F32, [1])
        P.op("pool", MEMSET(cmh[:], -0.5), writes=[b_onesf])

        def POWH(out, in0, ncol):
            return lambda e: e.tensor_tensor(out=out, in0=in0, in1=cmh[:, 0:1].to_broadcast([128, ncol]), op=ALU.pow)

        TWO_PI = 2.0 * math.pi
        C1 = 6.28125
        C2 = TWO_PI - C1

        def build_tables(fname, sname):
            P.dma("sp", posi[:], pos_d.partition_broadcast(128), writes=[b_tt])
            P.op("dve", CP(tmpf[:], posi[:]), reads=[b_tt], writes=[b_tt])
            P.op("dve", TS(tmpf[:], tmpf[:], pvb(fname), None, ALU.mult), reads=[b_tt, b_pvT], writes=[b_tt])
            P.op("dve", TS(cosF[:], tmpf[:], 1.0 / TWO_PI, None, ALU.mult), reads=[b_tt], writes=[b_tab])
            P.op("dve", CP(kfi[:], cosF[:]), reads=[b_tab], writes=[b_tt])
            P.op("dve", CP(cosF[:], kfi[:]), reads=[b_tt], writes=[b_tab])
            P.op("dve", STT(tmpf[:], cosF[:], -C1, tmpf[:], ALU.mult, ALU.add), reads=[b_tab, b_tt], writes=[b_tt])
            P.op("dve", STT(tmpf[:], cosF[:], -C2, tmpf[:], ALU.mult, ALU.add), reads=[b_tab, b_tt], writes=[b_tt])
            P.op("dve", TS(tmpf[:], tmpf[:], math.pi, -math.pi, ALU.min, ALU.max), reads=[b_tt], writes=[b_tt])
            P.op("act", ACT(sinF[:], tmpf[:], AF.Sin, scale=pvb(sname)), reads=[b_tt, b_pvT], writes=[b_tab])
            P.op("dve", TS(tmpf[:], tmpf[:], math.pi / 2, None, ALU.add), reads=[b_tt], writes=[b_tt])
            P.op("dve", TS(kff[:], tmpf[:], math.pi, -TWO_PI, ALU.is_gt, ALU.mult), reads=[b_tt], writes=[b_tt])
            P.op("dve", TT(tmpf[:], tmpf[:], kff[:], ALU.add), reads=[b_tt], writes=[b_tt])
            P.op("dve", TS(tmpf[:], tmpf[:], math.pi, -math.pi, ALU.min, ALU.max), reads=[b_tt], writes=[b_tt])
            P.op("act", ACT(cosF[:], tmpf[:], AF.Sin), reads=[b_tt], writes=[b_tab])

        def rstd_from(bank, rows, n, r):
            if _os.environ.get("MK_LN") != "1":
                P.op("act", ACT(rsb[r][rows, :], ps[rows, bank, :], AF.Sqrt, bias=epsc[rows, :], scale=1.0 / n),
                     reads=[b_ps[bank], b_const], writes=[b_rsb[r]])
                P.op("dve", RECIP(rsb[r][rows, :], rsb[r][rows, :]), reads=[b_rsb[r]], writes=[b_rsb[r]])
                return
            P.op("act", ACT(rsb[r][rows, :], ps[rows, bank, :], AF.Ln, bias=epsc[rows, :], scale=1.0 / n),
                 reads=[b_ps[bank], b_const], writes=[b_rsb[r]])
            P.op("act", ACT(rsb[r][rows, :], rsb[r][rows, :], AF.Exp, scale=-0.5), reads=[b_rsb[r]], writes=[b_rsb[r]])

        def rope_parts(nrows, gcol, gpcol, sl, out_ap, b_out, after=()):
            rows = slice(0, nrows)
            P.op("dve", STT(fsc[1][rows, :], ps[rows, 5, :], gpcol[rows, :], sinF[rows, sl], ALU.mult, ALU.mult),
                 reads=[b_ps[5], b_pvT, b_tab], writes=[b_fsc[1]])
            P.op("dve", STT(fsc[0][rows, :], ps[rows, 4, :], gcol[rows, :], cosF[rows, sl], ALU.mult, ALU.mult),
                 reads=[b_ps[4], b_pvT, b_tab] + list(after), writes=[b_fsc[0]])
            P.op("dve", TT(out_ap, fsc[0][rows, :], fsc[1][rows, :], ALU.add), reads=[b_fsc[0], b_fsc[1]], writes=[b_out])

        def tiny_rstd(i, which, g, ncomp, n, mult, stat_fn):
            base = 0 if which == "q" else 32
            dst_t, b_dst_t = (rq_t[i], b_rq[i]) if which == "q" else (rk_t[i], b_rk[i])
            c0 = g * 4 * ncomp
            if _os.environ.get("MK_SKIP") == which:
                P.op("pool", MEMSET(dst_t[:, c0:c0 + 4 * ncomp], 0.1), writes=[b_dst_t])
                return
            for j in range(4):
                stat_fn(j, ps[:, 6, base + c0 + j * ncomp:base + c0 + (j + 1) * ncomp])
            ti = 0 if which == "q" else 1
            P.op("dve", TS(t4[ti][:, 0:4 * ncomp], ps[:, 6, base + c0:base + c0 + 4 * ncomp], mult / n, EPS * mult, ALU.mult, ALU.add),
                 reads=[b_ps[6]], writes=[b_t4[ti]])
            P.op("pool", POWH(dst_t[:, c0:c0 + 4 * ncomp], t4[ti][:, 0:4 * ncomp], 4 * ncomp), reads=[b_t4[ti], b_onesf], writes=[b_dst_t])

        def qprep_gen(i, nrows, ncomp, n, gcol, gpcol, dst, b_dst, sl, g):
            rows = slice(0, nrows)
            P.op("act", ACT(sqs[0][rows, :], ps[rows, 4, :], AF.Square), reads=[b_ps[4]], writes=[b_sqs[0]])
            rope_parts(nrows, gcol, gpcol, sl, fsc[0][rows, :], b_fsc[0], after=[b_sqs[0]])
            yield

            def stat(j, out_ap):
                rhs = ones[rows, 0:1] if ncomp == 1 else sel2b[:, 0:2]
                P.op("pe", MM(out_ap, sqs[0][rows, j * 128:(j + 1) * 128], rhs, True, True),
                     reads=[b_sqs[0], b_const, b_onesf], writes=[b_ps[6]])
            tiny_rstd(i, "q", g, ncomp, n, 1.0, stat)
            yield
            c0 = g * 4 * ncomp
            for j in range(4):
                for c in range(ncomp):
                    d = j * ncomp + c
                    col = c0 + j * ncomp + c
                    P.op("dve", TS(dg[d][:], ident[:], rq_t[i][:, col:col + 1], None, ALU.mult),
                         reads=[b_const, b_rq[i]], writes=[b_dg[d]])
            for j in range(4):
                for c in range(ncomp):
                    d = j * ncomp + c
                    lhsT = onesf[:, 0:128] if ncomp == 1 else self_[c][:, 0:128]
                    P.op("pe", MM(ps[0:128, 4, j * 128:(j + 1) * 128], lhsT, dg[d][:], c == 0, c == ncomp - 1),
                         reads=[b_onesf, b_dg[d]], writes=[b_ps[4]], inc=(j == 3 and c == ncomp - 1))
            P.op("dve", TT(dst, fsc[0][rows, :], ps[rows, 4, :], ALU.mult), reads=[b_fsc[0], b_ps[4]], writes=[b_dst])
            yield

        ring_i = [0]
        pend = []

        def step_pend():
            for gen_ in list(pend):
                try:
                    next(gen_)
                except StopIteration:
                    pend.remove(gen_)

        def attention(kT, b_k, qT, b_q, rows, scale, vts, qc, filler, fill_at):
            qsl = slice(qc * GS, (qc + 1) * GS)

            def S(kt):
                it = ring_i[0]
                ring_i[0] += 1
                sb = it % 2
                pr = it % 4
                P.op("pe", MM(ps[:, sb, :], kT[rows, kt * 128:(kt + 1) * 128], qT[rows, qsl], True, True),
                     reads=[b_k, b_q], writes=[b_ps[sb]])
                sc_ap, b_sc = scale(kt)
                if _os.environ.get("MK_CONSTSC") == "1":
                    sc_ap = 0.1
                P.op("act", ACT(PT[pr][:], ps[:, sb, :], AF.Exp, scale=sc_ap), reads=[b_ps[sb], b_sc], writes=[b_PT[pr]])
                return pr

            prs = {0: S(0)}
            for kt in range(16):
                if kt + 1 < 16:
                    prs[kt + 1] = S(kt + 1)
                pr = prs[kt]
                for vi, (vfn, M, bank, b_v) in enumerate(vts):
                    P.op("pe", MM(ps[0:M, bank, :], vfn(kt), PT[pr][:], kt == 0, kt == 15),
                         reads=[b_v, b_PT[pr]], writes=[b_ps[bank]], inc=(kt == 15 and vi == len(vts) - 1))
                if kt in fill_at:
                    filler()
                if kt in (4, 8, 12):
                    step_pend()

        def recip_bcast(src, drow):
            P.op("dve", RECIP(rin[drow:drow + 1, :], src[drow:drow + 1, :]), reads=[b_oraw[0], b_oraw[1]], writes=[b_rin])
            P.op("pe", MM(ps[:, 7, :], onesf[drow:drow + 1, :], rin[drow:drow + 1, :], True, True),
                 reads=[b_onesf, b_rin], writes=[b_ps[7]])

        NOFILL = _os.environ.get("MK_NOFILL") == "1"

        def make_filler(gen):
            def f():
                if gen is not None and not NOFILL:
                    next(gen, None)
            return f

        def drain(gen):
            if gen is not None:
                for _ in gen:
                    pass

        for hh in range(2):
            P.dma("pool", wlat[:, hh * 4:(hh + 1) * 4, :], win_v[:, hh * 4:(hh + 1) * 4, 0:640], writes=[b_w["lat"]])
        P.op("pool", MEMSET(wkr[:], 0.0), writes=[b_w["kr"]])
        P.op("pool", MEMSET(wkrp[:], 0.0), writes=[b_w["kr"]])
        P.dma("pool", wkr[:, :, 64:96], win_v[:, :, 640:672], writes=[b_w["kr"]])
        P.dma("pool", wkrp[:, :, 64:80], win_v[:, :, 656:672], writes=[b_w["kr"]])
        P.dma("pool", wkrp[:, :, 80:96], win_v[:, :, 640:656], writes=[b_w["kr"]])
        P.dma("pool", wuq[:], wuq_d.rearrange("(kc p) n -> p kc n", p=128), writes=[b_w["uq"]])
        P.op("pool", MEMSET(wuqp[:], 0.0), writes=[b_w["uqp"]])
        wuq_v4 = wuq_d.rearrange("(kc p) (h d) -> p kc h d", p=128, d=96)
        for kc in range(3):
            P.dma("pool", wuqp[:, kc, :, 64:80], wuq_v4[:, kc, :, 80:96], writes=[b_w["uqp"]])
            P.dma("pool", wuqp[:, kc, :, 80:96], wuq_v4[:, kc, :, 64:80], writes=[b_w["uqp"]])
        wukv_v4 = wukv_d.rearrange("(kc p) (h d) -> p kc h d", p=128, d=128)
        wukv_s = wukv.rearrange("p kc (h d) -> p kc h d", d=64)
        for kc in range(2):
            P.dma("pool", wukk[:, kc, :, :], wukv_v4[:, kc, :, 0:64], writes=[b_w["ukk"]])
            P.dma("pool", wukv_s[:, kc, :, :], wukv_v4[:, kc, :, 64:128], writes=[b_w["ukv"]])

        build_tables("freqa", "signa")
        P.retire(b_tt)
        b_cq = [[P.buf("cq") for g in range(NG)] for c in range(3)]
        b_ckv = [[P.buf("ckv") for g in range(NG)] for c in range(2)]
        b_vh = P.bufs("vh", 2)
        b_qh = P.bufs("qh", 2)
        b_kh = P.bufs("kh", 2)

        zst = [oraw[0], oraw[1], dn[0]]
        b_zst = [b_oraw[0], b_oraw[1], b_dn[0]]
        for g in range(NG):
            sl = slice(g * GS, (g + 1) * GS)
            for (lc0, nl, dstT, b_dst, nname, nfeat, sbank, r) in ((0, 3, cqT, b_cq, "qnorm", 384.0, 6, 0), (3, 2, ckvT, b_ckv, "kvnorm", 256.0, 7, 1)):
                for l in range(nl):
                    bk = 4 + l % 2
                    for kc in range(KC):
                        P.op("pe", MM(ps[:, bk, :], wlat[:, kc, (lc0 + l) * 128:(lc0 + l + 1) * 128], nT[:, kc, sl], kc == 0, kc == KC - 1),
                             reads=[b_w["lat"], b_n[kc][g]], writes=[b_ps[bk]], inc=(kc == KC - 1))
                    P.op("act", ACT(zst[l][:], ps[:, bk, :], AF.Copy), reads=[b_ps[bk]], writes=[b_zst[l]])
                    P.op("act", ACT(sqs[l % 2][:], ps[:, bk, :], AF.Square), reads=[b_ps[bk]], writes=[b_sqs[l % 2]])
                    P.op("pe", MM(ps[:, sbank, :], ones[:], sqs[l % 2][:], l == 0, l == nl - 1),
                         reads=[b_const, b_sqs[l % 2]], writes=[b_ps[sbank]])
                rstd_from(sbank, slice(0, 128), nfeat, r)
                for l in range(nl):
                    P.op("dve", STT(dstT[:, l, sl], zst[l][:], pvb(nname, l), rsb[r][:], ALU.mult, ALU.mult),
                         reads=[b_zst[l], b_pvT, b_rsb[r]], writes=[b_dst[l][g]])
            for (wt, bk) in ((wkr, 4), (wkrp, 5)):
                for kc in range(KC):
                    P.op("pe", MM(ps[0:96, bk, :], wt[:, kc, :], nT[:, kc, sl], kc == 0, kc == KC - 1),
                         reads=[b_w["kr"], b_n[kc][g]], writes=[b_ps[bk]], inc=(kc == KC - 1))
            R96 = slice(0, 96)
            P.op("act", ACT(sqpe[R96, sl], ps[R96, 4, :], AF.Square), reads=[b_ps[4]], writes=[b_kpe[g]])
            P.op("dve", STT(fsc[0][R96, :], ps[R96, 4, :], pvb("kg")[R96, :], cosF[R96, sl], ALU.mult, ALU.mult),
                 reads=[b_ps[4], b_pvT, b_tab, b_kpe[g]], writes=[b_fsc[0]])
            P.op("dve", STT(fsc[1][R96, :], ps[R96, 5, :], pvb("kgp")[R96, :], sinF[R96, sl], ALU.mult, ALU.mult),
                 reads=[b_ps[5], b_pvT, b_tab], writes=[b_fsc[1]])
            P.op("dve", TT(kpeR[R96, sl], fsc[0][R96, :], fsc[1][R96, :], ALU.add), reads=[b_fsc[0], b_fsc[1]], writes=[b_kpe[g]])

        R96 = slice(0, 96)

        def mla_prep(h):
            i = h % 2
            kindB = (h % 2 == 1)
            for g in range(NG):
                sl = slice(g * GS, (g + 1) * GS)
                Mq = 128 if h < 7 else 96
                Mk = 128 if h < 7 else 64
                for kc in range(3):
                    P.op("pe", MM(ps[0:Mq, 4, :], wuq[:, kc, h * 96:h * 96 + Mq], cqT[:, kc, sl], kc == 0, kc == 2),
                         reads=[b_w["uq"], b_cq[kc][g]], writes=[b_ps[4]], inc=(kc == 2))
                for kc in range(3):
                    P.op("pe", MM(ps[0:Mq, 5, :], wuqp_f[:, kc, h * 96:h * 96 + Mq], cqT[:, kc, sl], kc == 0, kc == 2),
                         reads=[b_w["uqp"], b_cq[kc][g]], writes=[b_ps[5]], inc=(kc == 2))
                yield from qprep_gen(i, 96, 1, 96.0, pvb("qg"), pvb("qgp"), qh[i][R96, sl], b_qh[i], sl, g)
                for kc in range(2):
                    P.op("pe", MM(ps[0:Mk, 5, :], wukk_f[:, kc, h * 64:h * 64 + Mk], ckvT[:, kc, sl], kc == 0, kc == 1),
                         reads=[b_w["ukk"], b_ckv[kc][g]], writes=[b_ps[5]], inc=(kc == 1))
                P.op("act", ACT(sqs[1][0:64, :], ps[0:64, 5, :], AF.Square), reads=[b_ps[5]], writes=[b_sqs[1]])
                P.op("dve", TS(kh[i][0:64, sl], ps[0:64, 5, :], pvb("kg")[0:64, :], None, ALU.mult),
                     reads=[b_ps[5], b_pvT, b_sqs[1]], writes=[b_kh[i]])
                P.op("dve", CP(kh[i][64:96, sl], kpeR[64:96, sl]), reads=[b_kpe[g]], writes=[b_kh[i]])
                yield

                def kstat(j, out_ap, g=g):
                    P.op("pe", MM(out_ap, sqs[1][0:64, j * 128:(j + 1) * 128], ones[0:64, 0:1], True, False),
                         reads=[b_sqs[1], b_const], writes=[b_ps[6]], inc=False)
                    P.op("pe", MM(out_ap, sqpe[64:96, g * GS + j * 128:g * GS + (j + 1) * 128], ones[64:96, 0:1], False, True),
                         reads=[b_kpe[g], b_const], writes=[b_ps[6]])
                tiny_rstd(i, "k", g, 1, 96.0, 96.0, kstat)
                yield
            P.op("pool", MEMSET(vh[i][:], 0.0), writes=[b_vh[i]])
            onescol = 32 if kindB else 64
            c0 = 64 if kindB else 0
            P.op("pool", MEMSET(vh[i][:, :, onescol:onescol + 1], 1.0), writes=[b_vh[i]])
            for t0 in range(0, 16, 8):
                for tt in range(t0, t0 + 8):
                    for kc in range(2):
                        P.op("pe", MM(ps[:, 5, (tt - t0) * 64:(tt - t0 + 1) * 64], ckvT[:, kc, tt * 128:(tt + 1) * 128],
                                      wukv[:, kc, h * 64:(h + 1) * 64], kc == 0, kc == 1),
                             reads=[b_ckv[kc][tt // 4], b_w["ukv"]], writes=[b_ps[5]], inc=(tt == t0 + 7 and kc == 1))
                P.op("dve", CP(vh[i][:, t0:t0 + 8, c0:c0 + 64], ps[:, 5, :].rearrange("p (a b) -> p a b", b=64)),
                     reads=[b_ps[5]], writes=[b_vh[i]])
                yield

        def mla_attn(h, filler):
            i = h % 2
            kindB = (h % 2 == 1)
            if kindB:
                vt = (lambda kt, i=i: vh[i][:, kt, 0:128], 128, 3, b_vh[i])
                drow, ob = 32, 1
                drows = slice(64, 128)
            else:
                vt = (lambda kt, i=i: vh[i][:, kt, 0:128], 128, 2, b_vh[i])
                drow, ob = 64, 0
                drows = slice(0, 64)
            for qc in range(NG):
                qsl = slice(qc * GS, (qc + 1) * GS)
                attention(kh[i], b_kh[i], qh[i], b_qh[i], R96, (lambda kt, i=i: (rk_t[i][:, kt:kt + 1], b_rk[i])), [vt], qc, filler, (1, 3, 5, 7, 9, 11, 13))
                if kindB:
                    P.op("dve", CP(oraw[ob][32:33, :], ps[32:33, 3, :]), reads=[b_ps[3]], writes=[b_oraw[ob]])
                    P.op("dve", CP(oraw[ob][64:128, :], ps[64:128, 3, :]), reads=[b_ps[3]], writes=[b_oraw[ob]])
                else:
                    P.op("dve", CP(oraw[ob][0:65, :], ps[0:65, 2, :]), reads=[b_ps[2]], writes=[b_oraw[ob]])
                P.op("dve", RECIP(rin[drow:drow + 1, :], oraw[ob][drow:drow + 1, :]), reads=[b_oraw[ob]], writes=[b_rin])

                def epi(ob=ob, drow=drow, drows=drows, qsl=qsl, qc=qc):
                    P.op("pe", MM(ps[:, 7, :], onesf[drow:drow + 1, :], rin[drow:drow + 1, :], True, True),
                         reads=[b_onesf, b_rin], writes=[b_ps[7]])
                    P.op("dve", TT(oaT[drows, h // 2, qsl], oraw[ob][drows, :], ps[drows, 7, :], ALU.mult),
                         reads=[b_oraw[ob], b_ps[7]], writes=[b_oa[h // 2][qc]])
                    yield
                pend.append(epi())

        wdh = [A.view(M2 + i * 10240, BF16, [KC, 640]) for i in range(2)]
        vb = [A.view(M2 + 26624 + i * 6400, BF16, [16, 200]) for i in range(2)]
        QB0, KB0, VB0 = 672, 1184, 1696
        dstate = {}

        def diff_setup():
            P.retire(*[b for row in b_cq for b in row], *[b for row in b_ckv for b in row], *b_vh, *b_w.values(), *b_kpe)
            nonlocal_b_tt = P.buf("tabtmp2")
            dstate["b_tt"] = nonlocal_b_tt
            yield

        def build_tables2(fname, sname, bt):
            P.dma("sp", posi[:], pos_d.partition_broadcast(128), writes=[bt])
            P.op("dve", CP(tmpf[:], posi[:]), reads=[bt], writes=[bt])
            P.op("dve", TS(tmpf[:], tmpf[:], pvb(fname), None, ALU.mult), reads=[bt, b_pvT], writes=[bt])
            yield
            P.op("dve", TS(cosF[:], tmpf[:], 1.0 / TWO_PI, None, ALU.mult), reads=[bt], writes=[b_tab])
            P.op("dve", CP(kfi[:], cosF[:]), reads=[b_tab], writes=[bt])
            P.op("dve", CP(cosF[:], kfi[:]), reads=[bt], writes=[b_tab])
            yield
            P.op("dve", STT(tmpf[:], cosF[:], -C1, tmpf[:], ALU.mult, ALU.add), reads=[b_tab, bt], writes=[bt])
            P.op("dve", STT(tmpf[:], cosF[:], -C2, tmpf[:], ALU.mult, ALU.add), reads=[b_tab, bt], writes=[bt])
            P.op("dve", TS(tmpf[:], tmpf[:], math.pi, -math.pi, ALU.min, ALU.max), reads=[bt], writes=[bt])
            P.op("act", ACT(sinF[:], tmpf[:], AF.Sin, scale=pvb(sname)), reads=[bt, b_pvT], writes=[b_tab])
            yield
            P.op("dve", TS(tmpf[:], tmpf[:], math.pi / 2, None, ALU.add), reads=[bt], writes=[bt])
            P.op("dve", TS(kff[:], tmpf[:], math.pi, -TWO_PI, ALU.is_gt, ALU.mult), reads=[bt], writes=[bt])
            P.op("dve", TT(tmpf[:], tmpf[:], kff[:], ALU.add), reads=[bt], writes=[bt])
            P.op("dve", TS(tmpf[:], tmpf[:], math.pi, -math.pi, ALU.min, ALU.max), reads=[bt], writes=[bt])
            P.op("act", ACT(cosF[:], tmpf[:], AF.Sin), reads=[bt], writes=[b_tab])
            yield

        def diff_prep(h):
            i = h % 2
            if h == 0:
                P.retire(*[b for row in b_cq for b in row], *[b for row in b_ckv for b in row], *b_vh, *b_w.values(), *b_kpe)
                bt = P.buf("tabtmp2")
                yield from build_tables2("freqb", "signb", bt)
                P.retire(bt)
                dstate["b_wdh"] = P.bufs("wdh", 2)
                dstate["b_vb"] = P.bufs("vb", 2)
                P.op("dve", TT(der[:, 74:75], pvb("lq1"), pvb("lk1"), ALU.mult), reads=[b_pvT], writes=[b_der])
                P.op("dve", TT(der[:, 75:76], pvb("lq2"), pvb("lk2"), ALU.mult), reads=[b_pvT], writes=[b_der])
                yield
                P.op("pe", MM(ps[:, 6, 0:2], onesf[:], der[:, 74:76], True, True), reads=[b_onesf, b_der], writes=[b_ps[6]])
                P.op("act", ACT(der[:, 74:76], ps[:, 6, 0:2], AF.Exp), reads=[b_ps[6]], writes=[b_der])
                P.op("dve", TT(der[:, 72:73], der[:, 75:76], der[:, 74:75], ALU.subtract), reads=[b_der], writes=[b_der])
                P.op("dve", TS(der[:, 72:73], der[:, 72:73], -LAMBDA_INIT, None, ALU.add), reads=[b_der], writes=[b_der])
                P.op("dve", TS(der[:, 73:74], pvb("subln"), 1.0 - LAMBDA_INIT, None, ALU.mult), reads=[b_pvT], writes=[b_der])
            b_wdh, b_vb = dstate["b_wdh"], dstate["b_vb"]

            def diff_load(hh):
                ii = hh % 2
                P.op("pool", MEMSET(wdh[ii][:, :, 128:256], 0.0), writes=[b_wdh[ii]])
                P.op("pool", MEMSET(wdh[ii][:, :, 384:512], 0.0), writes=[b_wdh[ii]])
                for (src0, d0) in ((QB0, 0), (KB0, 256)):
                    s0 = src0 + hh * 128
                    P.dma("pool", wdh[ii][:, :, d0:d0 + 128], win_v[:, :, s0:s0 + 128], writes=[b_wdh[ii]])
                    for base in (0, 64):
                        P.op("pool", CP(wdh[ii][:, :, d0 + 128 + base:d0 + 128 + base + 8], wdh[ii][:, :, d0 + base + 8:d0 + base + 16]),
                             reads=[b_wdh[ii]], writes=[b_wdh[ii]])
                        P.op("pool", CP(wdh[ii][:, :, d0 + 128 + base + 8:d0 + 128 + base + 16], wdh[ii][:, :, d0 + base:d0 + base + 8]),
                             reads=[b_wdh[ii]], writes=[b_wdh[ii]])
                P.dma("pool", wdh[ii][:, :, 512:640], win_v[:, :, VB0 + hh * 128:VB0 + (hh + 1) * 128], writes=[b_wdh[ii]])

            if h == 0:
                diff_load(0)
                diff_load(1)
            elif h + 1 < 4:
                diff_load(h + 1)
            P.op("pool", MEMSET(vb[i][:], 0.0), writes=[b_vb[i]])
            P.op("pool", MEMSET(vb[i][:, :, 64:65], 1.0), writes=[b_vb[i]])
            P.op("pool", MEMSET(vb[i][:, :, 72 + 32:72 + 33], 1.0), writes=[b_vb[i]])
            yield
            for t0 in range(0, 16, 4):
                for tt in range(t0, t0 + 4):
                    for kc in range(KC):
                        P.op("pe", MM(ps[:, 5, (tt - t0) * 128:(tt - t0 + 1) * 128], nT[:, kc, tt * 128:(tt + 1) * 128],
                                      wdh[i][:, kc, 512:640], kc == 0, kc == KC - 1),
                             reads=[b_n[kc][tt // 4], b_wdh[i]], writes=[b_ps[5]], inc=(tt == t0 + 3 and kc == KC - 1))
                pv = ps[:, 5, :].rearrange("p (a b) -> p a b", b=128)
                P.op("dve", CP(vb[i][:, t0:t0 + 4, 0:64], pv[:, :, 0:64]), reads=[b_ps[5]], writes=[b_vb[i]])
                P.op("dve", CP(vb[i][:, t0:t0 + 4, 136:200], pv[:, :, 64:128]), reads=[b_ps[5]], writes=[b_vb[i]])
                yield
            for g in range(NG):
                sl = slice(g * GS, (g + 1) * GS)
                for (d0, gname, gpname, dst, b_dst) in ((0, "dqg", "dqgp", qh[i], b_qh[i]), (256, "dkg", "dkgp", kh[i], b_kh[i])):
                    for kc in range(KC):
                        P.op("pe", MM(ps[:, 4, :], wdh[i][:, kc, d0:d0 + 128], nT[:, kc, sl], kc == 0, kc == KC - 1),
                             reads=[b_wdh[i], b_n[kc][g]], writes=[b_ps[4]], inc=(kc == KC - 1))
                    for kc in range(KC):
                        P.op("pe", MM(ps[:, 5, :], wdh[i][:, kc, d0 + 128:d0 + 256], nT[:, kc, sl], kc == 0, kc == KC - 1),
                             reads=[b_wdh[i], b_n[kc][g]], writes=[b_ps[5]], inc=(kc == KC - 1))
                    if d0 == 0:
                        yield from qprep_gen(i, 128, 2, 64.0, pvb(gname), pvb(gpname), dst[:, sl], b_dst, sl, g)
                    else:
                        P.op("act", ACT(sqs[1][:], ps[:, 4, :], AF.Square), reads=[b_ps[4]], writes=[b_sqs[1]])
                        rope_parts(128, pvb(gname), pvb(gpname), sl, dst[:, sl], b_dst, after=[b_sqs[1]])
                        yield

                        def kstat2(j, out_ap):
                            P.op("pe", MM(out_ap, sqs[1][:, j * 128:(j + 1) * 128], sel2b[:, 0:2], True, True),
                                 reads=[b_sqs[1], b_onesf], writes=[b_ps[6]])
                        tiny_rstd(i, "k", g, 2, 64.0, 64.0, kstat2)
                        yield

        def diff_attn(h, filler):
            i = h % 2
            b_vb = dstate["b_vb"]
            RA = slice(0, 128)
            vtA = (lambda kt, i=i: vb[i][:, kt, 0:128], 128, 2, b_vb[i])
            vtB = (lambda kt, i=i: vb[i][:, kt, 72:200], 128, 3, b_vb[i])
            for qc in range(NG):
                qsl = slice(qc * GS, (qc + 1) * GS)
                for comp in range(2):
                    rows = slice(comp * 64, comp * 64 + 64)
                    attention(kh[i], b_kh[i], qh[i], b_qh[i], rows, (lambda kt, i=i, comp=comp: (rk_t[i][:, kt * 2 + comp:kt * 2 + comp + 1], b_rk[i])), [vtA, vtB], qc, filler, (3, 7, 11, 15))
                    P.op("dve", CP(oraw[0][0:65, :], ps[0:65, 2, :]), reads=[b_ps[2]], writes=[b_oraw[0]])
                    P.op("dve", CP(oraw[1][64:128, :], ps[64:128, 3, :]), reads=[b_ps[3]], writes=[b_oraw[1]])
                    P.op("dve", RECIP(rin[64:65, :], oraw[0][64:65, :]), reads=[b_oraw[0]], writes=[b_rin])

                    def epi(comp=comp, qsl=qsl, qc=qc):
                        P.op("pe", MM(ps[:, 7, :], onesf[64:65, :], rin[64:65, :], True, True),
                             reads=[b_onesf, b_rin], writes=[b_ps[7]])
                        P.op("dve", TT(dn[comp][0:64, :], oraw[0][0:64, :], ps[0:64, 7, :], ALU.mult),
                             reads=[b_oraw[0], b_ps[7]], writes=[b_dn[comp]])
                        P.op("dve", TT(dn[comp][64:128, :], oraw[1][64:128, :], ps[64:128, 7, :], ALU.mult),
                             reads=[b_oraw[1], b_ps[7]], writes=[b_dn[comp]])
                        if comp == 0:
                            yield
                            return
                        P.op("dve", STT(od_e[:], dn[1][:], der[:, 72:73], dn[0][:], ALU.mult, ALU.add),
                             reads=[b_dn[0], b_dn[1], b_der], writes=[b_ode])
                        P.op("act", ACT(sq_e[:], od_e[:], AF.Square), reads=[b_ode], writes=[b_sqe])
                        yield
                        P.op("pe", MM(ps[:, 7, :], ones[:], sq_e[:], True, True), reads=[b_const, b_sqe], writes=[b_ps[7]])
                        P.op("act", ACT(rs_e[:], ps[:, 7, :], AF.Sqrt, bias=epsc[:], scale=1.0 / 128.0),
                             reads=[b_ps[7], b_const], writes=[b_rse])
                        P.op("dve", RECIP(rs_e[:], rs_e[:]), reads=[b_rse], writes=[b_rse])
                        yield
                        P.op("dve", STT(obT[:, h, qsl], od_e[:], der[:, 73:74], rs_e[:], ALU.mult, ALU.mult),
                             reads=[b_ode, b_der, b_rse], writes=[b_ob[h][qc]])
                        yield
                    pend.append(epi())

        units = [("mla", h) for h in range(8)] + [("diff", h) for h in range(4)]

        def prep_of(u):
            return mla_prep(u[1]) if u[0] == "mla" else diff_prep(u[1])

        NU = int(_os.environ.get("MK_UNITS", "12"))
        if NU < 12:
            for c in range(4):
                for g in range(NG):
                    P.op("pool", MEMSET(oaT[:, c, g * GS:(g + 1) * GS], 0.0), writes=[b_oa[c][g]])
                    P.op("pool", MEMSET(obT[:, c, g * GS:(g + 1) * GS], 0.0), writes=[b_ob[c][g]])
            if NU <= 8:
                dstate["b_wdh"] = P.bufs("wdh", 2)
                dstate["b_vb"] = P.bufs("vb", 2)
        units = units[:NU]
        def warm(n):
            for k in range(n):
                P.op("pe", MM(ps[:, 7, :], ones[:], nT[:, k % KC, 0:GS], True, True),
                     reads=[b_const, b_n[k % KC][0]], writes=[b_ps[7]], inc=(k == n - 1))

        NWARM = int(_os.environ.get("MK_WARM", "0"))
        drain(prep_of(units[0]))
        for ui, u in enumerate(units):
            if NWARM:
                warm(NWARM)
            gen = prep_of(units[ui + 1]) if ui + 1 < len(units) else None
            if NOFILL and _os.environ.get("MK_PREFIRST") == "1":
                drain(gen)
            filler = make_filler(gen)
            if u[0] == "mla":
                mla_attn(u[1], filler)
            else:
                diff_attn(u[1], filler)
            drain(gen)
        while pend:
            step_pend()
        b_wdh, b_vb = dstate["b_wdh"], dstate["b_vb"]

        P.retire(*b_wdh, *b_vb, *b_qh, *b_kh, *b_PT, *b_oraw, *b_dn, b_rin, b_onesf, b_tab, b_sqe, b_ode, b_rse, *b_rq, *b_rk, *b_t4, *b_dg)
        woa = A.view(M2 + 0, BF16, [4, D])
        wob = A.view(M2 + 8192, BF16, [4, D])
        wout = A.view(M2 + 16384, BF16, [KC, D])
        wgt = [A.view(M2 + 32768 + i * 4096, BF16, [KC, 256]) for i in range(2)]
        hst = [A.view(M2 + 40960 + i * 2048, F32, [GS]) for i in range(2)]
        gA = A.view(M2 + 45056, F32, [GS])
        gB = A.view(M2 + 47104, F32, [GS])
        mT = A.view(M2 + 49152, BF16, [KC, T])
        b_woa, b_wob, b_wout = P.buf("woa"), P.buf("wob"), P.buf("wout")
        b_wgt = P.bufs("wgt", 2)
        b_hst = P.bufs("hst", 2)
        b_gA, b_gB = P.buf("gA"), P.buf("gB")
        b_m = [[P.buf("m") for g in range(NG)] for c in range(KC)]
        P.dma("pool", woa[:], woa_d.rearrange("(c p) n -> p c n", p=128), writes=[b_woa])
        P.dma("pool", wob[:], wob_d.rearrange("(c p) n -> p c n", p=128), writes=[b_wob])
        for hh in range(2):
            P.dma("pool", wout[:, hh * 4:(hh + 1) * 4, :], wout_d.rearrange("(c p) n -> p c n", p=128)[:, hh * 4:(hh + 1) * 4, :], writes=[b_wout])
        GA0, GB0 = 2208, 3232
        wad = A.view(M2 + 81920, BF16, [KC, D])
        b_wad = P.buf("wad")
        wada_block(5, wad, b_wad)
        late = {0: (5, 6), 2: (6, 7), 4: (7, 8), 6: (8, None)}
        nit = 0
        def load_wgt(oc_):
            wi_ = oc_ % 2
            P.dma("pool", wgt[wi_][:, :, 0:128], win_v[:, :, GA0 + oc_ * 128:GA0 + (oc_ + 1) * 128], writes=[b_wgt[wi_]])
            P.dma("pool", wgt[wi_][:, :, 128:256], win_v[:, :, GB0 + oc_ * 128:GB0 + (oc_ + 1) * 128], writes=[b_wgt[wi_]])

        load_wgt(0)
        for oc in range(KC):
            wi = oc % 2
            if oc + 1 < KC:
                load_wgt(oc + 1)
            for g in range(NG):
                sl = slice(g * GS, (g + 1) * GS)
                b0 = 4 * (nit % 2)
                nit += 1
                for c in range(4):
                    P.op("pe", MM(ps[:, b0, :], woa[:, c, oc * 128:(oc + 1) * 128], oaT[:, c, sl], c == 0, c == 3),
                         reads=[b_woa, b_oa[c][g]], writes=[b_ps[b0]], inc=(c == 3))
                for c in range(4):
                    P.op("pe", MM(ps[:, b0 + 1, :], wob[:, c, oc * 128:(oc + 1) * 128], obT[:, c, sl], c == 0, c == 3),
                         reads=[b_wob, b_ob[c][g]], writes=[b_ps[b0 + 1]], inc=(c == 3))
                for kc in range(KC):
                    P.op("pe", MM(ps[:, b0 + 2, :], wgt[wi][:, kc, 0:128], nT[:, kc, sl], kc == 0, kc == KC - 1),
                         reads=[b_wgt[wi], b_n[kc][g]], writes=[b_ps[b0 + 2]], inc=(kc == KC - 1))
                for kc in range(KC):
                    P.op("pe", MM(ps[:, b0 + 3, :], wgt[wi][:, kc, 128:256], nT[:, kc, sl], kc == 0, kc == KC - 1),
                         reads=[b_wgt[wi], b_n[kc][g]], writes=[b_ps[b0 + 3]], inc=(kc == KC - 1))
                P.op("act", ACT(gA[:], ps[:, b0 + 2, :], AF.Sigmoid), reads=[b_ps[b0 + 2]], writes=[b_gA])
                P.op("act", ACT(gB[:], ps[:, b0 + 3, :], AF.Sigmoid), reads=[b_ps[b0 + 3]], writes=[b_gB])
                P.op("dve", TT(gA[:], gA[:], ps[:, b0, :], ALU.mult), reads=[b_gA, b_ps[b0]], writes=[b_gA])
                P.op("dve", TT(gB[:], gB[:], ps[:, b0 + 1, :], ALU.mult), reads=[b_gB, b_ps[b0 + 1]], writes=[b_gB])
                P.op("dve", TT(mT[:, oc, sl], gA[:], gB[:], ALU.add), reads=[b_gA, b_gB], writes=[b_m[oc][g]])
            if oc in late:
                mcur, mnext = late[oc]
                mod_block(mcur, wad, b_wad)
                if mnext is not None:
                    wada_block(mnext, wad, b_wad)
        derive_G("G2", 5, 1.0)
        derive_A("A3", "g3", 7, 6, "SH3")
        derive_G("G3", 8, 0.5)
        P.retire(*[b for row in b_oa for b in row], *[b for row in b_ob for b in row])
        b_h = [[P.buf("h%d_%d" % (c, g)) for g in range(NG)] for c in range(KC)]
        nit = 0
        for g in range(NG):
            sl = slice(g * GS, (g + 1) * GS)
            for oc in range(KC):
                bk = nit % 2
                hi = nit % 2
                nit += 1
                P.dma("sp", hst[hi][:], hsp[:, oc, sl], reads=[b_hsp], writes=[b_hst[hi]], owner=b_hst[hi])
                for c in range(KC):
                    P.op("pe", MM(ps[:, bk, :], wout[:, c, oc * 128:(oc + 1) * 128], mT[:, c, sl], c == 0, c == KC - 1),
                         reads=[b_wout, b_m[c][g]], writes=[b_ps[bk]], inc=(c == KC - 1))
                P.op("dve", STT(hT[:, oc, sl], ps[:, bk, :], dcol("G2", oc), hst[hi][:], ALU.mult, ALU.add),
                     reads=[b_ps[bk], b_der, b_hst[hi]], writes=[b_h[oc][g]])
        P.retire(*[b for row in b_m for b in row], b_woa, b_wob, b_wout, *b_wgt, *b_hst, b_gA, b_gB, b_wad)
        P.retire(*[b for row in b_n for b in row])
        b_n = [[P.buf("n%d_%d" % (c, g)) for g in range(NG)] for c in range(KC)]
        ring["b_wd"] = P.bufs("wdp", 2)
        ring["b_gu"] = P.bufs("gu", 2)

    if "ffn2" in stages:
        for g in range(NG):
            norm_group(g, "A3", "SH3")
        ffn(f2g_d, f2u_d, f2d_d, "G3", ring, {})

    ost = [A.view(AT_OFF + i * 4096, F32, [D]) for i in range(2)]
    b_ost = P.bufs("ost", 2)
    ytmp = [A.view(AT_OFF + 8192 + i * 2048, F32, [GS]) for i in range(8)]
    b_ytmp = P.bufs("ytmp", 8)
    b_out = P.buf("out")
    for g in range(NG):
        r = norm_group(g, None, None)
        sl = slice(g * GS, (g + 1) * GS)
        for c in range(KC):
            P.op("dve", STT(ytmp[c][:], hT[:, c, sl], pvb("gf", c), rsb[r][:], ALU.mult, ALU.mult),
                 reads=[b_h[c][g], b_pvT, b_rsb[r]], writes=[b_ytmp[c]])
        for j in range(4):
            tt = g * 4 + j
            o = tt % 2
            for half in range(2):
                bk = (tt * 2 + half) % 4
                for cc in range(4):
                    c = half * 4 + cc
                    P.op("pe", TR(ps[:, bk, cc * 128:(cc + 1) * 128], ytmp[c][:, j * 128:(j + 1) * 128], ident[:]),
                         reads=[b_ytmp[c], b_const], writes=[b_ps[bk]], inc=(cc == 3))
                P.op("act", ACT(ost[o][:, half * 512:(half + 1) * 512], ps[:, bk, :], AF.Copy),
                     reads=[b_ps[bk]], writes=[b_ost[o]])
            P.dma("sp", out_d[tt * 128:(tt + 1) * 128, :], ost[o][:], reads=[b_ost[o]], writes=[b_out], owner=b_ost[o])
    P.wait_all("sp", b_ost + [b_out])
    P.emit()
    return nc


def _prep_inputs(inp):
    L = 0
    f32 = np.float32

    def pad128(v):
        o = np.zeros(128, f32)
        o[:v.shape[0]] = v
        return o

    def mla_perm(g):
        o = np.zeros(128, f32)
        o[64:80] = g[80:96]
        o[80:96] = g[64:80]
        return o

    def diff_rep(g):
        return np.concatenate([g, g]).astype(f32)

    def diff_perm(g):
        o = np.zeros(128, f32)
        for base in (0, 64):
            o[base:base + 8] = g[8:16]
            o[base + 8:base + 16] = g[0:8]
        return o

    freqa = np.zeros(128, f32)
    signa = np.ones(128, f32)
    fa = (1.0 / (np.float32(10000.0) ** (np.arange(16, dtype=f32) / np.float32(16)))).astype(f32)
    freqa[64:80] = fa
    freqa[80:96] = fa
    signa[64:80] = -1.0
    freqb = np.zeros(128, f32)
    signb = np.ones(128, f32)
    fb = (1.0 / (np.float32(500000.0) ** (np.arange(8, dtype=f32) / np.float32(8)))).astype(f32)
    for base in (0, 64):
        freqb[base:base + 8] = fb
        freqb[base + 8:base + 16] = fb
        signb[base:base + 8] = -1.0

    rows = []
    for nm in ("ffn1_norm", "mix_norm", "ffn2_norm", "final_norm"):
        rows.append(np.asarray(inp[nm][L], f32).reshape(8, 128))
    rows.append(np.asarray(inp["mla_q_norm"][L], f32).reshape(3, 128))
    rows.append(np.asarray(inp["mla_kv_norm"][L], f32).reshape(2, 128))
    qg = np.asarray(inp["mla_q_gain"][L], f32)
    kg = np.asarray(inp["mla_k_gain"][L], f32)
    dqg = np.asarray(inp["diff_q_gain"][L], f32)
    dkg = np.asarray(inp["diff_k_gain"][L], f32)
    singles = [pad128(qg), mla_perm(qg), pad128(kg), mla_perm(kg),
               diff_rep(dqg), diff_perm(dqg), diff_rep(dkg), diff_perm(dkg),
               pad128(np.asarray(inp["diff_lambda_q1"][L], f32)), pad128(np.asarray(inp["diff_lambda_k1"][L], f32)),
               pad128(np.asarray(inp["diff_lambda_q2"][L], f32)), pad128(np.asarray(inp["diff_lambda_k2"][L], f32)),
               np.asarray(inp["diff_subln"][L], f32), freqa, signa, freqb, signb]
    rows.append(np.stack(singles))
    pvb = np.ascontiguousarray(np.concatenate(rows, axis=0), dtype=f32)
    assert pvb.shape == (NPVB, 128), pvb.shape
    shared = {
        "pvb": pvb,
        "ident": np.eye(128, dtype=f32),
        "w_ada": np.ascontiguousarray(inp["w_ada"][L], f32),
        "f1g": np.ascontiguousarray(inp["ffn1_w_gate"][L], f32),
        "f1u": np.ascontiguousarray(inp["ffn1_w_up"][L], f32),
        "f1d": np.ascontiguousarray(inp["ffn1_w_down"][L], f32),
        "f2g": np.ascontiguousarray(inp["ffn2_w_gate"][L], f32),
        "f2u": np.ascontiguousarray(inp["ffn2_w_up"][L], f32),
        "f2d": np.ascontiguousarray(inp["ffn2_w_down"][L], f32),
        "w_in": np.ascontiguousarray(inp["w_in"][L], f32),
        "w_uq": np.ascontiguousarray(inp["mla_w_uq"][L], f32),
        "w_ukv": np.ascontiguousarray(inp["mla_w_ukv"][L], f32),
        "w_oa": np.ascontiguousarray(inp["mla_w_o"][L], f32),
        "w_ob": np.ascontiguousarray(inp["diff_w_o"][L], f32),
        "w_out": np.ascontiguousarray(inp["w_out"][L], f32),
    }
    b_ada = np.asarray(inp["b_ada"][L], f32).reshape(72, 128)
    maps = []
    for b in range(8):
        m = dict(shared)
        m["x"] = np.ascontiguousarray(inp["x"][b], f32)
        m["pos"] = np.ascontiguousarray(np.asarray(inp["positions"][b]).reshape(1, T).astype(np.int32))
        m["pva"] = np.ascontiguousarray(np.concatenate([np.asarray(inp["c"][b], f32).reshape(8, 128), b_ada], axis=0))
        maps.append(m)
    return maps


_NC_CACHE = {}


def kernel(**inputs):
    maps = _prep_inputs(inputs)
    if "nc" not in _NC_CACHE:
        _NC_CACHE["nc"] = build_nc()
    nc = _NC_CACHE["nc"]
    res = run_bass_kernel_spmd(nc, maps, core_ids=list(range(8)))
    out = np.stack([np.asarray(r["out"], np.float32) for r in res.results], axis=0)
    return out
```

```python
import math
import os as _os
import numpy as np
import concourse.bass as bass
import concourse.mybir as mybir
from concourse.bass_utils import run_bass_kernel_spmd

F32 = mybir.dt.float32
BF16 = mybir.dt.bfloat16
I32 = mybir.dt.int32
AF = mybir.ActivationFunctionType
ALU = mybir.AluOpType

T = 2048
D = 1024
DFF = 2816
NG = 4
GS = 512
KC = 8
EPS = 1e-6
IN_COLS = 4256
LAMBDA_INIT = 0.8 - 0.6 * math.exp(-0.3 * 0)


class Buf:
    __slots__ = ("name", "w", "r", "dsem", "dcnt")

    def __init__(self, name):
        self.name = name
        self.w = None
        self.r = {}
        self.dsem = None
        self.dcnt = 0


class Prog:
    ENG = ("pe", "act", "dve", "pool", "sp")

    def __init__(self, nc, same_engine_sync=True):
        self.nc = nc
        self.streams = {e: [] for e in self.ENG}
        self.sems = {}
        self.cnt = {}
        self.waited = {e: {} for e in self.ENG}
        self.same = same_engine_sync
        for e in self.ENG:
            self.sems["E_" + e] = nc.alloc_semaphore("sem_e_" + e)
            self.cnt[e] = 0
        self.nd = 0
        self.retired = {}

    def new_dsem(self, name):
        k = "D_%d_%s" % (self.nd, name)
        self.nd += 1
        self.sems[k] = self.nc.alloc_semaphore("sem_d%d" % self.nd)
        return k

    def buf(self, name):
        b = Buf(name)
        b.r = dict(self.retired)
        return b

    def bufs(self, name, n):
        return [self.buf("%s%d" % (name, i)) for i in range(n)]

    def retire(self, *bufs):
        for b in bufs:
            if b.w is not None:
                s, v = b.w
                self.retired[s] = max(self.retired.get(s, 0), v)
            for s, v in b.r.items():
                self.retired[s] = max(self.retired.get(s, 0), v)

    def _waits(self, eng, reads, writes):
        waits = {}
        own = "E_" + eng

        def need(s, v):
            if s == own and (eng == "pe" or not self.same):
                return
            if self.waited[eng].get(s, 0) >= v:
                return
            if waits.get(s, 0) < v:
                waits[s] = v

        for b in reads:
            if b.w is not None:
                need(*b.w)
        for b in writes:
            if b.w is not None:
                need(*b.w)
            for s, v in b.r.items():
                need(s, v)
        for s, v in waits.items():
            self.waited[eng][s] = v
        return list(waits.items())

    def _mark(self, tok, reads, writes):
        s, v = tok
        for b in reads:
            if b.r.get(s, 0) < v:
                b.r[s] = v
        for b in writes:
            b.w = tok
            b.r = {}

    def op(self, eng, fn, reads=(), writes=(), inc=True):
        waits = self._waits(eng, reads, writes)
        own = "E_" + eng
        if inc:
            self.cnt[eng] += 1
            tok = (own, self.cnt[eng])
        else:
            tok = (own, self.cnt[eng] + 1)
        self._mark(tok, reads, writes)
        self.streams[eng].append((waits, fn, (own, 1) if inc else None))

    def dma(self, q, out_ap, in_ap, reads=(), writes=(), owner=None):
        if owner is None:
            owner = writes[0] if writes else reads[0]
        if owner.dsem is None:
            owner.dsem = self.new_dsem(owner.name)
        waits = self._waits(q, reads, writes)
        owner.dcnt += 16
        tok = (owner.dsem, owner.dcnt)
        self._mark(tok, reads, writes)
        self.streams[q].append(
            (waits, lambda e: e.dma_start(out=out_ap, in_=in_ap), (owner.dsem, 16)))

    def wait_all(self, eng, bufs):
        waits = self._waits(eng, (), bufs)
        self.streams[eng].append((waits, None, None))

    def emit(self):
        nc = self.nc
        sems = self.sems
        streams = self.streams
        with nc.Block() as block:
            def run(e, name):
                for waits, fn, inc in streams[name]:
                    for s, v in waits:
                        e.wait_ge(sems[s], v)
                    if fn is None:
                        continue
                    ins = fn(e)
                    if inc is not None:
                        ins.then_inc(sems[inc[0]], inc[1])

            @block.tensor
            def _(e):
                run(e, "pe")

            @block.scalar
            def _(e):
                run(e, "act")

            @block.vector
            def _(e):
                run(e, "dve")

            @block.gpsimd
            def _(e):
                run(e, "pool")

            @block.sync
            def _(e):
                run(e, "sp")


class Arena:
    def __init__(self, nc, reserve=1024):
        rem = nc.sbuf_bytes_remaining
        self.nbytes = ((rem - reserve) // 64) * 64
        self.t = nc.alloc_sbuf_tensor("arena", [128, self.nbytes // 4], F32)

    def view(self, off, dtype, shape):
        assert off % 4 == 0
        esz = 2 if dtype == BF16 else 4
        n = 1
        for s in shape:
            n *= s
        nb = n * esz
        assert nb % 4 == 0 and off + nb <= self.nbytes, (off, nb, self.nbytes)
        a = self.t[:, off // 4:(off + nb) // 4]
        if dtype != F32:
            a = a.bitcast(dtype)
        if len(shape) == 2:
            a = a.rearrange("p (a b) -> p a b", a=shape[0])
        elif len(shape) == 3:
            a = a.rearrange("p (a b c) -> p a b c", a=shape[0], b=shape[1])
        return a


def MM(out, lhsT, rhs, start, stop):
    return lambda e: e.matmul(out, lhsT=lhsT, rhs=rhs, start=start, stop=stop)


def TR(out, in_, ident):
    return lambda e: e.transpose(out, in_, ident)


def ACT(out, in_, func, bias=None, scale=None):
    kw = {}
    if bias is not None:
        kw["bias"] = bias
    if scale is not None:
        kw["scale"] = scale
    return lambda e: e.activation(out=out, in_=in_, func=func, **kw)


def TT(out, in0, in1, op):
    return lambda e: e.tensor_tensor(out=out, in0=in0, in1=in1, op=op)


def TS(out, in0, s1, s2, op0, op1=None):
    if op1 is None:
        return lambda e: e.tensor_scalar(out=out, in0=in0, scalar1=s1, scalar2=None, op0=op0)
    return lambda e: e.tensor_scalar(out=out, in0=in0, scalar1=s1, scalar2=s2, op0=op0, op1=op1)


def STT(out, in0, scalar, in1, op0, op1):
    return lambda e: e.scalar_tensor_tensor(out=out, in0=in0, scalar=scalar, in1=in1, op0=op0, op1=op1)


def CP(out, in_):
    return lambda e: e.tensor_copy(out=out, in_=in_)


def RECIP(out, in_):
    return lambda e: e.reciprocal(out=out, in_=in_)


def MEMSET(ap, v):
    return lambda e: e.memset(ap, v)


PVB = {}
_n = 0
for _name, _cnt in [("g1", 8), ("g2", 8), ("g3", 8), ("gf", 8), ("qnorm", 3), ("kvnorm", 2),
                    ("qg", 1), ("qgp", 1), ("kg", 1), ("kgp", 1), ("dqg", 1), ("dqgp", 1),
                    ("dkg", 1), ("dkgp", 1), ("lq1", 1), ("lk1", 1), ("lq2", 1), ("lk2", 1),
                    ("subln", 1), ("freqa", 1), ("signa", 1), ("freqb", 1), ("signb", 1)]:
    PVB[_name] = _n
    _n += _cnt
NPVB = _n
NPVA = 80


def build_nc(stages=("ffn1", "mix", "ffn2"), debug=None):
    nc = bass.Bass("TRN2", target_bir_lowering=False)
    P = Prog(nc)
    A = Arena(nc)

    def dram_in(name, shape, dt=F32):
        return nc.dram_tensor(name, shape, dt, kind="ExternalInput").ap()

    x_d = dram_in("x", [T, D])
    pos_d = dram_in("pos", [1, T], I32)
    pva_d = dram_in("pva", [NPVA, 128])
    pvb_d = dram_in("pvb", [NPVB, 128])
    ident_d = dram_in("ident", [128, 128])
    wada_d = dram_in("w_ada", [D, 9 * D])
    f1g_d = dram_in("f1g", [D, DFF])
    f1u_d = dram_in("f1u", [D, DFF])
    f1d_d = dram_in("f1d", [DFF, D])
    f2g_d = dram_in("f2g", [D, DFF])
    f2u_d = dram_in("f2u", [D, DFF])
    f2d_d = dram_in("f2d", [DFF, D])
    win_d = dram_in("w_in", [D, IN_COLS])
    wuq_d = dram_in("w_uq", [384, 768])
    wukv_d = dram_in("w_ukv", [256, 1024])
    woa_d = dram_in("w_oa", [512, D])
    wob_d = dram_in("w_ob", [512, D])
    wout_d = dram_in("w_out", [D, D])
    out_d = nc.dram_tensor("out", [T, D], F32, kind="ExternalOutput").ap()
    dbg_d = None
    if debug is not None:
        dbg_d = nc.dram_tensor("dbg", [128] + list(debug[1]), F32, kind="ExternalOutput").ap()

    H_OFF = 0
    MISC = 65536
    NT_OFF = MISC + 14336
    R_OFF = NT_OFF + 32768
    R_SIZE = A.nbytes - R_OFF
    assert R_SIZE >= 98304, R_SIZE

    hT = A.view(H_OFF, F32, [KC, T])
    nT = A.view(NT_OFF, BF16, [KC, T])
    ident = A.view(MISC + 0, F32, [128])
    ones = A.view(MISC + 512, BF16, [128])
    ones2 = A.view(MISC + 768, BF16, [128])
    pvT = A.view(MISC + 1024, F32, [256])
    modT = A.view(MISC + 2048, F32, [72])
    der = A.view(MISC + 2560, F32, [128])
    sqs = [A.view(MISC + 3072 + i * 1024, BF16, [GS]) for i in range(2)]
    fsc = [A.view(MISC + 5120 + i * 2048, F32, [GS]) for i in range(2)]
    rsb = [A.view(MISC + 9216 + i * 2048, F32, [GS]) for i in range(2)]
    condb = A.view(MISC + 13312, BF16, [8])
    ps = nc.alloc_psum_tensor("ps", [128, 8, GS], F32)

    b_h = [[P.buf("h%d_%d" % (c, g)) for g in range(NG)] for c in range(KC)]
    b_n = [[P.buf("n%d_%d" % (c, g)) for g in range(NG)] for c in range(KC)]
    b_ps = P.bufs("ps", 8)
    b_const = P.buf("const")
    b_pvT = P.buf("pvT")
    b_mod = P.buf("mod")
    b_der = P.buf("der")
    b_sqs = P.bufs("sqs", 2)
    b_fsc = P.bufs("fsc", 2)
    b_rsb = P.bufs("rsb", 2)
    b_condb = P.buf("condb")

    def pvb(name, i=0):
        c = 80 + PVB[name] + i
        return pvT[:, c:c + 1]

    DER = {"A1": 0, "G1": 8, "A2": 16, "G2": 24, "A3": 32, "G3": 40, "SH1": 48, "SH2": 56, "SH3": 64,
           "nlam": 72, "subs": 73, "lamt": 74}

    def dcol(name, i=0):
        c = DER[name] + i
        return der[:, c:c + 1]

    P.dma("sp", ident[:], ident_d, writes=[b_const])
    P.op("pool", MEMSET(ones[:], 1.0), writes=[b_const])
    P.op("pool", MEMSET(ones2[:], 0.0), writes=[b_const])
    P.op("pool", MEMSET(ones2[0:64, 0:64], 1.0), writes=[b_const])
    P.op("pool", MEMSET(ones2[64:128, 64:128], 1.0), writes=[b_const])
    pstage = A.view(R_OFF + 32768, F32, [2, 128])
    b_pstage = P.buf("pstage")
    P.op("pool", MEMSET(pstage[:], 0.0), writes=[b_pstage])
    P.dma("sp", pstage[0:NPVA, 0, :], pva_d, writes=[b_pstage])
    P.dma("sp", pstage[0:NPVB, 1, :], pvb_d, writes=[b_pstage])
    P.op("pe", TR(ps[:, 7, 0:NPVA], pstage[0:NPVA, 0, :], ident[0:NPVA, 0:NPVA]),
         reads=[b_pstage, b_const], writes=[b_ps[7]], inc=False)
    P.op("pe", TR(ps[:, 7, 128:128 + NPVB], pstage[0:NPVB, 1, :], ident[0:NPVB, 0:NPVB]),
         reads=[b_pstage, b_const], writes=[b_ps[7]])
    P.op("dve", CP(pvT[:, 0:NPVA], ps[:, 7, 0:NPVA]), reads=[b_ps[7]], writes=[b_pvT])
    P.op("dve", CP(pvT[:, 80:80 + NPVB], ps[:, 7, 128:128 + NPVB]), reads=[b_ps[7]], writes=[b_pvT])
    P.retire(b_pstage)
    P.op("act", ACT(condb[:], pvT[:, 0:8], AF.Silu), reads=[b_pvT], writes=[b_condb])

    ring = {"wd_i": 0, "gu_i": 0, "b_wd": P.bufs("wdp", 2), "b_gu": P.bufs("gu", 2)}

    AT_OFF = R_OFF
    GU_OFF = R_OFF + 32768
    WD_OFF = R_OFF + 65536

    def wada_block(m, dst, b_dst):
        src = wada_d.rearrange("(kc p) n -> p kc n", p=128)[:, :, m * 1024:(m + 1) * 1024]
        for hh in range(2):
            P.dma("pool", dst[:, hh * 4:(hh + 1) * 4, :], src[:, hh * 4:(hh + 1) * 4, :], writes=[b_dst])

    def mod_block(m, wt, b_wt):
        for j in range(8):
            for kc in range(KC):
                P.op("pe", MM(ps[:, 6, m * 8 + j:m * 8 + j + 1], wt[:, kc, j * 128:(j + 1) * 128],
                              condb[:, kc:kc + 1], kc == 0, kc == KC - 1),
                     reads=[b_wt, b_condb], writes=[b_ps[6]], inc=(j == 7 and kc == KC - 1))
        P.op("dve", TT(modT[:, m * 8:(m + 1) * 8], ps[:, 6, m * 8:(m + 1) * 8],
                       pvT[:, 8 + m * 8:8 + (m + 1) * 8], ALU.add),
             reads=[b_ps[6], b_pvT], writes=[b_mod])

    def derive_A(dst, gname, m_sc, m_sh, shname):
        P.op("dve", STT(der[:, DER[dst]:DER[dst] + 8], modT[:, m_sc * 8:(m_sc + 1) * 8], 1.0,
                        pvT[:, 80 + PVB[gname]:80 + PVB[gname] + 8], ALU.add, ALU.mult),
             reads=[b_mod, b_pvT], writes=[b_der])
        P.op("dve", CP(der[:, DER[shname]:DER[shname] + 8], modT[:, m_sh * 8:(m_sh + 1) * 8]),
             reads=[b_mod], writes=[b_der])

    def derive_G(dst, m_gt, scale):
        P.op("dve", TS(der[:, DER[dst]:DER[dst] + 8], modT[:, m_gt * 8:(m_gt + 1) * 8], float(scale), None, ALU.mult),
             reads=[b_mod], writes=[b_der])

    def norm_group(g, Aname, SHname, sq_from_psum=None):
        sl = slice(g * GS, (g + 1) * GS)
        sb = 6 + (g % 2)
        for c in range(KC):
            i = c % 2
            P.op("act", ACT(sqs[i][:], hT[:, c, sl], AF.Square), reads=[b_h[c][g]], writes=[b_sqs[i]])
            P.op("pe", MM(ps[:, sb, :], ones[:], sqs[i][:], c == 0, c == KC - 1),
                 reads=[b_const, b_sqs[i]], writes=[b_ps[sb]], inc=True)
        r = g % 2
        P.op("act", ACT(rsb[r][:], ps[:, sb, :], AF.Sqrt, bias=epsc[:], scale=1.0 / D),
             reads=[b_ps[sb], b_const], writes=[b_rsb[r]])
        P.op("dve", RECIP(rsb[r][:], rsb[r][:]), reads=[b_rsb[r]], writes=[b_rsb[r]])
        if Aname is None:
            return r
        for c in range(KC):
            i = c % 2
            P.op("dve", TT(fsc[i][:], hT[:, c, sl], rsb[r][:], ALU.mult),
                 reads=[b_h[c][g], b_rsb[r]], writes=[b_fsc[i]])
            P.op("dve", TS(nT[:, c, sl], fsc[i][:], dcol(Aname, c), dcol(SHname, c), ALU.mult, ALU.add),
                 reads=[b_fsc[i], b_der], writes=[b_n[c][g]])
        return r

    epsc = A.view(MISC + 13312 + 64, F32, [1])
    P.op("pool", MEMSET(epsc[:], EPS), writes=[b_const])

    aT = A.view(AT_OFF, BF16, [8, T])
    gu = [A.view(GU_OFF + i * 16384, BF16, [2, KC, GS]) for i in range(2)]
    wdp = [A.view(WD_OFF + i * 16384, BF16, [8, D]) for i in range(2)]
    PASSES = [(0, 8), (8, 16), (16, 22)]

    def ffn(wg_d, wu_d, wd_d, Gname, st, pre_hooks):
        wg_v = wg_d.rearrange("(kc p) n -> p kc n", p=128)
        wu_v = wu_d.rearrange("(kc p) n -> p kc n", p=128)
        wd_v = wd_d.rearrange("(j p) n -> p j n", p=128)
        b_a = [[P.buf("a%d_%d" % (j, g)) for g in range(NG)] for j in range(8)]
        it = 0
        for pi, (c0, c1) in enumerate(PASSES):
            npc = c1 - c0
            wslot = st["wd_i"] % 2
            st["wd_i"] += 1
            b_wd = st["b_wd"][wslot]
            for hh in range(0, npc, 4):
                h1 = min(hh + 4, npc)
                P.dma("pool", wdp[wslot][:, hh:h1, :], wd_v[:, c0 + hh:c0 + h1, :], writes=[b_wd])
            groups = [(a, min(a + 4, c1)) for a in range(c0, c1, 4)]
            loaded = []
            for (a, b) in groups:
                gslot = st["gu_i"] % 2
                st["gu_i"] += 1
                b_g = st["b_gu"][gslot]
                ncol = (b - a) * 128
                P.dma("pool", gu[gslot][:, 0, :, 0:ncol], wg_v[:, :, a * 128:b * 128], writes=[b_g])
                P.dma("pool", gu[gslot][:, 1, :, 0:ncol], wu_v[:, :, a * 128:b * 128], writes=[b_g])
                loaded.append((a, b, gslot, b_g))
                for j in range(a, b):
                    jl = j - c0
                    jo = (j - a) * 128
                    for g in range(NG):
                        sl = slice(g * GS, (g + 1) * GS)
                        bg = it % 2
                        bu = 2 + it % 2
                        it += 1
                        for kc in range(KC):
                            P.op("pe", MM(ps[:, bg, :], gu[gslot][:, 0, kc, jo:jo + 128], nT[:, kc, sl], kc == 0, kc == KC - 1),
                                 reads=[b_g, b_n[kc][g]], writes=[b_ps[bg]], inc=(kc == KC - 1))
                        for kc in range(KC):
                            P.op("pe", MM(ps[:, bu, :], gu[gslot][:, 1, kc, jo:jo + 128], nT[:, kc, sl], kc == 0, kc == KC - 1),
                                 reads=[b_g, b_n[kc][g]], writes=[b_ps[bu]], inc=(kc == KC - 1))
                        i = it % 2
                        P.op("act", ACT(fsc[i][:], ps[:, bg, :], AF.Silu), reads=[b_ps[bg]], writes=[b_fsc[i]])
                        P.op("dve", TT(aT[:, jl, sl], fsc[i][:], ps[:, bu, :], ALU.mult),
                             reads=[b_fsc[i], b_ps[bu]], writes=[b_a[jl][g]])
                hk = pre_hooks.get((pi, len(loaded) - 1))
                if hk is not None:
                    hk()
            hk = pre_hooks.get((pi, "down"))
            if hk is not None:
                hk()
            dn = 0
            for g in range(NG):
                sl = slice(g * GS, (g + 1) * GS)
                for oc in range(KC):
                    bd = 4 + dn % 2
                    dn += 1
                    for jl in range(npc):
                        P.op("pe", MM(ps[:, bd, :], wdp[wslot][:, jl, oc * 128:(oc + 1) * 128], aT[:, jl, sl], jl == 0, jl == npc - 1),
                             reads=[b_wd, b_a[jl][g]], writes=[b_ps[bd]], inc=(jl == npc - 1))
                    P.op("dve", STT(hT[:, oc, sl], ps[:, bd, :], dcol(Gname, oc), hT[:, oc, sl], ALU.mult, ALU.add),
                         reads=[b_ps[bd], b_der, b_h[oc][g]], writes=[b_h[oc][g]])
        P.retire(*[b for row in b_a for b in row])


    xs = [A.view(AT_OFF + i * 4096, F32, [D]) for i in range(8)]
    b_xs = P.bufs("xs", 8)
    wa = [A.view(WD_OFF + i * 16384, BF16, [KC, D]) for i in range(2)]
    wada_block(0, wa[0], ring["b_wd"][0])
    wada_block(1, wa[1], ring["b_wd"][1])
    x_v = x_d.rearrange("(t p) d -> t p d", p=128)
    for g in range(NG):
        for j in range(4):
            tt = g * 4 + j
            s = tt % 8
            P.dma("sp", xs[s][:], x_v[tt], writes=[b_xs[s]])
        for c in range(KC):
            bk = c % 2
            for j in range(4):
                s = (g * 4 + j) % 8
                P.op("pe", TR(ps[:, bk, j * 128:(j + 1) * 128], xs[s][:, c * 128:(c + 1) * 128], ident[:]),
                     reads=[b_xs[s], b_const], writes=[b_ps[bk]], inc=(j == 3))
            P.op("act", ACT(hT[:, c, g * GS:(g + 1) * GS], ps[:, bk, :], AF.Copy), reads=[b_ps[bk]], writes=[b_h[c][g]])
        if g == 0:
            mod_block(0, wa[0], ring["b_wd"][0])
            mod_block(1, wa[1], ring["b_wd"][1])
            derive_A("A1", "g1", 1, 0, "SH1")
    P.retire(*b_xs)
    ring["wd_i"] = 0

    for g in range(NG):
        norm_group(g, "A1", "SH1")

    def make_mod_hook(ms):
        def hk():
            for m in ms:
                wslot = ring["wd_i"] % 2
                b_w_ = ring["b_wd"][wslot]
                wv = A.view(WD_OFF + wslot * 16384, BF16, [KC, D])
                wada_block(m, wv, b_w_)
                mod_block(m, wv, b_w_)
        return hk

    def hook_g1():
        derive_G("G1", 2, 0.5)

    if "ffn1" in stages:
        ffn(f1g_d, f1u_d, f1d_d, "G1", ring,
            {(0, 0): make_mod_hook([2]), (0, "down"): hook_g1,
             (0, 1): make_mod_hook([3]), (1, 0): make_mod_hook([4])})
    else:
        make_mod_hook([2, 3, 4])()
    derive_A("A2", "g2", 4, 3, "SH2")
    LATE_MOD = "mix" in stages
    if not LATE_MOD:
        make_mod_hook([5, 6, 7, 8])()
        derive_G("G2", 5, 1.0)
        derive_A("A3", "g3", 7, 6, "SH3")
        derive_G("G3", 8, 0.5)

    if "mix" in stages:
        for g in range(NG):
            norm_group(g, "A2", "SH2")
        hsp = nc.dram_tensor("hspill", [128, KC, T], F32).ap()
        b_hsp = P.buf("hsp")
        for c in range(KC):
            P.dma("sp", hsp[:, c, :], hT[:, c, :], reads=[b_h[c][g] for g in range(NG)], writes=[b_hsp])
        P.retire(*[b for row in b_h for b in row])
        P.retire(*ring["b_wd"])
        P.retire(*ring["b_gu"])
        M2 = R_OFF
        SC_A = 1.0 / math.sqrt(96.0)
        SC_B = 1.0 / math.sqrt(64.0)
        win_v = win_d.rearrange("(kc p) n -> p kc n", p=128)
        oaT = A.view(0, BF16, [4, T])
        obT = A.view(16384, BF16, [4, T])
        cosF = A.view(32768, F32, [T])
        sinF = A.view(40960, F32, [T])
        kpeR = A.view(49152, F32, [T])
        sqpe = A.view(57344, BF16, [T])
        b_oa = [[P.buf("oa") for g in range(NG)] for c in range(4)]
        b_ob = [[P.buf("ob") for g in range(NG)] for c in range(4)]
        b_tab = P.buf("tab")
        b_kpe = P.bufs("kpe", NG)
        wlat = A.view(M2 + 0, BF16, [KC, 640])
        wkr = A.view(M2 + 10240, BF16, [KC, 96])
        wkrp = A.view(M2 + 11776, BF16, [KC, 96])
        wuq = A.view(M2 + 13312, BF16, [3, 768])
        wuqp = A.view(M2 + 17920, BF16, [3, 8, 96])
        wukk = A.view(M2 + 22528, BF16, [2, 8, 64])
        wukv = A.view(M2 + 24576, BF16, [2, 512])
        wuqp_f = A.view(M2 + 17920, BF16, [3, 768])
        wukk_f = A.view(M2 + 22528, BF16, [2, 512])
        cqT = A.view(M2 + 26624, BF16, [3, T])
        ckvT = A.view(M2 + 38912, BF16, [2, T])
        vh = [A.view(M2 + 47104 + i * 4096, BF16, [16, 128]) for i in range(2)]
        qh = [A.view(M2 + 55296 + i * 4096, BF16, [T]) for i in range(2)]
        kh = [A.view(M2 + 63488 + i * 4096, BF16, [T]) for i in range(2)]
        PT = [A.view(M2 + 71680 + i * 1024, BF16, [GS]) for i in range(4)]
        oraw = [A.view(M2 + 75776 + i * 2048, F32, [GS]) for i in range(2)]
        dn = [A.view(M2 + 79872 + i * 2048, F32, [GS]) for i in range(2)]
        onesf = A.view(M2 + 83968, F32, [128])
        rin = A.view(M2 + 84480, F32, [GS])
        posi = A.view(M2 + 26624, I32, [T])
        tmpf = A.view(M2 + 34816, F32, [T])
        kfi = A.view(M2 + 43008, I32, [T])
        kff = A.view(M2 + 43008, F32, [T])
        b_tt = P.buf("tabtmp")
        b_w = {k: P.buf("w_" + k) for k in ("lat", "kr", "uq", "uqp", "ukk", "ukv")}
        b_PT = P.bufs("PT", 4)
        b_oraw = P.bufs("oraw", 2)
        b_dn = P.bufs("dn", 2)
        b_rin = P.buf("rin")
        sq_e = A.view(M2 + 86528, BF16, [GS])
        od_e = A.view(M2 + 87552, F32, [GS])
        rs_e = A.view(M2 + 89600, F32, [GS])
        b_sqe, b_ode, b_rse = P.buf("sqe"), P.buf("ode"), P.buf("rse")
        b_onesf = P.buf("onesf")
        P.op("pool", MEMSET(onesf[:], 1.0), writes=[b_onesf])
        self_ = [A.view(M2 + 91648 + i * 512, F32, [128]) for i in range(2)]
        rq_t = [A.view(M2 + 92672 + i * 128, F32, [32]) for i in range(2)]
        rk_t = [A.view(M2 + 92928 + i * 128, F32, [32]) for i in range(2)]
        t4 = [A.view(M2 + 93184 + i * 32, F32, [8]) for i in range(2)]
        dg = [A.view(M2 + 93248 + i * 512, F32, [128]) for i in range(8)]
        sel2b = A.view(M2 + 97344, BF16, [2])
        b_rq, b_rk = P.bufs("rq", 2), P.bufs("rk", 2)
        b_t4 = P.bufs("t4", 2)
        b_dg = P.bufs("dg", 8)
        P.op("pool", MEMSET(self_[0][:], 0.0), writes=[b_onesf])
        P.op("pool", MEMSET(self_[1][:], 0.0), writes=[b_onesf])
        P.op("pool", MEMSET(self_[0][:, 0:64], 1.0), writes=[b_onesf])
        P.op("pool", MEMSET(self_[1][:, 64:128], 1.0), writes=[b_onesf])
        P.op("pool", MEMSET(sel2b[:], 0.0), writes=[b_onesf])
        P.op("pool", MEMSET(sel2b[0:64, 0:1], 1.0), writes=[b_onesf])
        P.op("pool", MEMSET(sel2b[64:128, 1:2], 1.0), writes=[b_onesf])
        cmh = A.view(MISC + 13312 + 72, F32, [1])
        P.op("pool", MEMSET(cmh[:], -0.5), writes=[b_onesf])

        def POWH(out, in0, ncol):
            return lambda e: e.tensor_tensor(out=out, in0=in0, in1=cmh[:, 0:1].to_broadcast([128, ncol]), op=ALU.pow)

        TWO_PI = 2.0 * math.pi
        C1 = 6.28125
        C2 = TWO_PI - C1

        def build_tables(fname, sname):
            P.dma("sp", posi[:], pos_d.partition_broadcast(128), writes=[b_tt])
            P.op("dve", CP(tmpf[:], posi[:]), reads=[b_tt], writes=[b_tt])
            P.op("dve", TS(tmpf[:], tmpf[:], pvb(fname), None, ALU.mult), reads=[b_tt, b_pvT], writes=[b_tt])
            P.op("dve", TS(cosF[:], tmpf[:], 1.0 / TWO_PI, None, ALU.mult), reads=[b_tt], writes=[b_tab])
            P.op("dve", CP(kfi[:], cosF[:]), reads=[b_tab], writes=[b_tt])
            P.op("dve", CP(cosF[:], kfi[:]), reads=[b_tt], writes=[b_tab])
            P.op("dve", STT(tmpf[:], cosF[:], -C1, tmpf[:], ALU.mult, ALU.add), reads=[b_tab, b_tt], writes=[b_tt])
            P.op("dve", STT(tmpf[:], cosF[:], -C2, tmpf[:], ALU.mult, ALU.add), reads=[b_tab, b_tt], writes=[b_tt])
            P.op("dve", TS(tmpf[:], tmpf[:], math.pi, -math.pi, ALU.min, ALU.max), reads=[b_tt], writes=[b_tt])
            P.op("act", ACT(sinF[:], tmpf[:], AF.Sin, scale=pvb(sname)), reads=[b_tt, b_pvT], writes=[b_tab])
            P.op("dve", TS(tmpf[:], tmpf[:], math.pi / 2, None, ALU.add), reads=[b_tt], writes=[b_tt])
            P.op("dve", TS(kff[:], tmpf[:], math.pi, -TWO_PI, ALU.is_gt, ALU.mult), reads=[b_tt], writes=[b_tt])
            P.op("dve", TT(tmpf[:], tmpf[:], kff[:], ALU.add), reads=[b_tt], writes=[b_tt])
            P.op("dve", TS(tmpf[:], tmpf[:], math.pi, -math.pi, ALU.min, ALU.max), reads=[b_tt], writes=[b_tt])
            P.op("act", ACT(cosF[:], tmpf[:], AF.Sin), reads=[b_tt], writes=[b_tab])

        def rstd_from(bank, rows, n, r):
            if _os.environ.get("MK_LN") != "1":
                P.op("act", ACT(rsb[r][rows, :], ps[rows, bank, :], AF.Sqrt, bias=epsc[rows, :], scale=1.0 / n),
                     reads=[b_ps[bank], b_const], writes=[b_rsb[r]])
                P.op("dve", RECIP(rsb[r][rows, :], rsb[r][rows, :]), reads=[b_rsb[r]], writes=[b_rsb[r]])
                return
            P.op("act", ACT(rsb[r][rows, :], ps[rows, bank, :], AF.Ln, bias=epsc[rows, :], scale=1.0 / n),
                 reads=[b_ps[bank], b_const], writes=[b_rsb[r]])
            P.op("act", ACT(rsb[r][rows, :], rsb[r][rows, :], AF.Exp, scale=-0.5), reads=[b_rsb[r]], writes=[b_rsb[r]])

        def rope_parts(nrows, gcol, gpcol, sl, out_ap, b_out, after=()):
            rows = slice(0, nrows)
            P.op("dve", STT(fsc[1][rows, :], ps[rows, 5, :], gpcol[rows, :], sinF[rows, sl], ALU.mult, ALU.mult),
                 reads=[b_ps[5], b_pvT, b_tab], writes=[b_fsc[1]])
            P.op("dve", STT(fsc[0][rows, :], ps[rows, 4, :], gcol[rows, :], cosF[rows, sl], ALU.mult, ALU.mult),
                 reads=[b_ps[4], b_pvT, b_tab] + list(after), writes=[b_fsc[0]])
            P.op("dve", TT(out_ap, fsc[0][rows, :], fsc[1][rows, :], ALU.add), reads=[b_fsc[0], b_fsc[1]], writes=[b_out])

        def tiny_rstd(i, which, g, ncomp, n, mult, stat_fn):
            base = 0 if which == "q" else 32
            dst_t, b_dst_t = (rq_t[i], b_rq[i]) if which == "q" else (rk_t[i], b_rk[i])
            c0 = g * 4 * ncomp
            if _os.environ.get("MK_SKIP") == which:
                P.op("pool", MEMSET(dst_t[:, c0:c0 + 4 * ncomp], 0.1), writes=[b_dst_t])
                return
            for j in range(4):
                stat_fn(j, ps[:, 6, base + c0 + j * ncomp:base + c0 + (j + 1) * ncomp])
            ti = 0 if which == "q" else 1
            P.op("dve", TS(t4[ti][:, 0:4 * ncomp], ps[:, 6, base + c0:base + c0 + 4 * ncomp], mult / n, EPS * mult, ALU.mult, ALU.add),
                 reads=[b_ps[6]], writes=[b_t4[ti]])
            P.op("pool", POWH(dst_t[:, c0:c0 + 4 * ncomp], t4[ti][:, 0:4 * ncomp], 4 * ncomp), reads=[b_t4[ti], b_onesf], writes=[b_dst_t])

        def qprep_gen(i, nrows, ncomp, n, gcol, gpcol, dst, b_dst, sl, g):
            rows = slice(0, nrows)
            P.op("act", ACT(sqs[0][rows, :], ps[rows, 4, :], AF.Square), reads=[b_ps[4]], writes=[b_sqs[0]])
            rope_parts(nrows, gcol, gpcol, sl, fsc[0][rows, :], b_fsc[0], after=[b_sqs[0]])
            yield

            def stat(j, out_ap):
                rhs = ones[rows, 0:1] if ncomp == 1 else sel2b[:, 0:2]
                P.op("pe", MM(out_ap, sqs[0][rows, j * 128:(j + 1) * 128], rhs, True, True),
                     reads=[b_sqs[0], b_const, b_onesf], writes=[b_ps[6]])
            tiny_rstd(i, "q", g, ncomp, n, 1.0, stat)
            yield
            c0 = g * 4 * ncomp
            for j in range(4):
                for c in range(ncomp):
                    d = j * ncomp + c
                    col = c0 + j * ncomp + c
                    P.op("dve", TS(dg[d][:], ident[:], rq_t[i][:, col:col + 1], None, ALU.mult),
                         reads=[b_const, b_rq[i]], writes=[b_dg[d]])
            for j in range(4):
                for c in range(ncomp):
                    d = j * ncomp + c
                    lhsT = onesf[:, 0:128] if ncomp == 1 else self_[c][:, 0:128]
                    P.op("pe", MM(ps[0:128, 4, j * 128:(j + 1) * 128], lhsT, dg[d][:], c == 0, c == ncomp - 1),
                         reads=[b_onesf, b_dg[d]], writes=[b_ps[4]], inc=(j == 3 and c == ncomp - 1))
            P.op("dve", TT(dst, fsc[0][rows, :], ps[rows, 4, :], ALU.mult), reads=[b_fsc[0], b_ps[4]], writes=[b_dst])
            yield

        ring_i = [0]
        pend = []

        def step_pend():
            for gen_ in list(pend):
                try:
                    next(gen_)
                except StopIteration:
                    pend.remove(gen_)

        def attention(kT, b_k, qT, b_q, rows, scale, vts, qc, filler, fill_at, sbanks=(0, 1)):
            qsl = slice(qc * GS, (qc + 1) * GS)

            def S(kt):
                it = ring_i[0]
                ring_i[0] += 1
                sb = sbanks[it % len(sbanks)]
                pr = it % 4
                P.op("pe", MM(ps[:, sb, :], kT[rows, kt * 128:(kt + 1) * 128], qT[rows, qsl], True, True),
                     reads=[b_k, b_q], writes=[b_ps[sb]])
                sc_ap, b_sc = scale(kt)
                if _os.environ.get("MK_CONSTSC") == "1":
                    sc_ap = 0.1
                P.op("act", ACT(PT[pr][:], ps[:, sb, :], AF.Exp, scale=sc_ap), reads=[b_ps[sb], b_sc], writes=[b_PT[pr]])
                return pr

            depth = len(sbanks) - 1
            prs = {}
            for k0 in range(depth):
                prs[k0] = S(k0)
            for kt in range(16):
                if kt + depth < 16:
                    prs[kt + depth] = S(kt + depth)
                pr = prs[kt]
                for vi, (vfn, M, bank, b_v) in enumerate(vts):
                    P.op("pe", MM(ps[0:M, bank, :], vfn(kt), PT[pr][:], kt == 0, kt == 15),
                         reads=[b_v, b_PT[pr]], writes=[b_ps[bank]], inc=(kt == 15 and vi == len(vts) - 1))
                if kt in fill_at:
                    filler()
                if kt in (4, 8, 12):
                    step_pend()

        def recip_bcast(src, drow):
            P.op("dve", RECIP(rin[drow:drow + 1, :], src[drow:drow + 1, :]), reads=[b_oraw[0], b_oraw[1]], writes=[b_rin])
            P.op("pe", MM(ps[:, 7, :], onesf[drow:drow + 1, :], rin[drow:drow + 1, :], True, True),
                 reads=[b_onesf, b_rin], writes=[b_ps[7]])

        NOFILL = _os.environ.get("MK_NOFILL") == "1"

        def make_filler(gen):
            def f():
                if gen is not None and not NOFILL:
                    next(gen, None)
            return f

        def drain(gen):
            if gen is not None:
                for _ in gen:
                    pass

        for hh in range(2):
            P.dma("pool", wlat[:, hh * 4:(hh + 1) * 4, :], win_v[:, hh * 4:(hh + 1) * 4, 0:640], writes=[b_w["lat"]])
        P.op("pool", MEMSET(wkr[:], 0.0), writes=[b_w["kr"]])
        P.op("pool", MEMSET(wkrp[:], 0.0), writes=[b_w["kr"]])
        P.dma("pool", wkr[:, :, 64:96], win_v[:, :, 640:672], writes=[b_w["kr"]])
        P.dma("pool", wkrp[:, :, 64:80], win_v[:, :, 656:672], writes=[b_w["kr"]])
        P.dma("pool", wkrp[:, :, 80:96], win_v[:, :, 640:656], writes=[b_w["kr"]])
        P.dma("pool", wuq[:], wuq_d.rearrange("(kc p) n -> p kc n", p=128), writes=[b_w["uq"]])
        P.op("pool", MEMSET(wuqp[:], 0.0), writes=[b_w["uqp"]])
        wuq_v4 = wuq_d.rearrange("(kc p) (h d) -> p kc h d", p=128, d=96)
        for kc in range(3):
            P.dma("pool", wuqp[:, kc, :, 64:80], wuq_v4[:, kc, :, 80:96], writes=[b_w["uqp"]])
            P.dma("pool", wuqp[:, kc, :, 80:96], wuq_v4[:, kc, :, 64:80], writes=[b_w["uqp"]])
        wukv_v4 = wukv_d.rearrange("(kc p) (h d) -> p kc h d", p=128, d=128)
        wukv_s = wukv.rearrange("p kc (h d) -> p kc h d", d=64)
        for kc in range(2):
            P.dma("pool", wukk[:, kc, :, :], wukv_v4[:, kc, :, 0:64], writes=[b_w["ukk"]])
            P.dma("pool", wukv_s[:, kc, :, :], wukv_v4[:, kc, :, 64:128], writes=[b_w["ukv"]])

        build_tables("freqa", "signa")
        P.retire(b_tt)
        b_cq = [[P.buf("cq") for g in range(NG)] for c in range(3)]
        b_ckv = [[P.buf("ckv") for g in range(NG)] for c in range(2)]
        b_vh = P.bufs("vh", 2)
        b_qh = P.bufs("qh", 2)
        b_kh = P.bufs("kh", 2)

        zst = [oraw[0], oraw[1], dn[0]]
        b_zst = [b_oraw[0], b_oraw[1], b_dn[0]]
        for g in range(NG):
            sl = slice(g * GS, (g + 1) * GS)
            for (lc0, nl, dstT, b_dst, nname, nfeat, sbank, r) in ((0, 3, cqT, b_cq, "qnorm", 384.0, 6, 0), (3, 2, ckvT, b_ckv, "kvnorm", 256.0, 7, 1)):
                for l in range(nl):
                    bk = 4 + l % 2
                    for kc in range(KC):
                        P.op("pe", MM(ps[:, bk, :], wlat[:, kc, (lc0 + l) * 128:(lc0 + l + 1) * 128], nT[:, kc, sl], kc == 0, kc == KC - 1),
                             reads=[b_w["lat"], b_n[kc][g]], writes=[b_ps[bk]], inc=(kc == KC - 1))
                    P.op("act", ACT(zst[l][:], ps[:, bk, :], AF.Copy), reads=[b_ps[bk]], writes=[b_zst[l]])
                    P.op("act", ACT(sqs[l % 2][:], ps[:, bk, :], AF.Square), reads=[b_ps[bk]], writes=[b_sqs[l % 2]])
                    P.op("pe", MM(ps[:, sbank, :], ones[:], sqs[l % 2][:], l == 0, l == nl - 1),
                         reads=[b_const, b_sqs[l % 2]], writes=[b_ps[sbank]])
                rstd_from(sbank, slice(0, 128), nfeat, r)
                for l in range(nl):
                    P.op("dve", STT(dstT[:, l, sl], zst[l][:], pvb(nname, l), rsb[r][:], ALU.mult, ALU.mult),
                         reads=[b_zst[l], b_pvT, b_rsb[r]], writes=[b_dst[l][g]])
            for (wt, bk) in ((wkr, 4), (wkrp, 5)):
                for kc in range(KC):
                    P.op("pe", MM(ps[0:96, bk, :], wt[:, kc, :], nT[:, kc, sl], kc == 0, kc == KC - 1),
                         reads=[b_w["kr"], b_n[kc][g]], writes=[b_ps[bk]], inc=(kc == KC - 1))
            R96 = slice(0, 96)
            P.op("act", ACT(sqpe[R96, sl], ps[R96, 4, :], AF.Square), reads=[b_ps[4]], writes=[b_kpe[g]])
            P.op("dve", STT(fsc[0][R96, :], ps[R96, 4, :], pvb("kg")[R96, :], cosF[R96, sl], ALU.mult, ALU.mult),
                 reads=[b_ps[4], b_pvT, b_tab, b_kpe[g]], writes=[b_fsc[0]])
            P.op("dve", STT(fsc[1][R96, :], ps[R96, 5, :], pvb("kgp")[R96, :], sinF[R96, sl], ALU.mult, ALU.mult),
                 reads=[b_ps[5], b_pvT, b_tab], writes=[b_fsc[1]])
            P.op("dve", TT(kpeR[R96, sl], fsc[0][R96, :], fsc[1][R96, :], ALU.add), reads=[b_fsc[0], b_fsc[1]], writes=[b_kpe[g]])

        R96 = slice(0, 96)

        def mla_prep(h):
            i = h % 2
            kindB = (h % 2 == 1)
            for g in range(NG):
                sl = slice(g * GS, (g + 1) * GS)
                Mq = 128 if h < 7 else 96
                Mk = 128 if h < 7 else 64
                for kc in range(3):
                    P.op("pe", MM(ps[0:Mq, 4, :], wuq[:, kc, h * 96:h * 96 + Mq], cqT[:, kc, sl], kc == 0, kc == 2),
                         reads=[b_w["uq"], b_cq[kc][g]], writes=[b_ps[4]], inc=(kc == 2))
                for kc in range(3):
                    P.op("pe", MM(ps[0:Mq, 5, :], wuqp_f[:, kc, h * 96:h * 96 + Mq], cqT[:, kc, sl], kc == 0, kc == 2),
                         reads=[b_w["uqp"], b_cq[kc][g]], writes=[b_ps[5]], inc=(kc == 2))
                yield from qprep_gen(i, 96, 1, 96.0, pvb("qg"), pvb("qgp"), qh[i][R96, sl], b_qh[i], sl, g)
                for kc in range(2):
                    P.op("pe", MM(ps[0:Mk, 5, :], wukk_f[:, kc, h * 64:h * 64 + Mk], ckvT[:, kc, sl], kc == 0, kc == 1),
                         reads=[b_w["ukk"], b_ckv[kc][g]], writes=[b_ps[5]], inc=(kc == 1))
                P.op("act", ACT(sqs[1][0:64, :], ps[0:64, 5, :], AF.Square), reads=[b_ps[5]], writes=[b_sqs[1]])
                P.op("dve", TS(kh[i][0:64, sl], ps[0:64, 5, :], pvb("kg")[0:64, :], None, ALU.mult),
                     reads=[b_ps[5], b_pvT, b_sqs[1]], writes=[b_kh[i]])
                P.op("dve", CP(kh[i][64:96, sl], kpeR[64:96, sl]), reads=[b_kpe[g]], writes=[b_kh[i]])
                yield

                def kstat(j, out_ap, g=g):
                    P.op("pe", MM(out_ap, sqs[1][0:64, j * 128:(j + 1) * 128], ones[0:64, 0:1], True, False),
                         reads=[b_sqs[1], b_const], writes=[b_ps[6]], inc=False)
                    P.op("pe", MM(out_ap, sqpe[64:96, g * GS + j * 128:g * GS + (j + 1) * 128], ones[64:96, 0:1], False, True),
                         reads=[b_kpe[g], b_const], writes=[b_ps[6]])
                tiny_rstd(i, "k", g, 1, 96.0, 96.0, kstat)
                yield
            P.op("pool", MEMSET(vh[i][:], 0.0), writes=[b_vh[i]])
            onescol = 32 if kindB else 64
            c0 = 64 if kindB else 0
            P.op("pool", MEMSET(vh[i][:, :, onescol:onescol + 1], 1.0), writes=[b_vh[i]])
            for t0 in range(0, 16, 8):
                for tt in range(t0, t0 + 8):
                    for kc in range(2):
                        P.op("pe", MM(ps[:, 5, (tt - t0) * 64:(tt - t0 + 1) * 64], ckvT[:, kc, tt * 128:(tt + 1) * 128],
                                      wukv[:, kc, h * 64:(h + 1) * 64], kc == 0, kc == 1),
                             reads=[b_ckv[kc][tt // 4], b_w["ukv"]], writes=[b_ps[5]], inc=(tt == t0 + 7 and kc == 1))
                P.op("dve", CP(vh[i][:, t0:t0 + 8, c0:c0 + 64], ps[:, 5, :].rearrange("p (a b) -> p a b", b=64)),
                     reads=[b_ps[5]], writes=[b_vh[i]])
                yield

        def mla_attn(h, filler):
            i = h % 2
            kindB = (h % 2 == 1)
            if kindB:
                vt = (lambda kt, i=i: vh[i][:, kt, 0:128], 128, 3, b_vh[i])
                drow, ob = 32, 1
                drows = slice(64, 128)
            else:
                vt = (lambda kt, i=i: vh[i][:, kt, 0:128], 128, 2, b_vh[i])
                drow, ob = 64, 0
                drows = slice(0, 64)
            for qc in range(NG):
                qsl = slice(qc * GS, (qc + 1) * GS)
                attention(kh[i], b_kh[i], qh[i], b_qh[i], R96, (lambda kt, i=i: (rk_t[i][:, kt:kt + 1], b_rk[i])), [vt], qc, filler, (1, 3, 5, 7, 9, 11, 13), sbanks=((0, 1, 2) if kindB else (0, 1, 3)))
                if kindB:
                    P.op("dve", CP(oraw[ob][32:33, :], ps[32:33, 3, :]), reads=[b_ps[3]], writes=[b_oraw[ob]])
                    P.op("dve", CP(oraw[ob][64:128, :], ps[64:128, 3, :]), reads=[b_ps[3]], writes=[b_oraw[ob]])
                else:
                    P.op("dve", CP(oraw[ob][0:65, :], ps[0:65, 2, :]), reads=[b_ps[2]], writes=[b_oraw[ob]])
                P.op("dve", RECIP(rin[drow:drow + 1, :], oraw[ob][drow:drow + 1, :]), reads=[b_oraw[ob]], writes=[b_rin])

                def epi(ob=ob, drow=drow, drows=drows, qsl=qsl, qc=qc):
                    P.op("pe", MM(ps[:, 7, :], onesf[drow:drow + 1, :], rin[drow:drow + 1, :], True, True),
                         reads=[b_onesf, b_rin], writes=[b_ps[7]])
                    P.op("dve", TT(oaT[drows, h // 2, qsl], oraw[ob][drows, :], ps[drows, 7, :], ALU.mult),
                         reads=[b_oraw[ob], b_ps[7]], writes=[b_oa[h // 2][qc]])
                    yield
                pend.append(epi())

        wdh = [A.view(M2 + i * 10240, BF16, [KC, 640]) for i in range(2)]
        vb = [A.view(M2 + 26624 + i * 6400, BF16, [16, 200]) for i in range(2)]
        QB0, KB0, VB0 = 672, 1184, 1696
        dstate = {}

        def diff_setup():
            P.retire(*[b for row in b_cq for b in row], *[b for row in b_ckv for b in row], *b_vh, *b_w.values(), *b_kpe)
            nonlocal_b_tt = P.buf("tabtmp2")
            dstate["b_tt"] = nonlocal_b_tt
            yield

        def build_tables2(fname, sname, bt):
            P.dma("sp", posi[:], pos_d.partition_broadcast(128), writes=[bt])
            P.op("dve", CP(tmpf[:], posi[:]), reads=[bt], writes=[bt])
            P.op("dve", TS(tmpf[:], tmpf[:], pvb(fname), None, ALU.mult), reads=[bt, b_pvT], writes=[bt])
            yield
            P.op("dve", TS(cosF[:], tmpf[:], 1.0 / TWO_PI, None, ALU.mult), reads=[bt], writes=[b_tab])
            P.op("dve", CP(kfi[:], cosF[:]), reads=[b_tab], writes=[bt])
            P.op("dve", CP(cosF[:], kfi[:]), reads=[bt], writes=[b_tab])
            yield
            P.op("dve", STT(tmpf[:], cosF[:], -C1, tmpf[:], ALU.mult, ALU.add), reads=[b_tab, bt], writes=[bt])
            P.op("dve", STT(tmpf[:], cosF[:], -C2, tmpf[:], ALU.mult, ALU.add), reads=[b_tab, bt], writes=[bt])
            P.op("dve", TS(tmpf[:], tmpf[:], math.pi, -math.pi, ALU.min, ALU.max), reads=[bt], writes=[bt])
            P.op("act", ACT(sinF[:], tmpf[:], AF.Sin, scale=pvb(sname)), reads=[bt, b_pvT], writes=[b_tab])
            yield
            P.op("dve", TS(tmpf[:], tmpf[:], math.pi / 2, None, ALU.add), reads=[bt], writes=[bt])
            P.op("dve", TS(kff[:], tmpf[:], math.pi, -TWO_PI, ALU.is_gt, ALU.mult), reads=[bt], writes=[bt])
            P.op("dve", TT(tmpf[:], tmpf[:], kff[:], ALU.add), reads=[bt], writes=[bt])
            P.op("dve", TS(tmpf[:], tmpf[:], math.pi, -math.pi, ALU.min, ALU.max), reads=[bt], writes=[bt])
            P.op("act", ACT(cosF[:], tmpf[:], AF.Sin), reads=[bt], writes=[b_tab])
            yield

        def diff_prep(h):
            i = h % 2
            if h == 0:
                P.retire(*[b for row in b_cq for b in row], *[b for row in b_ckv for b in row], *b_vh, *b_w.values(), *b_kpe)
                bt = P.buf("tabtmp2")
                yield from build_tables2("freqb", "signb", bt)
                P.retire(bt)
                dstate["b_wdh"] = P.bufs("wdh", 2)
                dstate["b_vb"] = P.bufs("vb", 2)
                P.op("dve", TT(der[:, 74:75], pvb("lq1"), pvb("lk1"), ALU.mult), reads=[b_pvT], writes=[b_der])
                P.op("dve", TT(der[:, 75:76], pvb("lq2"), pvb("lk2"), ALU.mult), reads=[b_pvT], writes=[b_der])
                yield
                P.op("pe", MM(ps[:, 6, 0:2], onesf[:], der[:, 74:76], True, True), reads=[b_onesf, b_der], writes=[b_ps[6]])
                P.op("act", ACT(der[:, 74:76], ps[:, 6, 0:2], AF.Exp), reads=[b_ps[6]], writes=[b_der])
                P.op("dve", TT(der[:, 72:73], der[:, 75:76], der[:, 74:75], ALU.subtract), reads=[b_der], writes=[b_der])
                P.op("dve", TS(der[:, 72:73], der[:, 72:73], -LAMBDA_INIT, None, ALU.add), reads=[b_der], writes=[b_der])
                P.op("dve", TS(der[:, 73:74], pvb("subln"), 1.0 - LAMBDA_INIT, None, ALU.mult), reads=[b_pvT], writes=[b_der])
            b_wdh, b_vb = dstate["b_wdh"], dstate["b_vb"]

            def diff_load(hh):
                ii = hh % 2
                P.op("pool", MEMSET(wdh[ii][:, :, 128:256], 0.0), writes=[b_wdh[ii]])
                P.op("pool", MEMSET(wdh[ii][:, :, 384:512], 0.0), writes=[b_wdh[ii]])
                for (src0, d0) in ((QB0, 0), (KB0, 256)):
                    s0 = src0 + hh * 128
                    P.dma("pool", wdh[ii][:, :, d0:d0 + 128], win_v[:, :, s0:s0 + 128], writes=[b_wdh[ii]])
                    for base in (0, 64):
                        P.op("pool", CP(wdh[ii][:, :, d0 + 128 + base:d0 + 128 + base + 8], wdh[ii][:, :, d0 + base + 8:d0 + base + 16]),
                             reads=[b_wdh[ii]], writes=[b_wdh[ii]])
                        P.op("pool", CP(wdh[ii][:, :, d0 + 128 + base + 8:d0 + 128 + base + 16], wdh[ii][:, :, d0 + base:d0 + base + 8]),
                             reads=[b_wdh[ii]], writes=[b_wdh[ii]])
                P.dma("pool", wdh[ii][:, :, 512:640], win_v[:, :, VB0 + hh * 128:VB0 + (hh + 1) * 128], writes=[b_wdh[ii]])

            if h == 0:
                diff_load(0)
                diff_load(1)
            elif h + 1 < 4:
                diff_load(h + 1)
            P.op("pool", MEMSET(vb[i][:], 0.0), writes=[b_vb[i]])
            P.op("pool", MEMSET(vb[i][:, :, 64:65], 1.0), writes=[b_vb[i]])
            P.op("pool", MEMSET(vb[i][:, :, 72 + 32:72 + 33], 1.0), writes=[b_vb[i]])
            yield
            for t0 in range(0, 16, 4):
                for tt in range(t0, t0 + 4):
                    for kc in range(KC):
                        P.op("pe", MM(ps[:, 5, (tt - t0) * 128:(tt - t0 + 1) * 128], nT[:, kc, tt * 128:(tt + 1) * 128],
                                      wdh[i][:, kc, 512:640], kc == 0, kc == KC - 1),
                             reads=[b_n[kc][tt // 4], b_wdh[i]], writes=[b_ps[5]], inc=(tt == t0 + 3 and kc == KC - 1))
                pv = ps[:, 5, :].rearrange("p (a b) -> p a b", b=128)
                P.op("dve", CP(vb[i][:, t0:t0 + 4, 0:64], pv[:, :, 0:64]), reads=[b_ps[5]], writes=[b_vb[i]])
                P.op("dve", CP(vb[i][:, t0:t0 + 4, 136:200], pv[:, :, 64:128]), reads=[b_ps[5]], writes=[b_vb[i]])
                yield
            for g in range(NG):
                sl = slice(g * GS, (g + 1) * GS)
                for (d0, gname, gpname, dst, b_dst) in ((0, "dqg", "dqgp", qh[i], b_qh[i]), (256, "dkg", "dkgp", kh[i], b_kh[i])):
                    for kc in range(KC):
                        P.op("pe", MM(ps[:, 4, :], wdh[i][:, kc, d0:d0 + 128], nT[:, kc, sl], kc == 0, kc == KC - 1),
                             reads=[b_wdh[i], b_n[kc][g]], writes=[b_ps[4]], inc=(kc == KC - 1))
                    for kc in range(KC):
                        P.op("pe", MM(ps[:, 5, :], wdh[i][:, kc, d0 + 128:d0 + 256], nT[:, kc, sl], kc == 0, kc == KC - 1),
                             reads=[b_wdh[i], b_n[kc][g]], writes=[b_ps[5]], inc=(kc == KC - 1))
                    if d0 == 0:
                        yield from qprep_gen(i, 128, 2, 64.0, pvb(gname), pvb(gpname), dst[:, sl], b_dst, sl, g)
                    else:
                        P.op("act", ACT(sqs[1][:], ps[:, 4, :], AF.Square), reads=[b_ps[4]], writes=[b_sqs[1]])
                        rope_parts(128, pvb(gname), pvb(gpname), sl, dst[:, sl], b_dst, after=[b_sqs[1]])
                        yield

                        def kstat2(j, out_ap):
                            P.op("pe", MM(out_ap, sqs[1][:, j * 128:(j + 1) * 128], sel2b[:, 0:2], True, True),
                                 reads=[b_sqs[1], b_onesf], writes=[b_ps[6]])
                        tiny_rstd(i, "k", g, 2, 64.0, 64.0, kstat2)
                        yield

        def diff_attn(h, filler):
            i = h % 2
            b_vb = dstate["b_vb"]
            RA = slice(0, 128)
            vtA = (lambda kt, i=i: vb[i][:, kt, 0:128], 128, 2, b_vb[i])
            vtB = (lambda kt, i=i: vb[i][:, kt, 72:200], 128, 3, b_vb[i])
            for qc in range(NG):
                qsl = slice(qc * GS, (qc + 1) * GS)
                for comp in range(2):
                    rows = slice(comp * 64, comp * 64 + 64)
                    attention(kh[i], b_kh[i], qh[i], b_qh[i], rows, (lambda kt, i=i, comp=comp: (rk_t[i][:, kt * 2 + comp:kt * 2 + comp + 1], b_rk[i])), [vtA, vtB], qc, filler, (3, 7, 11, 15))
                    P.op("dve", CP(oraw[0][0:65, :], ps[0:65, 2, :]), reads=[b_ps[2]], writes=[b_oraw[0]])
                    P.op("dve", CP(oraw[1][64:128, :], ps[64:128, 3, :]), reads=[b_ps[3]], writes=[b_oraw[1]])
                    P.op("dve", RECIP(rin[64:65, :], oraw[0][64:65, :]), reads=[b_oraw[0]], writes=[b_rin])

                    def epi(comp=comp, qsl=qsl, qc=qc):
                        P.op("pe", MM(ps[:, 7, :], onesf[64:65, :], rin[64:65, :], True, True),
                             reads=[b_onesf, b_rin], writes=[b_ps[7]])
                        P.op("dve", TT(dn[comp][0:64, :], oraw[0][0:64, :], ps[0:64, 7, :], ALU.mult),
                             reads=[b_oraw[0], b_ps[7]], writes=[b_dn[comp]])
                        P.op("dve", TT(dn[comp][64:128, :], oraw[1][64:128, :], ps[64:128, 7, :], ALU.mult),
                             reads=[b_oraw[1], b_ps[7]], writes=[b_dn[comp]])
                        if comp == 0:
                            yield
                            return
                        P.op("dve", STT(od_e[:], dn[1][:], der[:, 72:73], dn[0][:], ALU.mult, ALU.add),
                             reads=[b_dn[0], b_dn[1], b_der], writes=[b_ode])
                        P.op("act", ACT(sq_e[:], od_e[:], AF.Square), reads=[b_ode], writes=[b_sqe])
                        yield
                        P.op("pe", MM(ps[:, 7, :], ones[:], sq_e[:], True, True), reads=[b_const, b_sqe], writes=[b_ps[7]])
                        P.op("act", ACT(rs_e[:], ps[:, 7, :], AF.Sqrt, bias=epsc[:], scale=1.0 / 128.0),
                             reads=[b_ps[7], b_const], writes=[b_rse])
                        P.op("dve", RECIP(rs_e[:], rs_e[:]), reads=[b_rse], writes=[b_rse])
                        yield
                        P.op("dve", STT(obT[:, h, qsl], od_e[:], der[:, 73:74], rs_e[:], ALU.mult, ALU.mult),
                             reads=[b_ode, b_der, b_rse], writes=[b_ob[h][qc]])
                        yield
                    pend.append(epi())

        units = [("mla", h) for h in range(8)] + [("diff", h) for h in range(4)]

        def prep_of(u):
            return mla_prep(u[1]) if u[0] == "mla" else diff_prep(u[1])

        NU = int(_os.environ.get("MK_UNITS", "12"))
        if NU < 12:
            for c in range(4):
                for g in range(NG):
                    P.op("pool", MEMSET(oaT[:, c, g * GS:(g + 1) * GS], 0.0), writes=[b_oa[c][g]])
                    P.op("pool", MEMSET(obT[:, c, g * GS:(g + 1) * GS], 0.0), writes=[b_ob[c][g]])
            if NU <= 8:
                dstate["b_wdh"] = P.bufs("wdh", 2)
                dstate["b_vb"] = P.bufs("vb", 2)
        units = units[:NU]
        def warm(n):
            for k in range(n):
                P.op("pe", MM(ps[:, 7, :], ones[:], nT[:, k % KC, 0:GS], True, True),
                     reads=[b_const, b_n[k % KC][0]], writes=[b_ps[7]], inc=(k == n - 1))

        NWARM = int(_os.environ.get("MK_WARM", "0"))
        drain(prep_of(units[0]))
        for ui, u in enumerate(units):
            if NWARM:
                warm(NWARM)
            gen = prep_of(units[ui + 1]) if ui + 1 < len(units) else None
            if NOFILL and _os.environ.get("MK_PREFIRST") == "1":
                drain(gen)
            filler = make_filler(gen)
            if u[0] == "mla":
                mla_attn(u[1], filler)
            else:
                diff_attn(u[1], filler)
            drain(gen)
        while pend:
            step_pend()
        b_wdh, b_vb = dstate["b_wdh"], dstate["b_vb"]

        P.retire(*b_wdh, *b_vb, *b_qh, *b_kh, *b_PT, *b_oraw, *b_dn, b_rin, b_onesf, b_tab, b_sqe, b_ode, b_rse, *b_rq, *b_rk, *b_t4, *b_dg)
        woa = A.view(M2 + 0, BF16, [4, D])
        wob = A.view(M2 + 8192, BF16, [4, D])
        wout = A.view(M2 + 16384, BF16, [KC, D])
        wgt = [A.view(M2 + 32768 + i * 4096, BF16, [KC, 256]) for i in range(2)]
        hst = [A.view(M2 + 40960 + i * 2048, F32, [GS]) for i in range(2)]
        gA = A.view(M2 + 45056, F32, [GS])
        gB = A.view(M2 + 47104, F32, [GS])
        mT = A.view(M2 + 49152, BF16, [KC, T])
        b_woa, b_wob, b_wout = P.buf("woa"), P.buf("wob"), P.buf("wout")
        b_wgt = P.bufs("wgt", 2)
        b_hst = P.bufs("hst", 2)
        b_gA, b_gB = P.buf("gA"), P.buf("gB")
        b_m = [[P.buf("m") for g in range(NG)] for c in range(KC)]
        P.dma("pool", woa[:], woa_d.rearrange("(c p) n -> p c n", p=128), writes=[b_woa])
        P.dma("pool", wob[:], wob_d.rearrange("(c p) n -> p c n", p=128), writes=[b_wob])
        for hh in range(2):
            P.dma("pool", wout[:, hh * 4:(hh + 1) * 4, :], wout_d.rearrange("(c p) n -> p c n", p=128)[:, hh * 4:(hh + 1) * 4, :], writes=[b_wout])
        GA0, GB0 = 2208, 3232
        wad = A.view(M2 + 81920, BF16, [KC, D])
        b_wad = P.buf("wad")
        wada_block(5, wad, b_wad)
        late = {0: (5, 6), 2: (6, 7), 4: (7, 8), 6: (8, None)}
        nit = 0
        def load_wgt(oc_):
            wi_ = oc_ % 2
            P.dma("pool", wgt[wi_][:, :, 0:128], win_v[:, :, GA0 + oc_ * 128:GA0 + (oc_ + 1) * 128], writes=[b_wgt[wi_]])
            P.dma("pool", wgt[wi_][:, :, 128:256], win_v[:, :, GB0 + oc_ * 128:GB0 + (oc_ + 1) * 128], writes=[b_wgt[wi_]])

        load_wgt(0)
        for oc in range(KC):
            wi = oc % 2
            if oc + 1 < KC:
                load_wgt(oc + 1)
            for g in range(NG):
                sl = slice(g * GS, (g + 1) * GS)
                b0 = 4 * (nit % 2)
                nit += 1
                for c in range(4):
                    P.op("pe", MM(ps[:, b0, :], woa[:, c, oc * 128:(oc + 1) * 128], oaT[:, c, sl], c == 0, c == 3),
                         reads=[b_woa, b_oa[c][g]], writes=[b_ps[b0]], inc=(c == 3))
                for c in range(4):
                    P.op("pe", MM(ps[:, b0 + 1, :], wob[:, c, oc * 128:(oc + 1) * 128], obT[:, c, sl], c == 0, c == 3),
                         reads=[b_wob, b_ob[c][g]], writes=[b_ps[b0 + 1]], inc=(c == 3))
                for kc in range(KC):
                    P.op("pe", MM(ps[:, b0 + 2, :], wgt[wi][:, kc, 0:128], nT[:, kc, sl], kc == 0, kc == KC - 1),
                         reads=[b_wgt[wi], b_n[kc][g]], writes=[b_ps[b0 + 2]], inc=(kc == KC - 1))
                for kc in range(KC):
                    P.op("pe", MM(ps[:, b0 + 3, :], wgt[wi][:, kc, 128:256], nT[:, kc, sl], kc == 0, kc == KC - 1),
                         reads=[b_wgt[wi], b_n[kc][g]], writes=[b_ps[b0 + 3]], inc=(kc == KC - 1))
                P.op("act", ACT(gA[:], ps[:, b0 + 2, :], AF.Sigmoid), reads=[b_ps[b0 + 2]], writes=[b_gA])
                P.op("act", ACT(gB[:], ps[:, b0 + 3, :], AF.Sigmoid), reads=[b_ps[b0 + 3]], writes=[b_gB])
                P.op("dve", TT(gA[:], gA[:], ps[:, b0, :], ALU.mult), reads=[b_gA, b_ps[b0]], writes=[b_gA])
                P.op("dve", TT(gB[:], gB[:], ps[:, b0 + 1, :], ALU.mult), reads=[b_gB, b_ps[b0 + 1]], writes=[b_gB])
                P.op("dve", TT(mT[:, oc, sl], gA[:], gB[:], ALU.add), reads=[b_gA, b_gB], writes=[b_m[oc][g]])
            if oc in late:
                mcur, mnext = late[oc]
                mod_block(mcur, wad, b_wad)
                if mnext is not None:
                    wada_block(mnext, wad, b_wad)
        derive_G("G2", 5, 1.0)
        derive_A("A3", "g3", 7, 6, "SH3")
        derive_G("G3", 8, 0.5)
        P.retire(*[b for row in b_oa for b in row], *[b for row in b_ob for b in row])
        b_h = [[P.buf("h%d_%d" % (c, g)) for g in range(NG)] for c in range(KC)]
        nit = 0
        for g in range(NG):
            sl = slice(g * GS, (g + 1) * GS)
            for oc in range(KC):
                bk = nit % 2
                hi = nit % 2
                nit += 1
                P.dma("sp", hst[hi][:], hsp[:, oc, sl], reads=[b_hsp], writes=[b_hst[hi]], owner=b_hst[hi])
                for c in range(KC):
                    P.op("pe", MM(ps[:, bk, :], wout[:, c, oc * 128:(oc + 1) * 128], mT[:, c, sl], c == 0, c == KC - 1),
                         reads=[b_wout, b_m[c][g]], writes=[b_ps[bk]], inc=(c == KC - 1))
                P.op("dve", STT(hT[:, oc, sl], ps[:, bk, :], dcol("G2", oc), hst[hi][:], ALU.mult, ALU.add),
                     reads=[b_ps[bk], b_der, b_hst[hi]], writes=[b_h[oc][g]])
        P.retire(*[b for row in b_m for b in row], b_woa, b_wob, b_wout, *b_wgt, *b_hst, b_gA, b_gB, b_wad)
        P.retire(*[b for row in b_n for b in row])
        b_n = [[P.buf("n%d_%d" % (c, g)) for g in range(NG)] for c in range(KC)]
        ring["b_wd"] = P.bufs("wdp", 2)
        ring["b_gu"] = P.bufs("gu", 2)

    if "ffn2" in stages:
        for g in range(NG):
            norm_group(g, "A3", "SH3")
        ffn(f2g_d, f2u_d, f2d_d, "G3", ring, {})

    ost = [A.view(AT_OFF + i * 4096, F32, [D]) for i in range(2)]
    b_ost = P.bufs("ost", 2)
    ytmp = [A.view(AT_OFF + 8192 + i * 2048, F32, [GS]) for i in range(8)]
    b_ytmp = P.bufs("ytmp", 8)
    b_out = P.buf("out")
    for g in range(NG):
        r = norm_group(g, None, None)
        sl = slice(g * GS, (g + 1) * GS)
        for c in range(KC):
            P.op("dve", STT(ytmp[c][:], hT[:, c, sl], pvb("gf", c), rsb[r][:], ALU.mult, ALU.mult),
                 reads=[b_h[c][g], b_pvT, b_rsb[r]], writes=[b_ytmp[c]])
        for j in range(4):
            tt = g * 4 + j
            o = tt % 2
            for half in range(2):
                bk = (tt * 2 + half) % 4
                for cc in range(4):
                    c = half * 4 + cc
                    P.op("pe", TR(ps[:, bk, cc * 128:(cc + 1) * 128], ytmp[c][:, j * 128:(j + 1) * 128], ident[:]),
                         reads=[b_ytmp[c], b_const], writes=[b_ps[bk]], inc=(cc == 3))
                P.op("act", ACT(ost[o][:, half * 512:(half + 1) * 512], ps[:, bk, :], AF.Copy),
                     reads=[b_ps[bk]], writes=[b_ost[o]])
            P.dma("sp", out_d[tt * 128:(tt + 1) * 128, :], ost[o][:], reads=[b_ost[o]], writes=[b_out], owner=b_ost[o])
    P.wait_all("sp", b_ost + [b_out])
    P.emit()
    return nc


def _prep_inputs(inp):
    L = 0
    f32 = np.float32

    def pad128(v):
        o = np.zeros(128, f32)
        o[:v.shape[0]] = v
        return o

    def mla_perm(g):
        o = np.zeros(128, f32)
        o[64:80] = g[80:96]
        o[80:96] = g[64:80]
        return o

    def diff_rep(g):
        return np.concatenate([g, g]).astype(f32)

    def diff_perm(g):
        o = np.zeros(128, f32)
        for base in (0, 64):
            o[base:base + 8] = g[8:16]
            o[base + 8:base + 16] = g[0:8]
        return o

    freqa = np.zeros(128, f32)
    signa = np.ones(128, f32)
    fa = (1.0 / (np.float32(10000.0) ** (np.arange(16, dtype=f32) / np.float32(16)))).astype(f32)
    freqa[64:80] = fa
    freqa[80:96] = fa
    signa[64:80] = -1.0
    freqb = np.zeros(128, f32)
    signb = np.ones(128, f32)
    fb = (1.0 / (np.float32(500000.0) ** (np.arange(8, dtype=f32) / np.float32(8)))).astype(f32)
    for base in (0, 64):
        freqb[base:base + 8] = fb
        freqb[base + 8:base + 16] = fb
        signb[base:base + 8] = -1.0

    rows = []
    for nm in ("ffn1_norm", "mix_norm", "ffn2_norm", "final_norm"):
        rows.append(np.asarray(inp[nm][L], f32).reshape(8, 128))
    rows.append(np.asarray(inp["mla_q_norm"][L], f32).reshape(3, 128))
    rows.append(np.asarray(inp["mla_kv_norm"][L], f32).reshape(2, 128))
    qg = np.asarray(inp["mla_q_gain"][L], f32)
    kg = np.asarray(inp["mla_k_gain"][L], f32)
    dqg = np.asarray(inp["diff_q_gain"][L], f32)
    dkg = np.asarray(inp["diff_k_gain"][L], f32)
    singles = [pad128(qg), mla_perm(qg), pad128(kg), mla_perm(kg),
               diff_rep(dqg), diff_perm(dqg), diff_rep(dkg), diff_perm(dkg),
               pad128(np.asarray(inp["diff_lambda_q1"][L], f32)), pad128(np.asarray(inp["diff_lambda_k1"][L], f32)),
               pad128(np.asarray(inp["diff_lambda_q2"][L], f32)), pad128(np.asarray(inp["diff_lambda_k2"][L], f32)),
               np.asarray(inp["diff_subln"][L], f32), freqa, signa, freqb, signb]
    rows.append(np.stack(singles))
    pvb = np.ascontiguousarray(np.concatenate(rows, axis=0), dtype=f32)
    assert pvb.shape == (NPVB, 128), pvb.shape
    shared = {
        "pvb": pvb,
        "ident": np.eye(128, dtype=f32),
        "w_ada": np.ascontiguousarray(inp["w_ada"][L], f32),
        "f1g": np.ascontiguousarray(inp["ffn1_w_gate"][L], f32),
        "f1u": np.ascontiguousarray(inp["ffn1_w_up"][L], f32),
        "f1d": np.ascontiguousarray(inp["ffn1_w_down"][L], f32),
        "f2g": np.ascontiguousarray(inp["ffn2_w_gate"][L], f32),
        "f2u": np.ascontiguousarray(inp["ffn2_w_up"][L], f32),
        "f2d": np.ascontiguousarray(inp["ffn2_w_down"][L], f32),
        "w_in": np.ascontiguousarray(inp["w_in"][L], f32),
        "w_uq": np.ascontiguousarray(inp["mla_w_uq"][L], f32),
        "w_ukv": np.ascontiguousarray(inp["mla_w_ukv"][L], f32),
        "w_oa": np.ascontiguousarray(inp["mla_w_o"][L], f32),
        "w_ob": np.ascontiguousarray(inp["diff_w_o"][L], f32),
        "w_out": np.ascontiguousarray(inp["w_out"][L], f32),
    }
    b_ada = np.asarray(inp["b_ada"][L], f32).reshape(72, 128)
    maps = []
    for b in range(8):
        m = dict(shared)
        m["x"] = np.ascontiguousarray(inp["x"][b], f32)
        m["pos"] = np.ascontiguousarray(np.asarray(inp["positions"][b]).reshape(1, T).astype(np.int32))
        m["pva"] = np.ascontiguousarray(np.concatenate([np.asarray(inp["c"][b], f32).reshape(8, 128), b_ada], axis=0))
        maps.append(m)
    return maps


_NC_CACHE = {}


def kernel(**inputs):
    maps = _prep_inputs(inputs)
    if "nc" not in _NC_CACHE:
        _NC_CACHE["nc"] = build_nc()
    nc = _NC_CACHE["nc"]
    res = run_bass_kernel_spmd(nc, maps, core_ids=list(range(8)))
    out = np.stack([np.asarray(r["out"], np.float32) for r in res.results], axis=0)
    return out
```

```python
import math
import os as _os
import numpy as np
import concourse.bass as bass
import concourse.mybir as mybir
from concourse.bass_utils import run_bass_kernel_spmd

F32 = mybir.dt.float32
BF16 = mybir.dt.bfloat16
I32 = mybir.dt.int32
AF = mybir.ActivationFunctionType
ALU = mybir.AluOpType

T = 2048
D = 1024
DFF = 2816
NG = 4
GS = 512
KC = 8
EPS = 1e-6
IN_COLS = 4256
LAMBDA_INIT = 0.8 - 0.6 * math.exp(-0.3 * 0)


class Buf:
    __slots__ = ("name", "w", "r", "dsem", "dcnt")

    def __init__(self, name):
        self.name = name
        self.w = None
        self.r = {}
        self.dsem = None
        self.dcnt = 0


class Prog:
    ENG = ("pe", "act", "dve", "pool", "sp")

    def __init__(self, nc, same_engine_sync=True):
        self.nc = nc
        self.streams = {e: [] for e in self.ENG}
        self.sems = {}
        self.cnt = {}
        self.waited = {e: {} for e in self.ENG}
        self.same = same_engine_sync
        for e in self.ENG:
            self.sems["E_" + e] = nc.alloc_semaphore("sem_e_" + e)
            self.cnt[e] = 0
        self.nd = 0
        self.retired = {}

    def new_dsem(self, name):
        k = "D_%d_%s" % (self.nd, name)
        self.nd += 1
        self.sems[k] = self.nc.alloc_semaphore("sem_d%d" % self.nd)
        return k

    def buf(self, name):
        b = Buf(name)
        b.r = dict(self.retired)
        return b

    def bufs(self, name, n):
        return [self.buf("%s%d" % (name, i)) for i in range(n)]

    def retire(self, *bufs):
        for b in bufs:
            if b.w is not None:
                s, v = b.w
                self.retired[s] = max(self.retired.get(s, 0), v)
            for s, v in b.r.items():
                self.retired[s] = max(self.retired.get(s, 0), v)

    def _waits(self, eng, reads, writes):
        waits = {}
        own = "E_" + eng

        def need(s, v):
            if s == own and (eng == "pe" or not self.same):
                return
            if self.waited[eng].get(s, 0) >= v:
                return
            if waits.get(s, 0) < v:
                waits[s] = v

        for b in reads:
            if b.w is not None:
                need(*b.w)
        for b in writes:
            if b.w is not None:
                need(*b.w)
            for s, v in b.r.items():
                need(s, v)
        for s, v in waits.items():
            self.waited[eng][s] = v
        return list(waits.items())

    def _mark(self, tok, reads, writes):
        s, v = tok
        for b in reads:
            if b.r.get(s, 0) < v:
                b.r[s] = v
        for b in writes:
            b.w = tok
            b.r = {}

    def op(self, eng, fn, reads=(), writes=(), inc=True):
        waits = self._waits(eng, reads, writes)
        own = "E_" + eng
        if inc:
            self.cnt[eng] += 1
            tok = (own, self.cnt[eng])
        else:
            tok = (own, self.cnt[eng] + 1)
        self._mark(tok, reads, writes)
        self.streams[eng].append((waits, fn, (own, 1) if inc else None))

    def dma(self, q, out_ap, in_ap, reads=(), writes=(), owner=None):
        if owner is None:
            owner = writes[0] if writes else reads[0]
        if owner.dsem is None:
            owner.dsem = self.new_dsem(owner.name)
        waits = self._waits(q, reads, writes)
        owner.dcnt += 16
        tok = (owner.dsem, owner.dcnt)
        self._mark(tok, reads, writes)
        self.streams[q].append(
            (waits, lambda e: e.dma_start(out=out_ap, in_=in_ap), (owner.dsem, 16)))

    def wait_all(self, eng, bufs):
        waits = self._waits(eng, (), bufs)
        self.streams[eng].append((waits, None, None))

    def emit(self):
        nc = self.nc
        sems = self.sems
        streams = self.streams
        with nc.Block() as block:
            def run(e, name):
                for waits, fn, inc in streams[name]:
                    for s, v in waits:
                        e.wait_ge(sems[s], v)
                    if fn is None:
                        continue
                    ins = fn(e)
                    if inc is not None:
                        ins.then_inc(sems[inc[0]], inc[1])

            @block.tensor
            def _(e):
                run(e, "pe")

            @block.scalar
            def _(e):
                run(e, "act")

            @block.vector
            def _(e):
                run(e, "dve")

            @block.gpsimd
            def _(e):
                run(e, "pool")

            @block.sync
            def _(e):
                run(e, "sp")


class Arena:
    def __init__(self, nc, reserve=1024):
        rem = nc.sbuf_bytes_remaining
        self.nbytes = ((rem - reserve) // 64) * 64
        self.t = nc.alloc_sbuf_tensor("arena", [128, self.nbytes // 4], F32)

    def view(self, off, dtype, shape):
        assert off % 4 == 0
        esz = 2 if dtype == BF16 else 4
        n = 1
        for s in shape:
            n *= s
        nb = n * esz
        assert nb % 4 == 0 and off + nb <= self.nbytes, (off, nb, self.nbytes)
        a = self.t[:, off // 4:(off + nb) // 4]
        if dtype != F32:
            a = a.bitcast(dtype)
        if len(shape) == 2:
            a = a.rearrange("p (a b) -> p a b", a=shape[0])
        elif len(shape) == 3:
            a = a.rearrange("p (a b c) -> p a b c", a=shape[0], b=shape[1])
        return a


def MM(out, lhsT, rhs, start, stop):
    return lambda e: e.matmul(out, lhsT=lhsT, rhs=rhs, start=start, stop=stop)


def TR(out, in_, ident):
    return lambda e: e.transpose(out, in_, ident)


def ACT(out, in_, func, bias=None, scale=None):
    kw = {}
    if bias is not None:
        kw["bias"] = bias
    if scale is not None:
        kw["scale"] = scale
    return lambda e: e.activation(out=out, in_=in_, func=func, **kw)


def TT(out, in0, in1, op):
    return lambda e: e.tensor_tensor(out=out, in0=in0, in1=in1, op=op)


def TS(out, in0, s1, s2, op0, op1=None):
    if op1 is None:
        return lambda e: e.tensor_scalar(out=out, in0=in0, scalar1=s1, scalar2=None, op0=op0)
    return lambda e: e.tensor_scalar(out=out, in0=in0, scalar1=s1, scalar2=s2, op0=op0, op1=op1)


def STT(out, in0, scalar, in1, op0, op1):
    return lambda e: e.scalar_tensor_tensor(out=out, in0=in0, scalar=scalar, in1=in1, op0=op0, op1=op1)


def CP(out, in_):
    return lambda e: e.tensor_copy(out=out, in_=in_)


def RECIP(out, in_):
    return lambda e: e.reciprocal(out=out, in_=in_)


def MEMSET(ap, v):
    return lambda e: e.memset(ap, v)


PVB = {}
_n = 0
for _name, _cnt in [("g1", 8), ("g2", 8), ("g3", 8), ("gf", 8), ("qnorm", 3), ("kvnorm", 2),
                    ("qg", 1), ("qgp", 1), ("kg", 1), ("kgp", 1), ("dqg", 1), ("dqgp", 1),
                    ("dkg", 1), ("dkgp", 1), ("lq1", 1), ("lk1", 1), ("lq2", 1), ("lk2", 1),
                    ("subln", 1), ("freqa", 1), ("signa", 1), ("freqb", 1), ("signb", 1)]:
    PVB[_name] = _n
    _n += _cnt
NPVB = _n
NPVA = 80


def build_nc(stages=("ffn1", "mix", "ffn2"), debug=None):
    nc = bass.Bass("TRN2", target_bir_lowering=False)
    P = Prog(nc)
    A = Arena(nc)

    def dram_in(name, shape, dt=F32):
        return nc.dram_tensor(name, shape, dt, kind="ExternalInput").ap()

    x_d = dram_in("x", [T, D])
    pos_d = dram_in("pos", [1, T], I32)
    pva_d = dram_in("pva", [NPVA, 128])
    pvb_d = dram_in("pvb", [NPVB, 128])
    ident_d = dram_in("ident", [128, 128])
    wada_d = dram_in("w_ada", [D, 9 * D])
    f1g_d = dram_in("f1g", [D, DFF])
    f1u_d = dram_in("f1u", [D, DFF])
    f1d_d = dram_in("f1d", [DFF, D])
    f2g_d = dram_in("f2g", [D, DFF])
    f2u_d = dram_in("f2u", [D, DFF])
    f2d_d = dram_in("f2d", [DFF, D])
    win_d = dram_in("w_in", [D, IN_COLS])
    wuq_d = dram_in("w_uq", [384, 768])
    wukv_d = dram_in("w_ukv", [256, 1024])
    woa_d = dram_in("w_oa", [512, D])
    wob_d = dram_in("w_ob", [512, D])
    wout_d = dram_in("w_out", [D, D])
    out_d = nc.dram_tensor("out", [T, D], F32, kind="ExternalOutput").ap()
    dbg_d = None
    if debug is not None:
        dbg_d = nc.dram_tensor("dbg", [128] + list(debug[1]), F32, kind="ExternalOutput").ap()

    H_OFF = 0
    MISC = 65536
    NT_OFF = MISC + 14336
    R_OFF = NT_OFF + 32768
    R_SIZE = A.nbytes - R_OFF
    assert R_SIZE >= 98304, R_SIZE

    hT = A.view(H_OFF, F32, [KC, T])
    nT = A.view(NT_OFF, BF16, [KC, T])
    ident = A.view(MISC + 0, F32, [128])
    ones = A.view(MISC + 512, BF16, [128])
    ones2 = A.view(MISC + 768, BF16, [128])
    pvT = A.view(MISC + 1024, F32, [256])
    modT = A.view(MISC + 2048, F32, [72])
    der = A.view(MISC + 2560, F32, [128])
    sqs = [A.view(MISC + 3072 + i * 1024, BF16, [GS]) for i in range(2)]
    fsc = [A.view(MISC + 5120 + i * 2048, F32, [GS]) for i in range(2)]
    rsb = [A.view(MISC + 9216 + i * 2048, F32, [GS]) for i in range(2)]
    condb = A.view(MISC + 13312, BF16, [8])
    ps = nc.alloc_psum_tensor("ps", [128, 8, GS], F32)

    b_h = [[P.buf("h%d_%d" % (c, g)) for g in range(NG)] for c in range(KC)]
    b_n = [[P.buf("n%d_%d" % (c, g)) for g in range(NG)] for c in range(KC)]
    b_ps = P.bufs("ps", 8)
    b_const = P.buf("const")
    b_pvT = P.buf("pvT")
    b_mod = P.buf("mod")
    b_der = P.buf("der")
    b_sqs = P.bufs("sqs", 2)
    b_fsc = P.bufs("fsc", 2)
    b_rsb = P.bufs("rsb", 2)
    b_condb = P.buf("condb")

    def pvb(name, i=0):
        c = 80 + PVB[name] + i
        return pvT[:, c:c + 1]

    DER = {"A1": 0, "G1": 8, "A2": 16, "G2": 24, "A3": 32, "G3": 40, "SH1": 48, "SH2": 56, "SH3": 64,
           "nlam": 72, "subs": 73, "lamt": 74}

    def dcol(name, i=0):
        c = DER[name] + i
        return der[:, c:c + 1]

    P.dma("sp", ident[:], ident_d, writes=[b_const])
    P.op("pool", MEMSET(ones[:], 1.0), writes=[b_const])
    P.op("pool", MEMSET(ones2[:], 0.0), writes=[b_const])
    P.op("pool", MEMSET(ones2[0:64, 0:64], 1.0), writes=[b_const])
    P.op("pool", MEMSET(ones2[64:128, 64:128], 1.0), writes=[b_const])
    pstage = A.view(R_OFF + 32768, F32, [2, 128])
    b_pstage = P.buf("pstage")
    P.op("pool", MEMSET(pstage[:], 0.0), writes=[b_pstage])
    P.dma("sp", pstage[0:NPVA, 0, :], pva_d, writes=[b_pstage])
    P.dma("sp", pstage[0:NPVB, 1, :], pvb_d, writes=[b_pstage])
    P.op("pe", TR(ps[:, 7, 0:NPVA], pstage[0:NPVA, 0, :], ident[0:NPVA, 0:NPVA]),
         reads=[b_pstage, b_const], writes=[b_ps[7]], inc=False)
    P.op("pe", TR(ps[:, 7, 128:128 + NPVB], pstage[0:NPVB, 1, :], ident[0:NPVB, 0:NPVB]),
         reads=[b_pstage, b_const], writes=[b_ps[7]])
    P.op("dve", CP(pvT[:, 0:NPVA], ps[:, 7, 0:NPVA]), reads=[b_ps[7]], writes=[b_pvT])
    P.op("dve", CP(pvT[:, 80:80 + NPVB], ps[:, 7, 128:128 + NPVB]), reads=[b_ps[7]], writes=[b_pvT])
    P.retire(b_pstage)
    P.op("act", ACT(condb[:], pvT[:, 0:8], AF.Silu), reads=[b_pvT], writes=[b_condb])

    ring = {"wd_i": 0, "gu_i": 0, "b_wd": P.bufs("wdp", 2), "b_gu": P.bufs("gu", 2)}

    AT_OFF = R_OFF
    GU_OFF = R_OFF + 32768
    WD_OFF = R_OFF + 65536

    def wada_block(m, dst, b_dst):
        src = wada_d.rearrange("(kc p) n -> p kc n", p=128)[:, :, m * 1024:(m + 1) * 1024]
        for hh in range(2):
            P.dma("pool", dst[:, hh * 4:(hh + 1) * 4, :], src[:, hh * 4:(hh + 1) * 4, :], writes=[b_dst])

    def mod_block(m, wt, b_wt):
        for j in range(8):
            for kc in range(KC):
                P.op("pe", MM(ps[:, 6, m * 8 + j:m * 8 + j + 1], wt[:, kc, j * 128:(j + 1) * 128],
                              condb[:, kc:kc + 1], kc == 0, kc == KC - 1),
                     reads=[b_wt, b_condb], writes=[b_ps[6]], inc=(j == 7 and kc == KC - 1))
        P.op("dve", TT(modT[:, m * 8:(m + 1) * 8], ps[:, 6, m * 8:(m + 1) * 8],
                       pvT[:, 8 + m * 8:8 + (m + 1) * 8], ALU.add),
             reads=[b_ps[6], b_pvT], writes=[b_mod])

    def derive_A(dst, gname, m_sc, m_sh, shname):
        P.op("dve", STT(der[:, DER[dst]:DER[dst] + 8], modT[:, m_sc * 8:(m_sc + 1) * 8], 1.0,
                        pvT[:, 80 + PVB[gname]:80 + PVB[gname] + 8], ALU.add, ALU.mult),
             reads=[b_mod, b_pvT], writes=[b_der])
        P.op("dve", CP(der[:, DER[shname]:DER[shname] + 8], modT[:, m_sh * 8:(m_sh + 1) * 8]),
             reads=[b_mod], writes=[b_der])

    def derive_G(dst, m_gt, scale):
        P.op("dve", TS(der[:, DER[dst]:DER[dst] + 8], modT[:, m_gt * 8:(m_gt + 1) * 8], float(scale), None, ALU.mult),
             reads=[b_mod], writes=[b_der])

    def norm_group(g, Aname, SHname, sq_from_psum=None):
        sl = slice(g * GS, (g + 1) * GS)
        sb = 6 + (g % 2)
        for c in range(KC):
            i = c % 2
            P.op("act", ACT(sqs[i][:], hT[:, c, sl], AF.Square), reads=[b_h[c][g]], writes=[b_sqs[i]])
            P.op("pe", MM(ps[:, sb, :], ones[:], sqs[i][:], c == 0, c == KC - 1),
                 reads=[b_const, b_sqs[i]], writes=[b_ps[sb]], inc=True)
        r = g % 2
        P.op("act", ACT(rsb[r][:], ps[:, sb, :], AF.Sqrt, bias=epsc[:], scale=1.0 / D),
             reads=[b_ps[sb], b_const], writes=[b_rsb[r]])
        P.op("dve", RECIP(rsb[r][:], rsb[r][:]), reads=[b_rsb[r]], writes=[b_rsb[r]])
        if Aname is None:
            return r
        for c in range(KC):
            i = c % 2
            P.op("dve", TT(fsc[i][:], hT[:, c, sl], rsb[r][:], ALU.mult),
                 reads=[b_h[c][g], b_rsb[r]], writes=[b_fsc[i]])
            P.op("dve", TS(nT[:, c, sl], fsc[i][:], dcol(Aname, c), dcol(SHname, c), ALU.mult, ALU.add),
                 reads=[b_fsc[i], b_der], writes=[b_n[c][g]])
        return r

    epsc = A.view(MISC + 13312 + 64, F32, [1])
    P.op("pool", MEMSET(epsc[:], EPS), writes=[b_const])

    aT = A.view(AT_OFF, BF16, [8, T])
    gu = [A.view(GU_OFF + i * 16384, BF16, [2, KC, GS]) for i in range(2)]
    wdp = [A.view(WD_OFF + i * 16384, BF16, [8, D]) for i in range(2)]
    PASSES = [(0, 8), (8, 16), (16, 22)]

    def ffn(wg_d, wu_d, wd_d, Gname, st, pre_hooks):
        wg_v = wg_d.rearrange("(kc p) n -> p kc n", p=128)
        wu_v = wu_d.rearrange("(kc p) n -> p kc n", p=128)
        wd_v = wd_d.rearrange("(j p) n -> p j n", p=128)
        b_a = [[P.buf("a%d_%d" % (j, g)) for g in range(NG)] for j in range(8)]
        it = 0
        for pi, (c0, c1) in enumerate(PASSES):
            npc = c1 - c0
            wslot = st["wd_i"] % 2
            st["wd_i"] += 1
            b_wd = st["b_wd"][wslot]
            for hh in range(0, npc, 4):
                h1 = min(hh + 4, npc)
                P.dma("pool", wdp[wslot][:, hh:h1, :], wd_v[:, c0 + hh:c0 + h1, :], writes=[b_wd])
            groups = [(a, min(a + 4, c1)) for a in range(c0, c1, 4)]
            loaded = []
            for (a, b) in groups:
                gslot = st["gu_i"] % 2
                st["gu_i"] += 1
                b_g = st["b_gu"][gslot]
                ncol = (b - a) * 128
                P.dma("pool", gu[gslot][:, 0, :, 0:ncol], wg_v[:, :, a * 128:b * 128], writes=[b_g])
                P.dma("pool", gu[gslot][:, 1, :, 0:ncol], wu_v[:, :, a * 128:b * 128], writes=[b_g])
                loaded.append((a, b, gslot, b_g))
                for j in range(a, b):
                    jl = j - c0
                    jo = (j - a) * 128
                    for g in range(NG):
                        sl = slice(g * GS, (g + 1) * GS)
                        bg = it % 2
                        bu = 2 + it % 2
                        it += 1
                        for kc in range(KC):
                            P.op("pe", MM(ps[:, bg, :], gu[gslot][:, 0, kc, jo:jo + 128], nT[:, kc, sl], kc == 0, kc == KC - 1),
                                 reads=[b_g, b_n[kc][g]], writes=[b_ps[bg]], inc=(kc == KC - 1))
                        for kc in range(KC):
                            P.op("pe", MM(ps[:, bu, :], gu[gslot][:, 1, kc, jo:jo + 128], nT[:, kc, sl], kc == 0, kc == KC - 1),
                                 reads=[b_g, b_n[kc][g]], writes=[b_ps[bu]], inc=(kc == KC - 1))
                        i = it % 2
                        P.op("act", ACT(fsc[i][:], ps[:, bg, :], AF.Silu), reads=[b_ps[bg]], writes=[b_fsc[i]])
                        P.op("dve", TT(aT[:, jl, sl], fsc[i][:], ps[:, bu, :], ALU.mult),
                             reads=[b_fsc[i], b_ps[bu]], writes=[b_a[jl][g]])
                hk = pre_hooks.get((pi, len(loaded) - 1))
                if hk is not None:
                    hk()
            hk = pre_hooks.get((pi, "down"))
            if hk is not None:
                hk()
            dn = 0
            for g in range(NG):
                sl = slice(g * GS, (g + 1) * GS)
                for oc in range(KC):
                    bd = 4 + dn % 2
                    dn += 1
                    for jl in range(npc):
                        P.op("pe", MM(ps[:, bd, :], wdp[wslot][:, jl, oc * 128:(oc + 1) * 128], aT[:, jl, sl], jl == 0, jl == npc - 1),
                             reads=[b_wd, b_a[jl][g]], writes=[b_ps[bd]], inc=(jl == npc - 1))
                    P.op("dve", STT(hT[:, oc, sl], ps[:, bd, :], dcol(Gname, oc), hT[:, oc, sl], ALU.mult, ALU.add),
                         reads=[b_ps[bd], b_der, b_h[oc][g]], writes=[b_h[oc][g]])
        P.retire(*[b for row in b_a for b in row])


    xs = [A.view(AT_OFF + i * 4096, F32, [D]) for i in range(8)]
    b_xs = P.bufs("xs", 8)
    wa = [A.view(WD_OFF + i * 16384, BF16, [KC, D]) for i in range(2)]
    wada_block(0, wa[0], ring["b_wd"][0])
    wada_block(1, wa[1], ring["b_wd"][1])
    x_v = x_d.rearrange("(t p) d -> t p d", p=128)
    for g in range(NG):
        for j in range(4):
            tt = g * 4 + j
            s = tt % 8
            P.dma("sp", xs[s][:], x_v[tt], writes=[b_xs[s]])
        for c in range(KC):
            bk = c % 2
            for j in range(4):
                s = (g * 4 + j) % 8
                P.op("pe", TR(ps[:, bk, j * 128:(j + 1) * 128], xs[s][:, c * 128:(c + 1) * 128], ident[:]),
                     reads=[b_xs[s], b_const], writes=[b_ps[bk]], inc=(j == 3))
            P.op("act", ACT(hT[:, c, g * GS:(g + 1) * GS], ps[:, bk, :], AF.Copy), reads=[b_ps[bk]], writes=[b_h[c][g]])
        if g == 0:
            mod_block(0, wa[0], ring["b_wd"][0])
            mod_block(1, wa[1], ring["b_wd"][1])
            derive_A("A1", "g1", 1, 0, "SH1")
    P.retire(*b_xs)
    ring["wd_i"] = 0

    for g in range(NG):
        norm_group(g, "A1", "SH1")

    def make_mod_hook(ms):
        def hk():
            for m in ms:
                wslot = ring["wd_i"] % 2
                b_w_ = ring["b_wd"][wslot]
                wv = A.view(WD_OFF + wslot * 16384, BF16, [KC, D])
                wada_block(m, wv, b_w_)
                mod_block(m, wv, b_w_)
        return hk

    def hook_g1():
        derive_G("G1", 2, 0.5)

    if "ffn1" in stages:
        ffn(f1g_d, f1u_d, f1d_d, "G1", ring,
            {(0, 0): make_mod_hook([2]), (0, "down"): hook_g1,
             (0, 1): make_mod_hook([3]), (1, 0): make_mod_hook([4])})
    else:
        make_mod_hook([2, 3, 4])()
    derive_A("A2", "g2", 4, 3, "SH2")
    LATE_MOD = "mix" in stages
    if not LATE_MOD:
        make_mod_hook([5, 6, 7, 8])()
        derive_G("G2", 5, 1.0)
        derive_A("A3", "g3", 7, 6, "SH3")
        derive_G("G3", 8, 0.5)

    if "mix" in stages:
        for g in range(NG):
            norm_group(g, "A2", "SH2")
        hsp = nc.dram_tensor("hspill", [128, KC, T], F32).ap()
        b_hsp = P.buf("hsp")
        for c in range(KC):
            P.dma("sp", hsp[:, c, :], hT[:, c, :], reads=[b_h[c][g] for g in range(NG)], writes=[b_hsp])
        P.retire(*[b for row in b_h for b in row])
        P.retire(*ring["b_wd"])
        P.retire(*ring["b_gu"])
        M2 = R_OFF
        SC_A = 1.0 / math.sqrt(96.0)
        SC_B = 1.0 / math.sqrt(64.0)
        win_v = win_d.rearrange("(kc p) n -> p kc n", p=128)
        oaT = A.view(0, BF16, [4, T])
        obT = A.view(16384, BF16, [4, T])
        cosF = A.view(32768, F32, [T])
        sinF = A.view(40960, F32, [T])
        kpeR = A.view(49152, F32, [T])
        sqpe = A.view(57344, BF16, [T])
        b_oa = [[P.buf("oa") for g in range(NG)] for c in range(4)]
        b_ob = [[P.buf("ob") for g in range(NG)] for c in range(4)]
        b_tab = P.buf("tab")
        b_kpe = P.bufs("kpe", NG)
        wlat = A.view(M2 + 0, BF16, [KC, 640])
        wkr = A.view(M2 + 10240, BF16, [KC, 96])
        wkrp = A.view(M2 + 11776, BF16, [KC, 96])
        wuq = A.view(M2 + 13312, BF16, [3, 768])
        wuqp = A.view(M2 + 17920, BF16, [3, 8, 96])
        wukk = A.view(M2 + 22528, BF16, [2, 8, 64])
        wukv = A.view(M2 + 24576, BF16, [2, 512])
        wuqp_f = A.view(M2 + 17920, BF16, [3, 768])
        wukk_f = A.view(M2 + 22528, BF16, [2, 512])
        cqT = A.view(M2 + 26624, BF16, [3, T])
        ckvT = A.view(M2 + 38912, BF16, [2, T])
        vh = [A.view(M2 + 47104 + i * 4096, BF16, [16, 128]) for i in range(2)]
        qh = [A.view(M2 + 55296 + i * 4096, BF16, [T]) for i in range(2)]
        kh = [A.view(M2 + 63488 + i * 4096, BF16, [T]) for i in range(2)]
        PT = [A.view(M2 + 71680 + i * 1024, BF16, [GS]) for i in range(4)]
        oraw = [A.view(M2 + 75776 + i * 2048, F32, [GS]) for i in range(2)]
        dn = [A.view(M2 + 79872 + i * 2048, F32, [GS]) for i in range(2)]
        onesf = A.view(M2 + 83968, F32, [128])
        rin = A.view(M2 + 84480, F32, [GS])
        posi = A.view(M2 + 26624, I32, [T])
        tmpf = A.view(M2 + 34816, F32, [T])
        kfi = A.view(M2 + 43008, I32, [T])
        kff = A.view(M2 + 43008, F32, [T])
        b_tt = P.buf("tabtmp")
        b_w = {k: P.buf("w_" + k) for k in ("lat", "kr", "uq", "uqp", "ukk", "ukv")}
        b_PT = P.bufs("PT", 4)
        b_oraw = P.bufs("oraw", 2)
        b_dn = P.bufs("dn", 2)
        b_rin = P.buf("rin")
        sq_e = A.view(M2 + 86528, BF16, [GS])
        od_e = A.view(M2 + 87552, F32, [GS])
        rs_e = A.view(M2 + 89600, F32, [GS])
        b_sqe, b_ode, b_rse = P.buf("sqe"), P.buf("ode"), P.buf("rse")
        b_onesf = P.buf("onesf")
        P.op("pool", MEMSET(onesf[:], 1.0), writes=[b_onesf])
        self_ = [A.view(M2 + 91648 + i * 512, F32, [128]) for i in range(2)]
        rq_t = [A.view(M2 + 92672 + i * 128, F32, [32]) for i in range(2)]
        rk_t = [A.view(M2 + 92928 + i * 128, F32, [32]) for i in range(2)]
        t4 = [A.view(M2 + 93184 + i * 32, F32, [8]) for i in range(2)]
        dg = [A.view(M2 + 93248 + i * 512, F32, [128]) for i in range(8)]
        sel2b = A.view(M2 + 97344, BF16, [2])
        b_rq, b_rk = P.bufs("rq", 2), P.bufs("rk", 2)
        b_t4 = P.bufs("t4", 2)
        b_dg = P.bufs("dg", 8)
        P.op("pool", MEMSET(self_[0][:], 0.0), writes=[b_onesf])
        P.op("pool", MEMSET(self_[1][:], 0.0), writes=[b_onesf])
        P.op("pool", MEMSET(self_[0][:, 0:64], 1.0), writes=[b_onesf])
        P.op("pool", MEMSET(self_[1][:, 64:128], 1.0), writes=[b_onesf])
        P.op("pool", MEMSET(sel2b[:], 0.0), writes=[b_onesf])
        P.op("pool", MEMSET(sel2b[0:64, 0:1], 1.0), writes=[b_onesf])
        P.op("pool", MEMSET(sel2b[64:128, 1:2], 1.0), writes=[b_onesf])
        cmh = A.view(MISC + 13312 + 72, F32, [1])
        P.op("pool", MEMSET(cmh[:], -0.5), writes=[b_onesf])

        def POWH(out, in0, ncol):
            return lambda e: e.tensor_tensor(out=out, in0=in0, in1=cmh[:, 0:1].to_broadcast([128, ncol]), op=ALU.pow)

        TWO_PI = 2.0 * math.pi
        C1 = 6.28125
        C2 = TWO_PI - C1

        def build_tables(fname, sname):
            P.dma("sp", posi[:], pos_d.partition_broadcast(128), writes=[b_tt])
            P.op("dve", CP(tmpf[:], posi[:]), reads=[b_tt], writes=[b_tt])
            P.op("dve", TS(tmpf[:], tmpf[:], pvb(fname), None, ALU.mult), reads=[b_tt, b_pvT], writes=[b_tt])
            P.op("dve", TS(cosF[:], tmpf[:], 1.0 / TWO_PI, None, ALU.mult), reads=[b_tt], writes=[b_tab])
            P.op("dve", CP(kfi[:], cosF[:]), reads=[b_tab], writes=[b_tt])
            P.op("dve", CP(cosF[:], kfi[:]), reads=[b_tt], writes=[b_tab])
            P.op("dve", STT(tmpf[:], cosF[:], -C1, tmpf[:], ALU.mult, ALU.add), reads=[b_tab, b_tt], writes=[b_tt])
            P.op("dve", STT(tmpf[:], cosF[:], -C2, tmpf[:], ALU.mult, ALU.add), reads=[b_tab, b_tt], writes=[b_tt])
            P.op("dve", TS(tmpf[:], tmpf[:], math.pi, -math.pi, ALU.min, ALU.max), reads=[b_tt], writes=[b_tt])
            P.op("act", ACT(sinF[:], tmpf[:], AF.Sin, scale=pvb(sname)), reads=[b_tt, b_pvT], writes=[b_tab])
            P.op("dve", TS(tmpf[:], tmpf[:], math.pi / 2, None, ALU.add), reads=[b_tt], writes=[b_tt])
            P.op("dve", TS(kff[:], tmpf[:], math.pi, -TWO_PI, ALU.is_gt, ALU.mult), reads=[b_tt], writes=[b_tt])
            P.op("dve", TT(tmpf[:], tmpf[:], kff[:], ALU.add), reads=[b_tt], writes=[b_tt])
            P.op("dve", TS(tmpf[:], tmpf[:], math.pi, -math.pi, ALU.min, ALU.max), reads=[b_tt], writes=[b_tt])
            P.op("act", ACT(cosF[:], tmpf[:], AF.Sin), reads=[b_tt], writes=[b_tab])

        def rstd_from(bank, rows, n, r):
            if _os.environ.get("MK_LN") != "1":
                P.op("act", ACT(rsb[r][rows, :], ps[rows, bank, :], AF.Sqrt, bias=epsc[rows, :], scale=1.0 / n),
                     reads=[b_ps[bank], b_const], writes=[b_rsb[r]])
                P.op("dve", RECIP(rsb[r][rows, :], rsb[r][rows, :]), reads=[b_rsb[r]], writes=[b_rsb[r]])
                return
            P.op("act", ACT(rsb[r][rows, :], ps[rows, bank, :], AF.Ln, bias=epsc[rows, :], scale=1.0 / n),
                 reads=[b_ps[bank], b_const], writes=[b_rsb[r]])
            P.op("act", ACT(rsb[r][rows, :], rsb[r][rows, :], AF.Exp, scale=-0.5), reads=[b_rsb[r]], writes=[b_rsb[r]])

        def rope_parts(nrows, gcol, gpcol, sl, out_ap, b_out, after=()):
            rows = slice(0, nrows)
            P.op("dve", STT(fsc[1][rows, :], ps[rows, 5, :], gpcol[rows, :], sinF[rows, sl], ALU.mult, ALU.mult),
                 reads=[b_ps[5], b_pvT, b_tab], writes=[b_fsc[1]])
            P.op("dve", STT(fsc[0][rows, :], ps[rows, 4, :], gcol[rows, :], cosF[rows, sl], ALU.mult, ALU.mult),
                 reads=[b_ps[4], b_pvT, b_tab] + list(after), writes=[b_fsc[0]])
            P.op("dve", TT(out_ap, fsc[0][rows, :], fsc[1][rows, :], ALU.add), reads=[b_fsc[0], b_fsc[1]], writes=[b_out])

        def tiny_rstd(i, which, g, ncomp, n, mult, stat_fn):
            base = 0 if which == "q" else 32
            dst_t, b_dst_t = (rq_t[i], b_rq[i]) if which == "q" else (rk_t[i], b_rk[i])
            c0 = g * 4 * ncomp
            if _os.environ.get("MK_SKIP") == which:
                P.op("pool", MEMSET(dst_t[:, c0:c0 + 4 * ncomp], 0.1), writes=[b_dst_t])
                return
            for j in range(4):
                stat_fn(j, ps[:, 6, base + c0 + j * ncomp:base + c0 + (j + 1) * ncomp])
            ti = 0 if which == "q" else 1
            P.op("dve", TS(t4[ti][:, 0:4 * ncomp], ps[:, 6, base + c0:base + c0 + 4 * ncomp], mult / n, EPS * mult, ALU.mult, ALU.add),
                 reads=[b_ps[6]], writes=[b_t4[ti]])
            P.op("pool", POWH(dst_t[:, c0:c0 + 4 * ncomp], t4[ti][:, 0:4 * ncomp], 4 * ncomp), reads=[b_t4[ti], b_onesf], writes=[b_dst_t])

        def qprep_gen(i, nrows, ncomp, n, gcol, gpcol, dst, b_dst, sl, g):
            rows = slice(0, nrows)
            P.op("act", ACT(sqs[0][rows, :], ps[rows, 4, :], AF.Square), reads=[b_ps[4]], writes=[b_sqs[0]])
            rope_parts(nrows, gcol, gpcol, sl, fsc[0][rows, :], b_fsc[0], after=[b_sqs[0]])
            yield

            def stat(j, out_ap):
                rhs = ones[rows, 0:1] if ncomp == 1 else sel2b[:, 0:2]
                P.op("pe", MM(out_ap, sqs[0][rows, j * 128:(j + 1) * 128], rhs, True, True),
                     reads=[b_sqs[0], b_const, b_onesf], writes=[b_ps[6]])
            tiny_rstd(i, "q", g, ncomp, n, 1.0, stat)
            yield
            c0 = g * 4 * ncomp
            for j in range(4):
                for c in range(ncomp):
                    d = j * ncomp + c
                    col = c0 + j * ncomp + c
                    P.op("dve", TS(dg[d][:], ident[:], rq_t[i][:, col:col + 1], None, ALU.mult),
                         reads=[b_const, b_rq[i]], writes=[b_dg[d]])
            for j in range(4):
                for c in range(ncomp):
                    d = j * ncomp + c
                    lhsT = onesf[:, 0:128] if ncomp == 1 else self_[c][:, 0:128]
                    P.op("pe", MM(ps[0:128, 4, j * 128:(j + 1) * 128], lhsT, dg[d][:], c == 0, c == ncomp - 1),
                         reads=[b_onesf, b_dg[d]], writes=[b_ps[4]], inc=(j == 3 and c == ncomp - 1))
            P.op("dve", TT(dst, fsc[0][rows, :], ps[rows, 4, :], ALU.mult), reads=[b_fsc[0], b_ps[4]], writes=[b_dst])
            yield

        ring_i = [0]
        pend = []

        def step_pend():
            for gen_ in list(pend):
                try:
                    next(gen_)
                except StopIteration:
                    pend.remove(gen_)

        def attention(kT, b_k, qT, b_q, rows, scale, vts, qc, filler, fill_at, sbanks=(0, 1)):
            qsl = slice(qc * GS, (qc + 1) * GS)

            def S(kt):
                it = ring_i[0]
                ring_i[0] += 1
                sb = sbanks[it % len(sbanks)]
                pr = it % 4
                P.op("pe", MM(ps[:, sb, :], kT[rows, kt * 128:(kt + 1) * 128], qT[rows, qsl], True, True),
                     reads=[b_k, b_q], writes=[b_ps[sb]])
                sc_ap, b_sc = scale(kt)
                if _os.environ.get("MK_CONSTSC") == "1":
                    sc_ap = 0.1
                P.op("act", ACT(PT[pr][:], ps[:, sb, :], AF.Exp, scale=sc_ap), reads=[b_ps[sb], b_sc], writes=[b_PT[pr]])
                return pr

            depth = len(sbanks) - 1
            prs = {}
            for k0 in range(depth):
                prs[k0] = S(k0)
            for kt in range(16):
                if kt + depth < 16:
                    prs[kt + depth] = S(kt + depth)
                pr = prs[kt]
                for vi, (vfn, M, bank, b_v) in enumerate(vts):
                    P.op("pe", MM(ps[0:M, bank, :], vfn(kt), PT[pr][:], kt == 0, kt == 15),
                         reads=[b_v, b_PT[pr]], writes=[b_ps[bank]], inc=(kt == 15 and vi == len(vts) - 1))
                if kt in fill_at:
                    filler()
                if kt in (4, 8, 12):
                    step_pend()

        def recip_bcast(src, drow):
            P.op("dve", RECIP(rin[drow:drow + 1, :], src[drow:drow + 1, :]), reads=[b_oraw[0], b_oraw[1]], writes=[b_rin])
            P.op("pe", MM(ps[:, 7, :], onesf[drow:drow + 1, :], rin[drow:drow + 1, :], True, True),
                 reads=[b_onesf, b_rin], writes=[b_ps[7]])

        NOFILL = _os.environ.get("MK_NOFILL") == "1"

        def make_filler(gen):
            def f():
                if gen is not None and not NOFILL:
                    next(gen, None)
            return f

        def drain(gen):
            if gen is not None:
                for _ in gen:
                    pass

        for hh in range(2):
            P.dma("pool", wlat[:, hh * 4:(hh + 1) * 4, :], win_v[:, hh * 4:(hh + 1) * 4, 0:640], writes=[b_w["lat"]])
        P.op("pool", MEMSET(wkr[:], 0.0), writes=[b_w["kr"]])
        P.op("pool", MEMSET(wkrp[:], 0.0), writes=[b_w["kr"]])
        P.dma("pool", wkr[:, :, 64:96], win_v[:, :, 640:672], writes=[b_w["kr"]])
        P.dma("pool", wkrp[:, :, 64:80], win_v[:, :, 656:672], writes=[b_w["kr"]])
        P.dma("pool", wkrp[:, :, 80:96], win_v[:, :, 640:656], writes=[b_w["kr"]])
        P.dma("pool", wuq[:], wuq_d.rearrange("(kc p) n -> p kc n", p=128), writes=[b_w["uq"]])
        P.op("pool", MEMSET(wuqp[:], 0.0), writes=[b_w["uqp"]])
        wuq_v4 = wuq_d.rearrange("(kc p) (h d) -> p kc h d", p=128, d=96)
        for kc in range(3):
            P.dma("pool", wuqp[:, kc, :, 64:80], wuq_v4[:, kc, :, 80:96], writes=[b_w["uqp"]])
            P.dma("pool", wuqp[:, kc, :, 80:96], wuq_v4[:, kc, :, 64:80], writes=[b_w["uqp"]])
        wukv_v4 = wukv_d.rearrange("(kc p) (h d) -> p kc h d", p=128, d=128)
        wukv_s = wukv.rearrange("p kc (h d) -> p kc h d", d=64)
        for kc in range(2):
            P.dma("pool", wukk[:, kc, :, :], wukv_v4[:, kc, :, 0:64], writes=[b_w["ukk"]])
            P.dma("pool", wukv_s[:, kc, :, :], wukv_v4[:, kc, :, 64:128], writes=[b_w["ukv"]])

        build_tables("freqa", "signa")
        P.retire(b_tt)
        b_cq = [[P.buf("cq") for g in range(NG)] for c in range(3)]
        b_ckv = [[P.buf("ckv") for g in range(NG)] for c in range(2)]
        b_vh = P.bufs("vh", 2)
        b_qh = P.bufs("qh", 2)
        b_kh = P.bufs("kh", 2)

        zst = [oraw[0], oraw[1], dn[0]]
        b_zst = [b_oraw[0], b_oraw[1], b_dn[0]]
        for g in range(NG):
            sl = slice(g * GS, (g + 1) * GS)
            for (lc0, nl, dstT, b_dst, nname, nfeat, sbank, r) in ((0, 3, cqT, b_cq, "qnorm", 384.0, 6, 0), (3, 2, ckvT, b_ckv, "kvnorm", 256.0, 7, 1)):
                for l in range(nl):
                    bk = 4 + l % 2
                    for kc in range(KC):
                        P.op("pe", MM(ps[:, bk, :], wlat[:, kc, (lc0 + l) * 128:(lc0 + l + 1) * 128], nT[:, kc, sl], kc == 0, kc == KC - 1),
                             reads=[b_w["lat"], b_n[kc][g]], writes=[b_ps[bk]], inc=(kc == KC - 1))
                    P.op("act", ACT(zst[l][:], ps[:, bk, :], AF.Copy), reads=[b_ps[bk]], writes=[b_zst[l]])
                    P.op("act", ACT(sqs[l % 2][:], ps[:, bk, :], AF.Square), reads=[b_ps[bk]], writes=[b_sqs[l % 2]])
                    P.op("pe", MM(ps[:, sbank, :], ones[:], sqs[l % 2][:], l == 0, l == nl - 1),
                         reads=[b_const, b_sqs[l % 2]], writes=[b_ps[sbank]])
                rstd_from(sbank, slice(0, 128), nfeat, r)
                for l in range(nl):
                    P.op("dve", STT(dstT[:, l, sl], zst[l][:], pvb(nname, l), rsb[r][:], ALU.mult, ALU.mult),
                         reads=[b_zst[l], b_pvT, b_rsb[r]], writes=[b_dst[l][g]])
            for (wt, bk) in ((wkr, 4), (wkrp, 5)):
                for kc in range(KC):
                    P.op("pe", MM(ps[0:96, bk, :], wt[:, kc, :], nT[:, kc, sl], kc == 0, kc == KC - 1),
                         reads=[b_w["kr"], b_n[kc][g]], writes=[b_ps[bk]], inc=(kc == KC - 1))
            R96 = slice(0, 96)
            P.op("act", ACT(sqpe[R96, sl], ps[R96, 4, :], AF.Square), reads=[b_ps[4]], writes=[b_kpe[g]])
            P.op("dve", STT(fsc[0][R96, :], ps[R96, 4, :], pvb("kg")[R96, :], cosF[R96, sl], ALU.mult, ALU.mult),
                 reads=[b_ps[4], b_pvT, b_tab, b_kpe[g]], writes=[b_fsc[0]])
            P.op("dve", STT(fsc[1][R96, :], ps[R96, 5, :], pvb("kgp")[R96, :], sinF[R96, sl], ALU.mult, ALU.mult),
                 reads=[b_ps[5], b_pvT, b_tab], writes=[b_fsc[1]])
            P.op("dve", TT(kpeR[R96, sl], fsc[0][R96, :], fsc[1][R96, :], ALU.add), reads=[b_fsc[0], b_fsc[1]], writes=[b_kpe[g]])

        R96 = slice(0, 96)

        def mla_prep(h):
            i = h % 2
            kindB = (h % 2 == 1)
            for g in range(NG):
                sl = slice(g * GS, (g + 1) * GS)
                Mq = 128 if h < 7 else 96
                Mk = 128 if h < 7 else 64
                for kc in range(3):
                    P.op("pe", MM(ps[0:Mq, 4, :], wuq[:, kc, h * 96:h * 96 + Mq], cqT[:, kc, sl], kc == 0, kc == 2),
                         reads=[b_w["uq"], b_cq[kc][g]], writes=[b_ps[4]], inc=(kc == 2))
                for kc in range(3):
                    P.op("pe", MM(ps[0:Mq, 5, :], wuqp_f[:, kc, h * 96:h * 96 + Mq], cqT[:, kc, sl], kc == 0, kc == 2),
                         reads=[b_w["uqp"], b_cq[kc][g]], writes=[b_ps[5]], inc=(kc == 2))
                yield from qprep_gen(i, 96, 1, 96.0, pvb("qg"), pvb("qgp"), qh[i][R96, sl], b_qh[i], sl, g)
                for kc in range(2):
                    P.op("pe", MM(ps[0:Mk, 5, :], wukk_f[:, kc, h * 64:h * 64 + Mk], ckvT[:, kc, sl], kc == 0, kc == 1),
                         reads=[b_w["ukk"], b_ckv[kc][g]], writes=[b_ps[5]], inc=(kc == 1))
                P.op("act", ACT(sqs[1][0:64, :], ps[0:64, 5, :], AF.Square), reads=[b_ps[5]], writes=[b_sqs[1]])
                P.op("dve", TS(kh[i][0:64, sl], ps[0:64, 5, :], pvb("kg")[0:64, :], None, ALU.mult),
                     reads=[b_ps[5], b_pvT, b_sqs[1]], writes=[b_kh[i]])
                P.op("dve", CP(kh[i][64:96, sl], kpeR[64:96, sl]), reads=[b_kpe[g]], writes=[b_kh[i]])
                yield

                def kstat(j, out_ap, g=g):
                    P.op("pe", MM(out_ap, sqs[1][0:64, j * 128:(j + 1) * 128], ones[0:64, 0:1], True, False),
                         reads=[b_sqs[1], b_const], writes=[b_ps[6]], inc=False)
                    P.op("pe", MM(out_ap, sqpe[64:96, g * GS + j * 128:g * GS + (j + 1) * 128], ones[64:96, 0:1], False, True),
                         reads=[b_kpe[g], b_const], writes=[b_ps[6]])
                tiny_rstd(i, "k", g, 1, 96.0, 96.0, kstat)
                yield
            P.op("pool", MEMSET(vh[i][:], 0.0), writes=[b_vh[i]])
            onescol = 32 if kindB else 64
            c0 = 64 if kindB else 0
            P.op("pool", MEMSET(vh[i][:, :, onescol:onescol + 1], 1.0), writes=[b_vh[i]])
            for t0 in range(0, 16, 8):
                for tt in range(t0, t0 + 8):
                    for kc in range(2):
                        P.op("pe", MM(ps[:, 5, (tt - t0) * 64:(tt - t0 + 1) * 64], ckvT[:, kc, tt * 128:(tt + 1) * 128],
                                      wukv[:, kc, h * 64:(h + 1) * 64], kc == 0, kc == 1),
                             reads=[b_ckv[kc][tt // 4], b_w["ukv"]], writes=[b_ps[5]], inc=(tt == t0 + 7 and kc == 1))
                P.op("dve", CP(vh[i][:, t0:t0 + 8, c0:c0 + 64], ps[:, 5, :].rearrange("p (a b) -> p a b", b=64)),
                     reads=[b_ps[5]], writes=[b_vh[i]])
                yield

        def mla_attn(h, filler):
            i = h % 2
            kindB = (h % 2 == 1)
            if kindB:
                vt = (lambda kt, i=i: vh[i][:, kt, 0:128], 128, 3, b_vh[i])
                drow, ob = 32, 1
                drows = slice(64, 128)
            else:
                vt = (lambda kt, i=i: vh[i][:, kt, 0:128], 128, 2, b_vh[i])
                drow, ob = 64, 0
                drows = slice(0, 64)
            for qc in range(NG):
                qsl = slice(qc * GS, (qc + 1) * GS)
                attention(kh[i], b_kh[i], qh[i], b_qh[i], R96, (lambda kt, i=i: (rk_t[i][:, kt:kt + 1], b_rk[i])), [vt], qc, filler, (1, 3, 5, 7, 9, 11, 13), sbanks=((0, 1, 2) if kindB else (0, 1, 3)))
                if kindB:
                    P.op("dve", CP(oraw[ob][32:33, :], ps[32:33, 3, :]), reads=[b_ps[3]], writes=[b_oraw[ob]])
                    P.op("dve", CP(oraw[ob][64:128, :], ps[64:128, 3, :]), reads=[b_ps[3]], writes=[b_oraw[ob]])
                else:
                    P.op("dve", CP(oraw[ob][0:65, :], ps[0:65, 2, :]), reads=[b_ps[2]], writes=[b_oraw[ob]])
                P.op("dve", RECIP(rin[drow:drow + 1, :], oraw[ob][drow:drow + 1, :]), reads=[b_oraw[ob]], writes=[b_rin])

                def epi(ob=ob, drow=drow, drows=drows, qsl=qsl, qc=qc):
                    P.op("pe", MM(ps[:, 7, :], onesf[drow:drow + 1, :], rin[drow:drow + 1, :], True, True),
                         reads=[b_onesf, b_rin], writes=[b_ps[7]])
                    P.op("dve", TT(oaT[drows, h // 2, qsl], oraw[ob][drows, :], ps[drows, 7, :], ALU.mult),
                         reads=[b_oraw[ob], b_ps[7]], writes=[b_oa[h // 2][qc]])
                    yield
                pend.append(epi())

        wdh = [A.view(M2 + i * 10240, BF16, [KC, 640]) for i in range(2)]
        vb = [A.view(M2 + 26624 + i * 6400, BF16, [16, 200]) for i in range(2)]
        QB0, KB0, VB0 = 672, 1184, 1696
        dstate = {}

        def diff_setup():
            P.retire(*[b for row in b_cq for b in row], *[b for row in b_ckv for b in row], *b_vh, *b_w.values(), *b_kpe)
            nonlocal_b_tt = P.buf("tabtmp2")
            dstate["b_tt"] = nonlocal_b_tt
            yield

        def build_tables2(fname, sname, bt):
            P.dma("sp", posi[:], pos_d.partition_broadcast(128), writes=[bt])
            P.op("dve", CP(tmpf[:], posi[:]), reads=[bt], writes=[bt])
            P.op("dve", TS(tmpf[:], tmpf[:], pvb(fname), None, ALU.mult), reads=[bt, b_pvT], writes=[bt])
            yield
            P.op("dve", TS(cosF[:], tmpf[:], 1.0 / TWO_PI, None, ALU.mult), reads=[bt], writes=[b_tab])
            P.op("dve", CP(kfi[:], cosF[:]), reads=[b_tab], writes=[bt])
            P.op("dve", CP(cosF[:], kfi[:]), reads=[bt], writes=[b_tab])
            yield
            P.op("dve", STT(tmpf[:], cosF[:], -C1, tmpf[:], ALU.mult, ALU.add), reads=[b_tab, bt], writes=[bt])
            P.op("dve", STT(tmpf[:], cosF[:], -C2, tmpf[:], ALU.mult, ALU.add), reads=[b_tab, bt], writes=[bt])
            P.op("dve", TS(tmpf[:], tmpf[:], math.pi, -math.pi, ALU.min, ALU.max), reads=[bt], writes=[bt])
            P.op("act", ACT(sinF[:], tmpf[:], AF.Sin, scale=pvb(sname)), reads=[bt, b_pvT], writes=[b_tab])
            yield
            P.op("dve", TS(tmpf[:], tmpf[:], math.pi / 2, None, ALU.add), reads=[bt], writes=[bt])
            P.op("dve", TS(kff[:], tmpf[:], math.pi, -TWO_PI, ALU.is_gt, ALU.mult), reads=[bt], writes=[bt])
            P.op("dve", TT(tmpf[:], tmpf[:], kff[:], ALU.add), reads=[bt], writes=[bt])
            P.op("dve", TS(tmpf[:], tmpf[:], math.pi, -math.pi, ALU.min, ALU.max), reads=[bt], writes=[bt])
            P.op("act", ACT(cosF[:], tmpf[:], AF.Sin), reads=[bt], writes=[b_tab])
            yield

        def diff_prep(h):
            i = h % 2
            if h == 0:
                P.retire(*[b for row in b_cq for b in row], *[b for row in b_ckv for b in row], *b_vh, *b_w.values(), *b_kpe)
                bt = P.buf("tabtmp2")
                yield from build_tables2("freqb", "signb", bt)
                P.retire(bt)
                dstate["b_wdh"] = P.bufs("wdh", 2)
                dstate["b_vb"] = P.bufs("vb", 2)
                P.op("dve", TT(der[:, 74:75], pvb("lq1"), pvb("lk1"), ALU.mult), reads=[b_pvT], writes=[b_der])
                P.op("dve", TT(der[:, 75:76], pvb("lq2"), pvb("lk2"), ALU.mult), reads=[b_pvT], writes=[b_der])
                yield
                P.op("pe", MM(ps[:, 6, 0:2], onesf[:], der[:, 74:76], True, True), reads=[b_onesf, b_der], writes=[b_ps[6]])
                P.op("act", ACT(der[:, 74:76], ps[:, 6, 0:2], AF.Exp), reads=[b_ps[6]], writes=[b_der])
                P.op("dve", TT(der[:, 72:73], der[:, 75:76], der[:, 74:75], ALU.subtract), reads=[b_der], writes=[b_der])
                P.op("dve", TS(der[:, 72:73], der[:, 72:73], -LAMBDA_INIT, None, ALU.add), reads=[b_der], writes=[b_der])
                P.op("dve", TS(der[:, 73:74], pvb("subln"), 1.0 - LAMBDA_INIT, None, ALU.mult), reads=[b_pvT], writes=[b_der])
            b_wdh, b_vb = dstate["b_wdh"], dstate["b_vb"]

            def diff_load(hh):
                ii = hh % 2
                P.op("pool", MEMSET(wdh[ii][:, :, 128:256], 0.0), writes=[b_wdh[ii]])
                P.op("pool", MEMSET(wdh[ii][:, :, 384:512], 0.0), writes=[b_wdh[ii]])
                for (src0, d0) in ((QB0, 0), (KB0, 256)):
                    s0 = src0 + hh * 128
                    P.dma("pool", wdh[ii][:, :, d0:d0 + 128], win_v[:, :, s0:s0 + 128], writes=[b_wdh[ii]])
                    for base in (0, 64):
                        P.op("pool", CP(wdh[ii][:, :, d0 + 128 + base:d0 + 128 + base + 8], wdh[ii][:, :, d0 + base + 8:d0 + base + 16]),
                             reads=[b_wdh[ii]], writes=[b_wdh[ii]])
                        P.op("pool", CP(wdh[ii][:, :, d0 + 128 + base + 8:d0 + 128 + base + 16], wdh[ii][:, :, d0 + base:d0 + base + 8]),
                             reads=[b_wdh[ii]], writes=[b_wdh[ii]])
                P.dma("pool", wdh[ii][:, :, 512:640], win_v[:, :, VB0 + hh * 128:VB0 + (hh + 1) * 128], writes=[b_wdh[ii]])

            if h == 0:
                diff_load(0)
                diff_load(1)
            elif h + 1 < 4:
                diff_load(h + 1)
            P.op("pool", MEMSET(vb[i][:], 0.0), writes=[b_vb[i]])
            P.op("pool", MEMSET(vb[i][:, :, 64:65], 1.0), writes=[b_vb[i]])
            P.op("pool", MEMSET(vb[i][:, :, 72 + 32:72 + 33], 1.0), writes=[b_vb[i]])
            yield
            for t0 in range(0, 16, 4):
                for tt in range(t0, t0 + 4):
                    for kc in range(KC):
                        P.op("pe", MM(ps[:, 5, (tt - t0) * 128:(tt - t0 + 1) * 128], nT[:, kc, tt * 128:(tt + 1) * 128],
                                      wdh[i][:, kc, 512:640], kc == 0, kc == KC - 1),
                             reads=[b_n[kc][tt // 4], b_wdh[i]], writes=[b_ps[5]], inc=(tt == t0 + 3 and kc == KC - 1))
                pv = ps[:, 5, :].rearrange("p (a b) -> p a b", b=128)
                P.op("dve", CP(vb[i][:, t0:t0 + 4, 0:64], pv[:, :, 0:64]), reads=[b_ps[5]], writes=[b_vb[i]])
                P.op("dve", CP(vb[i][:, t0:t0 + 4, 136:200], pv[:, :, 64:128]), reads=[b_ps[5]], writes=[b_vb[i]])
                yield
            for g in range(NG):
                sl = slice(g * GS, (g + 1) * GS)
                for (d0, gname, gpname, dst, b_dst) in ((0, "dqg", "dqgp", qh[i], b_qh[i]), (256, "dkg", "dkgp", kh[i], b_kh[i])):
                    for kc in range(KC):
                        P.op("pe", MM(ps[:, 4, :], wdh[i][:, kc, d0:d0 + 128], nT[:, kc, sl], kc == 0, kc == KC - 1),
                             reads=[b_wdh[i], b_n[kc][g]], writes=[b_ps[4]], inc=(kc == KC - 1))
                    for kc in range(KC):
                        P.op("pe", MM(ps[:, 5, :], wdh[i][:, kc, d0 + 128:d0 + 256], nT[:, kc, sl], kc == 0, kc == KC - 1),
                             reads=[b_wdh[i], b_n[kc][g]], writes=[b_ps[5]], inc=(kc == KC - 1))
                    if d0 == 0:
                        yield from qprep_gen(i, 128, 2, 64.0, pvb(gname), pvb(gpname), dst[:, sl], b_dst, sl, g)
                    else:
                        P.op("act", ACT(sqs[1][:], ps[:, 4, :], AF.Square), reads=[b_ps[4]], writes=[b_sqs[1]])
                        rope_parts(128, pvb(gname), pvb(gpname), sl, dst[:, sl], b_dst, after=[b_sqs[1]])
                        yield

                        def kstat2(j, out_ap):
                            P.op("pe", MM(out_ap, sqs[1][:, j * 128:(j + 1) * 128], sel2b[:, 0:2], True, True),
                                 reads=[b_sqs[1], b_onesf], writes=[b_ps[6]])
                        tiny_rstd(i, "k", g, 2, 64.0, 64.0, kstat2)
                        yield

        def diff_attn(h, filler):
            i = h % 2
            b_vb = dstate["b_vb"]
            RA = slice(0, 128)
            vtA = (lambda kt, i=i: vb[i][:, kt, 0:128], 128, 2, b_vb[i])
            vtB = (lambda kt, i=i: vb[i][:, kt, 72:200], 128, 3, b_vb[i])
            for qc in range(NG):
                qsl = slice(qc * GS, (qc + 1) * GS)
                for comp in range(2):
                    rows = slice(comp * 64, comp * 64 + 64)
                    attention(kh[i], b_kh[i], qh[i], b_qh[i], rows, (lambda kt, i=i, comp=comp: (rk_t[i][:, kt * 2 + comp:kt * 2 + comp + 1], b_rk[i])), [vtA, vtB], qc, filler, (3, 7, 11, 15), sbanks=(0, 1, 6))
                    P.op("dve", CP(oraw[0][0:65, :], ps[0:65, 2, :]), reads=[b_ps[2]], writes=[b_oraw[0]])
                    P.op("dve", CP(oraw[1][64:128, :], ps[64:128, 3, :]), reads=[b_ps[3]], writes=[b_oraw[1]])
                    P.op("dve", RECIP(rin[64:65, :], oraw[0][64:65, :]), reads=[b_oraw[0]], writes=[b_rin])

                    def epi(comp=comp, qsl=qsl, qc=qc):
                        P.op("pe", MM(ps[:, 7, :], onesf[64:65, :], rin[64:65, :], True, True),
                             reads=[b_onesf, b_rin], writes=[b_ps[7]])
                        P.op("dve", TT(dn[comp][0:64, :], oraw[0][0:64, :], ps[0:64, 7, :], ALU.mult),
                             reads=[b_oraw[0], b_ps[7]], writes=[b_dn[comp]])
                        P.op("dve", TT(dn[comp][64:128, :], oraw[1][64:128, :], ps[64:128, 7, :], ALU.mult),
                             reads=[b_oraw[1], b_ps[7]], writes=[b_dn[comp]])
                        if comp == 0:
                            yield
                            return
                        P.op("dve", STT(od_e[:], dn[1][:], der[:, 72:73], dn[0][:], ALU.mult, ALU.add),
                             reads=[b_dn[0], b_dn[1], b_der], writes=[b_ode])
                        P.op("act", ACT(sq_e[:], od_e[:], AF.Square), reads=[b_ode], writes=[b_sqe])
                        yield
                        P.op("pe", MM(ps[:, 7, :], ones[:], sq_e[:], True, True), reads=[b_const, b_sqe], writes=[b_ps[7]])
                        P.op("act", ACT(rs_e[:], ps[:, 7, :], AF.Sqrt, bias=epsc[:], scale=1.0 / 128.0),
                             reads=[b_ps[7], b_const], writes=[b_rse])
                        P.op("dve", RECIP(rs_e[:], rs_e[:]), reads=[b_rse], writes=[b_rse])
                        yield
                        P.op("dve", STT(obT[:, h, qsl], od_e[:], der[:, 73:74], rs_e[:], ALU.mult, ALU.mult),
                             reads=[b_ode, b_der, b_rse], writes=[b_ob[h][qc]])
                        yield
                    pend.append(epi())

        units = [("mla", h) for h in range(8)] + [("diff", h) for h in range(4)]

        def prep_of(u):
            return mla_prep(u[1]) if u[0] == "mla" else diff_prep(u[1])

        NU = int(_os.environ.get("MK_UNITS", "12"))
        if NU < 12:
            for c in range(4):
                for g in range(NG):
                    P.op("pool", MEMSET(oaT[:, c, g * GS:(g + 1) * GS], 0.0), writes=[b_oa[c][g]])
                    P.op("pool", MEMSET(obT[:, c, g * GS:(g + 1) * GS], 0.0), writes=[b_ob[c][g]])
            if NU <= 8:
                dstate["b_wdh"] = P.bufs("wdh", 2)
                dstate["b_vb"] = P.bufs("vb", 2)
        units = units[:NU]
        def warm(n):
            for k in range(n):
                P.op("pe", MM(ps[:, 7, :], ones[:], nT[:, k % KC, 0:GS], True, True),
                     reads=[b_const, b_n[k % KC][0]], writes=[b_ps[7]], inc=(k == n - 1))

        NWARM = int(_os.environ.get("MK_WARM", "0"))
        drain(prep_of(units[0]))
        for ui, u in enumerate(units):
            if NWARM:
                warm(NWARM)
            gen = prep_of(units[ui + 1]) if ui + 1 < len(units) else None
            if NOFILL and _os.environ.get("MK_PREFIRST") == "1":
                drain(gen)
            filler = make_filler(gen)
            if u[0] == "mla":
                mla_attn(u[1], filler)
            else:
                diff_attn(u[1], filler)
            drain(gen)
        while pend:
            step_pend()
        b_wdh, b_vb = dstate["b_wdh"], dstate["b_vb"]

        P.retire(*b_wdh, *b_vb, *b_qh, *b_kh, *b_PT, *b_oraw, *b_dn, b_rin, b_onesf, b_tab, b_sqe, b_ode, b_rse, *b_rq, *b_rk, *b_t4, *b_dg)
        woa = A.view(M2 + 0, BF16, [4, D])
        wob = A.view(M2 + 8192, BF16, [4, D])
        wout = A.view(M2 + 16384, BF16, [KC, D])
        wgt = [A.view(M2 + 32768 + i * 4096, BF16, [KC, 256]) for i in range(2)]
        hst = [A.view(M2 + 40960 + i * 2048, F32, [GS]) for i in range(2)]
        gA = A.view(M2 + 45056, F32, [GS])
        gB = A.view(M2 + 47104, F32, [GS])
        mT = A.view(M2 + 49152, BF16, [KC, T])
        b_woa, b_wob, b_wout = P.buf("woa"), P.buf("wob"), P.buf("wout")
        b_wgt = P.bufs("wgt", 2)
        b_hst = P.bufs("hst", 2)
        b_gA, b_gB = P.buf("gA"), P.buf("gB")
        b_m = [[P.buf("m") for g in range(NG)] for c in range(KC)]
        P.dma("pool", woa[:], woa_d.rearrange("(c p) n -> p c n", p=128), writes=[b_woa])
        P.dma("pool", wob[:], wob_d.rearrange("(c p) n -> p c n", p=128), writes=[b_wob])
        for hh in range(2):
            P.dma("pool", wout[:, hh * 4:(hh + 1) * 4, :], wout_d.rearrange("(c p) n -> p c n", p=128)[:, hh * 4:(hh + 1) * 4, :], writes=[b_wout])
        GA0, GB0 = 2208, 3232
        wad = A.view(M2 + 81920, BF16, [KC, D])
        b_wad = P.buf("wad")
        wada_block(5, wad, b_wad)
        late = {0: (5, 6), 2: (6, 7), 4: (7, 8), 6: (8, None)}
        nit = 0
        def load_wgt(oc_):
            wi_ = oc_ % 2
            P.dma("pool", wgt[wi_][:, :, 0:128], win_v[:, :, GA0 + oc_ * 128:GA0 + (oc_ + 1) * 128], writes=[b_wgt[wi_]])
            P.dma("pool", wgt[wi_][:, :, 128:256], win_v[:, :, GB0 + oc_ * 128:GB0 + (oc_ + 1) * 128], writes=[b_wgt[wi_]])

        load_wgt(0)
        for oc in range(KC):
            wi = oc % 2
            if oc + 1 < KC:
                load_wgt(oc + 1)
            for g in range(NG):
                sl = slice(g * GS, (g + 1) * GS)
                b0 = 4 * (nit % 2)
                nit += 1
                for c in range(4):
                    P.op("pe", MM(ps[:, b0, :], woa[:, c, oc * 128:(oc + 1) * 128], oaT[:, c, sl], c == 0, c == 3),
                         reads=[b_woa, b_oa[c][g]], writes=[b_ps[b0]], inc=(c == 3))
                for c in range(4):
                    P.op("pe", MM(ps[:, b0 + 1, :], wob[:, c, oc * 128:(oc + 1) * 128], obT[:, c, sl], c == 0, c == 3),
                         reads=[b_wob, b_ob[c][g]], writes=[b_ps[b0 + 1]], inc=(c == 3))
                for kc in range(KC):
                    P.op("pe", MM(ps[:, b0 + 2, :], wgt[wi][:, kc, 0:128], nT[:, kc, sl], kc == 0, kc == KC - 1),
                         reads=[b_wgt[wi], b_n[kc][g]], writes=[b_ps[b0 + 2]], inc=(kc == KC - 1))
                for kc in range(KC):
                    P.op("pe", MM(ps[:, b0 + 3, :], wgt[wi][:, kc, 128:256], nT[:, kc, sl], kc == 0, kc == KC - 1),
                         reads=[b_wgt[wi], b_n[kc][g]], writes=[b_ps[b0 + 3]], inc=(kc == KC - 1))
                P.op("act", ACT(gA[:], ps[:, b0 + 2, :], AF.Sigmoid), reads=[b_ps[b0 + 2]], writes=[b_gA])
                P.op("act", ACT(gB[:], ps[:, b0 + 3, :], AF.Sigmoid), reads=[b_ps[b0 + 3]], writes=[b_gB])
                P.op("dve", TT(gA[:], gA[:], ps[:, b0, :], ALU.mult), reads=[b_gA, b_ps[b0]], writes=[b_gA])
                P.op("dve", TT(gB[:], gB[:], ps[:, b0 + 1, :], ALU.mult), reads=[b_gB, b_ps[b0 + 1]], writes=[b_gB])
                P.op("dve", TT(mT[:, oc, sl], gA[:], gB[:], ALU.add), reads=[b_gA, b_gB], writes=[b_m[oc][g]])
            if oc in late:
                mcur, mnext = late[oc]
                mod_block(mcur, wad, b_wad)
                if mnext is not None:
                    wada_block(mnext, wad, b_wad)
        derive_G("G2", 5, 1.0)
        derive_A("A3", "g3", 7, 6, "SH3")
        derive_G("G3", 8, 0.5)
        P.retire(*[b for row in b_oa for b in row], *[b for row in b_ob for b in row])
        b_h = [[P.buf("h%d_%d" % (c, g)) for g in range(NG)] for c in range(KC)]
        nit = 0
        for g in range(NG):
            sl = slice(g * GS, (g + 1) * GS)
            for oc in range(KC):
                bk = nit % 2
                hi = nit % 2
                nit += 1
                P.dma("sp", hst[hi][:], hsp[:, oc, sl], reads=[b_hsp], writes=[b_hst[hi]], owner=b_hst[hi])
                for c in range(KC):
                    P.op("pe", MM(ps[:, bk, :], wout[:, c, oc * 128:(oc + 1) * 128], mT[:, c, sl], c == 0, c == KC - 1),
                         reads=[b_wout, b_m[c][g]], writes=[b_ps[bk]], inc=(c == KC - 1))
                P.op("dve", STT(hT[:, oc, sl], ps[:, bk, :], dcol("G2", oc), hst[hi][:], ALU.mult, ALU.add),
                     reads=[b_ps[bk], b_der, b_hst[hi]], writes=[b_h[oc][g]])
        P.retire(*[b for row in b_m for b in row], b_woa, b_wob, b_wout, *b_wgt, *b_hst, b_gA, b_gB, b_wad)
        P.retire(*[b for row in b_n for b in row])
        b_n = [[P.buf("n%d_%d" % (c, g)) for g in range(NG)] for c in range(KC)]
        ring["b_wd"] = P.bufs("wdp", 2)
        ring["b_gu"] = P.bufs("gu", 2)

    if "ffn2" in stages:
        for g in range(NG):
            norm_group(g, "A3", "SH3")
        ffn(f2g_d, f2u_d, f2d_d, "G3", ring, {})

    ost = [A.view(AT_OFF + i * 4096, F32, [D]) for i in range(2)]
    b_ost = P.bufs("ost", 2)
    ytmp = [A.view(AT_OFF + 8192 + i * 2048, F32, [GS]) for i in range(8)]
    b_ytmp = P.bufs("ytmp", 8)
    b_out = P.buf("out")
    for g in range(NG):
        r = norm_group(g, None, None)
        sl = slice(g * GS, (g + 1) * GS)
        for c in range(KC):
            P.op("dve", STT(ytmp[c][:], hT[:, c, sl], pvb("gf", c), rsb[r][:], ALU.mult, ALU.mult),
                 reads=[b_h[c][g], b_pvT, b_rsb[r]], writes=[b_ytmp[c]])
        for j in range(4):
            tt = g * 4 + j
            o = tt % 2
            for half in range(2):
                bk = (tt * 2 + half) % 4
                for cc in range(4):
                    c = half * 4 + cc
                    P.op("pe", TR(ps[:, bk, cc * 128:(cc + 1) * 128], ytmp[c][:, j * 128:(j + 1) * 128], ident[:]),
                         reads=[b_ytmp[c], b_const], writes=[b_ps[bk]], inc=(cc == 3))
                P.op("act", ACT(ost[o][:, half * 512:(half + 1) * 512], ps[:, bk, :], AF.Copy),
                     reads=[b_ps[bk]], writes=[b_ost[o]])
            P.dma("sp", out_d[tt * 128:(tt + 1) * 128, :], ost[o][:], reads=[b_ost[o]], writes=[b_out], owner=b_ost[o])
    P.wait_all("sp", b_ost + [b_out])
    P.emit()
    return nc


def _prep_inputs(inp):
    L = 0
    f32 = np.float32

    def pad128(v):
        o = np.zeros(128, f32)
        o[:v.shape[0]] = v
        return o

    def mla_perm(g):
        o = np.zeros(128, f32)
        o[64:80] = g[80:96]
        o[80:96] = g[64:80]
        return o

    def diff_rep(g):
        return np.concatenate([g, g]).astype(f32)

    def diff_perm(g):
        o = np.zeros(128, f32)
        for base in (0, 64):
            o[base:base + 8] = g[8:16]
            o[base + 8:base + 16] = g[0:8]
        return o

    freqa = np.zeros(128, f32)
    signa = np.ones(128, f32)
    fa = (1.0 / (np.float32(10000.0) ** (np.arange(16, dtype=f32) / np.float32(16)))).astype(f32)
    freqa[64:80] = fa
    freqa[80:96] = fa
    signa[64:80] = -1.0
    freqb = np.zeros(128, f32)
    signb = np.ones(128, f32)
    fb = (1.0 / (np.float32(500000.0) ** (np.arange(8, dtype=f32) / np.float32(8)))).astype(f32)
    for base in (0, 64):
        freqb[base:base + 8] = fb
        freqb[base + 8:base + 16] = fb
        signb[base:base + 8] = -1.0

    rows = []
    for nm in ("ffn1_norm", "mix_norm", "ffn2_norm", "final_norm"):
        rows.append(np.asarray(inp[nm][L], f32).reshape(8, 128))
    rows.append(np.asarray(inp["mla_q_norm"][L], f32).reshape(3, 128))
    rows.append(np.asarray(inp["mla_kv_norm"][L], f32).reshape(2, 128))
    qg = np.asarray(inp["mla_q_gain"][L], f32)
    kg = np.asarray(inp["mla_k_gain"][L], f32)
    dqg = np.asarray(inp["diff_q_gain"][L], f32)
    dkg = np.asarray(inp["diff_k_gain"][L], f32)
    singles = [pad128(qg), mla_perm(qg), pad128(kg), mla_perm(kg),
               diff_rep(dqg), diff_perm(dqg), diff_rep(dkg), diff_perm(dkg),
               pad128(np.asarray(inp["diff_lambda_q1"][L], f32)), pad128(np.asarray(inp["diff_lambda_k1"][L], f32)),
               pad128(np.asarray(inp["diff_lambda_q2"][L], f32)), pad128(np.asarray(inp["diff_lambda_k2"][L], f32)),
               np.asarray(inp["diff_subln"][L], f32), freqa, signa, freqb, signb]
    rows.append(np.stack(singles))
    pvb = np.ascontiguousarray(np.concatenate(rows, axis=0), dtype=f32)
    assert pvb.shape == (NPVB, 128), pvb.shape
    shared = {
        "pvb": pvb,
        "ident": np.eye(128, dtype=f32),
        "w_ada": np.ascontiguousarray(inp["w_ada"][L], f32),
        "f1g": np.ascontiguousarray(inp["ffn1_w_gate"][L], f32),
        "f1u": np.ascontiguousarray(inp["ffn1_w_up"][L], f32),
        "f1d": np.ascontiguousarray(inp["ffn1_w_down"][L], f32),
        "f2g": np.ascontiguousarray(inp["ffn2_w_gate"][L], f32),
        "f2u": np.ascontiguousarray(inp["ffn2_w_up"][L], f32),
        "f2d": np.ascontiguousarray(inp["ffn2_w_down"][L], f32),
        "w_in": np.ascontiguousarray(inp["w_in"][L], f32),
        "w_uq": np.ascontiguousarray(inp["mla_w_uq"][L], f32),
        "w_ukv": np.ascontiguousarray(inp["mla_w_ukv"][L], f32),
        "w_oa": np.ascontiguousarray(inp["mla_w_o"][L], f32),
        "w_ob": np.ascontiguousarray(inp["diff_w_o"][L], f32),
        "w_out": np.ascontiguousarray(inp["w_out"][L], f32),
    }
    b_ada = np.asarray(inp["b_ada"][L], f32).reshape(72, 128)
    maps = []
    for b in range(8):
        m = dict(shared)
        m["x"] = np.ascontiguousarray(inp["x"][b], f32)
        m["pos"] = np.ascontiguousarray(np.asarray(inp["positions"][b]).reshape(1, T).astype(np.int32))
        m["pva"] = np.ascontiguousarray(np.concatenate([np.asarray(inp["c"][b], f32).reshape(8, 128), b_ada], axis=0))
        maps.append(m)
    return maps


_NC_CACHE = {}


def kernel(**inputs):
    maps = _prep_inputs(inputs)
    if "nc" not in _NC_CACHE:
        _NC_CACHE["nc"] = build_nc()
    nc = _NC_CACHE["nc"]
    res = run_bass_kernel_spmd(nc, maps, core_ids=list(range(8)))
    out = np.stack([np.asarray(r["out"], np.float32) for r in res.results], axis=0)
    return out
```

```python
import math
import os as _os
import numpy as np
import concourse.bass as bass
import concourse.mybir as mybir
from concourse.bass_utils import run_bass_kernel_spmd

F32 = mybir.dt.float32
BF16 = mybir.dt.bfloat16
I32 = mybir.dt.int32
AF = mybir.ActivationFunctionType
ALU = mybir.AluOpType

T = 2048
D = 1024
DFF = 2816
NG = 4
GS = 512
KC = 8
EPS = 1e-6
IN_COLS = 4256
LAMBDA_INIT = 0.8 - 0.6 * math.exp(-0.3 * 0)


class Buf:
    __slots__ = ("name", "w", "r", "dsem", "dcnt")

    def __init__(self, name):
        self.name = name
        self.w = None
        self.r = {}
        self.dsem = None
        self.dcnt = 0


class Prog:
    ENG = ("pe", "act", "dve", "pool", "sp")

    def __init__(self, nc, same_engine_sync=True):
        self.nc = nc
        self.streams = {e: [] for e in self.ENG}
        self.sems = {}
        self.cnt = {}
        self.waited = {e: {} for e in self.ENG}
        self.same = same_engine_sync
        for e in self.ENG:
            self.sems["E_" + e] = nc.alloc_semaphore("sem_e_" + e)
            self.cnt[e] = 0
        self.nd = 0
        self.retired = {}

    def new_dsem(self, name):
        k = "D_%d_%s" % (self.nd, name)
        self.nd += 1
        self.sems[k] = self.nc.alloc_semaphore("sem_d%d" % self.nd)
        return k

    def buf(self, name):
        b = Buf(name)
        b.r = dict(self.retired)
        return b

    def bufs(self, name, n):
        return [self.buf("%s%d" % (name, i)) for i in range(n)]

    def retire(self, *bufs):
        for b in bufs:
            if b.w is not None:
                s, v = b.w
                self.retired[s] = max(self.retired.get(s, 0), v)
            for s, v in b.r.items():
                self.retired[s] = max(self.retired.get(s, 0), v)

    def _waits(self, eng, reads, writes):
        waits = {}
        own = "E_" + eng

        def need(s, v):
            if s == own and (eng == "pe" or not self.same):
                return
            if self.waited[eng].get(s, 0) >= v:
                return
            if waits.get(s, 0) < v:
                waits[s] = v

        for b in reads:
            if b.w is not None:
                need(*b.w)
        for b in writes:
            if b.w is not None:
                need(*b.w)
            for s, v in b.r.items():
                need(s, v)
        for s, v in waits.items():
            self.waited[eng][s] = v
        return list(waits.items())

    def _mark(self, tok, reads, writes):
        s, v = tok
        for b in reads:
            if b.r.get(s, 0) < v:
                b.r[s] = v
        for b in writes:
            b.w = tok
            b.r = {}

    def op(self, eng, fn, reads=(), writes=(), inc=True):
        waits = self._waits(eng, reads, writes)
        own = "E_" + eng
        if inc:
            self.cnt[eng] += 1
            tok = (own, self.cnt[eng])
        else:
            tok = (own, self.cnt[eng] + 1)
        self._mark(tok, reads, writes)
        self.streams[eng].append((waits, fn, (own, 1) if inc else None))

    def dma(self, q, out_ap, in_ap, reads=(), writes=(), owner=None):
        if owner is None:
            owner = writes[0] if writes else reads[0]
        if owner.dsem is None:
            owner.dsem = self.new_dsem(owner.name)
        waits = self._waits(q, reads, writes)
        owner.dcnt += 16
        tok = (owner.dsem, owner.dcnt)
        self._mark(tok, reads, writes)
        self.streams[q].append(
            (waits, lambda e: e.dma_start(out=out_ap, in_=in_ap), (owner.dsem, 16)))

    def wait_all(self, eng, bufs):
        waits = self._waits(eng, (), bufs)
        self.streams[eng].append((waits, None, None))

    def emit(self):
        nc = self.nc
        sems = self.sems
        streams = self.streams
        with nc.Block() as block:
            def run(e, name):
                for waits, fn, inc in streams[name]:
                    for s, v in waits:
                        e.wait_ge(sems[s], v)
                    if fn is None:
                        continue
                    ins = fn(e)
                    if inc is not None:
                        ins.then_inc(sems[inc[0]], inc[1])

            @block.tensor
            def _(e):
                run(e, "pe")

            @block.scalar
            def _(e):
                run(e, "act")

            @block.vector
            def _(e):
                run(e, "dve")

            @block.gpsimd
            def _(e):
                run(e, "pool")

            @block.sync
            def _(e):
                run(e, "sp")


class Arena:
    def __init__(self, nc, reserve=1024):
        rem = nc.sbuf_bytes_remaining
        self.nbytes = ((rem - reserve) // 64) * 64
        self.t = nc.alloc_sbuf_tensor("arena", [128, self.nbytes // 4], F32)

    def view(self, off, dtype, shape):
        assert off % 4 == 0
        esz = 2 if dtype == BF16 else 4
        n = 1
        for s in shape:
            n *= s
        nb = n * esz
        assert nb % 4 == 0 and off + nb <= self.nbytes, (off, nb, self.nbytes)
        a = self.t[:, off // 4:(off + nb) // 4]
        if dtype != F32:
            a = a.bitcast(dtype)
        if len(shape) == 2:
            a = a.rearrange("p (a b) -> p a b", a=shape[0])
        elif len(shape) == 3:
            a = a.rearrange("p (a b c) -> p a b c", a=shape[0], b=shape[1])
        return a


def MM(out, lhsT, rhs, start, stop):
    return lambda e: e.matmul(out, lhsT=lhsT, rhs=rhs, start=start, stop=stop)


def TR(out, in_, ident):
    return lambda e: e.transpose(out, in_, ident)


def ACT(out, in_, func, bias=None, scale=None):
    kw = {}
    if bias is not None:
        kw["bias"] = bias
    if scale is not None:
        kw["scale"] = scale
    return lambda e: e.activation(out=out, in_=in_, func=func, **kw)


def TT(out, in0, in1, op):
    return lambda e: e.tensor_tensor(out=out, in0=in0, in1=in1, op=op)


def TS(out, in0, s1, s2, op0, op1=None):
    if op1 is None:
        return lambda e: e.tensor_scalar(out=out, in0=in0, scalar1=s1, scalar2=None, op0=op0)
    return lambda e: e.tensor_scalar(out=out, in0=in0, scalar1=s1, scalar2=s2, op0=op0, op1=op1)


def STT(out, in0, scalar, in1, op0, op1):
    return lambda e: e.scalar_tensor_tensor(out=out, in0=in0, scalar=scalar, in1=in1, op0=op0, op1=op1)


def CP(out, in_):
    return lambda e: e.tensor_copy(out=out, in_=in_)


def RECIP(out, in_):
    return lambda e: e.reciprocal(out=out, in_=in_)


def MEMSET(ap, v):
    return lambda e: e.memset(ap, v)


PVB = {}
_n = 0
for _name, _cnt in [("g1", 8), ("g2", 8), ("g3", 8), ("gf", 8), ("qnorm", 3), ("kvnorm", 2),
                    ("qg", 1), ("qgp", 1), ("kg", 1), ("kgp", 1), ("dqg", 1), ("dqgp", 1),
                    ("dkg", 1), ("dkgp", 1), ("lq1", 1), ("lk1", 1), ("lq2", 1), ("lk2", 1),
                    ("subln", 1), ("freqa", 1), ("signa", 1), ("freqb", 1), ("signb", 1)]:
    PVB[_name] = _n
    _n += _cnt
NPVB = _n
NPVA = 80


def build_nc(stages=("ffn1", "mix", "ffn2"), debug=None):
    nc = bass.Bass("TRN2", target_bir_lowering=False)
    P = Prog(nc)
    A = Arena(nc)

    def dram_in(name, shape, dt=F32):
        return nc.dram_tensor(name, shape, dt, kind="ExternalInput").ap()

    x_d = dram_in("x", [T, D])
    pos_d = dram_in("pos", [1, T], I32)
    pva_d = dram_in("pva", [NPVA, 128])
    pvb_d = dram_in("pvb", [NPVB, 128])
    ident_d = dram_in("ident", [128, 128])
    wada_d = dram_in("w_ada", [D, 9 * D])
    f1g_d = dram_in("f1g", [D, DFF])
    f1u_d = dram_in("f1u", [D, DFF])
    f1d_d = dram_in("f1d", [DFF, D])
    f2g_d = dram_in("f2g", [D, DFF])
    f2u_d = dram_in("f2u", [D, DFF])
    f2d_d = dram_in("f2d", [DFF, D])
    win_d = dram_in("w_in", [D, IN_COLS])
    wuq_d = dram_in("w_uq", [384, 768])
    wukv_d = dram_in("w_ukv", [256, 1024])
    woa_d = dram_in("w_oa", [512, D])
    wob_d = dram_in("w_ob", [512, D])
    wout_d = dram_in("w_out", [D, D])
    out_d = nc.dram_tensor("out", [T, D], F32, kind="ExternalOutput").ap()
    dbg_d = None
    if debug is not None:
        dbg_d = nc.dram_tensor("dbg", [128] + list(debug[1]), F32, kind="ExternalOutput").ap()

    H_OFF = 0
    MISC = 65536
    NT_OFF = MISC + 14336
    R_OFF = NT_OFF + 32768
    R_SIZE = A.nbytes - R_OFF
    assert R_SIZE >= 98304, R_SIZE

    hT = A.view(H_OFF, F32, [KC, T])
    nT = A.view(NT_OFF, BF16, [KC, T])
    ident = A.view(MISC + 0, F32, [128])
    ones = A.view(MISC + 512, BF16, [128])
    ones2 = A.view(MISC + 768, BF16, [128])
    pvT = A.view(MISC + 1024, F32, [256])
    modT = A.view(MISC + 2048, F32, [72])
    der = A.view(MISC + 2560, F32, [128])
    sqs = [A.view(MISC + 3072 + i * 1024, BF16, [GS]) for i in range(2)]
    fsc = [A.view(MISC + 5120 + i * 2048, F32, [GS]) for i in range(2)]
    rsb = [A.view(MISC + 9216 + i * 2048, F32, [GS]) for i in range(2)]
    condb = A.view(MISC + 13312, BF16, [8])
    ps = nc.alloc_psum_tensor("ps", [128, 8, GS], F32)

    b_h = [[P.buf("h%d_%d" % (c, g)) for g in range(NG)] for c in range(KC)]
    b_n = [[P.buf("n%d_%d" % (c, g)) for g in range(NG)] for c in range(KC)]
    b_ps = P.bufs("ps", 8)
    b_const = P.buf("const")
    b_pvT = P.buf("pvT")
    b_mod = P.buf("mod")
    b_der = P.buf("der")
    b_sqs = P.bufs("sqs", 2)
    b_fsc = P.bufs("fsc", 2)
    b_rsb = P.bufs("rsb", 2)
    b_condb = P.buf("condb")

    def pvb(name, i=0):
        c = 80 + PVB[name] + i
        return pvT[:, c:c + 1]

    DER = {"A1": 0, "G1": 8, "A2": 16, "G2": 24, "A3": 32, "G3": 40, "SH1": 48, "SH2": 56, "SH3": 64,
           "nlam": 72, "subs": 73, "lamt": 74}

    def dcol(name, i=0):
        c = DER[name] + i
        return der[:, c:c + 1]

    P.dma("sp", ident[:], ident_d, writes=[b_const])
    P.op("pool", MEMSET(ones[:], 1.0), writes=[b_const])
    P.op("pool", MEMSET(ones2[:], 0.0), writes=[b_const])
    P.op("pool", MEMSET(ones2[0:64, 0:64], 1.0), writes=[b_const])
    P.op("pool", MEMSET(ones2[64:128, 64:128], 1.0), writes=[b_const])
    pstage = A.view(R_OFF + 32768, F32, [2, 128])
    b_pstage = P.buf("pstage")
    P.op("pool", MEMSET(pstage[:], 0.0), writes=[b_pstage])
    P.dma("sp", pstage[0:NPVA, 0, :], pva_d, writes=[b_pstage])
    P.dma("sp", pstage[0:NPVB, 1, :], pvb_d, writes=[b_pstage])
    P.op("pe", TR(ps[:, 7, 0:NPVA], pstage[0:NPVA, 0, :], ident[0:NPVA, 0:NPVA]),
         reads=[b_pstage, b_const], writes=[b_ps[7]], inc=False)
    P.op("pe", TR(ps[:, 7, 128:128 + NPVB], pstage[0:NPVB, 1, :], ident[0:NPVB, 0:NPVB]),
         reads=[b_pstage, b_const], writes=[b_ps[7]])
    P.op("dve", CP(pvT[:, 0:NPVA], ps[:, 7, 0:NPVA]), reads=[b_ps[7]], writes=[b_pvT])
    P.op("dve", CP(pvT[:, 80:80 + NPVB], ps[:, 7, 128:128 + NPVB]), reads=[b_ps[7]], writes=[b_pvT])
    P.retire(b_pstage)
    P.op("act", ACT(condb[:], pvT[:, 0:8], AF.Silu), reads=[b_pvT], writes=[b_condb])

    ring = {"wd_i": 0, "gu_i": 0, "b_wd": P.bufs("wdp", 2), "b_gu": P.bufs("gu", 2)}

    AT_OFF = R_OFF
    GU_OFF = R_OFF + 32768
    WD_OFF = R_OFF + 65536

    def wada_block(m, dst, b_dst):
        src = wada_d.rearrange("(kc p) n -> p kc n", p=128)[:, :, m * 1024:(m + 1) * 1024]
        for hh in range(2):
            P.dma("pool", dst[:, hh * 4:(hh + 1) * 4, :], src[:, hh * 4:(hh + 1) * 4, :], writes=[b_dst])

    def mod_block(m, wt, b_wt):
        for j in range(8):
            for kc in range(KC):
                P.op("pe", MM(ps[:, 6, m * 8 + j:m * 8 + j + 1], wt[:, kc, j * 128:(j + 1) * 128],
                              condb[:, kc:kc + 1], kc == 0, kc == KC - 1),
                     reads=[b_wt, b_condb], writes=[b_ps[6]], inc=(j == 7 and kc == KC - 1))
        P.op("dve", TT(modT[:, m * 8:(m + 1) * 8], ps[:, 6, m * 8:(m + 1) * 8],
                       pvT[:, 8 + m * 8:8 + (m + 1) * 8], ALU.add),
             reads=[b_ps[6], b_pvT], writes=[b_mod])

    def derive_A(dst, gname, m_sc, m_sh, shname):
        P.op("dve", STT(der[:, DER[dst]:DER[dst] + 8], modT[:, m_sc * 8:(m_sc + 1) * 8], 1.0,
                        pvT[:, 80 + PVB[gname]:80 + PVB[gname] + 8], ALU.add, ALU.mult),
             reads=[b_mod, b_pvT], writes=[b_der])
        P.op("dve", CP(der[:, DER[shname]:DER[shname] + 8], modT[:, m_sh * 8:(m_sh + 1) * 8]),
             reads=[b_mod], writes=[b_der])

    def derive_G(dst, m_gt, scale):
        P.op("dve", TS(der[:, DER[dst]:DER[dst] + 8], modT[:, m_gt * 8:(m_gt + 1) * 8], float(scale), None, ALU.mult),
             reads=[b_mod], writes=[b_der])

    def norm_group(g, Aname, SHname, sq_from_psum=None):
        sl = slice(g * GS, (g + 1) * GS)
        sb = 6 + (g % 2)
        for c in range(KC):
            i = c % 2
            P.op("act", ACT(sqs[i][:], hT[:, c, sl], AF.Square), reads=[b_h[c][g]], writes=[b_sqs[i]])
            P.op("pe", MM(ps[:, sb, :], ones[:], sqs[i][:], c == 0, c == KC - 1),
                 reads=[b_const, b_sqs[i]], writes=[b_ps[sb]], inc=True)
        r = g % 2
        P.op("act", ACT(rsb[r][:], ps[:, sb, :], AF.Sqrt, bias=epsc[:], scale=1.0 / D),
             reads=[b_ps[sb], b_const], writes=[b_rsb[r]])
        P.op("dve", RECIP(rsb[r][:], rsb[r][:]), reads=[b_rsb[r]], writes=[b_rsb[r]])
        if Aname is None:
            return r
        for c in range(KC):
            i = c % 2
            P.op("dve", TT(fsc[i][:], hT[:, c, sl], rsb[r][:], ALU.mult),
                 reads=[b_h[c][g], b_rsb[r]], writes=[b_fsc[i]])
            P.op("dve", TS(nT[:, c, sl], fsc[i][:], dcol(Aname, c), dcol(SHname, c), ALU.mult, ALU.add),
                 reads=[b_fsc[i], b_der], writes=[b_n[c][g]])
        return r

    epsc = A.view(MISC + 13312 + 64, F32, [1])
    P.op("pool", MEMSET(epsc[:], EPS), writes=[b_const])

    aT = A.view(AT_OFF, BF16, [8, T])
    gu = [A.view(GU_OFF + i * 16384, BF16, [2, KC, GS]) for i in range(2)]
    wdp = [A.view(WD_OFF + i * 16384, BF16, [8, D]) for i in range(2)]
    PASSES = [(0, 8), (8, 16), (16, 22)]

    def ffn(wg_d, wu_d, wd_d, Gname, st, pre_hooks):
        wg_v = wg_d.rearrange("(kc p) n -> p kc n", p=128)
        wu_v = wu_d.rearrange("(kc p) n -> p kc n", p=128)
        wd_v = wd_d.rearrange("(j p) n -> p j n", p=128)
        b_a = [[P.buf("a%d_%d" % (j, g)) for g in range(NG)] for j in range(8)]
        it = 0
        for pi, (c0, c1) in enumerate(PASSES):
            npc = c1 - c0
            wslot = st["wd_i"] % 2
            st["wd_i"] += 1
            b_wd = st["b_wd"][wslot]
            for hh in range(0, npc, 4):
                h1 = min(hh + 4, npc)
                P.dma("pool", wdp[wslot][:, hh:h1, :], wd_v[:, c0 + hh:c0 + h1, :], writes=[b_wd])
            groups = [(a, min(a + 4, c1)) for a in range(c0, c1, 4)]
            loaded = []
            for (a, b) in groups:
                gslot = st["gu_i"] % 2
                st["gu_i"] += 1
                b_g = st["b_gu"][gslot]
                ncol = (b - a) * 128
                P.dma("pool", gu[gslot][:, 0, :, 0:ncol], wg_v[:, :, a * 128:b * 128], writes=[b_g])
                P.dma("pool", gu[gslot][:, 1, :, 0:ncol], wu_v[:, :, a * 128:b * 128], writes=[b_g])
                loaded.append((a, b, gslot, b_g))
                for j in range(a, b):
                    jl = j - c0
                    jo = (j - a) * 128
                    for g in range(NG):
                        sl = slice(g * GS, (g + 1) * GS)
                        bg = it % 2
                        bu = 2 + it % 2
                        it += 1
                        for kc in range(KC):
                            P.op("pe", MM(ps[:, bg, :], gu[gslot][:, 0, kc, jo:jo + 128], nT[:, kc, sl], kc == 0, kc == KC - 1),
                                 reads=[b_g, b_n[kc][g]], writes=[b_ps[bg]], inc=(kc == KC - 1))
                        for kc in range(KC):
                            P.op("pe", MM(ps[:, bu, :], gu[gslot][:, 1, kc, jo:jo + 128], nT[:, kc, sl], kc == 0, kc == KC - 1),
                                 reads=[b_g, b_n[kc][g]], writes=[b_ps[bu]], inc=(kc == KC - 1))
                        i = it % 2
                        P.op("act", ACT(fsc[i][:], ps[:, bg, :], AF.Silu), reads=[b_ps[bg]], writes=[b_fsc[i]])
                        P.op("dve", TT(aT[:, jl, sl], fsc[i][:], ps[:, bu, :], ALU.mult),
                             reads=[b_fsc[i], b_ps[bu]], writes=[b_a[jl][g]])
                hk = pre_hooks.get((pi, len(loaded) - 1))
                if hk is not None:
                    hk()
            hk = pre_hooks.get((pi, "down"))
            if hk is not None:
                hk()
            dn = 0
            for g in range(NG):
                sl = slice(g * GS, (g + 1) * GS)
                for oc in range(KC):
                    bd = 4 + dn % 2
                    dn += 1
                    for jl in range(npc):
                        P.op("pe", MM(ps[:, bd, :], wdp[wslot][:, jl, oc * 128:(oc + 1) * 128], aT[:, jl, sl], jl == 0, jl == npc - 1),
                             reads=[b_wd, b_a[jl][g]], writes=[b_ps[bd]], inc=(jl == npc - 1))
                    P.op("dve", STT(hT[:, oc, sl], ps[:, bd, :], dcol(Gname, oc), hT[:, oc, sl], ALU.mult, ALU.add),
                         reads=[b_ps[bd], b_der, b_h[oc][g]], writes=[b_h[oc][g]])
        P.retire(*[b for row in b_a for b in row])


    xs = [A.view(AT_OFF + i * 4096, F32, [D]) for i in range(8)]
    b_xs = P.bufs("xs", 8)
    wa = [A.view(WD_OFF + i * 16384, BF16, [KC, D]) for i in range(2)]
    wada_block(0, wa[0], ring["b_wd"][0])
    wada_block(1, wa[1], ring["b_wd"][1])
    x_v = x_d.rearrange("(t p) d -> t p d", p=128)
    for g in range(NG):
        for j in range(4):
            tt = g * 4 + j
            s = tt % 8
            P.dma("sp", xs[s][:], x_v[tt], writes=[b_xs[s]])
        for c in range(KC):
            bk = c % 2
            for j in range(4):
                s = (g * 4 + j) % 8
                P.op("pe", TR(ps[:, bk, j * 128:(j + 1) * 128], xs[s][:, c * 128:(c + 1) * 128], ident[:]),
                     reads=[b_xs[s], b_const], writes=[b_ps[bk]], inc=(j == 3))
            P.op("act", ACT(hT[:, c, g * GS:(g + 1) * GS], ps[:, bk, :], AF.Copy), reads=[b_ps[bk]], writes=[b_h[c][g]])
        if g == 0:
            mod_block(0, wa[0], ring["b_wd"][0])
            mod_block(1, wa[1], ring["b_wd"][1])
            derive_A("A1", "g1", 1, 0, "SH1")
    P.retire(*b_xs)
    ring["wd_i"] = 0

    for g in range(NG):
        norm_group(g, "A1", "SH1")

    def make_mod_hook(ms):
        def hk():
            for m in ms:
                wslot = ring["wd_i"] % 2
                b_w_ = ring["b_wd"][wslot]
                wv = A.view(WD_OFF + wslot * 16384, BF16, [KC, D])
                wada_block(m, wv, b_w_)
                mod_block(m, wv, b_w_)
        return hk

    def hook_g1():
        derive_G("G1", 2, 0.5)

    if "ffn1" in stages:
        ffn(f1g_d, f1u_d, f1d_d, "G1", ring,
            {(0, 0): make_mod_hook([2]), (0, "down"): hook_g1,
             (0, 1): make_mod_hook([3]), (1, 0): make_mod_hook([4])})
    else:
        make_mod_hook([2, 3, 4])()
    derive_A("A2", "g2", 4, 3, "SH2")
    LATE_MOD = "mix" in stages
    if not LATE_MOD:
        make_mod_hook([5, 6, 7, 8])()
        derive_G("G2", 5, 1.0)
        derive_A("A3", "g3", 7, 6, "SH3")
        derive_G("G3", 8, 0.5)

    if "mix" in stages:
        for g in range(NG):
            norm_group(g, "A2", "SH2")
        hsp = nc.dram_tensor("hspill", [128, KC, T], F32).ap()
        b_hsp = P.buf("hsp")
        for c in range(KC):
            P.dma("sp", hsp[:, c, :], hT[:, c, :], reads=[b_h[c][g] for g in range(NG)], writes=[b_hsp])
        P.retire(*[b for row in b_h for b in row])
        P.retire(*ring["b_wd"])
        P.retire(*ring["b_gu"])
        M2 = R_OFF
        SC_A = 1.0 / math.sqrt(96.0)
        SC_B = 1.0 / math.sqrt(64.0)
        win_v = win_d.rearrange("(kc p) n -> p kc n", p=128)
        oaT = A.view(0, BF16, [4, T])
        obT = A.view(16384, BF16, [4, T])
        cosF = A.view(32768, F32, [T])
        sinF = A.view(40960, F32, [T])
        kpeR = A.view(49152, F32, [T])
        sqpe = A.view(57344, BF16, [T])
        b_oa = [[P.buf("oa") for g in range(NG)] for c in range(4)]
        b_ob = [[P.buf("ob") for g in range(NG)] for c in range(4)]
        b_tab = P.buf("tab")
        b_kpe = P.bufs("kpe", NG)
        wlat = A.view(M2 + 0, BF16, [KC, 640])
        wkr = A.view(M2 + 10240, BF16, [KC, 96])
        wkrp = A.view(M2 + 11776, BF16, [KC, 96])
        wuq = A.view(M2 + 13312, BF16, [3, 768])
        wuqp = A.view(M2 + 17920, BF16, [3, 8, 96])
        wukk = A.view(M2 + 22528, BF16, [2, 8, 64])
        wukv = A.view(M2 + 24576, BF16, [2, 512])
        wuqp_f = A.view(M2 + 17920, BF16, [3, 768])
        wukk_f = A.view(M2 + 22528, BF16, [2, 512])
        cqT = A.view(M2 + 26624, BF16, [3, T])
        ckvT = A.view(M2 + 38912, BF16, [2, T])
        vh = [A.view(M2 + 47104 + i * 4096, BF16, [16, 128]) for i in range(2)]
        qh = [A.view(M2 + 55296 + i * 4096, BF16, [T]) for i in range(2)]
        kh = [A.view(M2 + 63488 + i * 4096, BF16, [T]) for i in range(2)]
        PT = [A.view(M2 + 71680 + i * 1024, BF16, [GS]) for i in range(4)]
        oraw = [A.view(M2 + 75776 + i * 2048, F32, [GS]) for i in range(2)]
        dn = [A.view(M2 + 79872 + i * 2048, F32, [GS]) for i in range(2)]
        onesf = A.view(M2 + 83968, F32, [128])
        rin = A.view(M2 + 84480, F32, [GS])
        posi = A.view(M2 + 26624, I32, [T])
        tmpf = A.view(M2 + 34816, F32, [T])
        kfi = A.view(M2 + 43008, I32, [T])
        kff = A.view(M2 + 43008, F32, [T])
        b_tt = P.buf("tabtmp")
        b_w = {k: P.buf("w_" + k) for k in ("lat", "kr", "uq", "uqp", "ukk", "ukv")}
        b_PT = P.bufs("PT", 4)
        b_oraw = P.bufs("oraw", 2)
        b_dn = P.bufs("dn", 2)
        b_rin = P.buf("rin")
        sq_e = A.view(M2 + 86528, BF16, [GS])
        od_e = A.view(M2 + 87552, F32, [GS])
        rs_e = A.view(M2 + 89600, F32, [GS])
        b_sqe, b_ode, b_rse = P.buf("sqe"), P.buf("ode"), P.buf("rse")
        b_onesf = P.buf("onesf")
        P.op("pool", MEMSET(onesf[:], 1.0), writes=[b_onesf])
        self_ = [A.view(M2 + 91648 + i * 512, F32, [128]) for i in range(2)]
        rq_t = [A.view(M2 + 92672 + i * 128, F32, [32]) for i in range(2)]
        rk_t = [A.view(M2 + 92928 + i * 128, F32, [32]) for i in range(2)]
        t4 = [A.view(M2 + 93184 + i * 32, F32, [8]) for i in range(2)]
        dg = [A.view(M2 + 93248 + i * 512, F32, [128]) for i in range(8)]
        sel2b = A.view(M2 + 97344, BF16, [2])
        b_rq, b_rk = P.bufs("rq", 2), P.bufs("rk", 2)
        b_t4 = P.bufs("t4", 2)
        b_dg = P.bufs("dg", 8)
        P.op("pool", MEMSET(self_[0][:], 0.0), writes=[b_onesf])
        P.op("pool", MEMSET(self_[1][:], 0.0), writes=[b_onesf])
        P.op("pool", MEMSET(self_[0][:, 0:64], 1.0), writes=[b_onesf])
        P.op("pool", MEMSET(self_[1][:, 64:128], 1.0), writes=[b_onesf])
        P.op("pool", MEMSET(sel2b[:], 0.0), writes=[b_onesf])
        P.op("pool", MEMSET(sel2b[0:64, 0:1], 1.0), writes=[b_onesf])
        P.op("pool", MEMSET(sel2b[64:128, 1:2], 1.0), writes=[b_onesf])
        cmh = A.view(MISC + 13312 + 72, F32, [1])
        P.op("pool", MEMSET(cmh[:], -0.5), writes=[b_onesf])

        def POWH(out, in0, ncol):
            return lambda e: e.tensor_tensor(out=out, in0=in0, in1=cmh[:, 0:1].to_broadcast([128, ncol]), op=ALU.pow)

        TWO_PI = 2.0 * math.pi
        C1 = 6.28125
        C2 = TWO_PI - C1

        def build_tables(fname, sname):
            P.dma("sp", posi[:], pos_d.partition_broadcast(128), writes=[b_tt])
            P.op("dve", CP(tmpf[:], posi[:]), reads=[b_tt], writes=[b_tt])
            P.op("dve", TS(tmpf[:], tmpf[:], pvb(fname), None, ALU.mult), reads=[b_tt, b_pvT], writes=[b_tt])
            P.op("dve", TS(cosF[:], tmpf[:], 1.0 / TWO_PI, None, ALU.mult), reads=[b_tt], writes=[b_tab])
            P.op("dve", CP(kfi[:], cosF[:]), reads=[b_tab], writes=[b_tt])
            P.op("dve", CP(cosF[:], kfi[:]), reads=[b_tt], writes=[b_tab])
            P.op("dve", STT(tmpf[:], cosF[:], -C1, tmpf[:], ALU.mult, ALU.add), reads=[b_tab, b_tt], writes=[b_tt])
            P.op("dve", STT(tmpf[:], cosF[:], -C2, tmpf[:], ALU.mult, ALU.add), reads=[b_tab, b_tt], writes=[b_tt])
            P.op("dve", TS(tmpf[:], tmpf[:], math.pi, -math.pi, ALU.min, ALU.max), reads=[b_tt], writes=[b_tt])
            P.op("act", ACT(sinF[:], tmpf[:], AF.Sin, scale=pvb(sname)), reads=[b_tt, b_pvT], writes=[b_tab])
            P.op("dve", TS(tmpf[:], tmpf[:], math.pi / 2, None, ALU.add), reads=[b_tt], writes=[b_tt])
            P.op("dve", TS(kff[:], tmpf[:], math.pi, -TWO_PI, ALU.is_gt, ALU.mult), reads=[b_tt], writes=[b_tt])
            P.op("dve", TT(tmpf[:], tmpf[:], kff[:], ALU.add), reads=[b_tt], writes=[b_tt])
            P.op("dve", TS(tmpf[:], tmpf[:], math.pi, -math.pi, ALU.min, ALU.max), reads=[b_tt], writes=[b_tt])
            P.op("act", ACT(cosF[:], tmpf[:], AF.Sin), reads=[b_tt], writes=[b_tab])

        def rstd_from(bank, rows, n, r):
            if _os.environ.get("MK_LN") != "1":
                P.op("act", ACT(rsb[r][rows, :], ps[rows, bank, :], AF.Sqrt, bias=epsc[rows, :], scale=1.0 / n),
                     reads=[b_ps[bank], b_const], writes=[b_rsb[r]])
                P.op("dve", RECIP(rsb[r][rows, :], rsb[r][rows, :]), reads=[b_rsb[r]], writes=[b_rsb[r]])
                return
            P.op("act", ACT(rsb[r][rows, :], ps[rows, bank, :], AF.Ln, bias=epsc[rows, :], scale=1.0 / n),
                 reads=[b_ps[bank], b_const], writes=[b_rsb[r]])
            P.op("act", ACT(rsb[r][rows, :], rsb[r][rows, :], AF.Exp, scale=-0.5), reads=[b_rsb[r]], writes=[b_rsb[r]])

        def rope_parts(nrows, gcol, gpcol, sl, out_ap, b_out, after=()):
            rows = slice(0, nrows)
            P.op("dve", STT(fsc[1][rows, :], ps[rows, 5, :], gpcol[rows, :], sinF[rows, sl], ALU.mult, ALU.mult),
                 reads=[b_ps[5], b_pvT, b_tab], writes=[b_fsc[1]])
            P.op("dve", STT(fsc[0][rows, :], ps[rows, 4, :], gcol[rows, :], cosF[rows, sl], ALU.mult, ALU.mult),
                 reads=[b_ps[4], b_pvT, b_tab] + list(after), writes=[b_fsc[0]])
            P.op("dve", TT(out_ap, fsc[0][rows, :], fsc[1][rows, :], ALU.add), reads=[b_fsc[0], b_fsc[1]], writes=[b_out])

        def tiny_rstd(i, which, g, ncomp, n, mult, stat_fn):
            base = 0 if which == "q" else 32
            dst_t, b_dst_t = (rq_t[i], b_rq[i]) if which == "q" else (rk_t[i], b_rk[i])
            c0 = g * 4 * ncomp
            if _os.environ.get("MK_SKIP") == which:
                P.op("pool", MEMSET(dst_t[:, c0:c0 + 4 * ncomp], 0.1), writes=[b_dst_t])
                return
            for j in range(4):
                stat_fn(j, ps[:, 6, base + c0 + j * ncomp:base + c0 + (j + 1) * ncomp])
            ti = 0 if which == "q" else 1
            P.op("dve", TS(t4[ti][:, 0:4 * ncomp], ps[:, 6, base + c0:base + c0 + 4 * ncomp], mult / n, EPS * mult, ALU.mult, ALU.add),
                 reads=[b_ps[6]], writes=[b_t4[ti]])
            P.op("pool", POWH(dst_t[:, c0:c0 + 4 * ncomp], t4[ti][:, 0:4 * ncomp], 4 * ncomp), reads=[b_t4[ti], b_onesf], writes=[b_dst_t])

        def qprep_gen(i, nrows, ncomp, n, gcol, gpcol, dst, b_dst, sl, g):
            rows = slice(0, nrows)
            P.op("act", ACT(sqs[0][rows, :], ps[rows, 4, :], AF.Square), reads=[b_ps[4]], writes=[b_sqs[0]])
            rope_parts(nrows, gcol, gpcol, sl, fsc[0][rows, :], b_fsc[0], after=[b_sqs[0]])
            yield

            def stat(j, out_ap):
                rhs = ones[rows, 0:1] if ncomp == 1 else sel2b[:, 0:2]
                P.op("pe", MM(out_ap, sqs[0][rows, j * 128:(j + 1) * 128], rhs, True, True),
                     reads=[b_sqs[0], b_const, b_onesf], writes=[b_ps[6]])
            tiny_rstd(i, "q", g, ncomp, n, 1.0, stat)
            yield
            c0 = g * 4 * ncomp
            for j in range(4):
                for c in range(ncomp):
                    d = j * ncomp + c
                    col = c0 + j * ncomp + c
                    P.op("dve", TS(dg[d][:], ident[:], rq_t[i][:, col:col + 1], None, ALU.mult),
                         reads=[b_const, b_rq[i]], writes=[b_dg[d]])
            for j in range(4):
                for c in range(ncomp):
                    d = j * ncomp + c
                    lhsT = onesf[:, 0:128] if ncomp == 1 else self_[c][:, 0:128]
                    P.op("pe", MM(ps[0:128, 4, j * 128:(j + 1) * 128], lhsT, dg[d][:], c == 0, c == ncomp - 1),
                         reads=[b_onesf, b_dg[d]], writes=[b_ps[4]], inc=(j == 3 and c == ncomp - 1))
            P.op("dve", TT(dst, fsc[0][rows, :], ps[rows, 4, :], ALU.mult), reads=[b_fsc[0], b_ps[4]], writes=[b_dst])
            yield

        ring_i = [0]
        pend = []

        def step_pend():
            for gen_ in list(pend):
                try:
                    next(gen_)
                except StopIteration:
                    pend.remove(gen_)

        def attention(kT, b_k, qT, b_q, rows, scale, vts, qc, filler, fill_at, sbanks=(0, 1)):
            qsl = slice(qc * GS, (qc + 1) * GS)

            def S(kt):
                it = ring_i[0]
                ring_i[0] += 1
                sb = sbanks[it % len(sbanks)]
                pr = it % 4
                P.op("pe", MM(ps[:, sb, :], kT[rows, kt * 128:(kt + 1) * 128], qT[rows, qsl], True, True),
                     reads=[b_k, b_q], writes=[b_ps[sb]])
                sc_ap, b_sc = scale(kt)
                if _os.environ.get("MK_CONSTSC") == "1":
                    sc_ap = 0.1
                P.op("act", ACT(PT[pr][:], ps[:, sb, :], AF.Exp, scale=sc_ap), reads=[b_ps[sb], b_sc], writes=[b_PT[pr]])
                return pr

            depth = len(sbanks) - 1
            prs = {}
            for k0 in range(depth):
                prs[k0] = S(k0)
            for kt in range(16):
                if kt + depth < 16:
                    prs[kt + depth] = S(kt + depth)
                pr = prs[kt]
                for vi, (vfn, M, bank, b_v) in enumerate(vts):
                    P.op("pe", MM(ps[0:M, bank, :], vfn(kt), PT[pr][:], kt == 0, kt == 15),
                         reads=[b_v, b_PT[pr]], writes=[b_ps[bank]], inc=(kt == 15 and vi == len(vts) - 1))
                if kt in fill_at:
                    filler()
                if kt in (4, 8, 12):
                    step_pend()

        def recip_bcast(src, drow):
            P.op("dve", RECIP(rin[drow:drow + 1, :], src[drow:drow + 1, :]), reads=[b_oraw[0], b_oraw[1]], writes=[b_rin])
            P.op("pe", MM(ps[:, 7, :], onesf[drow:drow + 1, :], rin[drow:drow + 1, :], True, True),
                 reads=[b_onesf, b_rin], writes=[b_ps[7]])

        NOFILL = _os.environ.get("MK_NOFILL") == "1"

        def make_filler(gen):
            def f():
                if gen is not None and not NOFILL:
                    next(gen, None)
            return f

        def drain(gen):
            if gen is not None:
                for _ in gen:
                    pass

        for hh in range(2):
            P.dma("pool", wlat[:, hh * 4:(hh + 1) * 4, :], win_v[:, hh * 4:(hh + 1) * 4, 0:640], writes=[b_w["lat"]])
        P.op("pool", MEMSET(wkr[:], 0.0), writes=[b_w["kr"]])
        P.op("pool", MEMSET(wkrp[:], 0.0), writes=[b_w["kr"]])
        P.dma("pool", wkr[:, :, 64:96], win_v[:, :, 640:672], writes=[b_w["kr"]])
        P.dma("pool", wkrp[:, :, 64:80], win_v[:, :, 656:672], writes=[b_w["kr"]])
        P.dma("pool", wkrp[:, :, 80:96], win_v[:, :, 640:656], writes=[b_w["kr"]])
        P.dma("pool", wuq[:], wuq_d.rearrange("(kc p) n -> p kc n", p=128), writes=[b_w["uq"]])
        P.op("pool", MEMSET(wuqp[:], 0.0), writes=[b_w["uqp"]])
        wuq_v4 = wuq_d.rearrange("(kc p) (h d) -> p kc h d", p=128, d=96)
        for kc in range(3):
            P.dma("pool", wuqp[:, kc, :, 64:80], wuq_v4[:, kc, :, 80:96], writes=[b_w["uqp"]])
            P.dma("pool", wuqp[:, kc, :, 80:96], wuq_v4[:, kc, :, 64:80], writes=[b_w["uqp"]])
        wukv_v4 = wukv_d.rearrange("(kc p) (h d) -> p kc h d", p=128, d=128)
        wukv_s = wukv.rearrange("p kc (h d) -> p kc h d", d=64)
        for kc in range(2):
            P.dma("pool", wukk[:, kc, :, :], wukv_v4[:, kc, :, 0:64], writes=[b_w["ukk"]])
            P.dma("pool", wukv_s[:, kc, :, :], wukv_v4[:, kc, :, 64:128], writes=[b_w["ukv"]])

        build_tables("freqa", "signa")
        P.retire(b_tt)
        b_cq = [[P.buf("cq") for g in range(NG)] for c in range(3)]
        b_ckv = [[P.buf("ckv") for g in range(NG)] for c in range(2)]
        b_vh = P.bufs("vh", 2)
        b_qh = P.bufs("qh", 2)
        b_kh = P.bufs("kh", 2)

        zst = [oraw[0], oraw[1], dn[0]]
        b_zst = [b_oraw[0], b_oraw[1], b_dn[0]]
        for g in range(NG):
            sl = slice(g * GS, (g + 1) * GS)
            for (lc0, nl, dstT, b_dst, nname, nfeat, sbank, r) in ((0, 3, cqT, b_cq, "qnorm", 384.0, 6, 0), (3, 2, ckvT, b_ckv, "kvnorm", 256.0, 7, 1)):
                for l in range(nl):
                    bk = 4 + l % 2
                    for kc in range(KC):
                        P.op("pe", MM(ps[:, bk, :], wlat[:, kc, (lc0 + l) * 128:(lc0 + l + 1) * 128], nT[:, kc, sl], kc == 0, kc == KC - 1),
                             reads=[b_w["lat"], b_n[kc][g]], writes=[b_ps[bk]], inc=(kc == KC - 1))
                    P.op("act", ACT(zst[l][:], ps[:, bk, :], AF.Copy), reads=[b_ps[bk]], writes=[b_zst[l]])
                    P.op("act", ACT(sqs[l % 2][:], ps[:, bk, :], AF.Square), reads=[b_ps[bk]], writes=[b_sqs[l % 2]])
                    P.op("pe", MM(ps[:, sbank, :], ones[:], sqs[l % 2][:], l == 0, l == nl - 1),
                         reads=[b_const, b_sqs[l % 2]], writes=[b_ps[sbank]])
                rstd_from(sbank, slice(0, 128), nfeat, r)
                for l in range(nl):
                    P.op("dve", STT(dstT[:, l, sl], zst[l][:], pvb(nname, l), rsb[r][:], ALU.mult, ALU.mult),
                         reads=[b_zst[l], b_pvT, b_rsb[r]], writes=[b_dst[l][g]])
            for (wt, bk) in ((wkr, 4), (wkrp, 5)):
                for kc in range(KC):
                    P.op("pe", MM(ps[0:96, bk, :], wt[:, kc, :], nT[:, kc, sl], kc == 0, kc == KC - 1),
                         reads=[b_w["kr"], b_n[kc][g]], writes=[b_ps[bk]], inc=(kc == KC - 1))
            R96 = slice(0, 96)
            P.op("act", ACT(sqpe[R96, sl], ps[R96, 4, :], AF.Square), reads=[b_ps[4]], writes=[b_kpe[g]])
            P.op("dve", STT(fsc[0][R96, :], ps[R96, 4, :], pvb("kg")[R96, :], cosF[R96, sl], ALU.mult, ALU.mult),
                 reads=[b_ps[4], b_pvT, b_tab, b_kpe[g]], writes=[b_fsc[0]])
            P.op("dve", STT(fsc[1][R96, :], ps[R96, 5, :], pvb("kgp")[R96, :], sinF[R96, sl], ALU.mult, ALU.mult),
                 reads=[b_ps[5], b_pvT, b_tab], writes=[b_fsc[1]])
            P.op("dve", TT(kpeR[R96, sl], fsc[0][R96, :], fsc[1][R96, :], ALU.add), reads=[b_fsc[0], b_fsc[1]], writes=[b_kpe[g]])

        R96 = slice(0, 96)

        def mla_prep(h):
            i = h % 2
            kindB = (h % 2 == 1)
            for g in range(NG):
                sl = slice(g * GS, (g + 1) * GS)
                Mq = 128 if h < 7 else 96
                Mk = 128 if h < 7 else 64
                for kc in range(3):
                    P.op("pe", MM(ps[0:Mq, 4, :], wuq[:, kc, h * 96:h * 96 + Mq], cqT[:, kc, sl], kc == 0, kc == 2),
                         reads=[b_w["uq"], b_cq[kc][g]], writes=[b_ps[4]], inc=(kc == 2))
                for kc in range(3):
                    P.op("pe", MM(ps[0:Mq, 5, :], wuqp_f[:, kc, h * 96:h * 96 + Mq], cqT[:, kc, sl], kc == 0, kc == 2),
                         reads=[b_w["uqp"], b_cq[kc][g]], writes=[b_ps[5]], inc=(kc == 2))
                yield from qprep_gen(i, 96, 1, 96.0, pvb("qg"), pvb("qgp"), qh[i][R96, sl], b_qh[i], sl, g)
                for kc in range(2):
                    P.op("pe", MM(ps[0:Mk, 5, :], wukk_f[:, kc, h * 64:h * 64 + Mk], ckvT[:, kc, sl], kc == 0, kc == 1),
                         reads=[b_w["ukk"], b_ckv[kc][g]], writes=[b_ps[5]], inc=(kc == 1))
                P.op("act", ACT(sqs[1][0:64, :], ps[0:64, 5, :], AF.Square), reads=[b_ps[5]], writes=[b_sqs[1]])
                P.op("dve", TS(kh[i][0:64, sl], ps[0:64, 5, :], pvb("kg")[0:64, :], None, ALU.mult),
                     reads=[b_ps[5], b_pvT, b_sqs[1]], writes=[b_kh[i]])
                P.op("dve", CP(kh[i][64:96, sl], kpeR[64:96, sl]), reads=[b_kpe[g]], writes=[b_kh[i]])
                yield

                def kstat(j, out_ap, g=g):
                    P.op("pe", MM(out_ap, sqs[1][0:64, j * 128:(j + 1) * 128], ones[0:64, 0:1], True, False),
                         reads=[b_sqs[1], b_const], writes=[b_ps[6]], inc=False)
                    P.op("pe", MM(out_ap, sqpe[64:96, g * GS + j * 128:g * GS + (j + 1) * 128], ones[64:96, 0:1], False, True),
                         reads=[b_kpe[g], b_const], writes=[b_ps[6]])
                tiny_rstd(i, "k", g, 1, 96.0, 96.0, kstat)
                yield
            P.op("pool", MEMSET(vh[i][:], 0.0), writes=[b_vh[i]])
            onescol = 32 if kindB else 64
            c0 = 64 if kindB else 0
            P.op("pool", MEMSET(vh[i][:, :, onescol:onescol + 1], 1.0), writes=[b_vh[i]])
            for t0 in range(0, 16, 8):
                for tt in range(t0, t0 + 8):
                    for kc in range(2):
                        P.op("pe", MM(ps[:, 5, (tt - t0) * 64:(tt - t0 + 1) * 64], ckvT[:, kc, tt * 128:(tt + 1) * 128],
                                      wukv[:, kc, h * 64:(h + 1) * 64], kc == 0, kc == 1),
                             reads=[b_ckv[kc][tt // 4], b_w["ukv"]], writes=[b_ps[5]], inc=(tt == t0 + 7 and kc == 1))
                P.op("dve", CP(vh[i][:, t0:t0 + 8, c0:c0 + 64], ps[:, 5, :].rearrange("p (a b) -> p a b", b=64)),
                     reads=[b_ps[5]], writes=[b_vh[i]])
                yield

        def mla_attn(h, filler):
            i = h % 2
            kindB = (h % 2 == 1)
            if kindB:
                vt = (lambda kt, i=i: vh[i][:, kt, 0:128], 128, 3, b_vh[i])
                drow, ob = 32, 1
                drows = slice(64, 128)
            else:
                vt = (lambda kt, i=i: vh[i][:, kt, 0:128], 128, 2, b_vh[i])
                drow, ob = 64, 0
                drows = slice(0, 64)
            for qc in range(NG):
                qsl = slice(qc * GS, (qc + 1) * GS)
                attention(kh[i], b_kh[i], qh[i], b_qh[i], R96, (lambda kt, i=i: (rk_t[i][:, kt:kt + 1], b_rk[i])), [vt], qc, filler, (1, 3, 5, 7, 9, 11, 13), sbanks=((0, 1, 2, 6) if kindB else (0, 1, 3, 6)))
                if kindB:
                    P.op("dve", CP(oraw[ob][32:33, :], ps[32:33, 3, :]), reads=[b_ps[3]], writes=[b_oraw[ob]])
                    P.op("dve", CP(oraw[ob][64:128, :], ps[64:128, 3, :]), reads=[b_ps[3]], writes=[b_oraw[ob]])
                else:
                    P.op("dve", CP(oraw[ob][0:65, :], ps[0:65, 2, :]), reads=[b_ps[2]], writes=[b_oraw[ob]])
                P.op("dve", RECIP(rin[drow:drow + 1, :], oraw[ob][drow:drow + 1, :]), reads=[b_oraw[ob]], writes=[b_rin])

                def epi(ob=ob, drow=drow, drows=drows, qsl=qsl, qc=qc):
                    P.op("pe", MM(ps[:, 7, :], onesf[drow:drow + 1, :], rin[drow:drow + 1, :], True, True),
                         reads=[b_onesf, b_rin], writes=[b_ps[7]])
                    P.op("dve", TT(oaT[drows, h // 2, qsl], oraw[ob][drows, :], ps[drows, 7, :], ALU.mult),
                         reads=[b_oraw[ob], b_ps[7]], writes=[b_oa[h // 2][qc]])
                    yield
                pend.append(epi())

        wdh = [A.view(M2 + i * 10240, BF16, [KC, 640]) for i in range(2)]
        vb = [A.view(M2 + 26624 + i * 6400, BF16, [16, 200]) for i in range(2)]
        QB0, KB0, VB0 = 672, 1184, 1696
        dstate = {}

        def diff_setup():
            P.retire(*[b for row in b_cq for b in row], *[b for row in b_ckv for b in row], *b_vh, *b_w.values(), *b_kpe)
            nonlocal_b_tt = P.buf("tabtmp2")
            dstate["b_tt"] = nonlocal_b_tt
            yield

        def build_tables2(fname, sname, bt):
            P.dma("sp", posi[:], pos_d.partition_broadcast(128), writes=[bt])
            P.op("dve", CP(tmpf[:], posi[:]), reads=[bt], writes=[bt])
            P.op("dve", TS(tmpf[:], tmpf[:], pvb(fname), None, ALU.mult), reads=[bt, b_pvT], writes=[bt])
            yield
            P.op("dve", TS(cosF[:], tmpf[:], 1.0 / TWO_PI, None, ALU.mult), reads=[bt], writes=[b_tab])
            P.op("dve", CP(kfi[:], cosF[:]), reads=[b_tab], writes=[bt])
            P.op("dve", CP(cosF[:], kfi[:]), reads=[bt], writes=[b_tab])
            yield
            P.op("dve", STT(tmpf[:], cosF[:], -C1, tmpf[:], ALU.mult, ALU.add), reads=[b_tab, bt], writes=[bt])
            P.op("dve", STT(tmpf[:], cosF[:], -C2, tmpf[:], ALU.mult, ALU.add), reads=[b_tab, bt], writes=[bt])
            P.op("dve", TS(tmpf[:], tmpf[:], math.pi, -math.pi, ALU.min, ALU.max), reads=[bt], writes=[bt])
            P.op("act", ACT(sinF[:], tmpf[:], AF.Sin, scale=pvb(sname)), reads=[bt, b_pvT], writes=[b_tab])
            yield
            P.op("dve", TS(tmpf[:], tmpf[:], math.pi / 2, None, ALU.add), reads=[bt], writes=[bt])
            P.op("dve", TS(kff[:], tmpf[:], math.pi, -TWO_PI, ALU.is_gt, ALU.mult), reads=[bt], writes=[bt])
            P.op("dve", TT(tmpf[:], tmpf[:], kff[:], ALU.add), reads=[bt], writes=[bt])
            P.op("dve", TS(tmpf[:], tmpf[:], math.pi, -math.pi, ALU.min, ALU.max), reads=[bt], writes=[bt])
            P.op("act", ACT(cosF[:], tmpf[:], AF.Sin), reads=[bt], writes=[b_tab])
            yield

        def diff_prep(h):
            i = h % 2
            if h == 0:
                P.retire(*[b for row in b_cq for b in row], *[b for row in b_ckv for b in row], *b_vh, *b_w.values(), *b_kpe)
                bt = P.buf("tabtmp2")
                yield from build_tables2("freqb", "signb", bt)
                P.retire(bt)
                dstate["b_wdh"] = P.bufs("wdh", 2)
                dstate["b_vb"] = P.bufs("vb", 2)
                P.op("dve", TT(der[:, 74:75], pvb("lq1"), pvb("lk1"), ALU.mult), reads=[b_pvT], writes=[b_der])
                P.op("dve", TT(der[:, 75:76], pvb("lq2"), pvb("lk2"), ALU.mult), reads=[b_pvT], writes=[b_der])
                yield
                P.op("pe", MM(ps[:, 6, 0:2], onesf[:], der[:, 74:76], True, True), reads=[b_onesf, b_der], writes=[b_ps[6]])
                P.op("act", ACT(der[:, 74:76], ps[:, 6, 0:2], AF.Exp), reads=[b_ps[6]], writes=[b_der])
                P.op("dve", TT(der[:, 72:73], der[:, 75:76], der[:, 74:75], ALU.subtract), reads=[b_der], writes=[b_der])
                P.op("dve", TS(der[:, 72:73], der[:, 72:73], -LAMBDA_INIT, None, ALU.add), reads=[b_der], writes=[b_der])
                P.op("dve", TS(der[:, 73:74], pvb("subln"), 1.0 - LAMBDA_INIT, None, ALU.mult), reads=[b_pvT], writes=[b_der])
            b_wdh, b_vb = dstate["b_wdh"], dstate["b_vb"]

            def diff_load(hh):
                ii = hh % 2
                P.op("pool", MEMSET(wdh[ii][:, :, 128:256], 0.0), writes=[b_wdh[ii]])
                P.op("pool", MEMSET(wdh[ii][:, :, 384:512], 0.0), writes=[b_wdh[ii]])
                for (src0, d0) in ((QB0, 0), (KB0, 256)):
                    s0 = src0 + hh * 128
                    P.dma("pool", wdh[ii][:, :, d0:d0 + 128], win_v[:, :, s0:s0 + 128], writes=[b_wdh[ii]])
                    for base in (0, 64):
                        P.op("pool", CP(wdh[ii][:, :, d0 + 128 + base:d0 + 128 + base + 8], wdh[ii][:, :, d0 + base + 8:d0 + base + 16]),
                             reads=[b_wdh[ii]], writes=[b_wdh[ii]])
                        P.op("pool", CP(wdh[ii][:, :, d0 + 128 + base + 8:d0 + 128 + base + 16], wdh[ii][:, :, d0 + base:d0 + base + 8]),
                             reads=[b_wdh[ii]], writes=[b_wdh[ii]])
                P.dma("pool", wdh[ii][:, :, 512:640], win_v[:, :, VB0 + hh * 128:VB0 + (hh + 1) * 128], writes=[b_wdh[ii]])

            if h == 0:
                diff_load(0)
                diff_load(1)
            elif h + 1 < 4:
                diff_load(h + 1)
            P.op("pool", MEMSET(vb[i][:], 0.0), writes=[b_vb[i]])
            P.op("pool", MEMSET(vb[i][:, :, 64:65], 1.0), writes=[b_vb[i]])
            P.op("pool", MEMSET(vb[i][:, :, 72 + 32:72 + 33], 1.0), writes=[b_vb[i]])
            yield
            for t0 in range(0, 16, 4):
                for tt in range(t0, t0 + 4):
                    for kc in range(KC):
                        P.op("pe", MM(ps[:, 5, (tt - t0) * 128:(tt - t0 + 1) * 128], nT[:, kc, tt * 128:(tt + 1) * 128],
                                      wdh[i][:, kc, 512:640], kc == 0, kc == KC - 1),
                             reads=[b_n[kc][tt // 4], b_wdh[i]], writes=[b_ps[5]], inc=(tt == t0 + 3 and kc == KC - 1))
                pv = ps[:, 5, :].rearrange("p (a b) -> p a b", b=128)
                P.op("dve", CP(vb[i][:, t0:t0 + 4, 0:64], pv[:, :, 0:64]), reads=[b_ps[5]], writes=[b_vb[i]])
                P.op("dve", CP(vb[i][:, t0:t0 + 4, 136:200], pv[:, :, 64:128]), reads=[b_ps[5]], writes=[b_vb[i]])
                yield
            for g in range(NG):
                sl = slice(g * GS, (g + 1) * GS)
                for (d0, gname, gpname, dst, b_dst) in ((0, "dqg", "dqgp", qh[i], b_qh[i]), (256, "dkg", "dkgp", kh[i], b_kh[i])):
                    for kc in range(KC):
                        P.op("pe", MM(ps[:, 4, :], wdh[i][:, kc, d0:d0 + 128], nT[:, kc, sl], kc == 0, kc == KC - 1),
                             reads=[b_wdh[i], b_n[kc][g]], writes=[b_ps[4]], inc=(kc == KC - 1))
                    for kc in range(KC):
                        P.op("pe", MM(ps[:, 5, :], wdh[i][:, kc, d0 + 128:d0 + 256], nT[:, kc, sl], kc == 0, kc == KC - 1),
                             reads=[b_wdh[i], b_n[kc][g]], writes=[b_ps[5]], inc=(kc == KC - 1))
                    if d0 == 0:
                        yield from qprep_gen(i, 128, 2, 64.0, pvb(gname), pvb(gpname), dst[:, sl], b_dst, sl, g)
                    else:
                        P.op("act", ACT(sqs[1][:], ps[:, 4, :], AF.Square), reads=[b_ps[4]], writes=[b_sqs[1]])
                        rope_parts(128, pvb(gname), pvb(gpname), sl, dst[:, sl], b_dst, after=[b_sqs[1]])
                        yield

                        def kstat2(j, out_ap):
                            P.op("pe", MM(out_ap, sqs[1][:, j * 128:(j + 1) * 128], sel2b[:, 0:2], True, True),
                                 reads=[b_sqs[1], b_onesf], writes=[b_ps[6]])
                        tiny_rstd(i, "k", g, 2, 64.0, 64.0, kstat2)
                        yield

        def diff_attn(h, filler):
            i = h % 2
            b_vb = dstate["b_vb"]
            RA = slice(0, 128)
            vtA = (lambda kt, i=i: vb[i][:, kt, 0:128], 128, 2, b_vb[i])
            vtB = (lambda kt, i=i: vb[i][:, kt, 72:200], 128, 3, b_vb[i])
            for qc in range(NG):
                qsl = slice(qc * GS, (qc + 1) * GS)
                for comp in range(2):
                    rows = slice(comp * 64, comp * 64 + 64)
                    attention(kh[i], b_kh[i], qh[i], b_qh[i], rows, (lambda kt, i=i, comp=comp: (rk_t[i][:, kt * 2 + comp:kt * 2 + comp + 1], b_rk[i])), [vtA, vtB], qc, filler, (3, 7, 11, 15), sbanks=(0, 1, 6, 7))
                    P.op("dve", CP(oraw[0][0:65, :], ps[0:65, 2, :]), reads=[b_ps[2]], writes=[b_oraw[0]])
                    P.op("dve", CP(oraw[1][64:128, :], ps[64:128, 3, :]), reads=[b_ps[3]], writes=[b_oraw[1]])
                    P.op("dve", RECIP(rin[64:65, :], oraw[0][64:65, :]), reads=[b_oraw[0]], writes=[b_rin])

                    def epi(comp=comp, qsl=qsl, qc=qc):
                        P.op("pe", MM(ps[:, 7, :], onesf[64:65, :], rin[64:65, :], True, True),
                             reads=[b_onesf, b_rin], writes=[b_ps[7]])
                        P.op("dve", TT(dn[comp][0:64, :], oraw[0][0:64, :], ps[0:64, 7, :], ALU.mult),
                             reads=[b_oraw[0], b_ps[7]], writes=[b_dn[comp]])
                        P.op("dve", TT(dn[comp][64:128, :], oraw[1][64:128, :], ps[64:128, 7, :], ALU.mult),
                             reads=[b_oraw[1], b_ps[7]], writes=[b_dn[comp]])
                        if comp == 0:
                            yield
                            return
                        P.op("dve", STT(od_e[:], dn[1][:], der[:, 72:73], dn[0][:], ALU.mult, ALU.add),
                             reads=[b_dn[0], b_dn[1], b_der], writes=[b_ode])
                        P.op("act", ACT(sq_e[:], od_e[:], AF.Square), reads=[b_ode], writes=[b_sqe])
                        yield
                        P.op("pe", MM(ps[:, 7, :], ones[:], sq_e[:], True, True), reads=[b_const, b_sqe], writes=[b_ps[7]])
                        P.op("act", ACT(rs_e[:], ps[:, 7, :], AF.Sqrt, bias=epsc[:], scale=1.0 / 128.0),
                             reads=[b_ps[7], b_const], writes=[b_rse])
                        P.op("dve", RECIP(rs_e[:], rs_e[:]), reads=[b_rse], writes=[b_rse])
                        yield
                        P.op("dve", STT(obT[:, h, qsl], od_e[:], der[:, 73:74], rs_e[:], ALU.mult, ALU.mult),
                             reads=[b_ode, b_der, b_rse], writes=[b_ob[h][qc]])
                        yield
                    pend.append(epi())

        units = [("mla", h) for h in range(8)] + [("diff", h) for h in range(4)]

        def prep_of(u):
            return mla_prep(u[1]) if u[0] == "mla" else diff_prep(u[1])

        NU = int(_os.environ.get("MK_UNITS", "12"))
        if NU < 12:
            for c in range(4):
                for g in range(NG):
                    P.op("pool", MEMSET(oaT[:, c, g * GS:(g + 1) * GS], 0.0), writes=[b_oa[c][g]])
                    P.op("pool", MEMSET(obT[:, c, g * GS:(g + 1) * GS], 0.0), writes=[b_ob[c][g]])
            if NU <= 8:
                dstate["b_wdh"] = P.bufs("wdh", 2)
                dstate["b_vb"] = P.bufs("vb", 2)
        units = units[:NU]
        def warm(n):
            for k in range(n):
                P.op("pe", MM(ps[:, 7, :], ones[:], nT[:, k % KC, 0:GS], True, True),
                     reads=[b_const, b_n[k % KC][0]], writes=[b_ps[7]], inc=(k == n - 1))

        NWARM = int(_os.environ.get("MK_WARM", "0"))
        drain(prep_of(units[0]))
        for ui, u in enumerate(units):
            if NWARM:
                warm(NWARM)
            gen = prep_of(units[ui + 1]) if ui + 1 < len(units) else None
            if NOFILL and _os.environ.get("MK_PREFIRST") == "1":
                drain(gen)
            filler = make_filler(gen)
            if u[0] == "mla":
                mla_attn(u[1], filler)
            else:
                diff_attn(u[1], filler)
            drain(gen)
        while pend:
            step_pend()
        b_wdh, b_vb = dstate["b_wdh"], dstate["b_vb"]

        P.retire(*b_wdh, *b_vb, *b_qh, *b_kh, *b_PT, *b_oraw, *b_dn, b_rin, b_onesf, b_tab, b_sqe, b_ode, b_rse, *b_rq, *b_rk, *b_t4, *b_dg)
        woa = A.view(M2 + 0, BF16, [4, D])
        wob = A.view(M2 + 8192, BF16, [4, D])
        wout = A.view(M2 + 16384, BF16, [KC, D])
        wgt = [A.view(M2 + 32768 + i * 4096, BF16, [KC, 256]) for i in range(2)]
        hst = [A.view(M2 + 40960 + i * 2048, F32, [GS]) for i in range(2)]
        gA = A.view(M2 + 45056, F32, [GS])
        gB = A.view(M2 + 47104, F32, [GS])
        mT = A.view(M2 + 49152, BF16, [KC, T])
        b_woa, b_wob, b_wout = P.buf("woa"), P.buf("wob"), P.buf("wout")
        b_wgt = P.bufs("wgt", 2)
        b_hst = P.bufs("hst", 2)
        b_gA, b_gB = P.buf("gA"), P.buf("gB")
        b_m = [[P.buf("m") for g in range(NG)] for c in range(KC)]
        P.dma("pool", woa[:], woa_d.rearrange("(c p) n -> p c n", p=128), writes=[b_woa])
        P.dma("pool", wob[:], wob_d.rearrange("(c p) n -> p c n", p=128), writes=[b_wob])
        for hh in range(2):
            P.dma("pool", wout[:, hh * 4:(hh + 1) * 4, :], wout_d.rearrange("(c p) n -> p c n", p=128)[:, hh * 4:(hh + 1) * 4, :], writes=[b_wout])
        GA0, GB0 = 2208, 3232
        wad = A.view(M2 + 81920, BF16, [KC, D])
        b_wad = P.buf("wad")
        wada_block(5, wad, b_wad)
        late = {0: (5, 6), 2: (6, 7), 4: (7, 8), 6: (8, None)}
        nit = 0
        def load_wgt(oc_):
            wi_ = oc_ % 2
            P.dma("pool", wgt[wi_][:, :, 0:128], win_v[:, :, GA0 + oc_ * 128:GA0 + (oc_ + 1) * 128], writes=[b_wgt[wi_]])
            P.dma("pool", wgt[wi_][:, :, 128:256], win_v[:, :, GB0 + oc_ * 128:GB0 + (oc_ + 1) * 128], writes=[b_wgt[wi_]])

        load_wgt(0)
        for oc in range(KC):
            wi = oc % 2
            if oc + 1 < KC:
                load_wgt(oc + 1)
            for g in range(NG):
                sl = slice(g * GS, (g + 1) * GS)
                b0 = 4 * (nit % 2)
                nit += 1
                for c in range(4):
                    P.op("pe", MM(ps[:, b0, :], woa[:, c, oc * 128:(oc + 1) * 128], oaT[:, c, sl], c == 0, c == 3),
                         reads=[b_woa, b_oa[c][g]], writes=[b_ps[b0]], inc=(c == 3))
                for c in range(4):
                    P.op("pe", MM(ps[:, b0 + 1, :], wob[:, c, oc * 128:(oc + 1) * 128], obT[:, c, sl], c == 0, c == 3),
                         reads=[b_wob, b_ob[c][g]], writes=[b_ps[b0 + 1]], inc=(c == 3))
                for kc in range(KC):
                    P.op("pe", MM(ps[:, b0 + 2, :], wgt[wi][:, kc, 0:128], nT[:, kc, sl], kc == 0, kc == KC - 1),
                         reads=[b_wgt[wi], b_n[kc][g]], writes=[b_ps[b0 + 2]], inc=(kc == KC - 1))
                for kc in range(KC):
                    P.op("pe", MM(ps[:, b0 + 3, :], wgt[wi][:, kc, 128:256], nT[:, kc, sl], kc == 0, kc == KC - 1),
                         reads=[b_wgt[wi], b_n[kc][g]], writes=[b_ps[b0 + 3]], inc=(kc == KC - 1))
                P.op("act", ACT(gA[:], ps[:, b0 + 2, :], AF.Sigmoid), reads=[b_ps[b0 + 2]], writes=[b_gA])
                P.op("act", ACT(gB[:], ps[:, b0 + 3, :], AF.Sigmoid), reads=[b_ps[b0 + 3]], writes=[b_gB])
                P.op("dve", TT(gA[:], gA[:], ps[:, b0, :], ALU.mult), reads=[b_gA, b_ps[b0]], writes=[b_gA])
                P.op("dve", TT(gB[:], gB[:], ps[:, b0 + 1, :], ALU.mult), reads=[b_gB, b_ps[b0 + 1]], writes=[b_gB])
                P.op("dve", TT(mT[:, oc, sl], gA[:], gB[:], ALU.add), reads=[b_gA, b_gB], writes=[b_m[oc][g]])
            if oc in late:
                mcur, mnext = late[oc]
                mod_block(mcur, wad, b_wad)
                if mnext is not None:
                    wada_block(mnext, wad, b_wad)
        derive_G("G2", 5, 1.0)
        derive_A("A3", "g3", 7, 6, "SH3")
        derive_G("G3", 8, 0.5)
        P.retire(*[b for row in b_oa for b in row], *[b for row in b_ob for b in row])
        b_h = [[P.buf("h%d_%d" % (c, g)) for g in range(NG)] for c in range(KC)]
        nit = 0
        for g in range(NG):
            sl = slice(g * GS, (g + 1) * GS)
            for oc in range(KC):
                bk = nit % 2
                hi = nit % 2
                nit += 1
                P.dma("sp", hst[hi][:], hsp[:, oc, sl], reads=[b_hsp], writes=[b_hst[hi]], owner=b_hst[hi])
                for c in range(KC):
                    P.op("pe", MM(ps[:, bk, :], wout[:, c, oc * 128:(oc + 1) * 128], mT[:, c, sl], c == 0, c == KC - 1),
                         reads=[b_wout, b_m[c][g]], writes=[b_ps[bk]], inc=(c == KC - 1))
                P.op("dve", STT(hT[:, oc, sl], ps[:, bk, :], dcol("G2", oc), hst[hi][:], ALU.mult, ALU.add),
                     reads=[b_ps[bk], b_der, b_hst[hi]], writes=[b_h[oc][g]])
        P.retire(*[b for row in b_m for b in row], b_woa, b_wob, b_wout, *b_wgt, *b_hst, b_gA, b_gB, b_wad)
        P.retire(*[b for row in b_n for b in row])
        b_n = [[P.buf("n%d_%d" % (c, g)) for g in range(NG)] for c in range(KC)]
        ring["b_wd"] = P.bufs("wdp", 2)
        ring["b_gu"] = P.bufs("gu", 2)

    if "ffn2" in stages:
        for g in range(NG):
            norm_group(g, "A3", "SH3")
        ffn(f2g_d, f2u_d, f2d_d, "G3", ring, {})

    ost = [A.view(AT_OFF + i * 4096, F32, [D]) for i in range(2)]
    b_ost = P.bufs("ost", 2)
    ytmp = [A.view(AT_OFF + 8192 + i * 2048, F32, [GS]) for i in range(8)]
    b_ytmp = P.bufs("ytmp", 8)
    b_out = P.buf("out")
    for g in range(NG):
        r = norm_group(g, None, None)
        sl = slice(g * GS, (g + 1) * GS)
        for c in range(KC):
            P.op("dve", STT(ytmp[c][:], hT[:, c, sl], pvb("gf", c), rsb[r][:], ALU.mult, ALU.mult),
                 reads=[b_h[c][g], b_pvT, b_rsb[r]], writes=[b_ytmp[c]])
        for j in range(4):
            tt = g * 4 + j
            o = tt % 2
            for half in range(2):
                bk = (tt * 2 + half) % 4
                for cc in range(4):
                    c = half * 4 + cc
                    P.op("pe", TR(ps[:, bk, cc * 128:(cc + 1) * 128], ytmp[c][:, j * 128:(j + 1) * 128], ident[:]),
                         reads=[b_ytmp[c], b_const], writes=[b_ps[bk]], inc=(cc == 3))
                P.op("act", ACT(ost[o][:, half * 512:(half + 1) * 512], ps[:, bk, :], AF.Copy),
                     reads=[b_ps[bk]], writes=[b_ost[o]])
            P.dma("sp", out_d[tt * 128:(tt + 1) * 128, :], ost[o][:], reads=[b_ost[o]], writes=[b_out], owner=b_ost[o])
    P.wait_all("sp", b_ost + [b_out])
    P.emit()
    return nc


def _prep_inputs(inp):
    L = 0
    f32 = np.float32

    def pad128(v):
        o = np.zeros(128, f32)
        o[:v.shape[0]] = v
        return o

    def mla_perm(g):
        o = np.zeros(128, f32)
        o[64:80] = g[80:96]
        o[80:96] = g[64:80]
        return o

    def diff_rep(g):
        return np.concatenate([g, g]).astype(f32)

    def diff_perm(g):
        o = np.zeros(128, f32)
        for base in (0, 64):
            o[base:base + 8] = g[8:16]
            o[base + 8:base + 16] = g[0:8]
        return o

    freqa = np.zeros(128, f32)
    signa = np.ones(128, f32)
    fa = (1.0 / (np.float32(10000.0) ** (np.arange(16, dtype=f32) / np.float32(16)))).astype(f32)
    freqa[64:80] = fa
    freqa[80:96] = fa
    signa[64:80] = -1.0
    freqb = np.zeros(128, f32)
    signb = np.ones(128, f32)
    fb = (1.0 / (np.float32(500000.0) ** (np.arange(8, dtype=f32) / np.float32(8)))).astype(f32)
    for base in (0, 64):
        freqb[base:base + 8] = fb
        freqb[base + 8:base + 16] = fb
        signb[base:base + 8] = -1.0

    rows = []
    for nm in ("ffn1_norm", "mix_norm", "ffn2_norm", "final_norm"):
        rows.append(np.asarray(inp[nm][L], f32).reshape(8, 128))
    rows.append(np.asarray(inp["mla_q_norm"][L], f32).reshape(3, 128))
    rows.append(np.asarray(inp["mla_kv_norm"][L], f32).reshape(2, 128))
    qg = np.asarray(inp["mla_q_gain"][L], f32)
    kg = np.asarray(inp["mla_k_gain"][L], f32)
    dqg = np.asarray(inp["diff_q_gain"][L], f32)
    dkg = np.asarray(inp["diff_k_gain"][L], f32)
    singles = [pad128(qg), mla_perm(qg), pad128(kg), mla_perm(kg),
               diff_rep(dqg), diff_perm(dqg), diff_rep(dkg), diff_perm(dkg),
               pad128(np.asarray(inp["diff_lambda_q1"][L], f32)), pad128(np.asarray(inp["diff_lambda_k1"][L], f32)),
               pad128(np.asarray(inp["diff_lambda_q2"][L], f32)), pad128(np.asarray(inp["diff_lambda_k2"][L], f32)),
               np.asarray(inp["diff_subln"][L], f32), freqa, signa, freqb, signb]
    rows.append(np.stack(singles))
    pvb = np.ascontiguousarray(np.concatenate(rows, axis=0), dtype=f32)
    assert pvb.shape == (NPVB, 128), pvb.shape
    shared = {
        "pvb": pvb,
        "ident": np.eye(128, dtype=f32),
        "w_ada": np.ascontiguousarray(inp["w_ada"][L], f32),
        "f1g": np.ascontiguousarray(inp["ffn1_w_gate"][L], f32),
        "f1u": np.ascontiguousarray(inp["ffn1_w_up"][L], f32),
        "f1d": np.ascontiguousarray(inp["ffn1_w_down"][L], f32),
        "f2g": np.ascontiguousarray(inp["ffn2_w_gate"][L], f32),
        "f2u": np.ascontiguousarray(inp["ffn2_w_up"][L], f32),
        "f2d": np.ascontiguousarray(inp["ffn2_w_down"][L], f32),
        "w_in": np.ascontiguousarray(inp["w_in"][L], f32),
        "w_uq": np.ascontiguousarray(inp["mla_w_uq"][L], f32),
        "w_ukv": np.ascontiguousarray(inp["mla_w_ukv"][L], f32),
        "w_oa": np.ascontiguousarray(inp["mla_w_o"][L], f32),
        "w_ob": np.ascontiguousarray(inp["diff_w_o"][L], f32),
        "w_out": np.ascontiguousarray(inp["w_out"][L], f32),
    }
    b_ada = np.asarray(inp["b_ada"][L], f32).reshape(72, 128)
    maps = []
    for b in range(8):
        m = dict(shared)
        m["x"] = np.ascontiguousarray(inp["x"][b], f32)
        m["pos"] = np.ascontiguousarray(np.asarray(inp["positions"][b]).reshape(1, T).astype(np.int32))
        m["pva"] = np.ascontiguousarray(np.concatenate([np.asarray(inp["c"][b], f32).reshape(8, 128), b_ada], axis=0))
        maps.append(m)
    return maps


_NC_CACHE = {}


def kernel(**inputs):
    maps = _prep_inputs(inputs)
    if "nc" not in _NC_CACHE:
        _NC_CACHE["nc"] = build_nc()
    nc = _NC_CACHE["nc"]
    res = run_bass_kernel_spmd(nc, maps, core_ids=list(range(8)))
    out = np.stack([np.asarray(r["out"], np.float32) for r in res.results], axis=0)
    return out
```
